# Optimizing a Trainium2 kernel written in Bass

```python
import math
import jax
import jax.numpy as jnp
from jax import lax
import numpy as np

D_MODEL = 1024
BATCH = 4
SEQ = 4096
DEPTH = 2
DEC_BATCH = 32
DEC_SEQ = 4
PAST_LEN = 8192
PAGE_SIZE = 128

HEAD_DIM = 64
N_EVEN = (DEPTH + 1) // 2
N_ODD = DEPTH // 2
NORM_EPS = 1e-6
ROPE_THETA = 10000.0
RWKV_HEADS = D_MODEL // (2 * HEAD_DIM)
RWKV_DIM = RWKV_HEADS * HEAD_DIM
DECAY_LORA = 64
AAA_LORA = 64
GATE_LORA = 128
RWKV_COLS = 3 * RWKV_DIM + DECAY_LORA + AAA_LORA + GATE_LORA
RWKV_GN_EPS = 64e-5
NSA_HEADS = D_MODEL // (2 * HEAD_DIM)
NSA_KV_HEADS = 2
NSA_DIM = NSA_HEADS * HEAD_DIM
CMP_STRIDE = 16
CMP_LEN = 2 * CMP_STRIDE
SEL_BLOCK = 64
SEL_TOPN = 16
WINDOW = 512
NSA_QBLK = 64
FORCE_BONUS = 100.0
NSA_COLS = NSA_DIM + 6 * NSA_KV_HEADS * HEAD_DIM + 3 * NSA_HEADS
EVEN_COLS = RWKV_COLS + NSA_COLS
MOBA_HEADS = D_MODEL // HEAD_DIM
MOBA_KV_HEADS = 4
MOBA_DIM = MOBA_HEADS * HEAD_DIM
MOBA_BLOCK = 256
MOBA_TOPK = 3
MOBA_QBLK = 32
ODD_COLS = MOBA_DIM + 2 * MOBA_KV_HEADS * HEAD_DIM
D_FF = -(-8 * D_MODEL // (3 * 256)) * 256

kernel_name = 'rwkv7_nsa_moba_hybrid_step'


def rms_norm(x, g):
    xf = x.astype(jnp.float32)
    y = xf * lax.rsqrt(jnp.mean(xf * xf, axis=-1, keepdims=True) + NORM_EPS)
    return (y * g.astype(jnp.float32)).astype(x.dtype)


def rope(x, pos):
    half = HEAD_DIM // 2
    inv = ROPE_THETA ** (-jnp.arange(half, dtype=jnp.float32) / half)
    ang = pos.astype(jnp.float32)[:, None] * inv[None, :]
    cos = jnp.cos(ang)[None, :, None, :]
    sin = jnp.sin(ang)[None, :, None, :]
    xf = x.astype(jnp.float32)
    x1, x2 = xf[..., :half], xf[..., half:]
    return jnp.concatenate([x1 * cos - x2 * sin, x2 * cos + x1 * sin], axis=-1).astype(x.dtype)


def masked_softmax(s, mask):
    s = jnp.where(mask, s.astype(jnp.float32), -jnp.inf)
    m = jnp.max(s, axis=-1, keepdims=True)
    e = jnp.exp(s - jnp.where(jnp.isfinite(m), m, 0.0))
    return e / jnp.maximum(jnp.sum(e, axis=-1, keepdims=True), 1e-30)


def swiglu(x, wg, wu, wd):
    return (jax.nn.silu(x @ wg) * (x @ wu)) @ wd


def even_project(xn, pos, w_in, gate_b):
    B, T, _ = xn.shape
    proj = xn @ w_in
    rw = proj[..., :RWKV_COLS]
    o = RWKV_COLS
    q = proj[..., o:o + NSA_DIM].reshape(B, T, NSA_HEADS, HEAD_DIM)
    o += NSA_DIM
    kv = proj[..., o:o + 6 * NSA_KV_HEADS * HEAD_DIM].reshape(B, T, 3, 2, NSA_KV_HEADS, HEAD_DIM)
    o += 6 * NSA_KV_HEADS * HEAD_DIM
    gates = jax.nn.sigmoid(proj[..., o:] + gate_b).reshape(B, T, NSA_HEADS, 3)
    k = rope(kv[:, :, :, 0].reshape(B, T, 3 * NSA_KV_HEADS, HEAD_DIM), pos).reshape(B, T, 3, NSA_KV_HEADS, HEAD_DIM)
    kv = jnp.stack([k, kv[:, :, :, 1]], axis=3).reshape(B, T, 6, NSA_KV_HEADS, HEAD_DIM)
    return rw, rope(q, pos), kv, gates


def rwkv_mix(rw, shift_prev, wkv0, p):
    mu, w0, w_up, a0, a_up, g_up, k_k, k_a, r_k, ln_g, ln_b = p
    B, T, _ = rw.shape
    f32 = jnp.float32
    prev = jnp.concatenate([shift_prev[:, None].astype(rw.dtype), rw[:, :-1]], axis=1)
    xm = rw + (prev - rw) * mu
    r, k, v, xw, xa, xg = jnp.split(xm, [RWKV_DIM, 2 * RWKV_DIM, 3 * RWKV_DIM, 3 * RWKV_DIM + DECAY_LORA, 3 * RWKV_DIM + DECAY_LORA + AAA_LORA], axis=-1)

    def heads(t):
        return t.astype(f32).reshape(B, T, RWKV_HEADS, HEAD_DIM)

    w = jnp.exp(-math.exp(-0.5) * jax.nn.sigmoid((w0 + jnp.tanh(xw) @ w_up).astype(f32)))
    a = jax.nn.sigmoid((a0 + xa @ a_up).astype(f32))
    g = jax.nn.sigmoid(xg) @ g_up
    kk = heads(k * k_k)
    kk = kk / jnp.maximum(jnp.sqrt(jnp.sum(kk * kk, axis=-1, keepdims=True)), 1e-12)
    k = k.astype(f32) * (1.0 + (a - 1.0) * k_a)
    r, w, k, v, a = heads(r), heads(w), heads(k), heads(v), heads(a)

    def step(S, inp):
        r_t, w_t, k_t, v_t, kk_t, a_t = inp
        sk = jnp.einsum('bhij,bhj->bhi', S, kk_t)
        S = S * w_t[:, :, None, :] - sk[..., None] * (kk_t * a_t)[:, :, None, :] + v_t[..., None] * k_t[:, :, None, :]
        return S, jnp.einsum('bhij,bhj->bhi', S, r_t)

    xs = tuple(jnp.moveaxis(t, 1, 0) for t in (r, w, k, v, kk, a))
    wkv, y = lax.scan(step, wkv0.astype(f32), xs)
    y = jnp.moveaxis(y, 0, 1)
    m = jnp.mean(y, axis=-1, keepdims=True)
    var = jnp.mean(jnp.square(y - m), axis=-1, keepdims=True)
    y = ((y - m) * lax.rsqrt(var + RWKV_GN_EPS)).reshape(B, T, RWKV_DIM) * ln_g + ln_b
    bonus = (jnp.sum(r * k * r_k, axis=-1, keepdims=True) * v).reshape(B, T, RWKV_DIM)
    out = (y + bonus) * g
    return out.astype(rw.dtype), rw[:, -1], wkv.astype(wkv0.dtype)


def nsa_compress(x, w1, pe, w2):
    B, L, G, dh = x.shape
    nch = L // CMP_STRIDE
    ch = x[:, :nch * CMP_STRIDE].reshape(B, nch, CMP_STRIDE, G, dh)
    h_a = jnp.einsum('bncgd,cde->bnge', ch, w1[:CMP_STRIDE])
    h_b = jnp.einsum('bncgd,cde->bnge', ch, w1[CMP_STRIDE:])
    bias = jnp.einsum('cd,cde->e', pe, w1)
    h = jax.nn.gelu(h_a[:, :-1] + h_b[:, 1:] + bias)
    return jnp.einsum('bnge,ef->bngf', h, w2)


def cmp_sel_overlap(nc, nsel):
    i = jnp.arange(nc, dtype=jnp.int32)[:, None]
    j = jnp.arange(nsel, dtype=jnp.int32)[None, :]
    lo = jnp.maximum(i * CMP_STRIDE, j * SEL_BLOCK)
    hi = jnp.minimum(i * CMP_STRIDE + CMP_LEN, (j + 1) * SEL_BLOCK)
    return jnp.clip(hi - lo, 0).astype(jnp.float32) / CMP_LEN


def nsa_prepare(kv4, cw):
    w1, pe, w2 = cw
    B, L = kv4.shape[:2]
    kc = nsa_compress(kv4[:, :, 0], w1[0], pe[0], w2[0]).transpose(0, 2, 1, 3)
    vc = nsa_compress(kv4[:, :, 1], w1[1], pe[1], w2[1]).transpose(0, 2, 1, 3)
    nsel = -(-L // SEL_BLOCK)
    sel = jnp.pad(kv4[:, :, 2:4], ((0, 0), (0, nsel * SEL_BLOCK - L), (0, 0), (0, 0), (0, 0)))
    sel = sel.reshape(B, nsel, SEL_BLOCK, 2, NSA_KV_HEADS, HEAD_DIM).transpose(3, 0, 4, 1, 2, 5)
    return kc, vc, sel[0], sel[1]


def nsa_block(q, q_pos, kc, vc, ks, vs, kw, vw, w_pos, gates):
    B, T = q.shape[:2]
    G, R = NSA_KV_HEADS, NSA_HEADS // NSA_KV_HEADS
    nc, nsel = kc.shape[2], ks.shape[2]
    qg = (q * HEAD_DIM ** -0.5).reshape(B, T, G, R, HEAD_DIM).transpose(0, 2, 3, 1, 4)
    t = q_pos[:, None]
    c_end = jnp.arange(nc, dtype=jnp.int32) * CMP_STRIDE + (CMP_LEN - 1)
    p_c = masked_softmax(jnp.einsum('bgrtd,bgnd->bgrtn', qg, kc), c_end[None, :] <= t)
    o_c = jnp.einsum('bgrtn,bgnd->bgrtd', p_c.astype(vc.dtype), vc)
    imp = jnp.einsum('bgrtn,nj->bgtj', p_c, cmp_sel_overlap(nc, nsel))
    blk = jnp.arange(nsel, dtype=jnp.int32)[None, :]
    cur = t // SEL_BLOCK
    forced = ((blk == cur) | (blk == cur - 1) | (blk == 0)).astype(jnp.float32)
    score = jnp.where(blk <= cur, imp + FORCE_BONUS * forced, -jnp.inf)
    n_top = min(SEL_TOPN, nsel)
    top_s, idx = lax.top_k(score, n_top)
    b_ix = jnp.arange(B)[:, None, None, None]
    g_ix = jnp.arange(G)[None, :, None, None]
    kg = ks[b_ix, g_ix, idx].reshape(B, G, T, n_top * SEL_BLOCK, HEAD_DIM)
    vg = vs[b_ix, g_ix, idx].reshape(B, G, T, n_top * SEL_BLOCK, HEAD_DIM)
    kpos = (idx[..., None] * SEL_BLOCK + jnp.arange(SEL_BLOCK, dtype=jnp.int32)).reshape(B, G, T, n_top * SEL_BLOCK)
    m_sel = jnp.repeat(jnp.isfinite(top_s), SEL_BLOCK, axis=-1) & (kpos <= t)
    p_s = masked_softmax(jnp.einsum('bgrtd,bgtsd->bgrts', qg, kg), m_sel[:, :, None])
    o_s = jnp.einsum('bgrts,bgtsd->bgrtd', p_s.astype(vg.dtype), vg)
    wp = w_pos[None, :]
    m_w = (wp <= t) & (wp > t - WINDOW) & (wp >= 0)
    p_w = masked_softmax(jnp.einsum('bgrtd,bgwd->bgrtw', qg, kw), m_w)
    o_w = jnp.einsum('bgrtw,bgwd->bgrtd', p_w.astype(vw.dtype), vw)
    gt = gates.reshape(B, T, G, R, 3).transpose(0, 2, 3, 1, 4).astype(jnp.float32)
    o = gt[..., 0:1] * o_c + gt[..., 1:2] * o_s + gt[..., 2:3] * o_w
    return o.transpose(0, 3, 1, 2, 4).reshape(B, T, NSA_DIM).astype(q.dtype)


def nsa_prompt(q, kv, gates, cw):
    B, T = q.shape[:2]
    kc, vc, ks, vs = nsa_prepare(kv[:, :, :4], cw)
    win = jnp.pad(kv[:, :, 4:6], ((0, 0), (WINDOW, 0), (0, 0), (0, 0), (0, 0))).transpose(2, 0, 3, 1, 4)

    def block(bi):
        q0 = bi * NSA_QBLK
        kw = lax.dynamic_slice_in_dim(win, q0, WINDOW + NSA_QBLK, axis=3)
        return nsa_block(lax.dynamic_slice_in_dim(q, q0, NSA_QBLK, axis=1),
                         q0 + jnp.arange(NSA_QBLK, dtype=jnp.int32), kc, vc, ks, vs, kw[0], kw[1],
                         q0 - WINDOW + jnp.arange(WINDOW + NSA_QBLK, dtype=jnp.int32),
                         lax.dynamic_slice_in_dim(gates, q0, NSA_QBLK, axis=1))

    out = lax.map(block, jnp.arange(T // NSA_QBLK, dtype=jnp.int32))
    return out.transpose(1, 0, 2, 3).reshape(B, T, NSA_DIM)


def nsa_sample(q, kv, gates, pos, cache, page_table, win_buf, past_len, cw):
    Bs, T = q.shape[:2]
    past = cache[page_table].reshape(Bs, past_len, 4, NSA_KV_HEADS, HEAD_DIM).astype(kv.dtype)
    kc, vc, ks, vs = nsa_prepare(jnp.concatenate([past, kv[:, :, :4]], axis=1), cw)
    nw = win_buf.shape[1]
    win = jnp.concatenate([win_buf.astype(kv.dtype), kv[:, :, 4:6]], axis=1)
    wt = win.transpose(2, 0, 3, 1, 4)
    wpos = past_len - nw + jnp.arange(nw + T, dtype=jnp.int32)
    out = nsa_block(q, pos, kc, vc, ks, vs, wt[0], wt[1], wpos, gates)
    return out, win[:, T:]


def odd_project(xn, pos, w_in):
    B, T, _ = xn.shape
    proj = xn @ w_in
    q = rope(proj[..., :MOBA_DIM].reshape(B, T, MOBA_HEADS, HEAD_DIM), pos)
    kv = proj[..., MOBA_DIM:].reshape(B, T, 2, MOBA_KV_HEADS, HEAD_DIM)
    kv = jnp.stack([rope(kv[:, :, 0], pos), kv[:, :, 1]], axis=2)
    return q, kv


def moba_prepare(kv):
    B, L = kv.shape[:2]
    nb = -(-L // MOBA_BLOCK)
    kvp = jnp.pad(kv, ((0, 0), (0, nb * MOBA_BLOCK - L), (0, 0), (0, 0), (0, 0)))
    kvp = kvp.reshape(B, nb, MOBA_BLOCK, 2, MOBA_KV_HEADS, HEAD_DIM).transpose(3, 0, 4, 1, 2, 5)
    means = jnp.mean(kvp[0].astype(jnp.float32), axis=3)
    return kvp[0], kvp[1], means


def moba_block(q, q_pos, kb, vb, means):
    B, T = q.shape[:2]
    G, R = MOBA_KV_HEADS, MOBA_HEADS // MOBA_KV_HEADS
    nb = kb.shape[2]
    qg = (q * HEAD_DIM ** -0.5).reshape(B, T, G, R, HEAD_DIM).transpose(0, 2, 3, 1, 4)
    t = q_pos[:, None]
    cur = t // MOBA_BLOCK
    gate = jnp.einsum('bgrtd,bgnd->bgrtn', qg.astype(jnp.float32), means)
    gate = jnp.where(jnp.arange(nb, dtype=jnp.int32)[None, :] < cur, gate, -jnp.inf)
    n_top = min(MOBA_TOPK, nb)
    top_s, idx = lax.top_k(gate, n_top)
    idx = jnp.concatenate([idx, jnp.broadcast_to(cur[None, None, None], (B, G, R, T, 1)).astype(idx.dtype)], axis=-1)
    ok = jnp.concatenate([jnp.isfinite(top_s), jnp.ones((B, G, R, T, 1), dtype=bool)], axis=-1)
    n_slot = n_top + 1
    b_ix = jnp.arange(B)[:, None, None, None]
    g_ix = jnp.arange(G)[None, :, None, None]
    s = jnp.concatenate([jnp.einsum('bgrtd,bgrtkd->bgrtk', qg, kb[b_ix, g_ix, idx[..., j]]) for j in range(n_slot)], axis=-1)
    kpos = (idx[..., None] * MOBA_BLOCK + jnp.arange(MOBA_BLOCK, dtype=jnp.int32)).reshape(B, G, R, T, n_slot * MOBA_BLOCK)
    mask = jnp.repeat(ok, MOBA_BLOCK, axis=-1) & (kpos <= t)
    p = masked_softmax(s, mask).reshape(B, G, R, T, n_slot, MOBA_BLOCK)
    o = jnp.einsum('bgrtk,bgrtkd->bgrtd', p[..., 0, :].astype(vb.dtype), vb[b_ix, g_ix, idx[..., 0]])
    for j in range(1, n_slot):
        o = o + jnp.einsum('bgrtk,bgrtkd->bgrtd', p[..., j, :].astype(vb.dtype), vb[b_ix, g_ix, idx[..., j]])
    return o.transpose(0, 3, 1, 2, 4).reshape(B, T, MOBA_DIM).astype(q.dtype)


def moba_prompt(q, kv):
    B, T = q.shape[:2]
    kb, vb, means = moba_prepare(kv)

    def block(bi):
        q0 = bi * MOBA_QBLK
        return moba_block(lax.dynamic_slice_in_dim(q, q0, MOBA_QBLK, axis=1),
                          q0 + jnp.arange(MOBA_QBLK, dtype=jnp.int32), kb, vb, means)

    out = lax.map(block, jnp.arange(T // MOBA_QBLK, dtype=jnp.int32))
    return out.transpose(1, 0, 2, 3).reshape(B, T, MOBA_DIM)


def moba_sample(q, kv, pos, cache, page_table, past_len):
    Bs = q.shape[0]
    past = cache[page_table].reshape(Bs, past_len, 2, MOBA_KV_HEADS, HEAD_DIM).astype(kv.dtype)
    kb, vb, means = moba_prepare(jnp.concatenate([past, kv], axis=1))
    return moba_block(q, pos, kb, vb, means)


def setup_inputs(seed: int = 0) -> dict:
    key = jax.random.key(seed)
    keys = jax.random.split(key, 40)

    def nrm(i, shape, scale):
        return jax.random.normal(keys[i], shape, jnp.float32) * scale

    n_pages = PAST_LEN // PAGE_SIZE
    n_used = DEC_BATCH * n_pages
    n_phys = n_used + n_used // 4
    win_buf = min(WINDOW, PAST_LEN)
    page_table = jax.random.permutation(keys[0], n_phys)[:n_used].reshape(DEC_BATCH, n_pages).astype(jnp.int32)
    return {
        'x_prompt': nrm(1, (BATCH, SEQ, D_MODEL), 1.0),
        'x_sample': nrm(2, (DEC_BATCH, DEC_SEQ, D_MODEL), 1.0),
        'cache_nsa_kv': nrm(3, (N_EVEN, n_phys, PAGE_SIZE, 4, NSA_KV_HEADS, HEAD_DIM), 1.0),
        'cache_moba_kv': nrm(4, (N_ODD, n_phys, PAGE_SIZE, 2, MOBA_KV_HEADS, HEAD_DIM), 1.0),
        'state_win_kv': nrm(5, (N_EVEN, DEC_BATCH, win_buf, 2, NSA_KV_HEADS, HEAD_DIM), 1.0),
        'state_wkv': nrm(6, (N_EVEN, DEC_BATCH, RWKV_HEADS, HEAD_DIM, HEAD_DIM), 0.5),
        'state_shift': nrm(7, (N_EVEN, DEC_BATCH, RWKV_COLS), 1.0),
        'page_table': page_table,
        'norm_mix': 1.0 + nrm(8, (DEPTH, D_MODEL), 0.1),
        'norm_ffn': 1.0 + nrm(9, (DEPTH, D_MODEL), 0.1),
        'norm_final': 1.0 + nrm(10, (D_MODEL,), 0.1),
        'even_w_in': nrm(11, (N_EVEN, D_MODEL, EVEN_COLS), D_MODEL ** -0.5),
        'even_w_out': nrm(12, (N_EVEN, RWKV_DIM + NSA_DIM, D_MODEL), (RWKV_DIM + NSA_DIM) ** -0.5),
        'rwkv_mu': jax.random.uniform(keys[13], (N_EVEN, RWKV_COLS), jnp.float32),
        'rwkv_w0': nrm(14, (N_EVEN, RWKV_DIM), 0.5),
        'rwkv_w_up': nrm(15, (N_EVEN, DECAY_LORA, RWKV_DIM), 0.1),
        'rwkv_a0': nrm(16, (N_EVEN, RWKV_DIM), 0.5),
        'rwkv_a_up': nrm(17, (N_EVEN, AAA_LORA, RWKV_DIM), 0.1),
        'rwkv_g_up': nrm(18, (N_EVEN, GATE_LORA, RWKV_DIM), GATE_LORA ** -0.5),
        'rwkv_k_k': 0.85 + nrm(19, (N_EVEN, RWKV_DIM), 0.05),
        'rwkv_k_a': 1.0 + nrm(20, (N_EVEN, RWKV_DIM), 0.05),
        'rwkv_r_k': nrm(21, (N_EVEN, RWKV_HEADS, HEAD_DIM), 0.1),
        'rwkv_ln_g': 1.0 + nrm(22, (N_EVEN, RWKV_DIM), 0.1),
        'rwkv_ln_b': nrm(23, (N_EVEN, RWKV_DIM), 0.02),
        'nsa_gate_b': nrm(24, (N_EVEN, 3 * NSA_HEADS), 0.1),
        'nsa_cmp_w1': nrm(25, (N_EVEN, 2, CMP_LEN, HEAD_DIM, HEAD_DIM), (CMP_LEN * HEAD_DIM) ** -0.5),
        'nsa_cmp_pe': nrm(26, (N_EVEN, 2, CMP_LEN, HEAD_DIM), 0.1),
        'nsa_cmp_w2': nrm(27, (N_EVEN, 2, HEAD_DIM, HEAD_DIM), HEAD_DIM ** -0.5),
        'odd_w_in': nrm(28, (N_ODD, D_MODEL, ODD_COLS), D_MODEL ** -0.5),
        'odd_w_out': nrm(29, (N_ODD, MOBA_DIM, D_MODEL), MOBA_DIM ** -0.5),
        'ffn_w_gate': nrm(30, (DEPTH, D_MODEL, D_FF), D_MODEL ** -0.5),
        'ffn_w_up': nrm(31, (DEPTH, D_MODEL, D_FF), D_MODEL ** -0.5),
        'ffn_w_down': nrm(32, (DEPTH, D_FF, D_MODEL), D_FF ** -0.5),
    }


def reference(x_prompt, x_sample, cache_nsa_kv, cache_moba_kv, state_win_kv, state_wkv, state_shift, page_table,
              norm_mix, norm_ffn, norm_final, even_w_in, even_w_out, rwkv_mu, rwkv_w0, rwkv_w_up, rwkv_a0,
              rwkv_a_up, rwkv_g_up, rwkv_k_k, rwkv_k_a, rwkv_r_k, rwkv_ln_g, rwkv_ln_b, nsa_gate_b,
              nsa_cmp_w1, nsa_cmp_pe, nsa_cmp_w2, odd_w_in, odd_w_out, ffn_w_gate, ffn_w_up, ffn_w_down):
    B, T = x_prompt.shape[:2]
    Bs, Ts = x_sample.shape[:2]
    past_len = page_table.shape[1] * cache_nsa_kv.shape[2]
    pos_p = jnp.arange(T, dtype=jnp.int32)
    pos_s = past_len + jnp.arange(Ts, dtype=jnp.int32)
    hp, hs = x_prompt, x_sample
    nsa_p, nsa_s, moba_p, moba_s = [], [], [], []
    win_p, win_s, wkv_p, wkv_s, sh_p, sh_s = [], [], [], [], [], []
    for layer in range(DEPTH):
        i = layer // 2
        xp = rms_norm(hp, norm_mix[layer])
        xs = rms_norm(hs, norm_mix[layer])
        if layer % 2 == 0:
            rp = (rwkv_mu[i], rwkv_w0[i], rwkv_w_up[i], rwkv_a0[i], rwkv_a_up[i], rwkv_g_up[i],
                  rwkv_k_k[i], rwkv_k_a[i], rwkv_r_k[i], rwkv_ln_g[i], rwkv_ln_b[i])
            cw = (nsa_cmp_w1[i], nsa_cmp_pe[i], nsa_cmp_w2[i])
            rw, q, kv, gates = even_project(xp, pos_p, even_w_in[i], nsa_gate_b[i])
            a_out, shift_new, wkv_new = rwkv_mix(rw, jnp.zeros((B, RWKV_COLS), rw.dtype),
                                                 jnp.zeros((B, RWKV_HEADS, HEAD_DIM, HEAD_DIM), rw.dtype), rp)
            b_out = nsa_prompt(q, kv, gates, cw)
            hp = hp + jnp.concatenate([a_out, b_out], axis=-1) @ even_w_out[i]
            nsa_p.append(kv[:, :, :4])
            win_p.append(kv[:, T - min(WINDOW, T):, 4:6])
            wkv_p.append(wkv_new)
            sh_p.append(shift_new)
            rw, q, kv, gates = even_project(xs, pos_s, even_w_in[i], nsa_gate_b[i])
            a_out, shift_new, wkv_new = rwkv_mix(rw, state_shift[i], state_wkv[i], rp)
            b_out, win_new = nsa_sample(q, kv, gates, pos_s, cache_nsa_kv[i], page_table, state_win_kv[i], past_len, cw)
            hs = hs + jnp.concatenate([a_out, b_out], axis=-1) @ even_w_out[i]
            nsa_s.append(kv[:, :, :4])
            win_s.append(win_new)
            wkv_s.append(wkv_new)
            sh_s.append(shift_new)
        else:
            q, kv = odd_project(xp, pos_p, odd_w_in[i])
            hp = hp + moba_prompt(q, kv) @ odd_w_out[i]
            moba_p.append(kv)
            q, kv = odd_project(xs, pos_s, odd_w_in[i])
            hs = hs + moba_sample(q, kv, pos_s, cache_moba_kv[i], page_table, past_len) @ odd_w_out[i]
            moba_s.append(kv)
        hp = hp + swiglu(rms_norm(hp, norm_ffn[layer]), ffn_w_gate[layer], ffn_w_up[layer], ffn_w_down[layer])
        hs = hs + swiglu(rms_norm(hs, norm_ffn[layer]), ffn_w_gate[layer], ffn_w_up[layer], ffn_w_down[layer])
    y_prompt = rms_norm(hp, norm_final)
    y_sample = rms_norm(hs, norm_final)
    return (y_prompt, y_sample, jnp.stack(nsa_p), jnp.stack(nsa_s), jnp.stack(moba_p), jnp.stack(moba_s),
            jnp.stack(win_p), jnp.stack(win_s), jnp.stack(wkv_p), jnp.stack(wkv_s), jnp.stack(sh_p), jnp.stack(sh_s))
```

```python
import numpy as np
import ml_dtypes
from contextlib import ExitStack, contextmanager
import concourse.bass as bass
import concourse.mybir as mybir
from concourse.bass_utils import run_bass_kernel_spmd

F32 = mybir.dt.float32
BF16 = mybir.dt.bfloat16
I32 = mybir.dt.int32
AF = mybir.ActivationFunctionType
ALU = mybir.AluOpType
AX = mybir.AxisListType

ENGS = ['pe', 'act', 'dve', 'pool', 'sp']
NDMA = 12
SAME_SYNC = {'pe': False, 'act': True, 'dve': True, 'pool': True, 'sp': False}
WRITE_KEYS = ('out', 'accum_out', 'out_max', 'out_indices', 'out_ap')


class T:
    def __init__(self, h, name):
        self.h = h
        self.name = name
        self.w = {}
        self.r = {}

    def __getitem__(self, idx):
        return V(self.h[idx], self)


class V:
    def __init__(self, ap, t):
        self.ap = ap
        self.t = t

    def __getitem__(self, idx):
        return V(self.ap[idx], self.t)

    def rearrange(self, s, **kw):
        return V(self.ap.rearrange(s, **kw), self.t)

    def bc(self, shape):
        return V(self.ap.to_broadcast(list(shape)), self.t)

    def pbc(self, n):
        return V(self.ap.partition_broadcast(n), self.t)

    def bitcast(self, dt):
        return V(self.ap.bitcast(dt), self.t)

    def unsq(self, ax):
        return V(self.ap.unsqueeze(ax), self.t)

    @property
    def shape(self):
        return self.ap.shape


def _merge(d, s):
    for k, v in s.items():
        if d.get(k, 0) < v:
            d[k] = v


class EngProxy:
    def __init__(self, kb, name):
        self.kb = kb
        self.name = name

    def __getattr__(self, opname):
        kb = self.kb
        name = self.name

        def call(**kw):
            reads, writes = [], []
            kw2 = {}
            for key, v in kw.items():
                if isinstance(v, V):
                    (writes if key in WRITE_KEYS else reads).append(v.t)
                    kw2[key] = v.ap
                else:
                    kw2[key] = v
            return kb.op(name, lambda e: getattr(e, opname)(**kw2), reads, writes)

        return call


class KB:
    def __init__(self):
        self.nc = bass.Bass("TRN2", target_bir_lowering=False)
        self.es = ExitStack()
        self.cnt = {}
        self.known = {e: {} for e in ENGS}
        self.sem = {}
        nc = self.nc
        self.eng = {'pe': nc.tensor, 'act': nc.scalar, 'dve': nc.vector, 'pool': nc.gpsimd, 'sp': nc.sync}
        for e in ENGS:
            self._mksem('c_' + e)
        for i in range(NDMA):
            self._mksem('d%d' % i)
        for i in range(4):
            self._mksem('g%d' % i)
        self.dma_rr = 0
        self.g_rr = 0
        self.pe = EngProxy(self, 'pe')
        self.act = EngProxy(self, 'act')
        self.dve = EngProxy(self, 'dve')
        self.pool = EngProxy(self, 'pool')
        self.stack = [self.es]
        self.ninst = 0
        self.uid = 0

    def _mksem(self, name):
        self.sem[name] = self.es.enter_context(self.nc.semaphore(name))
        self.cnt[name] = 0

    def dram(self, name, shape, dt, kind="Internal"):
        h = self.nc.dram_tensor(name, list(shape), dt, kind=kind)
        return T(h.ap(), name)

    def sb(self, name, shape, dt=F32):
        self.uid += 1
        h = self.stack[-1].enter_context(self.nc.sbuf_tensor("%s_%d" % (name, self.uid), list(shape), dt))
        return T(h, name)

    def ps(self, name, shape, dt=F32):
        self.uid += 1
        h = self.stack[-1].enter_context(self.nc.psum_tensor("%s_%d" % (name, self.uid), list(shape), dt))
        return T(h, name)

    @contextmanager
    def scope(self):
        es = ExitStack()
        self.stack.append(es)
        try:
            yield
        finally:
            self.barrier()
            self.stack.pop()
            es.close()

    def _deps(self, eng, reads, writes, own):
        deps = {}
        for t in reads:
            _merge(deps, t.w)
        for t in writes:
            _merge(deps, t.w)
            _merge(deps, t.r)
        kn = self.known[eng]
        e = self.eng[eng]
        for s, v in deps.items():
            if s == own and not SAME_SYNC[eng]:
                continue
            if kn.get(s, 0) >= v:
                continue
            e.wait_ge(self.sem[s], v)
            self.ninst += 1
            kn[s] = v

    def _mark(self, s, val, reads, writes):
        for t in reads:
            if t.r.get(s, 0) < val:
                t.r[s] = val
        for t in writes:
            if t.w.get(s, 0) < val:
                t.w[s] = val

    def op(self, eng, fn, reads=(), writes=()):
        own = 'c_' + eng
        self._deps(eng, reads, writes, own)
        self.cnt[own] += 1
        val = self.cnt[own]
        fn(self.eng[eng]).then_inc(self.sem[own], 1)
        self.ninst += 1
        self._mark(own, val, reads, writes)

    def dma(self, out, in_, q='sp', **kw):
        reads, writes = [in_.t], [out.t]
        s = 'd%d' % self.dma_rr
        self.dma_rr = (self.dma_rr + 1) % NDMA
        self._deps(q, reads, writes, None)
        self.cnt[s] += 16
        val = self.cnt[s]
        self.eng[q].dma_start(out=out.ap, in_=in_.ap, **kw).then_inc(self.sem[s], 16)
        self.ninst += 1
        self._mark(s, val, reads, writes)

    def raw16(self, q, fn, reads=(), writes=()):
        s = 'g%d' % self.g_rr
        self.g_rr = (self.g_rr + 1) % 4
        self._deps(q, reads, writes, None)
        self.cnt[s] += 16
        val = self.cnt[s]
        fn(self.eng[q]).then_inc(self.sem[s], 16)
        self.ninst += 1
        self._mark(s, val, reads, writes)

    def barrier(self, engs=ENGS):
        for e in engs:
            kn = self.known[e]
            for s, v in self.cnt.items():
                if v > 0 and kn.get(s, 0) < v and s != 'c_' + e:
                    self.eng[e].wait_ge(self.sem[s], v)
                    kn[s] = v

    def dbg(self, name, v, shape, dt=F32):
        if not getattr(self, 'dbg_on', False):
            return
        o = self.dram('dbg_' + name, shape, dt, "ExternalOutput")
        idx = tuple(slice(0, n) for n in shape)
        self.dma(o[idx], v)

    def finish(self):
        self.barrier(['sp'])
        self.es.close()
        return self.nc


RWC = 1792
EVC = 3096
ODC = 1536
DFF = 2816
NEG = -240000.0


def build(cfg):
    Tn, P, NPH = cfg['T'], cfg['P'], cfg['NPHYS']
    stages = cfg.get('stages', 'A')
    NT = Tn // 128
    LP = P * 128
    TR = Tn + 128
    k = KB()
    k.dbg_on = cfg.get('dbg', False)
    nc = k.nc
    IN = lambda n, s, d=F32: k.dram(n, s, d, "ExternalInput")
    OUT = lambda n, s, d=F32: k.dram(n, s, d, "ExternalOutput")
    xp = IN('xp', [Tn, 1024]); xs = IN('xs', [16, 1024])
    st_win = IN('st_win', [4, 512, 256]); st_wkv = IN('st_wkv', [4, 8, 64, 64]); st_shift = IN('st_shift', [4, RWC])
    ptab = IN('ptab', [4, P], I32)
    norm_mix = IN('norm_mix', [2, 1024]); norm_ffn = IN('norm_ffn', [2, 1024]); norm_final = IN('norm_final', [1, 1024])
    w_in0 = IN('w_in0', [1024, EVC]); w_out0 = IN('w_out0', [1024, 1024])
    gate_b = IN('gate_b', [1, 24])
    w_in1 = IN('w_in1', [1024, ODC]); w_out1 = IN('w_out1', [1024, 1024])
    ffn_g = [IN('ffn_g%d' % i, [1024, DFF]) for i in range(2)]; ffn_u = [IN('ffn_u%d' % i, [1024, DFF]) for i in range(2)]
    ffn_d = [IN('ffn_d%d' % i, [DFF, 1024]) for i in range(2)]
    NBLK = Tn // 256; NBLKP = max(8, NBLK)
    GBM = IN('GBM', [Tn, 2, NBLKP]); EEXP2 = IN('EEXP2', [64, max(Tn, LP)], BF16)
    cache_nsa = IN('cache_nsa', [NPH * 128, 512]); cache_moba = IN('cache_moba', [NPH * 128, 512])
    NSELS = LP // 64 + 1; NBS = LP // 16 - 1; NBTS = (NBS + 127) // 128
    NBLKS = LP // 256; NBLKSP = max(8, NBLKS)
    FBs = IN('FBs', [4, NSELS]); OVs = IN('OVs', [128, NBTS, NSELS], BF16)
    SMK = IN('SMK', [128, 2, 16], BF16)
    IOTA = IN('IOTA', [128, 1])
    rw_mu = IN('rw_mu', [1, RWC]); rw_vec = IN('rw_vec', [7, 512])
    rw_wup = IN('rw_wup', [64, 512]); rw_aup = IN('rw_aup', [64, 512]); rw_gup = IN('rw_gup', [128, 512])
    masks64 = IN('masks64', [64, 3, 64])
    NSEL = Tn // 64; NB = Tn // 16 - 1; NBT = (NB + 127) // 128
    cmp_w1 = IN('cmp_w1', [2, 32, 64, 64]); cmp_pe = IN('cmp_pe', [2, 32, 64]); cmp_w2 = IN('cmp_w2', [2, 64, 64])
    FBt = IN('FBt', [Tn, NSEL]); OVt = IN('OVt', [128, NBT, NSEL], BF16); CMt = IN('CMt', [128, 17, 128], BF16)
    TRIt = IN('TRIt', [128, 2, 128], BF16); EEXP = IN('EEXP', [128, max(Tn, LP)], BF16)
    ropecs = IN('ropecs', [TR, 64])
    identb_d = IN('identb', [128, 128], BF16); identf_d = IN('identf', [128, 128])
    y_p = OUT('y_p', [Tn, 1024]); y_s = OUT('y_s', [16, 1024])
    o_nsa_p = OUT('o_nsa_p', [Tn, 512]); o_nsa_s = OUT('o_nsa_s', [16, 512])
    o_moba_p = OUT('o_moba_p', [Tn, 512]); o_moba_s = OUT('o_moba_s', [16, 512])
    WN = min(512, Tn)
    o_win_p = OUT('o_win_p', [WN, 256]); o_win_s = OUT('o_win_s', [4, 512, 256])
    o_wkv_p = OUT('o_wkv_p', [8, 64, 64]); o_wkv_s = OUT('o_wkv_s', [4, 8, 64, 64])
    o_sh_p = OUT('o_sh_p', [1, RWC]); o_sh_s = OUT('o_sh_s', [4, RWC])
    RW = k.dram('RW', [TR, RWC], F32)
    QT = k.dram('QT', [64, 8, TR], BF16)
    KT = k.dram('KT', [64, 6, TR], BF16)
    VcT = k.dram('VcT', [64, 2, TR], BF16)
    VT = k.dram('VT', [TR, 2, 2, 65], BF16)
    GT = k.dram('GT', [TR, 24], F32)
    AO = k.dram('AO', [TR, 1024], BF16, "ExternalOutput" if cfg.get('dbg_ao') else "Internal")
    H1 = k.dram('H1', [TR, 1024], F32, "ExternalOutput" if cfg.get('dbg_ao') else "Internal")
    ACTT = k.dram('ACTT', [22, 128, TR], BF16)
    QT2 = k.dram('QT2', [64, 16, TR], BF16); KT2 = k.dram('KT2', [64, 4, TR], BF16); VT2 = k.dram('VT2', [TR, 4, 65], BF16)
    NS2 = k.dram('NS2', [Tn, 16, NBLKP], BF16)
    QF2 = k.dram('QF2', [64, 16, 16], F32)
    MO = k.dram('MO', [TR, 1024], BF16, "ExternalOutput" if cfg.get('dbg_ao') else "Internal")

    identb = k.sb('identb', [128, 128], BF16); identf = k.sb('identf', [128, 128], F32)
    k.dma(identb[:, :], identb_d[:, :]); k.dma(identf[:, :], identf_d[:, :])

    tiles = [(i * 128, 128) for i in range(NT)] + [(Tn, 16)]

    def rmsnorm_T(xt, rows, gbc, xn, pT, xnT, ss, junk):
        k.act.activation(out=junk[:rows, :], in_=xt[:rows, :], func=AF.Square, accum_out=ss[:rows, 0:1])
        k.dve.tensor_scalar(out=ss[:rows, 1:2], in0=ss[:rows, 0:1], scalar1=1.0 / 1024, scalar2=1e-6, op0=ALU.mult, op1=ALU.add)
        k.act.activation(out=ss[:rows, 3:4], in_=ss[:rows, 1:2], func=AF.Sqrt)
        k.dve.reciprocal(out=ss[:rows, 2:3], in_=ss[:rows, 3:4])
        k.dve.scalar_tensor_tensor(out=xn[:rows, :], in0=xt[:rows, :], scalar=ss[:rows, 2:3], in1=gbc[:rows, :], op0=ALU.mult, op1=ALU.mult)
        for kk in range(8):
            k.pe.transpose(out=pT[:, kk, :rows], in_=xn[:rows, kk * 128:(kk + 1) * 128], identity=identb[:rows, :rows])
        k.act.copy(out=xnT[:, :, :rows], in_=pT[:, :, :rows])

    def load_w_bf16(Wsb, wd, K, N):
        with k.scope():
            stg = [k.sb('wstg', [128, N], F32) for _ in range(2)]
            for kk in range(K // 128):
                s = stg[kk % 2]
                k.dma(s[:, :], wd[kk * 128:(kk + 1) * 128, :])
                (k.pool if kk % 2 else k.dve).tensor_copy(out=Wsb[:, kk, :], in_=s[:, :])

    with k.scope():
        W0 = k.sb('W0', [128, 8, EVC], BF16)
        load_w_bf16(W0, w_in0, 1024, EVC)
        gbc = k.sb('gbc', [128, 1024]); k.dma(gbc[:, :], norm_mix[0:1, :].pbc(128))
        gb = k.sb('gb', [128, 24]); k.dma(gb[:, :], gate_b[0:1, :].pbc(128))
        xt = [k.sb('xt', [128, 1024]) for _ in range(2)]
        junk = k.sb('junk', [128, 1024], BF16)
        xn = k.sb('xn', [128, 1024], BF16)
        ss = k.sb('ss', [128, 4])
        xnT = k.sb('xnT', [128, 8, 128], BF16)
        proj = [k.sb('proj', [128, EVC]) for _ in range(2)]
        cs = k.sb('cs', [128, 64])
        tmp = [k.sb('tmp%d' % i, [128, 8, 32]) for i in range(4)]
        qb = k.sb('qb', [128, 8, 64], BF16)
        kb = k.sb('kb', [128, 6, 64], BF16)
        vb = k.sb('vb', [128, 3, 2, 65], BF16)
        k.dve.memset(ap=vb[:, :, :, :], constant=1.0) if False else k.op('dve', lambda e: e.memset(vb[:, :, :, :].ap, 1.0), (), (vb,))
        gt = k.sb('gt', [128, 24])
        qT = k.sb('qT', [64, 8, 128], BF16); kT = k.sb('kT', [64, 6, 128], BF16); vcT = k.sb('vcT', [64, 2, 128], BF16)
        pT = k.ps('pT', [128, 8, 128], BF16)
        pA = [k.ps('pA', [128, 512]) for _ in range(2)]
        pQ = k.ps('pQ', [64, 8, 128], BF16)
        pK = k.ps('pK', [64, 8, 128], BF16)
        ci = 0
        for ti, (r0, rows) in enumerate(tiles):
            x = xt[ti % 2]
            pj = proj[ti % 2]
            src = xp[r0:r0 + rows, :] if r0 < Tn else xs[0:16, :]
            k.dma(x[:rows, :], src)
            k.dma(cs[:rows, :], ropecs[r0:r0 + rows, :])
            rmsnorm_T(x, rows, gbc, xn, pT, xnT, ss, junk)
            for c0 in range(0, EVC, 512):
                w = min(512, EVC - c0)
                ps = pA[ci % 2]
                for kk in range(8):
                    k.pe.matmul(out=ps[:rows, :w], lhsT=xnT[:, kk, :rows], rhs=W0[:, kk, c0:c0 + w], start=(kk == 0), stop=(kk == 7))
                if ci % 2:
                    k.act.copy(out=pj[:rows, c0:c0 + w], in_=ps[:rows, :w])
                else:
                    k.dve.tensor_copy(out=pj[:rows, c0:c0 + w], in_=ps[:rows, :w])
                ci += 1
            k.dma(RW[r0:r0 + rows, :], pj[:rows, 0:RWC])
            k.dve.tensor_tensor(out=gt[:rows, :], in0=pj[:rows, 3072:3096], in1=gb[:rows, :], op=ALU.add)
            k.act.activation(out=gt[:rows, :], in_=gt[:rows, :], func=AF.Sigmoid)
            k.dma(GT[r0:r0 + rows, :], gt[:rows, :])
            cosb = lambda n: cs[:rows, 0:32].unsq(1).bc([rows, n, 32])
            sinb = lambda n: cs[:rows, 32:64].unsq(1).bc([rows, n, 32])
            qv = pj[:rows, 1792:2304].rearrange("p (h d) -> p h d", h=8)
            views = [(qv, 8, qb[:rows, :, :])]
            for c in range(3):
                kv_ = pj[:rows, 2304 + c * 256:2304 + c * 256 + 128].rearrange("p (g d) -> p g d", g=2)
                views.append((kv_, 2, None))
            for vi, (xv, n, ob) in enumerate(views):
                E1 = k.dve if vi % 2 == 0 else k.pool
                x1 = xv[:, :, 0:32]; x2 = xv[:, :, 32:64]
                t = [tt[:rows, 0:n, :] for tt in tmp]
                E1.tensor_tensor(out=t[0], in0=x1, in1=cosb(n), op=ALU.mult)
                E1.tensor_tensor(out=t[1], in0=x2, in1=sinb(n), op=ALU.mult)
                E1.tensor_tensor(out=t[2], in0=x2, in1=cosb(n), op=ALU.mult)
                E1.tensor_tensor(out=t[3], in0=x1, in1=sinb(n), op=ALU.mult)
                if ob is not None:
                    E1.tensor_tensor(out=ob[:, :, 0:32], in0=t[0], in1=t[1], op=ALU.subtract)
                    E1.tensor_tensor(out=ob[:, :, 32:64], in0=t[2], in1=t[3], op=ALU.add)
                else:
                    E1.tensor_tensor(out=x1, in0=t[0], in1=t[1], op=ALU.subtract)
                    E1.tensor_tensor(out=x2, in0=t[2], in1=t[3], op=ALU.add)
            kvv = pj[:rows, 2304:3072].rearrange("p (c j g d) -> p c j g d", c=3, j=2, g=2)
            for c in range(3):
                k.dve.tensor_copy(out=kb[:rows, 2 * c:2 * c + 2, :], in_=kvv[:, c, 0, :, :])
                k.pool.tensor_copy(out=vb[:rows, c, :, 0:64], in_=kvv[:, c, 1, :, :])
            if r0 < Tn:
                k.dma(o_nsa_p[r0:r0 + rows, :], pj[:rows, 2304:2816])
                if r0 >= Tn - WN:
                    k.dma(o_win_p[r0 - (Tn - WN):r0 - (Tn - WN) + rows, :], pj[:rows, 2816:3072])
                if ti == NT - 1:
                    k.dma(o_sh_p[0:1, :], pj[127:128, 0:RWC])
            else:
                k.dma(o_nsa_s[0:16, :], pj[:16, 2304:2816])
                for bl in range(4):
                    k.dma(o_win_s[bl, 508:512, :], pj[bl * 4:bl * 4 + 4, 2816:3072])
                    k.dma(o_win_s[bl, 0:508, :], st_win[bl, 4:512, :])
                    k.dma(o_sh_s[bl:bl + 1, :], pj[bl * 4 + 3:bl * 4 + 4, 0:RWC])
            for h in range(8):
                k.pe.transpose(out=pQ[:, h, :rows], in_=qb[:rows, h, :], identity=identb[:rows, :rows])
            k.act.copy(out=qT[:, :, :rows], in_=pQ[:, :, :rows])
            for h in range(6):
                k.pe.transpose(out=pK[:, h, :rows], in_=kb[:rows, h, :], identity=identb[:rows, :rows])
            for g in range(2):
                k.pe.transpose(out=pK[:, 6 + g, :rows], in_=vb[:rows, 0, g, 0:64], identity=identb[:rows, :rows])
            k.dve.tensor_copy(out=kT[:, :, :rows], in_=pK[:, 0:6, :rows])
            k.dve.tensor_copy(out=vcT[:, :, :rows], in_=pK[:, 6:8, :rows])
            k.dma(QT[:, :, r0:r0 + rows], qT[:, :, :rows])
            k.dma(KT[:, :, r0:r0 + rows], kT[:, :, :rows])
            k.dma(VcT[:, :, r0:r0 + rows], vcT[:, :, :rows])
            k.dma(VT[r0:r0 + rows, :, :, :], vb[:rows, 1:3, :, :])
    if 'B' in stages:
        phase_rwkv(k, locals())
    if 'C' in stages:
        phase_nsa_prompt(k, locals())
    LL = locals()
    if 'S' in stages:
        phase_nsa_sample(k, LL)
    if 'D' in stages:
        layer_tail(k, LL, 0, AO, w_out0, None, H1, None)
    if 'E' in stages:
        phase_proj1(k, LL)
    if 'F' in stages:
        phase_moba_prompt(k, LL)
    if 'M' in stages:
        phase_moba_sample(k, LL)
    if 'G' in stages:
        layer_tail(k, LL, 1, MO, w_out1, H1, None, (y_p, y_s))
    nc2 = k.finish()
    return nc2, k


def phase_rwkv(k, L):
    Tn = L['Tn']; RW = L['RW']; AO = L['AO']; identf = L['identf']
    rw_mu = L['rw_mu']; rw_vec = L['rw_vec']; st_wkv = L['st_wkv']; st_shift = L['st_shift']
    with k.scope():
        mu = k.sb('mu', [64, RWC]); k.dma(mu[:, :], rw_mu[0:1, :].pbc(64))
        vec = k.sb('vec', [64, 7, 512])
        for i in range(7):
            k.dma(vec[:, i, :], rw_vec[i:i + 1, :].pbc(64))
        w0b, a0b, kkb, kab, rkb, lgb, lbb = [vec[:, i, :] for i in range(7)]
        wup = k.sb('wup', [64, 512]); k.dma(wup[:, :], L['rw_wup'][:, :])
        aup = k.sb('aup', [64, 512]); k.dma(aup[:, :], L['rw_aup'][:, :])
        gup = k.sb('gup', [128, 512]); k.dma(gup[:, :], L['rw_gup'][:, :])
        mk = k.sb('mk', [64, 3, 64]); k.dma(mk[:, :, :], L['masks64'][:, :, :])
        ones = k.sb('ones', [64, 1]); k.op('dve', lambda e: e.memset(ones[:, :].ap, 1.0), (), (ones,))
        banks = [k.ps('bank', [128, 512]) for _ in range(8)]
        bi = [0]

        def bank():
            b = banks[bi[0] % 8]
            bi[0] += 1
            return b

        ST = k.sb('ST', [64, 8, 64])
        cur = [k.sb('cur', [64, RWC]) for _ in range(2)]
        prv = [k.sb('prv', [64, RWC]) for _ in range(2)]
        Lt = k.sb('Lt', [64, 256]); LT = k.sb('LT', [128, 3, 64])
        lw = k.sb('lw', [64, 512]); av = k.sb('av', [64, 512]); gv = k.sb('gv', [64, 512])
        kk = k.sb('kk', [64, 512]); sq = k.sb('sq', [64, 512]); k2 = k.sb('k2', [64, 512]); bv = k.sb('bv', [64, 512])
        sm = k.sb('sm', [64, 8, 4])
        eP = k.sb('eP', [64, 512]); eN = k.sb('eN', [64, 512]); ePm = k.sb('ePm', [64, 512])
        F = k.sb('F', [64, 4, 512])
        FT = [k.sb('FT%d' % i, [64, 8, 64]) for i in range(4)]
        GC = k.sb('GC', [64, 8])
        Mb = [k.sb('Mb%d' % i, [64, 8, 64]) for i in range(5)]
        Bt = [k.sb('Bt%d' % i, [64, 8, 64]) for i in range(2)]
        BTt = [k.sb('BTt%d' % i, [64, 8, 64]) for i in range(2)]
        Nt = k.sb('Nt', [64, 8, 64])
        Zn = k.sb('Zn', [64, 8, 64]); UT = k.sb('UT', [64, 8, 64])
        yv = k.sb('yv', [64, 512]); yc = k.sb('yc', [64, 512]); t1 = k.sb('t1', [64, 512]); ob = k.sb('ob', [64, 512], BF16)
        Stmp = k.sb('Stmp', [64, 8, 64])
        ci = [0]

        def hv(t, C):
            return t[:C, :].rearrange("p (h d) -> p h d", h=8)

        def chunk(r0, C, first_prev):
            c = cur[ci[0] % 2]; p = prv[ci[0] % 2]; ci[0] += 1
            k.dma(c[:C, :], RW[r0:r0 + C, :])
            if first_prev is None:
                k.dma(p[:C, :], RW[r0 - 1:r0 - 1 + C, :])
            else:
                if first_prev == 'zero':
                    k.op('dve', lambda e: e.memset(p[0:1, :].ap, 0.0), (), (p,))
                else:
                    k.dma(p[0:1, :], first_prev)
                k.dma(p[1:C, :], RW[r0:r0 + C - 1, :])
            k.dve.tensor_tensor(out=p[:C, :], in0=p[:C, :], in1=c[:C, :], op=ALU.subtract)
            k.pool.tensor_tensor(out=p[:C, :], in0=p[:C, :], in1=mu[:C, :], op=ALU.mult)
            k.dve.tensor_tensor(out=c[:C, :], in0=c[:C, :], in1=p[:C, :], op=ALU.add)
            xm = c
            r_ = xm[:C, 0:512]; k_ = xm[:C, 512:1024]; v_ = xm[:C, 1024:1536]
            k.act.activation(out=Lt[:C, 0:64], in_=xm[:C, 1536:1600], func=AF.Tanh)
            k.act.activation(out=Lt[:C, 128:256], in_=xm[:C, 1664:1792], func=AF.Sigmoid)
            k.dve.tensor_copy(out=Lt[:C, 64:128], in_=xm[:C, 1600:1664])
            pb = bank()
            pl = pb[:, 0:192].rearrange("p (a t) -> p a t", a=3)
            k.pe.transpose(out=pl[0:64, 0, :C], in_=Lt[:C, 0:64], identity=identf[:C, :C])
            k.pe.transpose(out=pl[0:64, 1, :C], in_=Lt[:C, 64:128], identity=identf[:C, :C])
            k.pe.transpose(out=pl[:, 2, :C], in_=Lt[:C, 128:256], identity=identf[:C, :C])
            k.dve.tensor_copy(out=LT[0:64, 0:2, :C], in_=pl[0:64, 0:2, :C])
            k.dve.tensor_copy(out=LT[:, 2, :C], in_=pl[:, 2, :C])
            pW = bank(); pA = bank(); pG = bank()
            k.pe.matmul(out=pW[:C, :], lhsT=LT[0:64, 0, :C], rhs=wup[:, :], start=True, stop=True)
            k.pe.matmul(out=pA[:C, :], lhsT=LT[0:64, 1, :C], rhs=aup[:, :], start=True, stop=True)
            k.pe.matmul(out=pG[:C, :], lhsT=LT[:, 2, :C], rhs=gup[:, :], start=True, stop=True)
            k.dve.tensor_tensor(out=lw[:C, :], in0=pW[:C, :], in1=w0b[:C, :], op=ALU.add)
            k.act.activation(out=lw[:C, :], in_=lw[:C, :], func=AF.Sigmoid)
            k.dve.tensor_scalar(out=lw[:C, :], in0=lw[:C, :], scalar1=-0.6065306597126334, scalar2=None, op0=ALU.mult)
            k.dve.tensor_tensor(out=av[:C, :], in0=pA[:C, :], in1=a0b[:C, :], op=ALU.add)
            k.act.activation(out=av[:C, :], in_=av[:C, :], func=AF.Sigmoid)
            k.act.copy(out=gv[:C, :], in_=pG[:C, :])
            k.pool.tensor_tensor(out=kk[:C, :], in0=k_, in1=kkb[:C, :], op=ALU.mult)
            k.pool.tensor_tensor(out=sq[:C, :], in0=kk[:C, :], in1=kk[:C, :], op=ALU.mult)
            k.dve.reduce_sum(out=sm[:C, :, 0], in_=hv(sq, C), axis=AX.X)
            k.act.activation(out=sm[:C, :, 1], in_=sm[:C, :, 0], func=AF.Sqrt)
            k.dve.tensor_scalar(out=sm[:C, :, 1], in0=sm[:C, :, 1], scalar1=1e-12, scalar2=None, op0=ALU.max)
            k.dve.reciprocal(out=sm[:C, :, 2], in_=sm[:C, :, 1])
            if r0 == 0:
                k.dbg('kk0', kk[:C, :], [64, 512]); k.dbg('sq', sq[:C, :], [64, 512]); k.dbg('sm', sm[:C, :, :], [64, 8, 4])
            k.dve.tensor_tensor(out=hv(kk, C), in0=hv(kk, C), in1=sm[:C, :, 2:3].bc([C, 8, 64]), op=ALU.mult)
            k.dve.scalar_tensor_tensor(out=k2[:C, :], in0=av[:C, :], scalar=-1.0, in1=kab[:C, :], op0=ALU.add, op1=ALU.mult)
            k.dve.scalar_tensor_tensor(out=k2[:C, :], in0=k2[:C, :], scalar=1.0, in1=k_, op0=ALU.add, op1=ALU.mult)
            k.pool.tensor_tensor(out=bv[:C, :], in0=kk[:C, :], in1=av[:C, :], op=ALU.mult)
            pC = bank()
            k.pe.matmul(out=pC[:C, :], lhsT=mk[:C, 0, :C], rhs=lw[:C, :], start=True, stop=True)
            k.act.activation(out=eP[:C, :], in_=pC[:C, :], func=AF.Exp)
            k.act.activation(out=eN[:C, :], in_=pC[:C, :], func=AF.Exp, scale=-1.0)
            k.dve.tensor_tensor(out=ePm[:C, :], in0=pC[:C, :], in1=lw[:C, :], op=ALU.subtract)
            k.act.activation(out=ePm[:C, :], in_=ePm[:C, :], func=AF.Exp)
            k.dve.tensor_tensor(out=F[:C, 0, :], in0=kk[:C, :], in1=ePm[:C, :], op=ALU.mult)
            k.pool.tensor_tensor(out=F[:C, 1, :], in0=bv[:C, :], in1=eN[:C, :], op=ALU.mult)
            k.dve.tensor_tensor(out=F[:C, 2, :], in0=k2[:C, :], in1=eN[:C, :], op=ALU.mult)
            k.pool.tensor_tensor(out=F[:C, 3, :], in0=r_, in1=eP[:C, :], op=ALU.mult)
            pg = bank()
            for h in range(8):
                k.pe.matmul(out=pg[0:64, h:h + 1], lhsT=lw[:C, h * 64:(h + 1) * 64], rhs=ones[:C, 0:1], start=True, stop=True)
            k.act.activation(out=GC[:, :], in_=pg[0:64, 0:8], func=AF.Exp)
            for kind in range(4):
                pf = bank()
                pfv = pf[0:64, :].rearrange("p (h t) -> p h t", h=8)
                for h in range(8):
                    k.pe.transpose(out=pfv[:, h, :C], in_=F[:C, kind, h * 64:(h + 1) * 64], identity=identf[:C, :C])
                (k.act.copy if kind % 2 else k.dve.tensor_copy)(out=FT[kind][:, :, :C], in_=pfv[:, :, :C])
            aT, bT, khT, rT = FT
            combos = [(bT, aT, 1), (aT, bT, 2), (khT, aT, 1), (bT, rT, 0), (khT, rT, 0)]
            for i, (lt, rt, mi) in enumerate(combos):
                pm = bank()
                pmv = pm[0:64, :].rearrange("p (h t) -> p h t", h=8)
                for h in range(8):
                    k.pe.matmul(out=pmv[:C, h, :C], lhsT=lt[:, h, :C], rhs=rt[:, h, :C], start=True, stop=True)
                (k.dve if i % 2 == 0 else k.pool).tensor_tensor(out=Mb[i][:C, :, :C], in0=pmv[:C, :, :C], in1=mk[:C, mi:mi + 1, :C].bc([C, 8, C]), op=ALU.mult) if i % 2 == 0 else k.dve.tensor_tensor(out=Mb[i][:C, :, :C], in0=pmv[:C, :, :C], in1=mk[:C, mi:mi + 1, :C].bc([C, 8, C]), op=ALU.mult)
            A, AT, Mka, Mbr, Mkr = Mb
            k.dve.tensor_tensor(out=Nt[:C, :, :C], in0=identf[:C, 0:C].unsq(1).bc([C, 8, C]), in1=A[:C, :, :C], op=ALU.subtract)
            Bc, BTc = A, AT
            nlev = {64: 5, 4: 1}[C]
            for lev in range(nlev):
                Bn = Bt[lev % 2]; BTn = BTt[lev % 2]
                p1 = bank(); p2 = bank()
                p1v = p1[0:64, :].rearrange("p (h t) -> p h t", h=8); p2v = p2[0:64, :].rearrange("p (h t) -> p h t", h=8)
                for h in range(8):
                    k.pe.matmul(out=p1v[:C, h, :C], lhsT=BTc[:C, h, :C], rhs=Bc[:C, h, :C], start=True, stop=True)
                for h in range(8):
                    k.pe.matmul(out=p2v[:C, h, :C], lhsT=Bc[:C, h, :C], rhs=BTc[:C, h, :C], start=True, stop=True)
                k.dve.tensor_copy(out=Bn[:C, :, :C], in_=p1v[:C, :, :C])
                k.act.copy(out=BTn[:C, :, :C], in_=p2v[:C, :, :C])
                p3 = bank(); p3v = p3[0:64, :].rearrange("p (h t) -> p h t", h=8)
                for h in range(8):
                    k.pe.matmul(out=p3v[:C, h, :C], lhsT=BTn[:C, h, :C], rhs=Nt[:C, h, :C], start=True, stop=True)
                k.dve.tensor_tensor(out=Nt[:C, :, :C], in0=Nt[:C, :, :C], in1=p3v[:C, :, :C], op=ALU.add)
                Bc, BTc = Bn, BTn
            vh = lambda h: xm[:C, 1024 + h * 64:1024 + (h + 1) * 64]
            pz = bank(); pzv = pz[0:64, :].rearrange("p (h t) -> p h t", h=8)
            for h in range(8):
                k.pe.matmul(out=pzv[:C, h, :], lhsT=aT[:, h, :C], rhs=ST[:, h, :], start=True, stop=False)
                k.pe.matmul(out=pzv[:C, h, :], lhsT=Mka[:C, h, :C], rhs=vh(h), start=False, stop=True)
            k.dve.tensor_scalar(out=Zn[:C, :, :], in0=pzv[:C, :, :], scalar1=-1.0, scalar2=None, op0=ALU.mult)
            pu = bank(); puv = pu[0:64, :].rearrange("p (h t) -> p h t", h=8)
            for h in range(8):
                k.pe.matmul(out=puv[:C, h, :], lhsT=Nt[:C, h, :C], rhs=Zn[:C, h, :], start=True, stop=True)
            k.act.copy(out=UT[:C, :, :], in_=puv[:C, :, :])
            py = bank(); pyv = py[0:64, :].rearrange("p (h t) -> p h t", h=8)
            for h in range(8):
                k.pe.matmul(out=pyv[:C, h, :], lhsT=rT[:, h, :C], rhs=ST[:, h, :], start=True, stop=False)
                k.pe.matmul(out=pyv[:C, h, :], lhsT=Mbr[:C, h, :C], rhs=UT[:C, h, :], start=False, stop=False)
                k.pe.matmul(out=pyv[:C, h, :], lhsT=Mkr[:C, h, :C], rhs=vh(h), start=False, stop=True)
            k.act.copy(out=yv[:C, :], in_=py[:C, :])
            pS = bank(); pSv = pS[0:64, :].rearrange("p (h t) -> p h t", h=8)
            for h in range(8):
                k.pe.matmul(out=pSv[:, h, :], lhsT=F[:C, 1, h * 64:(h + 1) * 64], rhs=UT[:C, h, :], start=True, stop=False)
                k.pe.matmul(out=pSv[:, h, :], lhsT=F[:C, 2, h * 64:(h + 1) * 64], rhs=vh(h), start=False, stop=True)
            k.dve.tensor_tensor(out=ST[:, :, :], in0=ST[:, :, :], in1=pSv[:, :, :], op=ALU.add)
            k.dve.tensor_tensor(out=ST[:, :, :], in0=ST[:, :, :], in1=GC[:, :].unsq(2).bc([64, 8, 64]), op=ALU.mult)
            k.dve.reduce_sum(out=sm[:C, :, 0], in_=hv(yv, C), axis=AX.X)
            k.dve.tensor_scalar(out=sm[:C, :, 0], in0=sm[:C, :, 0], scalar1=1.0 / 64, scalar2=None, op0=ALU.mult)
            k.dve.tensor_tensor(out=hv(yc, C), in0=hv(yv, C), in1=sm[:C, :, 0:1].bc([C, 8, 64]), op=ALU.subtract)
            k.pool.tensor_tensor(out=t1[:C, :], in0=yc[:C, :], in1=yc[:C, :], op=ALU.mult)
            k.dve.reduce_sum(out=sm[:C, :, 1], in_=hv(t1, C), axis=AX.X)
            k.dve.tensor_scalar(out=sm[:C, :, 1], in0=sm[:C, :, 1], scalar1=1.0 / 64, scalar2=64e-5, op0=ALU.mult, op1=ALU.add)
            k.act.activation(out=sm[:C, :, 1], in_=sm[:C, :, 1], func=AF.Sqrt)
            k.dve.reciprocal(out=sm[:C, :, 2], in_=sm[:C, :, 1])
            k.dve.tensor_tensor(out=hv(yc, C), in0=hv(yc, C), in1=sm[:C, :, 2:3].bc([C, 8, 64]), op=ALU.mult)
            k.dve.tensor_tensor(out=yc[:C, :], in0=yc[:C, :], in1=lgb[:C, :], op=ALU.mult)
            k.dve.tensor_tensor(out=yc[:C, :], in0=yc[:C, :], in1=lbb[:C, :], op=ALU.add)
            k.pool.tensor_tensor(out=t1[:C, :], in0=r_, in1=k2[:C, :], op=ALU.mult)
            k.pool.tensor_tensor(out=t1[:C, :], in0=t1[:C, :], in1=rkb[:C, :], op=ALU.mult)
            k.dve.reduce_sum(out=sm[:C, :, 3], in_=hv(t1, C), axis=AX.X)
            k.dve.tensor_tensor(out=hv(t1, C), in0=xm[:C, 1024:1536].rearrange("p (h d) -> p h d", h=8), in1=sm[:C, :, 3:4].bc([C, 8, 64]), op=ALU.mult)
            k.dve.tensor_tensor(out=yc[:C, :], in0=yc[:C, :], in1=t1[:C, :], op=ALU.add)
            k.dve.tensor_tensor(out=ob[:C, :], in0=yc[:C, :], in1=gv[:C, :], op=ALU.mult)
            k.dma(AO[r0:r0 + C, 0:512], ob[:C, :])
            if r0 == 0:
                k.dbg('xm', xm[:C, :], [64, RWC]); k.dbg('lw', lw[:C, :], [64, 512]); k.dbg('av', av[:C, :], [64, 512])
                k.dbg('kk', kk[:C, :], [64, 512]); k.dbg('k2', k2[:C, :], [64, 512]); k.dbg('F', F[:C, :, :], [64, 4, 512])
                k.dbg('aT', FT[0][:, :, :], [64, 8, 64]); k.dbg('A', Mb[0][:, :, :], [64, 8, 64]); k.dbg('AT', Mb[1][:, :, :], [64, 8, 64])
                k.dbg('N', Nt[:, :, :], [64, 8, 64]); k.dbg('UT', UT[:, :, :], [64, 8, 64]); k.dbg('yv', yv[:C, :], [64, 512])
                k.dbg('ST', ST[:, :, :], [64, 8, 64]); k.dbg('GC', GC[:, :], [64, 8]); k.dbg('gv', gv[:C, :], [64, 512])
                k.dbg('ob', ob[:C, :], [64, 512], BF16)

        def store_state(dst):
            pt = bank(); ptv = pt[0:64, :].rearrange("p (h t) -> p h t", h=8)
            for h in range(8):
                k.pe.transpose(out=ptv[:, h, :], in_=ST[:, h, :], identity=identf[0:64, 0:64])
            k.dve.tensor_copy(out=Stmp[:, :, :], in_=ptv[:, :, :])
            k.dma(dst.rearrange("h i j -> i h j"), Stmp[:, :, :])

        k.op('dve', lambda e: e.memset(ST[:, :, :].ap, 0.0), (), (ST,))
        for ch in range(Tn // 64):
            chunk(ch * 64, 64, 'zero' if ch == 0 else None)
        store_state(L['o_wkv_p'][:, :, :])
        for bl in range(4):
            k.dma(Stmp[:, :, :], st_wkv[bl].rearrange("h i j -> i h j"))
            pt = bank(); ptv = pt[0:64, :].rearrange("p (h t) -> p h t", h=8)
            for h in range(8):
                k.pe.transpose(out=ptv[:, h, :], in_=Stmp[:, h, :], identity=identf[0:64, 0:64])
            k.dve.tensor_copy(out=ST[:, :, :], in_=ptv[:, :, :])
            chunk(Tn + bl * 4, 4, st_shift[bl:bl + 1, :])
            store_state(L['o_wkv_s'][bl])


def gelu_tanh(k, out_bf, x, tmpa, tmpb, shape_idx):
    k.dve.tensor_tensor(out=tmpa, in0=x, in1=x, op=ALU.mult)
    k.dve.tensor_scalar(out=tmpa, in0=tmpa, scalar1=0.044715, scalar2=1.0, op0=ALU.mult, op1=ALU.add)
    k.dve.tensor_tensor(out=tmpa, in0=tmpa, in1=x, op=ALU.mult)
    k.act.activation(out=tmpb, in_=tmpa, func=AF.Tanh, scale=0.7978845608028654)
    k.dve.tensor_scalar(out=tmpb, in0=tmpb, scalar1=1.0, scalar2=0.5, op0=ALU.add, op1=ALU.mult)
    k.dve.tensor_tensor(out=out_bf, in0=tmpb, in1=x, op=ALU.mult)


def load_cmp_weights(k, L, pbias):
    cmp_w1 = L['cmp_w1']; cmp_pe = L['cmp_pe']; cmp_w2 = L['cmp_w2']
    W = {}
    tiles_ = [(k.sb('cw1_%d' % kind, [64, 32, 64], BF16), k.sb('cw2_%d' % kind, [64, 64], BF16),
               k.sb('peT%d' % kind, [64, 32], BF16), k.sb('cbias%d' % kind, [64, 1])) for kind in range(2)]
    sc_ = k.scope(); sc_.__enter__()
    stg = k.sb('cw_stg', [64, 32, 64]); stg2 = k.sb('cw_stg2', [64, 64]); pes = k.sb('pe_stg', [64, 32])
    for kind in range(2):
        w1, w2, peT, bias = tiles_[kind]
        k.dma(stg[:, :, :], cmp_w1[kind].rearrange("c d e -> d c e"))
        k.dve.tensor_copy(out=w1[:, :, :], in_=stg[:, :, :])
        k.dma(stg2[:, :], cmp_w2[kind])
        k.dve.tensor_copy(out=w2[:, :], in_=stg2[:, :])
        k.dma(pes[:, :], cmp_pe[kind].rearrange("c d -> d c"), allow_slow_non_contiguous=True)
        k.dve.tensor_copy(out=peT[:, :], in_=pes[:, :])
        for c in range(32):
            k.pe.matmul(out=pbias[0:64, kind:kind + 1], lhsT=w1[:, c, :], rhs=peT[:, c:c + 1], start=(c == 0), stop=(c == 31))
        k.dve.tensor_copy(out=bias[:, :], in_=pbias[0:64, kind:kind + 1])
        W[kind] = (w1, w2, bias)
    sc_.__exit__(None, None, None)
    return W


def compress(k, W, kind, srcT, NBn, ps_bank, hx, ta, tb, hbf):
    w1, w2, bias = W[kind]
    for c in range(32):
        k.pe.matmul(out=ps_bank[0:64, 0:NBn], lhsT=w1[:, c, :], rhs=srcT[:, c:c + 16 * (NBn - 1) + 1:16], start=(c == 0), stop=(c == 31))
    k.act.activation(out=hx[:, 0:NBn], in_=ps_bank[0:64, 0:NBn], func=AF.Identity, bias=bias[:, 0:1])
    gelu_tanh(k, hbf[:, 0:NBn], hx[:, 0:NBn], ta[:, 0:NBn], tb[:, 0:NBn], None)


def phase_nsa_prompt(k, L):
    Tn = L['Tn']; NT = L['NT']; NSEL = L['NSEL']; NB = L['NB']; NBT = L['NBT']
    QT = L['QT']; KT = L['KT']; VcT = L['VcT']; VT = L['VT']; GT = L['GT']; AO = L['AO']; identb = L['identb']
    with k.scope():
        ps_s = [k.ps('ps_s', [128, 512]) for _ in range(3)]
        W = load_cmp_weights(k, L, ps_s[0])
        cm = k.sb('cm', [128, 17, 128], BF16); k.dma(cm[:, :, :], L['CMt'][:, :, :])
        tri = k.sb('tri', [128, 2, 128], BF16); k.dma(tri[:, :, :], L['TRIt'][:, :, :])
        eexp = k.sb('eexp', [NSEL, Tn], BF16); k.dma(eexp[:, :], L['EEXP'][0:NSEL, 0:Tn])
        po_sel = [k.ps('po_sel', [128, 4, 65]) for _ in range(1)]
        po_win = [k.ps('po_win', [128, 4, 65]) for _ in range(1)]
        po_c = [k.ps('po_c', [128, 65 + NSEL]) for _ in range(2)]
        ps_t = k.ps('ps_t', [128, 128], BF16)
        si = [0]

        def sbank():
            b = ps_s[si[0] % 3]; si[0] += 1
            return b

        KcT = k.sb('KcT', [64, Tn], BF16); VcTs = k.sb('VcTs', [64, Tn], BF16)
        KsT = k.sb('KsT', [64, Tn], BF16); KwT = k.sb('KwT', [64, Tn], BF16)
        Vs = k.sb('Vs', [128, NT, 65], BF16); Vw = k.sb('Vw', [128, NT, 65], BF16)
        KcC = k.sb('KcC', [64, NBT * 128], BF16)
        VcC = k.sb('VcC', [128, NBT, 65 + NSEL], BF16)
        hx = k.sb('hx', [64, 512]); ta = k.sb('ta', [64, 512]); tb = k.sb('tb', [64, 512])
        hk = k.sb('hk', [64, 512], BF16); hv_ = k.sb('hv_', [64, 512], BF16)
        q4 = [k.sb('q4', [64, 4, 128], BF16) for _ in range(2)]
        gtile = [k.sb('gtile', [128, 24]) for _ in range(2)]
        fbt = [k.sb('fbt', [128, NSEL]) for _ in range(2)]
        ebuf = [k.sb('ebuf', [128, 512], BF16) for _ in range(3)]
        ei = [0]

        def enext():
            b = ebuf[ei[0] % 3]; ei[0] += 1
            return b

        acc = k.sb('acc', [128, 4, 64]); accb = k.sb('accb', [128, 4, 64], BF16); tmp4 = k.sb('tmp4', [128, 4, 64])
        rr = k.sb('rr', [128, 16]); imp = k.sb('imp', [128, NSEL]); score = k.sb('score', [128, NSEL]); sc2 = k.sb('sc2', [128, NSEL])
        m8 = k.sb('m8', [128, 16]); negs = k.sb('negs', [128, NSEL], BF16)
        negT4 = k.sb('negT4', [NSEL, 4, 128], BF16)
        for g in range(2):
            k.dma(KcT[:, :], KT[:, 0 + g, 0:Tn]); k.dma(VcTs[:, :], VcT[:, g, 0:Tn])
            k.dma(KsT[:, :], KT[:, 2 + g, 0:Tn]); k.dma(KwT[:, :], KT[:, 4 + g, 0:Tn])
            k.dma(Vs[:, :, :], VT[0:Tn, 0, g, :].rearrange("(kt p) c -> p kt c", p=128))
            k.dma(Vw[:, :, :], VT[0:Tn, 1, g, :].rearrange("(kt p) c -> p kt c", p=128))
            k.dma(VcC[:, :, 65:65 + NSEL], L['OVt'][:, :, :])
            k.op('dve', lambda e: e.memset(VcC[:, :, 64:65].ap, 1.0), (), (VcC,))
            pb = sbank()
            compress(k, W, 0, KcT, NB, pb, hx, ta, tb, hk)
            pb2 = sbank()
            k.pe.matmul(out=pb2[0:64, 0:NB], lhsT=W[0][1][:, :], rhs=hk[:, 0:NB], start=True, stop=True)
            k.dve.tensor_copy(out=KcC[:, 0:NB], in_=pb2[0:64, 0:NB])
            pb = sbank()
            compress(k, W, 1, VcTs, NB, pb, hx, ta, tb, hv_)
            for ni in range(NBT):
                nn = min(128, NB - ni * 128)
                pb3 = sbank()
                k.pe.matmul(out=pb3[:nn, 0:64], lhsT=hv_[:, ni * 128:ni * 128 + nn], rhs=W[1][1][:, :], start=True, stop=True)
                k.dve.tensor_copy(out=VcC[:nn, ni, 0:64], in_=pb3[:nn, 0:64])
            for tt in range(NT):
                t0 = tt * 128
                q = q4[tt % 2]; gt_ = gtile[tt % 2]; fb = fbt[tt % 2]
                k.dma(q[:, :, :], QT[:, g * 4:(g + 1) * 4, t0:t0 + 128])
                k.dma(gt_[:, :], GT[t0:t0 + 128, :])
                k.dma(fb[:, :], L['FBt'][t0:t0 + 128, :])
                gv3 = gt_[:, :].rearrange("p (h b) -> p h b", b=3)
                nvalid = min(NB, (t0 + 96) // 16 + 1)
                nnt = (nvalid + 127) // 128
                for r in range(4):
                    po = po_c[r % 2]
                    for ni in range(nnt):
                        nn = min(128, NB - ni * 128)
                        pss = sbank()
                        k.pe.matmul(out=pss[:nn, 0:128], lhsT=KcC[:, ni * 128:ni * 128 + nn], rhs=q[:, r, :], start=True, stop=True)
                        e = enext()
                        k.act.activation(out=e[:nn, 0:128], in_=pss[:nn, 0:128], func=AF.Exp, scale=0.125)
                        delta = t0 - 2048 * ni
                        if delta < 17 * 128:
                            assert delta >= 0
                            k.pool.tensor_tensor(out=e[:nn, 0:128], in0=e[:nn, 0:128], in1=cm[:nn, delta // 128, :], op=ALU.mult)
                        k.pe.matmul(out=po[:, :], lhsT=e[:nn, 0:128], rhs=VcC[:nn, ni, :], start=(ni == 0), stop=(ni == nnt - 1))
                    k.dve.tensor_scalar(out=rr[:, r:r + 1], in0=po[:, 64:65], scalar1=1e-30, scalar2=None, op0=ALU.max)
                    k.dve.reciprocal(out=rr[:, 4 + r:5 + r], in_=rr[:, r:r + 1])
                    k.dve.tensor_tensor(out=rr[:, 8 + r:9 + r], in0=rr[:, 4 + r:5 + r], in1=gv3[:, g * 4 + r, 0:1], op=ALU.mult)
                    k.dve.tensor_scalar(out=acc[:, r, :], in0=po[:, 0:64], scalar1=rr[:, 8 + r:9 + r], scalar2=None, op0=ALU.mult)
                    if r == 0:
                        k.dve.tensor_scalar(out=imp[:, :], in0=po[:, 65:65 + NSEL], scalar1=rr[:, 4 + r:5 + r], scalar2=None, op0=ALU.mult)
                    else:
                        k.dve.scalar_tensor_tensor(out=imp[:, :], in0=po[:, 65:65 + NSEL], scalar=rr[:, 4 + r:5 + r], in1=imp[:, :], op0=ALU.mult, op1=ALU.add)
                k.dve.tensor_tensor(out=score[:, :], in0=imp[:, :], in1=fb[:, :], op=ALU.add)
                k.dve.max(out=m8[:, 0:8], in_=score[:, :])
                k.dve.match_replace(out=sc2[:, :], in_to_replace=m8[:, 0:8], in_values=score[:, :], imm_value=-3.0e38)
                k.dve.max(out=m8[:, 8:16], in_=sc2[:, :])
                k.dve.tensor_scalar(out=rr[:, 12:13], in0=m8[:, 15:16], scalar1=-1.0e8, scalar2=None, op0=ALU.max)
                k.dve.tensor_scalar(out=negs[:, :], in0=score[:, :], scalar1=rr[:, 12:13], scalar2=NEG, op0=ALU.is_lt, op1=ALU.mult)
                k.pe.transpose(out=ps_t[0:NSEL, :], in_=negs[:, :], identity=identb[:, :])
                k.dve.tensor_copy(out=negT4[:, :, :], in_=ps_t[0:NSEL, :].unsq(1).bc([NSEL, 4, 128]))
                qf = q[:, :, :].rearrange("p r t -> p (r t)")
                nf = negT4[:, :, :].rearrange("p r t -> p (r t)")
                po = po_sel[0]
                for kt in range(tt + 1):
                    pss = sbank()
                    k.pe.matmul(out=pss[:, :], lhsT=KsT[:, kt * 128:(kt + 1) * 128], rhs=qf, start=True, stop=False)
                    k.pe.matmul(out=pss[:, :], lhsT=eexp[:, kt * 128:(kt + 1) * 128], rhs=nf, start=False, stop=True)
                    e = enext()
                    k.act.activation(out=e[:, :], in_=pss[:, :], func=AF.Exp, scale=0.125)
                    if kt == tt:
                        ev = e[:, :].rearrange("p (r t) -> p r t", r=4)
                        k.pool.tensor_tensor(out=ev, in0=ev, in1=tri[:, 0:1, :].bc([128, 4, 128]), op=ALU.mult)
                    for r in range(4):
                        k.pe.matmul(out=po[:, r, :], lhsT=e[:, r * 128:(r + 1) * 128], rhs=Vs[:, kt, :], start=(kt == 0 and r == 0), stop=(kt == tt), skip_group_check=True)
                k.dve.tensor_scalar(out=rr[:, 0:4], in0=po[:, :, 64], scalar1=1e-30, scalar2=None, op0=ALU.max)
                k.dve.reciprocal(out=rr[:, 4:8], in_=rr[:, 0:4])
                k.dve.tensor_tensor(out=rr[:, 8:12], in0=rr[:, 4:8], in1=gv3[:, g * 4:(g + 1) * 4, 1], op=ALU.mult)
                k.dve.tensor_tensor(out=tmp4[:, :, :], in0=po[:, :, 0:64], in1=rr[:, 8:12].unsq(2).bc([128, 4, 64]), op=ALU.mult)
                k.dve.tensor_tensor(out=acc[:, :, :], in0=acc[:, :, :], in1=tmp4[:, :, :], op=ALU.add)
                po = po_win[0]
                k0 = max(0, tt - 4)
                for kt in range(k0, tt + 1):
                    pss = sbank()
                    k.pe.matmul(out=pss[:, :], lhsT=KwT[:, kt * 128:(kt + 1) * 128], rhs=qf, start=True, stop=True)
                    e = enext()
                    k.act.activation(out=e[:, :], in_=pss[:, :], func=AF.Exp, scale=0.125)
                    ev = e[:, :].rearrange("p (r t) -> p r t", r=4)
                    if kt == tt:
                        k.pool.tensor_tensor(out=ev, in0=ev, in1=tri[:, 0:1, :].bc([128, 4, 128]), op=ALU.mult)
                    elif kt == tt - 4:
                        k.pool.tensor_tensor(out=ev, in0=ev, in1=tri[:, 1:2, :].bc([128, 4, 128]), op=ALU.mult)
                    for r in range(4):
                        k.pe.matmul(out=po[:, r, :], lhsT=e[:, r * 128:(r + 1) * 128], rhs=Vw[:, kt, :], start=(kt == k0 and r == 0), stop=(kt == tt), skip_group_check=True)
                k.dve.tensor_scalar(out=rr[:, 0:4], in0=po[:, :, 64], scalar1=1e-30, scalar2=None, op0=ALU.max)
                k.dve.reciprocal(out=rr[:, 4:8], in_=rr[:, 0:4])
                k.dve.tensor_tensor(out=rr[:, 8:12], in0=rr[:, 4:8], in1=gv3[:, g * 4:(g + 1) * 4, 2], op=ALU.mult)
                k.dve.tensor_tensor(out=tmp4[:, :, :], in0=po[:, :, 0:64], in1=rr[:, 8:12].unsq(2).bc([128, 4, 64]), op=ALU.mult)
                k.dve.tensor_tensor(out=accb[:, :, :], in0=acc[:, :, :], in1=tmp4[:, :, :], op=ALU.add)
                k.dma(AO[t0:t0 + 128, 512 + g * 256:512 + (g + 1) * 256], accb[:, :, :].rearrange("p r d -> p (r d)"))


def layer_tail(k, L, layer, mix, w_out, h_in, h_out, y_out):
    Tn = L['Tn']; NT = L['NT']; identb = L['identb']; H1 = L['H1']; ACTT = L['ACTT']
    rmsnorm_T = L['rmsnorm_T']; load_w_bf16 = L['load_w_bf16']
    xp = L['xp']; xs = L['xs']
    groups = [(i * 512, min(512, Tn - i * 512)) for i in range((Tn + 511) // 512)] + [(Tn, 16)]
    with k.scope():
        Wo = k.sb('Wo', [128, 8, 1024], BF16); load_w_bf16(Wo, w_out, 1024, 1024)
        Wg = k.sb('Wg', [128, 8, DFF], BF16); load_w_bf16(Wg, L['ffn_g'][layer], 1024, DFF)
        Wu = k.sb('Wu', [128, 8, DFF], BF16); load_w_bf16(Wu, L['ffn_u'][layer], 1024, DFF)
        gbc = k.sb('gbc', [128, 1024]); k.dma(gbc[:, :], L['norm_ffn'][layer:layer + 1, :].pbc(128))
        mt = k.sb('mt', [128, 1024], BF16); mT = k.sb('mT', [128, 8, 128], BF16)
        ht = [k.sb('ht', [128, 1024]) for _ in range(2)]
        junk = k.sb('junk', [128, 1024], BF16); xn = k.sb('xn', [128, 1024], BF16); ss = k.sb('ss', [128, 4])
        xnT = k.sb('xnT', [128, 8, 512], BF16); xnT1 = k.sb('xnT1', [128, 8, 128], BF16)
        sg = k.sb('sg', [128, 512], BF16); at = [k.sb('at', [128, 512], BF16) for _ in range(2)]
        pT = k.ps('pT', [128, 8, 128], BF16)
        pA = [k.ps('pA', [128, 512]) for _ in range(2)]
        pG = [k.ps('pG', [128, 512]) for _ in range(2)]; pU = [k.ps('pU', [128, 512]) for _ in range(2)]
        ci = 0
        for (g0, gn) in groups:
            ntile = (gn + 127) // 128
            for j in range(ntile):
                r0 = g0 + j * 128; rows = min(128, gn - j * 128)
                h = ht[ci % 2]; ci += 1
                k.dma(mt[:rows, :], mix[r0:r0 + rows, :])
                if h_in is None:
                    k.dma(h[:rows, :], xp[r0:r0 + rows, :] if r0 < Tn else xs[0:16, :])
                else:
                    k.dma(h[:rows, :], h_in[r0:r0 + rows, :])
                for kk in range(8):
                    k.pe.transpose(out=pT[:, kk, :rows], in_=mt[:rows, kk * 128:(kk + 1) * 128], identity=identb[:rows, :rows])
                k.act.copy(out=mT[:, :, :rows], in_=pT[:, :, :rows])
                for c in range(2):
                    ps = pA[c]
                    for kk in range(8):
                        k.pe.matmul(out=ps[:rows, :], lhsT=mT[:, kk, :rows], rhs=Wo[:, kk, c * 512:(c + 1) * 512], start=(kk == 0), stop=(kk == 7))
                    k.dve.tensor_tensor(out=h[:rows, c * 512:(c + 1) * 512], in0=h[:rows, c * 512:(c + 1) * 512], in1=ps[:rows, :], op=ALU.add)
                k.dma(H1[r0:r0 + rows, :], h[:rows, :])
                rmsnorm_T(h, rows, gbc, xn, pT, xnT1, ss, junk)
                k.dve.tensor_copy(out=xnT[:, :, j * 128:j * 128 + rows], in_=xnT1[:, :, :rows])
            for hc in range(22):
                pg = pG[hc % 2]; pu = pU[hc % 2]
                for kk in range(8):
                    k.pe.matmul(out=pg[:, :gn], lhsT=Wg[:, kk, hc * 128:(hc + 1) * 128], rhs=xnT[:, kk, :gn], start=(kk == 0), stop=(kk == 7))
                for kk in range(8):
                    k.pe.matmul(out=pu[:, :gn], lhsT=Wu[:, kk, hc * 128:(hc + 1) * 128], rhs=xnT[:, kk, :gn], start=(kk == 0), stop=(kk == 7))
                a = at[hc % 2]
                k.act.activation(out=sg[:, :gn], in_=pg[:, :gn], func=AF.Silu)
                k.dve.tensor_tensor(out=a[:, :gn], in0=sg[:, :gn], in1=pu[:, :gn], op=ALU.mult)
                k.dma(ACTT[hc, :, g0:g0 + gn], a[:, :gn])
    with k.scope():
        Wd = k.sb('Wd', [128, 22, 1024], BF16); load_w_bf16(Wd, L['ffn_d'][layer], DFF, 1024)
        gbc = k.sb('gbc', [128, 1024])
        if y_out is not None:
            k.dma(gbc[:, :], L['norm_final'][0:1, :].pbc(128))
        aT = [k.sb('aT', [128, 22, 128], BF16) for _ in range(2)]
        ht = [k.sb('ht', [128, 1024]) for _ in range(2)]
        junk = k.sb('junk', [128, 1024]); ss = k.sb('ss', [128, 4]); yt = k.sb('yt', [128, 1024])
        pA = [k.ps('pA', [128, 512]) for _ in range(4)]
        ci = 0
        for ti, (r0, rows) in enumerate(L['tiles']):
            a = aT[ti % 2]; h = ht[ti % 2]
            k.dma(a[:, :, :rows], ACTT[:, :, r0:r0 + rows].rearrange("c p t -> p c t"))
            k.dma(h[:rows, :], H1[r0:r0 + rows, :])
            for c in range(2):
                ps = pA[ci % 4]; ci += 1
                for hc in range(22):
                    k.pe.matmul(out=ps[:rows, :], lhsT=a[:, hc, :rows], rhs=Wd[:, hc, c * 512:(c + 1) * 512], start=(hc == 0), stop=(hc == 21))
                k.dve.tensor_tensor(out=h[:rows, c * 512:(c + 1) * 512], in0=h[:rows, c * 512:(c + 1) * 512], in1=ps[:rows, :], op=ALU.add)
            if y_out is None:
                k.dma(H1[r0:r0 + rows, :], h[:rows, :])
            else:
                k.act.activation(out=junk[:rows, :], in_=h[:rows, :], func=AF.Square, accum_out=ss[:rows, 0:1])
                k.dve.tensor_scalar(out=ss[:rows, 1:2], in0=ss[:rows, 0:1], scalar1=1.0 / 1024, scalar2=1e-6, op0=ALU.mult, op1=ALU.add)
                k.act.activation(out=ss[:rows, 3:4], in_=ss[:rows, 1:2], func=AF.Sqrt)
                k.dve.reciprocal(out=ss[:rows, 2:3], in_=ss[:rows, 3:4])
                k.dve.scalar_tensor_tensor(out=yt[:rows, :], in0=h[:rows, :], scalar=ss[:rows, 2:3], in1=gbc[:rows, :], op0=ALU.mult, op1=ALU.mult)
                if r0 < Tn:
                    k.dma(y_out[0][r0:r0 + rows, :], yt[:rows, :])
                else:
                    k.dma(y_out[1][0:16, :], yt[:16, :])


def phase_proj1(k, L):
    Tn = L['Tn']; NT = L['NT']; identb = L['identb']; identf = L['identf']; H1 = L['H1']
    NBLK = L['NBLK']; NBLKP = L['NBLKP']
    rmsnorm_T = L['rmsnorm_T']; load_w_bf16 = L['load_w_bf16']
    QT2 = L['QT2']; KT2 = L['KT2']; VT2 = L['VT2']; NS2 = L['NS2']
    with k.scope():
        W1 = k.sb('W1', [128, 8, ODC], BF16); load_w_bf16(W1, L['w_in1'], 1024, ODC)
        gbc = k.sb('gbc', [128, 1024]); k.dma(gbc[:, :], L['norm_mix'][1:2, :].pbc(128))
        xt = [k.sb('xt', [128, 1024]) for _ in range(2)]
        junk = k.sb('junk', [128, 1024], BF16); xn = k.sb('xn', [128, 1024], BF16); ss = k.sb('ss', [128, 4])
        xnT = k.sb('xnT', [128, 8, 128], BF16)
        proj = [k.sb('proj', [128, ODC]) for _ in range(2)]
        cs = k.sb('cs', [128, 64])
        tmp = [k.sb('tmp%d' % i, [128, 16, 32]) for i in range(4)]
        qb = k.sb('qb', [128, 16, 64], BF16); kb = k.sb('kb', [128, 4, 64], BF16)
        vb = k.sb('vb', [128, 4, 65], BF16)
        k.op('dve', lambda e: e.memset(vb[:, :, :].ap, 1.0), (), (vb,))
        ones = k.sb('ones', [128, 1]); k.op('dve', lambda e: e.memset(ones[:, :].ap, 1.0), (), (ones,))
        qT = k.sb('qT', [64, 16, 128], BF16); kT = k.sb('kT', [64, 4, 128], BF16)
        qTf = k.sb('qTf', [64, 16, 128])
        meansT = k.sb('meansT', [64, 4, NBLKP])
        k.op('dve', lambda e: e.memset(meansT[:, :, :].ap, 0.0), (), (meansT,))
        gbm = k.sb('gbm', [128, 2, NBLKP])
        score = k.sb('score', [128, 16, NBLKP]); m8 = k.sb('m8', [128, 16, 8]); thr = k.sb('thr', [128, 16])
        negs = k.sb('negs', [128, 16, NBLKP]); negb = k.sb('negb', [128, 16, NBLKP], BF16)
        pT = k.ps('pT', [128, 8, 128], BF16)
        pA = [k.ps('pA', [128, 512]) for _ in range(2)]
        pQ = k.ps('pQ', [64, 8, 128], BF16)
        pK = k.ps('pK', [64, 4, 128], BF16)
        pQf = [k.ps('pQf', [64, 4, 128]) for _ in range(1)]
        pM = k.ps('pM', [64, 4, NBLKP])
        pGt = k.ps('pGt', [128, 16, NBLKP])
        ci = 0
        for ti, (r0, rows) in enumerate(L['tiles']):
            x = xt[ti % 2]; pj = proj[ti % 2]
            k.dma(x[:rows, :], H1[r0:r0 + rows, :])
            k.dma(cs[:rows, :], L['ropecs'][r0:r0 + rows, :])
            rmsnorm_T(x, rows, gbc, xn, pT, xnT, ss, junk)
            for c0 in range(0, ODC, 512):
                ps = pA[ci % 2]; ci += 1
                for kk in range(8):
                    k.pe.matmul(out=ps[:rows, :], lhsT=xnT[:, kk, :rows], rhs=W1[:, kk, c0:c0 + 512], start=(kk == 0), stop=(kk == 7))
                (k.act.copy if ci % 2 else k.dve.tensor_copy)(out=pj[:rows, c0:c0 + 512], in_=ps[:rows, :])
            cosb = lambda n: cs[:rows, 0:32].unsq(1).bc([rows, n, 32])
            sinb = lambda n: cs[:rows, 32:64].unsq(1).bc([rows, n, 32])
            for vi, (c0, n) in enumerate([(0, 16), (1024, 4)]):
                xv = pj[:rows, c0:c0 + n * 64].rearrange("p (h d) -> p h d", h=n)
                E1 = k.dve if vi == 0 else k.pool
                x1 = xv[:, :, 0:32]; x2 = xv[:, :, 32:64]
                t = [tt_[:rows, 0:n, :] for tt_ in tmp]
                E1.tensor_tensor(out=t[0], in0=x1, in1=cosb(n), op=ALU.mult)
                E1.tensor_tensor(out=t[1], in0=x2, in1=sinb(n), op=ALU.mult)
                E1.tensor_tensor(out=t[2], in0=x2, in1=cosb(n), op=ALU.mult)
                E1.tensor_tensor(out=t[3], in0=x1, in1=sinb(n), op=ALU.mult)
                E1.tensor_tensor(out=x1, in0=t[0], in1=t[1], op=ALU.subtract)
                E1.tensor_tensor(out=x2, in0=t[2], in1=t[3], op=ALU.add)
            qv = pj[:rows, 0:1024].rearrange("p (h d) -> p h d", h=16)
            kv_ = pj[:rows, 1024:1280].rearrange("p (h d) -> p h d", h=4)
            vv = pj[:rows, 1280:1536].rearrange("p (h d) -> p h d", h=4)
            k.dve.tensor_copy(out=qb[:rows, :, :], in_=qv)
            k.pool.tensor_copy(out=kb[:rows, :, :], in_=kv_)
            k.pool.tensor_copy(out=vb[:rows, :, 0:64], in_=vv)
            if r0 < Tn:
                k.dma(L['o_moba_p'][r0:r0 + rows, :], pj[:rows, 1024:1536])
            else:
                k.dma(L['o_moba_s'][0:16, :], pj[:16, 1024:1536])
            for hh in range(2):
                for h in range(8):
                    k.pe.transpose(out=pQ[:, h, :rows], in_=qb[:rows, hh * 8 + h, :], identity=identb[:rows, :rows])
                k.act.copy(out=qT[:, hh * 8:(hh + 1) * 8, :rows], in_=pQ[:, :, :rows])
            for h in range(4):
                k.pe.transpose(out=pK[:, h, :rows], in_=kb[:rows, h, :], identity=identb[:rows, :rows])
            k.dve.tensor_copy(out=kT[:, :, :rows], in_=pK[:, :, :rows])
            k.dma(QT2[:, :, r0:r0 + rows], qT[:, :, :rows])
            k.dma(KT2[:, :, r0:r0 + rows], kT[:, :, :rows])
            k.dma(VT2[r0:r0 + rows, :, :], vb[:rows, :, :])
            for hq in range(4):
                pq = pQf[0]
                for h in range(4):
                    k.pe.transpose(out=pq[:, h, :rows], in_=pj[:rows, (hq * 4 + h) * 64:(hq * 4 + h + 1) * 64], identity=identf[:rows, :rows])
                k.act.copy(out=qTf[:, hq * 4:(hq + 1) * 4, :rows], in_=pq[:, :, :rows])
            if r0 >= Tn:
                k.dma(L['QF2'][:, :, :], qTf[:, :, 0:16])
                continue
            blk = r0 // 256
            for h in range(16):
                k.pe.matmul(out=pGt[:rows, h, :], lhsT=qTf[:, h, :rows], rhs=meansT[:, h // 4, :], start=(h == 0), stop=(h == 15), skip_group_check=True)
            k.dma(gbm[:rows, :, :], L['GBM'][r0:r0 + rows, :, :])
            k.dve.tensor_tensor(out=score[:rows, :, :], in0=pGt[:rows, :, :], in1=gbm[:rows, 0:1, :].bc([rows, 16, NBLKP]), op=ALU.add)
            for h in range(16):
                k.dve.max(out=m8[:rows, h, :], in_=score[:rows, h, :])
            k.dve.tensor_scalar(out=thr[:rows, :], in0=m8[:rows, :, 2], scalar1=-1.0e8, scalar2=None, op0=ALU.max)
            k.dve.tensor_tensor(out=negs[:rows, :, :], in0=score[:rows, :, :], in1=thr[:rows, :].unsq(2).bc([rows, 16, NBLKP]), op=ALU.is_lt)
            k.dve.tensor_tensor(out=negs[:rows, :, :], in0=negs[:rows, :, :], in1=gbm[:rows, 1:2, :].bc([rows, 16, NBLKP]), op=ALU.mult)
            k.dve.tensor_scalar(out=negb[:rows, :, :], in0=negs[:rows, :, :], scalar1=NEG, scalar2=None, op0=ALU.mult)
            k.dma(NS2[r0:r0 + rows, :, :], negb[:rows, :, :])
            for g in range(4):
                first = (r0 % 256 == 0)
                k.pe.matmul(out=pM[:, g, blk:blk + 1], lhsT=pj[:rows, 1024 + g * 64:1024 + (g + 1) * 64], rhs=ones[:rows, 0:1],
                            start=(ti == 0 and g == 0), stop=not first, skip_group_check=True)
            if r0 % 256 == 128:
                k.dve.tensor_scalar(out=meansT[:, :, blk:blk + 1], in0=pM[:, :, blk:blk + 1], scalar1=1.0 / 2048, scalar2=None, op0=ALU.mult)


def phase_moba_prompt(k, L):
    Tn = L['Tn']; NT = L['NT']; NBLK = L['NBLK']; NBLKP = L['NBLKP']; identb = L['identb']
    QT2 = L['QT2']; KT2 = L['KT2']; VT2 = L['VT2']; NS2 = L['NS2']; MO = L['MO']
    with k.scope():
        tri = k.sb('tri', [128, 2, 128], BF16); k.dma(tri[:, :, :], L['TRIt'][:, :, :])
        eexp = k.sb('eexp', [NBLKP, Tn], BF16); k.dma(eexp[:, :], L['EEXP2'][0:NBLKP, 0:Tn])
        ps_s = [k.ps('ps_s', [128, 512]) for _ in range(3)]
        po_ = [k.ps('po', [128, 4, 65]) for _ in range(2)]
        ps_t = k.ps('ps_t', [NBLKP, 4, 128], BF16)
        K2 = k.sb('K2', [64, Tn], BF16); V2 = k.sb('V2', [128, NT, 65], BF16)
        q4 = [k.sb('q4', [64, 4, 128], BF16) for _ in range(2)]
        ns = [k.sb('ns', [128, 4, NBLKP], BF16) for _ in range(2)]
        negT4 = k.sb('negT4', [NBLKP, 4, 128], BF16)
        ebuf = [k.sb('ebuf', [128, 512], BF16) for _ in range(3)]
        rr = k.sb('rr', [128, 8]); accb = k.sb('accb', [128, 4, 64], BF16)
        si = 0; ei = 0
        for g in range(4):
            k.dma(K2[:, :], KT2[:, g, 0:Tn])
            k.dma(V2[:, :, :], VT2[0:Tn, g, :].rearrange("(kt p) c -> p kt c", p=128))
            for tt in range(NT):
                t0 = tt * 128
                q = q4[tt % 2]; n_ = ns[tt % 2]
                k.dma(q[:, :, :], QT2[:, g * 4:(g + 1) * 4, t0:t0 + 128])
                k.dma(n_[:, :, :], NS2[t0:t0 + 128, g * 4:(g + 1) * 4, :])
                for r in range(4):
                    k.pe.transpose(out=ps_t[:, r, :], in_=n_[:, r, :], identity=identb[:, :])
                k.dve.tensor_copy(out=negT4[:, :, :], in_=ps_t[:, :, :])
                qf = q[:, :, :].rearrange("p r t -> p (r t)")
                nf = negT4[:, :, :].rearrange("p r t -> p (r t)")
                po = po_[tt % 2]
                for kt in range(tt + 1):
                    pss = ps_s[si % 3]; si += 1
                    k.pe.matmul(out=pss[:, :], lhsT=K2[:, kt * 128:(kt + 1) * 128], rhs=qf, start=True, stop=False)
                    k.pe.matmul(out=pss[:, :], lhsT=eexp[:, kt * 128:(kt + 1) * 128], rhs=nf, start=False, stop=True)
                    e = ebuf[ei % 3]; ei += 1
                    k.act.activation(out=e[:, :], in_=pss[:, :], func=AF.Exp, scale=0.125)
                    if kt == tt:
                        ev = e[:, :].rearrange("p (r t) -> p r t", r=4)
                        k.pool.tensor_tensor(out=ev, in0=ev, in1=tri[:, 0:1, :].bc([128, 4, 128]), op=ALU.mult)
                    for r in range(4):
                        k.pe.matmul(out=po[:, r, :], lhsT=e[:, r * 128:(r + 1) * 128], rhs=V2[:, kt, :], start=(kt == 0 and r == 0), stop=(kt == tt), skip_group_check=True)
                k.dve.tensor_scalar(out=rr[:, 0:4], in0=po[:, :, 64], scalar1=1e-30, scalar2=None, op0=ALU.max)
                k.dve.reciprocal(out=rr[:, 4:8], in_=rr[:, 0:4])
                k.dve.tensor_tensor(out=accb[:, :, :], in0=po[:, :, 0:64], in1=rr[:, 4:8].unsq(2).bc([128, 4, 64]), op=ALU.mult)
                k.dma(MO[t0:t0 + 128, g * 256:(g + 1) * 256], accb[:, :, :].rearrange("p r d -> p (r d)"))


def page_indices(k, L):
    P = L['P']
    pti = k.sb('pti', [128, 4 * P], I32); ptf = k.sb('ptf', [128, 4 * P]); io = k.sb('io', [128, 1])
    idx = k.sb('idx', [128, 4 * P], I32)
    k.dma(pti[:, :], L['ptab'][:, :].rearrange("b p -> (b p)").unsq(0).pbc(128) if False else L['ptab'][:, :].rearrange("(o b) p -> o (b p)", o=1).pbc(128))
    k.dma(io[:, :], L['IOTA'][:, :])
    k.dve.tensor_copy(out=ptf[:, :], in_=pti[:, :])
    k.dve.tensor_scalar(out=ptf[:, :], in0=ptf[:, :], scalar1=128.0, scalar2=None, op0=ALU.mult)
    k.dve.tensor_tensor(out=ptf[:, :], in0=ptf[:, :], in1=io[:, 0:1].bc([128, 4 * P]), op=ALU.add)
    k.dve.tensor_copy(out=idx[:, :], in_=ptf[:, :])
    return idx


def gather_page(k, pg, cache, idx, col):
    k.raw16('pool', lambda e: e.indirect_dma_start(out=pg[:, :].ap, out_offset=None, in_=cache[:, :].ap,
                                                   in_offset=bass.IndirectOffsetOnAxis(ap=idx[:, col:col + 1].ap, axis=0)),
            reads=[cache.t if isinstance(cache, V) else cache, idx], writes=[pg])


def phase_nsa_sample(k, L):
    Tn = L['Tn']; P = L['P']; LP = L['LP']; NSELS = L['NSELS']; NBS = L['NBS']; NBTS = L['NBTS']
    QT = L['QT']; KT = L['KT']; VT = L['VT']; GT = L['GT']; AO = L['AO']; identb = L['identb']; identf = L['identf']
    cache = L['cache_nsa']; st_win = L['st_win']
    NJ = LP // 64
    with k.scope():
        ps_s = [k.ps('ps_s', [128, 512]) for _ in range(1)]
        W = load_cmp_weights(k, L, ps_s[0])
        idx = page_indices(k, L)
        eexp = k.sb('eexp', [NJ, LP], BF16); k.dma(eexp[:, :], L['EEXP'][0:NJ, 0:LP])
        smk = k.sb('smk', [128, 2, 16], BF16); k.dma(smk[:, :, :], L['SMK'][:, :, :])
        fbs = k.sb('fbs', [4, NSELS]); k.dma(fbs[:, :], L['FBs'][:, :])
        pX = [k.ps('pX', [64, 4, 128]) for _ in range(1)]
        pXb = [k.ps('pXb', [64, 8, 128], BF16) for _ in range(2)]
        stg = [k.sb('stg', [128, 384], BF16) for _ in range(2)]
        po_a = [k.ps('po_a', [4, 2, 65 + NSELS]) for _ in range(2)]
        po_b = k.ps('po_b', [4, 4, 65])
        ps_t = k.ps('ps_t', [128, 16], BF16)
        si = [0]

        def sbank():
            b = ps_s[si[0] % 1]; si[0] += 1
            return b

        pg = [k.sb('pg', [128, 512]) for _ in range(3)]
        KcTs = k.sb('KcTs', [64, 2, LP], BF16); VcTs = k.sb('VcTs', [64, 2, LP], BF16); KsTs = k.sb('KsTs', [64, 2, LP], BF16)
        Vss = k.sb('Vss', [128, P, 2, 65], BF16)
        k.op('dve', lambda e: e.memset(Vss[:, :, :, 64:65].ap, 1.0), (), (Vss,))
        KwTs = k.sb('KwTs', [64, 2, 512], BF16); Vws = k.sb('Vws', [128, 4, 2, 65], BF16)
        k.op('dve', lambda e: e.memset(Vws[:, :, :, 64:65].ap, 1.0), (), (Vws,))
        wt = [k.sb('wt', [128, 256]) for _ in range(2)]
        KcCs = k.sb('KcCs', [64, NBTS * 128], BF16)
        VcCs = k.sb('VcCs', [128, NBTS, 65 + NSELS], BF16)
        k.dma(VcCs[:, :, 65:65 + NSELS], L['OVs'][:, :, :])
        k.op('dve', lambda e: e.memset(VcCs[:, :, 64:65].ap, 1.0), (), (VcCs,))
        hx = k.sb('hx', [64, 512]); ta = k.sb('ta', [64, 512]); tb = k.sb('tb', [64, 512])
        hk = k.sb('hk', [64, 512], BF16); hv_ = k.sb('hv_', [64, 512], BF16)
        qs_all = k.sb('qs_all', [64, 8, 16], BF16); k.dma(qs_all[:, :, :], QT[:, :, Tn:Tn + 16])
        knew = k.sb('knew', [64, 6, 16], BF16); k.dma(knew[:, :, :], KT[:, :, Tn:Tn + 16])
        vnew = [k.sb('vnew', [4, 2, 2, 65], BF16) for _ in range(2)]
        gts = [k.sb('gts', [4, 24]) for _ in range(2)]
        q16 = k.sb('q16', [64, 4, 4], BF16)
        ebuf = [k.sb('ebuf', [128, 16], BF16) for _ in range(3)]
        ei = [0]

        def enext():
            b = ebuf[ei[0] % 3]; ei[0] += 1
            return b

        acc = k.sb('acc', [4, 8, 64]); accb = k.sb('accb', [4, 8, 64], BF16); tmp4 = k.sb('tmp4', [4, 4, 64])
        rr = k.sb('rr', [4, 16]); imp = k.sb('imp', [4, NSELS]); score = k.sb('score', [4, NSELS]); sc2 = k.sb('sc2', [4, NSELS])
        m8 = k.sb('m8', [4, 16]); negs = k.sb('negs', [4, NSELS], BF16)
        negT16 = k.sb('negT16', [NJ, 4, 4], BF16)
        SD = L['cfg'].get('sdbg', 99)
        for bl in range(4 if SD >= 99 else 1):
            vn = vnew[bl % 2]; gt_ = gts[bl % 2]
            if SD < 1:
                break
            import os
            if os.environ.get('SKIPVN') != '1':
                k.dma(vn[:, :, :, :], VT[Tn + bl * 4:Tn + bl * 4 + 4, :, :, :])
                k.dma(gt_[:, :], GT[Tn + bl * 4:Tn + bl * 4 + 4, :])
            gv3 = gt_[:, :].rearrange("p (h b) -> p h b", b=3)
            for lp in range(P if os.environ.get('SKIPG') != '1' else 0):
                pgt = pg[lp % 3]
                gather_page(k, pgt, cache, idx, bl * P + lp)
                sg_ = stg[lp % 2]
                k.dve.tensor_copy(out=sg_[:, :], in_=pgt[:, 0:384])
                k.pool.tensor_copy(out=Vss[:, lp, :, 0:64], in_=pgt[:, 384:512].rearrange("p (g d) -> p g d", g=2))
                px0 = pXb[lp % 2]
                for i in range(6):
                    k.pe.transpose(out=px0[:, i, :], in_=sg_[:, i * 64:(i + 1) * 64], identity=identb[:, :])
                k.dve.tensor_copy(out=KcTs[:, :, lp * 128:(lp + 1) * 128], in_=px0[:, 0:2, :])
                k.dve.tensor_copy(out=VcTs[:, :, lp * 128:(lp + 1) * 128], in_=px0[:, 2:4, :])
                k.dve.tensor_copy(out=KsTs[:, :, lp * 128:(lp + 1) * 128], in_=px0[:, 4:6, :])
            if SD < 2:
                break
            for wi in range(4):
                w_ = wt[wi % 2]
                k.dma(w_[:, :], st_win[bl, wi * 128:(wi + 1) * 128, :])
                px1 = pX[0]
                for g in range(2):
                    k.pe.transpose(out=px1[:, g, :], in_=w_[:, g * 64:(g + 1) * 64], identity=identf[:, :])
                k.dve.tensor_copy(out=KwTs[:, :, wi * 128:(wi + 1) * 128], in_=px1[:, 0:2, :])
                k.pool.tensor_copy(out=Vws[:, wi, :, 0:64], in_=w_[:, 128:256].rearrange("p (g d) -> p g d", g=2))
            if SD < 3:
                break
            for g in range(2):
                pb = sbank()
                compress(k, W, 0, KcTs[:, g, :], NBS, pb, hx, ta, tb, hk)
                pb2 = sbank()
                k.pe.matmul(out=pb2[0:64, 0:NBS], lhsT=W[0][1][:, :], rhs=hk[:, 0:NBS], start=True, stop=True)
                k.dve.tensor_copy(out=KcCs[:, 0:NBS], in_=pb2[0:64, 0:NBS])
                pb = sbank()
                compress(k, W, 1, VcTs[:, g, :], NBS, pb, hx, ta, tb, hv_)
                for ni in range(NBTS):
                    nn = min(128, NBS - ni * 128)
                    pb3 = sbank()
                    k.pe.matmul(out=pb3[:nn, 0:64], lhsT=hv_[:, ni * 128:ni * 128 + nn], rhs=W[1][1][:, :], start=True, stop=True)
                    k.dve.tensor_copy(out=VcCs[:nn, ni, 0:64], in_=pb3[:nn, 0:64])
                if SD < 4:
                    continue
                k.dve.tensor_copy(out=q16[:, :, :], in_=qs_all[:, g * 4:(g + 1) * 4, bl * 4:(bl + 1) * 4])
                qf = q16[:, :, :].rearrange("p r t -> p (r t)")
                for ni in range(NBTS):
                    nn = min(128, NBS - ni * 128)
                    pss = sbank()
                    k.pe.matmul(out=pss[:nn, 0:16], lhsT=KcCs[:, ni * 128:ni * 128 + nn], rhs=qf, start=True, stop=True)
                    e = enext()
                    k.act.activation(out=e[:nn, :], in_=pss[:nn, 0:16], func=AF.Exp, scale=0.125)
                    for r in range(4):
                        k.pe.matmul(out=po_a[r // 2][:, r % 2, :], lhsT=e[:nn, r * 4:(r + 1) * 4], rhs=VcCs[:nn, ni, :],
                                    start=(ni == 0 and r % 2 == 0), stop=(ni == NBTS - 1), skip_group_check=True)
                for r in range(4):
                    po = po_a[r // 2][:, r % 2, :]
                    k.dve.tensor_scalar(out=rr[:, r:r + 1], in0=po[:, 64:65], scalar1=1e-30, scalar2=None, op0=ALU.max)
                    k.dve.reciprocal(out=rr[:, 4 + r:5 + r], in_=rr[:, r:r + 1])
                    k.dve.tensor_tensor(out=rr[:, 8 + r:9 + r], in0=rr[:, 4 + r:5 + r], in1=gv3[:, g * 4 + r, 0:1], op=ALU.mult)
                    k.dve.tensor_scalar(out=acc[:, g * 4 + r, :], in0=po[:, 0:64], scalar1=rr[:, 8 + r:9 + r], scalar2=None, op0=ALU.mult)
                    if r == 0:
                        k.dve.tensor_scalar(out=imp[:, :], in0=po[:, 65:65 + NSELS], scalar1=rr[:, 4 + r:5 + r], scalar2=None, op0=ALU.mult)
                    else:
                        k.dve.scalar_tensor_tensor(out=imp[:, :], in0=po[:, 65:65 + NSELS], scalar=rr[:, 4 + r:5 + r], in1=imp[:, :], op0=ALU.mult, op1=ALU.add)
                if SD < 5:
                    continue
                k.dve.tensor_tensor(out=score[:, :], in0=imp[:, :], in1=fbs[:, :], op=ALU.add)
                k.dve.max(out=m8[:, 0:8], in_=score[:, :])
                k.dve.match_replace(out=sc2[:, :], in_to_replace=m8[:, 0:8], in_values=score[:, :], imm_value=-3.0e38)
                k.dve.max(out=m8[:, 8:16], in_=sc2[:, :])
                k.dve.tensor_scalar(out=rr[:, 12:13], in0=m8[:, 15:16], scalar1=-1.0e8, scalar2=None, op0=ALU.max)
                k.dve.tensor_scalar(out=negs[:, :], in0=score[:, :], scalar1=rr[:, 12:13], scalar2=NEG, op0=ALU.is_lt, op1=ALU.mult)
                k.pe.transpose(out=ps_t[0:NJ, 0:4], in_=negs[:, 0:NJ], identity=identb[0:4, 0:4])
                k.dve.tensor_copy(out=negT16[:, :, :], in_=ps_t[0:NJ, 0:4].unsq(1).bc([NJ, 4, 4]))
                nf = negT16[:, :, :].rearrange("p r t -> p (r t)")
                if SD < 6:
                    continue
                for kt in range(P + 1):
                    pss = sbank()
                    e = enext()
                    if kt < P:
                        k.pe.matmul(out=pss[:, 0:16], lhsT=KsTs[:, g, kt * 128:(kt + 1) * 128], rhs=qf, start=True, stop=False)
                        k.pe.matmul(out=pss[:, 0:16], lhsT=eexp[:, kt * 128:(kt + 1) * 128], rhs=nf, start=False, stop=True)
                        k.act.activation(out=e[:, :], in_=pss[:, 0:16], func=AF.Exp, scale=0.125)
                        nk = 128; vv = Vss[:, kt, g, :]
                    else:
                        k.pe.matmul(out=pss[0:4, 0:16], lhsT=knew[:, 2 + g, bl * 4:(bl + 1) * 4], rhs=qf, start=True, stop=True)
                        k.act.activation(out=e[0:4, :], in_=pss[0:4, 0:16], func=AF.Exp, scale=0.125)
                        k.pool.tensor_tensor(out=e[0:4, :], in0=e[0:4, :], in1=smk[0:4, 1, :], op=ALU.mult)
                        nk = 4; vv = vn[:, 0, g, :]
                    for r in range(4):
                        k.pe.matmul(out=po_b[:, r, :], lhsT=e[:nk, r * 4:(r + 1) * 4], rhs=vv[:nk] if nk == 128 else vv, start=(kt == 0 and r == 0), stop=(kt == P), skip_group_check=True)
                k.dve.tensor_scalar(out=rr[:, 0:4], in0=po_b[:, :, 64], scalar1=1e-30, scalar2=None, op0=ALU.max)
                k.dve.reciprocal(out=rr[:, 4:8], in_=rr[:, 0:4])
                k.dve.tensor_tensor(out=rr[:, 8:12], in0=rr[:, 4:8], in1=gv3[:, g * 4:(g + 1) * 4, 1], op=ALU.mult)
                k.dve.tensor_tensor(out=tmp4[:, :, :], in0=po_b[:, :, 0:64], in1=rr[:, 8:12].unsq(2).bc([4, 4, 64]), op=ALU.mult)
                k.dve.tensor_tensor(out=acc[:, g * 4:(g + 1) * 4, :], in0=acc[:, g * 4:(g + 1) * 4, :], in1=tmp4[:, :, :], op=ALU.add)
                if SD < 7:
                    continue
                for kt in range(5):
                    pss = sbank()
                    e = enext()
                    if kt < 4:
                        k.pe.matmul(out=pss[:, 0:16], lhsT=KwTs[:, g, kt * 128:(kt + 1) * 128], rhs=qf, start=True, stop=True)
                        k.act.activation(out=e[:, :], in_=pss[:, 0:16], func=AF.Exp, scale=0.125)
                        if kt == 0:
                            k.pool.tensor_tensor(out=e[:, :], in0=e[:, :], in1=smk[:, 0, :], op=ALU.mult)
                        nk = 128; vv = Vws[:, kt, g, :]
                    else:
                        k.pe.matmul(out=pss[0:4, 0:16], lhsT=knew[:, 4 + g, bl * 4:(bl + 1) * 4], rhs=qf, start=True, stop=True)
                        k.act.activation(out=e[0:4, :], in_=pss[0:4, 0:16], func=AF.Exp, scale=0.125)
                        k.pool.tensor_tensor(out=e[0:4, :], in0=e[0:4, :], in1=smk[0:4, 1, :], op=ALU.mult)
                        nk = 4; vv = vn[:, 1, g, :]
                    for r in range(4):
                        k.pe.matmul(out=po_b[:, r, :], lhsT=e[:nk, r * 4:(r + 1) * 4], rhs=vv, start=(kt == 0 and r == 0), stop=(kt == 4), skip_group_check=True)
                k.dve.tensor_scalar(out=rr[:, 0:4], in0=po_b[:, :, 64], scalar1=1e-30, scalar2=None, op0=ALU.max)
                k.dve.reciprocal(out=rr[:, 4:8], in_=rr[:, 0:4])
                k.dve.tensor_tensor(out=rr[:, 8:12], in0=rr[:, 4:8], in1=gv3[:, g * 4:(g + 1) * 4, 2], op=ALU.mult)
                k.dve.tensor_tensor(out=tmp4[:, :, :], in0=po_b[:, :, 0:64], in1=rr[:, 8:12].unsq(2).bc([4, 4, 64]), op=ALU.mult)
                k.dve.tensor_tensor(out=acc[:, g * 4:(g + 1) * 4, :], in0=acc[:, g * 4:(g + 1) * 4, :], in1=tmp4[:, :, :], op=ALU.add)
            k.dve.tensor_copy(out=accb[:, :, :], in_=acc[:, :, :])
            k.dma(AO[Tn + bl * 4:Tn + bl * 4 + 4, 512:1024], accb[:, :, :].rearrange("p h d -> p (h d)"))


def phase_moba_sample(k, L):
    Tn = L['Tn']; P = L['P']; LP = L['LP']; NBLKS = L['NBLKS']; NBLKSP = L['NBLKSP']
    QT2 = L['QT2']; KT2 = L['KT2']; VT2 = L['VT2']; MO = L['MO']; identb = L['identb']; identf = L['identf']
    cache = L['cache_moba']
    with k.scope():
        idx = page_indices(k, L)
        eexp = k.sb('eexp', [NBLKSP, LP], BF16); k.dma(eexp[:, :], L['EEXP2'][0:NBLKSP, 0:LP])
        smk = k.sb('smk', [128, 2, 16], BF16); k.dma(smk[:, :, :], L['SMK'][:, :, :])
        ones = k.sb('ones', [128, 1]); k.op('dve', lambda e: e.memset(ones[:, :].ap, 1.0), (), (ones,))
        ps_s = [k.ps('ps_s', [128, 512]) for _ in range(2)]
        pXb = [k.ps('pXb', [64, 4, 128], BF16) for _ in range(2)]
        stg = [k.sb('stg', [128, 256], BF16) for _ in range(2)]; stf = [k.sb('stf', [128, 256]) for _ in range(2)]
        pM = k.ps('pM', [64, 4, NBLKSP])
        pGt = k.ps('pGt', [4, 16, NBLKSP])
        po_b = k.ps('po_b', [4, 4, 65])
        ps_t = k.ps('ps_t', [NBLKSP, 16, 4], BF16)
        pg = [k.sb('pg', [128, 512]) for _ in range(3)]
        K2s = k.sb('K2s', [64, 4, LP], BF16); V2s = k.sb('V2s', [128, P, 4, 65], BF16)
        k.op('dve', lambda e: e.memset(V2s[:, :, :, 64:65].ap, 1.0), (), (V2s,))
        meansT = k.sb('meansT', [64, 4, NBLKSP])
        qf32 = k.sb('qf32', [64, 16, 16]); k.dma(qf32[:, :, :], L['QF2'][:, :, :])
        qs_all = k.sb('qs_all', [64, 16, 16], BF16); k.dma(qs_all[:, :, :], QT2[:, :, Tn:Tn + 16])
        knew = k.sb('knew', [64, 4, 16], BF16); k.dma(knew[:, :, :], KT2[:, :, Tn:Tn + 16])
        vnew = [k.sb('vnew', [4, 4, 65], BF16) for _ in range(2)]
        q16 = k.sb('q16', [64, 4, 4], BF16)
        score = k.sb('score', [4, 16, NBLKSP]); m8 = k.sb('m8', [4, 16, 8]); thr = k.sb('thr', [4, 16])
        negs = k.sb('negs', [4, 16, NBLKSP]); negb = k.sb('negb', [4, 16, NBLKSP], BF16)
        negT = k.sb('negT', [NBLKSP, 16, 4], BF16)
        ebuf = [k.sb('ebuf', [128, 16], BF16) for _ in range(3)]
        rr = k.sb('rr', [4, 8]); accb = k.sb('accb', [4, 16, 64], BF16)
        k.op('dve', lambda e: e.memset(score[:, :, :].ap, -2.0e9), (), (score,))
        si = 0; ei = 0
        for bl in range(4):
            vn = vnew[bl % 2]
            k.dma(vn[:, :, :], VT2[Tn + bl * 4:Tn + bl * 4 + 4, :, :])
            for lp in range(P):
                pgt = pg[lp % 3]
                gather_page(k, pgt, cache, idx, bl * P + lp)
                sg_ = stg[lp % 2]; sf_ = stf[lp % 2]
                k.dve.tensor_copy(out=sg_[:, :], in_=pgt[:, 0:256])
                k.dve.tensor_copy(out=sf_[:, :], in_=pgt[:, 0:256])
                k.pool.tensor_copy(out=V2s[:, lp, :, 0:64], in_=pgt[:, 256:512].rearrange("p (g d) -> p g d", g=4))
                px = pXb[lp % 2]
                for g in range(4):
                    k.pe.transpose(out=px[:, g, :], in_=sg_[:, g * 64:(g + 1) * 64], identity=identb[:, :])
                k.dve.tensor_copy(out=K2s[:, :, lp * 128:(lp + 1) * 128], in_=px[:, :, :])
                blk = lp // 2
                for g in range(4):
                    k.pe.matmul(out=pM[:, g, blk:blk + 1], lhsT=sf_[:, g * 64:(g + 1) * 64], rhs=ones[:, 0:1],
                                start=(lp == 0 and g == 0), stop=(lp % 2 == 1), skip_group_check=True)
            k.dve.tensor_scalar(out=meansT[:, :, 0:NBLKS], in0=pM[:, :, 0:NBLKS], scalar1=1.0 / 2048, scalar2=None, op0=ALU.mult)
            for h in range(16):
                k.pe.matmul(out=pGt[:, h, 0:NBLKS], lhsT=qf32[:, h, bl * 4:(bl + 1) * 4], rhs=meansT[:, h // 4, 0:NBLKS], start=(h == 0), stop=(h == 15), skip_group_check=True)
            k.dve.tensor_copy(out=score[:, :, 0:NBLKS], in_=pGt[:, :, 0:NBLKS])
            for h in range(16):
                k.dve.max(out=m8[:, h, :], in_=score[:, h, :])
            k.dve.tensor_scalar(out=thr[:, :], in0=m8[:, :, 2], scalar1=-1.0e8, scalar2=None, op0=ALU.max)
            k.dve.tensor_tensor(out=negs[:, :, :], in0=score[:, :, :], in1=thr[:, :].unsq(2).bc([4, 16, NBLKSP]), op=ALU.is_lt)
            k.dve.tensor_scalar(out=negb[:, :, :], in0=negs[:, :, :], scalar1=NEG, scalar2=None, op0=ALU.mult)
            for h in range(16):
                k.pe.transpose(out=ps_t[:, h, :], in_=negb[:, h, :], identity=identb[0:4, 0:4])
            k.dve.tensor_copy(out=negT[:, :, :], in_=ps_t[:, :, :])
            for g in range(4):
                k.dve.tensor_copy(out=q16[:, :, :], in_=qs_all[:, g * 4:(g + 1) * 4, bl * 4:(bl + 1) * 4])
                qf = q16[:, :, :].rearrange("p r t -> p (r t)")
                nf = negT[:, g * 4:(g + 1) * 4, :].rearrange("p r t -> p (r t)")
                for kt in range(P + 1):
                    pss = ps_s[si % 2]; si += 1
                    e = ebuf[ei % 3]; ei += 1
                    if kt < P:
                        k.pe.matmul(out=pss[:, 0:16], lhsT=K2s[:, g, kt * 128:(kt + 1) * 128], rhs=qf, start=True, stop=False)
                        k.pe.matmul(out=pss[:, 0:16], lhsT=eexp[:, kt * 128:(kt + 1) * 128], rhs=nf, start=False, stop=True)
                        k.act.activation(out=e[:, :], in_=pss[:, 0:16], func=AF.Exp, scale=0.125)
                        nk = 128; vv = V2s[:, kt, g, :]
                    else:
                        k.pe.matmul(out=pss[0:4, 0:16], lhsT=knew[:, g, bl * 4:(bl + 1) * 4], rhs=qf, start=True, stop=True)
                        k.act.activation(out=e[0:4, :], in_=pss[0:4, 0:16], func=AF.Exp, scale=0.125)
                        k.pool.tensor_tensor(out=e[0:4, :], in0=e[0:4, :], in1=smk[0:4, 1, :], op=ALU.mult)
                        nk = 4; vv = vn[:, g, :]
                    for r in range(4):
                        k.pe.matmul(out=po_b[:, r, :], lhsT=e[:nk, r * 4:(r + 1) * 4], rhs=vv, start=(kt == 0 and r == 0), stop=(kt == P), skip_group_check=True)
                k.dve.tensor_scalar(out=rr[:, 0:4], in0=po_b[:, :, 64], scalar1=1e-30, scalar2=None, op0=ALU.max)
                k.dve.reciprocal(out=rr[:, 4:8], in_=rr[:, 0:4])
                k.dve.tensor_tensor(out=accb[:, g * 4:(g + 1) * 4, :], in0=po_b[:, :, 0:64], in1=rr[:, 4:8].unsq(2).bc([4, 4, 64]), op=ALU.mult)
            k.dma(MO[Tn + bl * 4:Tn + bl * 4 + 4, :], accb[:, :, :].rearrange("p h d -> p (h d)"))


def host_consts(cfg):
    Tn, P = cfg['T'], cfg['P']
    TR = Tn + 128
    LP = P * 128
    pos = np.concatenate([np.arange(Tn), np.tile(LP + np.arange(4), 4), np.zeros(112)]).astype(np.float32)
    half = 32
    inv = (10000.0 ** (-np.arange(half, dtype=np.float32) / half)).astype(np.float32)
    ang = pos[:, None] * inv[None, :]
    ropecs = np.concatenate([np.cos(ang), np.sin(ang)], axis=1).astype(np.float32)
    return {
        'ropecs': ropecs,
        'identb': np.eye(128, dtype=np.float32).astype(ml_dtypes.bfloat16),
        'identf': np.eye(128, dtype=np.float32),
        **nsa_consts(Tn, LP),
        'masks64': np.stack([np.triu(np.ones((64, 64), np.float32)), np.triu(np.ones((64, 64), np.float32), 1), np.tril(np.ones((64, 64), np.float32), -1)], axis=1),
    }


def nsa_consts(Tn, LP):
    bf = ml_dtypes.bfloat16
    NBLK = Tn // 256; NBLKP = max(8, NBLK)
    tq = np.arange(Tn)[:, None]; nb = np.arange(NBLKP)[None, :]
    gb0 = np.where(nb < tq // 256, 0.0, -1.0e9).astype(np.float32)
    gb1 = (nb != tq // 256).astype(np.float32)
    GBM = np.stack([gb0, gb1], axis=1)
    Wd_ = max(Tn, LP)
    EE2 = (np.arange(Wd_)[None, :] // 256 == np.arange(64)[:, None]).astype(np.float32).astype(bf)
    NSEL = Tn // 64; NB = Tn // 16 - 1; NBT = (NB + 127) // 128
    t = np.arange(Tn)[:, None]; j = np.arange(NSEL)[None, :]
    cur = t // 64
    forced = ((j == cur) | (j == cur - 1) | (j == 0)).astype(np.float32)
    FB = np.where(j <= cur, 100.0 * forced, -1.0e9).astype(np.float32)
    n = np.arange(NBT * 128)[:, None]
    lo = np.maximum(n * 16, j * 64); hi = np.minimum(n * 16 + 32, (j + 1) * 64)
    ov = (np.clip(hi - lo, 0, None).astype(np.float32) / 32.0)
    ov[NB:] = 0
    OV = ov.reshape(NBT, 128, NSEL).transpose(1, 0, 2).astype(bf)
    nl = np.arange(128)[:, None, None]; idx = np.arange(17)[None, :, None]; tl = np.arange(128)[None, None, :]
    CM = (16 * nl + 31 - 128 * idx <= tl).astype(np.float32).astype(bf)
    s_ = np.arange(128)[:, None]; t_ = np.arange(128)[None, :]
    TRI = np.stack([(s_ <= t_), (s_ > t_)], axis=1).astype(np.float32).astype(bf)
    W = max(Tn, LP)
    EE = (np.arange(W)[None, :] // 64 == np.arange(128)[:, None]).astype(np.float32).astype(bf)
    NSELS = LP // 64 + 1; NBS = LP // 16 - 1; NBTS = (NBS + 127) // 128
    js = np.arange(NSELS)[None, :]
    FBs = np.tile((100.0 * ((js == 0) | (js == NSELS - 2) | (js == NSELS - 1))).astype(np.float32), (4, 1))
    ns_ = np.arange(NBTS * 128)[:, None]
    lo = np.maximum(ns_ * 16, js * 64); hi = np.minimum(ns_ * 16 + 32, (js + 1) * 64)
    ovs = (np.clip(hi - lo, 0, None).astype(np.float32) / 32.0); ovs[NBS:] = 0
    OVs = ovs.reshape(NBTS, 128, NSELS).transpose(1, 0, 2).astype(bf)
    i_ = np.arange(128)[:, None]; tq_ = np.tile(np.arange(4), 4)[None, :]
    SMK = np.stack([(i_ > tq_), (i_ <= tq_)], axis=1).astype(np.float32).astype(bf)
    return {'FBt': FB, 'OVt': OV, 'CMt': CM, 'TRIt': TRI, 'EEXP': EE, 'GBM': GBM, 'EEXP2': EE2,
            'FBs': FBs, 'OVs': OVs, 'SMK': SMK, 'IOTA': np.arange(128, dtype=np.float32).reshape(128, 1)}


def make_in_maps(cfg, inputs, ncores):
    Tn, P = cfg['T'], cfg['P']
    hc = host_consts(cfg)
    B = inputs['x_prompt'].shape[0]
    maps = []
    for c in range(ncores):
        b = c % B
        sl = slice(4 * c, 4 * c + 4)
        m = {
            'xp': np.ascontiguousarray(inputs['x_prompt'][b]),
            'xs': np.ascontiguousarray(inputs['x_sample'][sl]).reshape(16, 1024),
            'st_win': np.ascontiguousarray(inputs['state_win_kv'][0, sl]).reshape(4, 512, 256),
            'st_wkv': np.ascontiguousarray(inputs['state_wkv'][0, sl]),
            'st_shift': np.ascontiguousarray(inputs['state_shift'][0, sl]),
            'ptab': np.ascontiguousarray(inputs['page_table'][sl]).astype(np.int32),
            'norm_mix': inputs['norm_mix'], 'norm_ffn': inputs['norm_ffn'], 'norm_final': inputs['norm_final'].reshape(1, 1024),
            'w_in0': inputs['even_w_in'][0], 'w_out0': inputs['even_w_out'][0],
            'gate_b': inputs['nsa_gate_b'][0].reshape(1, 24),
            'cache_nsa': inputs['cache_nsa_kv'][0].reshape(-1, 512), 'cache_moba': inputs['cache_moba_kv'][0].reshape(-1, 512),
            'w_in1': inputs['odd_w_in'][0], 'w_out1': inputs['odd_w_out'][0],
            'ffn_g0': inputs['ffn_w_gate'][0], 'ffn_g1': inputs['ffn_w_gate'][1], 'ffn_u0': inputs['ffn_w_up'][0], 'ffn_u1': inputs['ffn_w_up'][1],
            'ffn_d0': inputs['ffn_w_down'][0], 'ffn_d1': inputs['ffn_w_down'][1],
            'cmp_w1': inputs['nsa_cmp_w1'][0], 'cmp_pe': inputs['nsa_cmp_pe'][0], 'cmp_w2': inputs['nsa_cmp_w2'][0],
            'rw_mu': inputs['rwkv_mu'][0].reshape(1, RWC),
            'rw_vec': np.stack([inputs['rwkv_w0'][0], inputs['rwkv_a0'][0], inputs['rwkv_k_k'][0], inputs['rwkv_k_a'][0],
                                inputs['rwkv_r_k'][0].reshape(512), inputs['rwkv_ln_g'][0], inputs['rwkv_ln_b'][0]]).astype(np.float32),
            'rw_wup': inputs['rwkv_w_up'][0], 'rw_aup': inputs['rwkv_a_up'][0], 'rw_gup': inputs['rwkv_g_up'][0],
        }
        m.update(hc)
        maps.append(m)
    return maps


CFG_FULL = {'T': 4096, 'P': 64, 'NPHYS': 2560, 'stages': 'ABCSDEFMG'}


def kernel(**inputs):
    cfg = dict(CFG_FULL)
    inputs = {k_: np.asarray(v) for k_, v in inputs.items()}
    nc, kb = build(cfg)
    maps = make_in_maps(cfg, inputs, 8)
    res = run_bass_kernel_spmd(nc, maps, core_ids=list(range(8)))
    R = res.results
    f32 = np.float32
    y_p = np.stack([R[c]['y_p'] for c in range(4)]).astype(f32)
    y_s = np.concatenate([R[c]['y_s'].reshape(4, 4, 1024) for c in range(8)]).astype(f32)
    nsa_p = np.stack([R[c]['o_nsa_p'].reshape(4096, 4, 2, 64) for c in range(4)])[None].astype(f32)
    nsa_s = np.concatenate([R[c]['o_nsa_s'].reshape(4, 4, 4, 2, 64) for c in range(8)])[None].astype(f32)
    moba_p = np.stack([R[c]['o_moba_p'].reshape(4096, 2, 4, 64) for c in range(4)])[None].astype(f32)
    moba_s = np.concatenate([R[c]['o_moba_s'].reshape(4, 4, 2, 4, 64) for c in range(8)])[None].astype(f32)
    win_p = np.stack([R[c]['o_win_p'].reshape(512, 2, 2, 64) for c in range(4)])[None].astype(f32)
    win_s = np.concatenate([R[c]['o_win_s'].reshape(4, 512, 2, 2, 64) for c in range(8)])[None].astype(f32)
    wkv_p = np.stack([R[c]['o_wkv_p'] for c in range(4)])[None].astype(f32)
    wkv_s = np.concatenate([R[c]['o_wkv_s'] for c in range(8)])[None].astype(f32)
    sh_p = np.concatenate([R[c]['o_sh_p'] for c in range(4)])[None].astype(f32)
    sh_s = np.concatenate([R[c]['o_sh_s'] for c in range(8)])[None].astype(f32)
    return (y_p, y_s, nsa_p, nsa_s, moba_p, moba_s, win_p, win_s, wkv_p, wkv_s, sh_p, sh_s)
```

```python
import numpy as np
import ml_dtypes
from contextlib import ExitStack, contextmanager
import concourse.bass as bass
import concourse.mybir as mybir
from concourse.bass_utils import run_bass_kernel_spmd

F32 = mybir.dt.float32
BF16 = mybir.dt.bfloat16
I32 = mybir.dt.int32
AF = mybir.ActivationFunctionType
ALU = mybir.AluOpType
AX = mybir.AxisListType

ENGS = ['pe', 'act', 'dve', 'pool', 'sp']
NDMA = 12
SAME_SYNC = {'pe': False, 'act': True, 'dve': True, 'pool': True, 'sp': False}
WRITE_KEYS = ('out', 'accum_out', 'out_max', 'out_indices', 'out_ap')


class T:
    def __init__(self, h, name):
        self.h = h
        self.name = name
        self.w = {}
        self.r = {}

    def __getitem__(self, idx):
        return V(self.h[idx], self)


class V:
    def __init__(self, ap, t):
        self.ap = ap
        self.t = t

    def __getitem__(self, idx):
        return V(self.ap[idx], self.t)

    def rearrange(self, s, **kw):
        return V(self.ap.rearrange(s, **kw), self.t)

    def bc(self, shape):
        return V(self.ap.to_broadcast(list(shape)), self.t)

    def pbc(self, n):
        return V(self.ap.partition_broadcast(n), self.t)

    def bitcast(self, dt):
        return V(self.ap.bitcast(dt), self.t)

    def unsq(self, ax):
        return V(self.ap.unsqueeze(ax), self.t)

    @property
    def shape(self):
        return self.ap.shape


def _merge(d, s):
    for k, v in s.items():
        if d.get(k, 0) < v:
            d[k] = v


class EngProxy:
    def __init__(self, kb, name):
        self.kb = kb
        self.name = name

    def __getattr__(self, opname):
        kb = self.kb
        name = self.name

        def call(**kw):
            reads, writes = [], []
            kw2 = {}
            for key, v in kw.items():
                if isinstance(v, V):
                    (writes if key in WRITE_KEYS else reads).append(v.t)
                    kw2[key] = v.ap
                else:
                    kw2[key] = v
            return kb.op(name, lambda e: getattr(e, opname)(**kw2), reads, writes)

        return call


class KB:
    def __init__(self):
        self.nc = bass.Bass("TRN2", target_bir_lowering=False)
        self.es = ExitStack()
        self.cnt = {}
        self.known = {e: {} for e in ENGS}
        self.sem = {}
        nc = self.nc
        self.eng = {'pe': nc.tensor, 'act': nc.scalar, 'dve': nc.vector, 'pool': nc.gpsimd, 'sp': nc.sync}
        for e in ENGS:
            self._mksem('c_' + e)
        for i in range(NDMA):
            self._mksem('d%d' % i)
        for i in range(4):
            self._mksem('g%d' % i)
        self.dma_rr = 0
        self.g_rr = 0
        self.pe = EngProxy(self, 'pe')
        self.act = EngProxy(self, 'act')
        self.dve = EngProxy(self, 'dve')
        self.pool = EngProxy(self, 'pool')
        self.stack = [self.es]
        self.ninst = 0
        self.uid = 0

    def _mksem(self, name):
        self.sem[name] = self.es.enter_context(self.nc.semaphore(name))
        self.cnt[name] = 0

    def dram(self, name, shape, dt, kind="Internal"):
        h = self.nc.dram_tensor(name, list(shape), dt, kind=kind)
        return T(h.ap(), name)

    def sb(self, name, shape, dt=F32):
        self.uid += 1
        h = self.stack[-1].enter_context(self.nc.sbuf_tensor("%s_%d" % (name, self.uid), list(shape), dt))
        return T(h, name)

    def ps(self, name, shape, dt=F32):
        self.uid += 1
        h = self.stack[-1].enter_context(self.nc.psum_tensor("%s_%d" % (name, self.uid), list(shape), dt))
        return T(h, name)

    @contextmanager
    def scope(self):
        es = ExitStack()
        self.stack.append(es)
        try:
            yield
        finally:
            self.barrier()
            self.stack.pop()
            es.close()

    def _deps(self, eng, reads, writes, own):
        deps = {}
        for t in reads:
            _merge(deps, t.w)
        for t in writes:
            _merge(deps, t.w)
            _merge(deps, t.r)
        kn = self.known[eng]
        e = self.eng[eng]
        for s, v in deps.items():
            if s == own and not SAME_SYNC[eng]:
                continue
            if kn.get(s, 0) >= v:
                continue
            e.wait_ge(self.sem[s], v)
            self.ninst += 1
            kn[s] = v

    def _mark(self, s, val, reads, writes):
        for t in reads:
            if t.r.get(s, 0) < val:
                t.r[s] = val
        for t in writes:
            if t.w.get(s, 0) < val:
                t.w[s] = val

    def op(self, eng, fn, reads=(), writes=()):
        own = 'c_' + eng
        self._deps(eng, reads, writes, own)
        self.cnt[own] += 1
        val = self.cnt[own]
        fn(self.eng[eng]).then_inc(self.sem[own], 1)
        self.ninst += 1
        self._mark(own, val, reads, writes)

    def dma(self, out, in_, q='sp', **kw):
        reads, writes = [in_.t], [out.t]
        s = 'd%d' % self.dma_rr
        self.dma_rr = (self.dma_rr + 1) % NDMA
        self._deps(q, reads, writes, None)
        self.cnt[s] += 16
        val = self.cnt[s]
        self.eng[q].dma_start(out=out.ap, in_=in_.ap, **kw).then_inc(self.sem[s], 16)
        self.ninst += 1
        self._mark(s, val, reads, writes)

    def raw16(self, q, fn, reads=(), writes=()):
        s = 'g%d' % self.g_rr
        self.g_rr = (self.g_rr + 1) % 4
        self._deps(q, reads, writes, None)
        self.cnt[s] += 16
        val = self.cnt[s]
        fn(self.eng[q]).then_inc(self.sem[s], 16)
        self.ninst += 1
        self._mark(s, val, reads, writes)

    def barrier(self, engs=ENGS):
        for e in engs:
            kn = self.known[e]
            for s, v in self.cnt.items():
                if v > 0 and kn.get(s, 0) < v and s != 'c_' + e:
                    self.eng[e].wait_ge(self.sem[s], v)
                    kn[s] = v

    def dbg(self, name, v, shape, dt=F32):
        if not getattr(self, 'dbg_on', False):
            return
        o = self.dram('dbg_' + name, shape, dt, "ExternalOutput")
        idx = tuple(slice(0, n) for n in shape)
        self.dma(o[idx], v)

    def finish(self):
        self.barrier(['sp'])
        self.es.close()
        return self.nc


RWC = 1792
EVC = 3096
ODC = 1536
DFF = 2816
NEG = -240000.0


def build(cfg):
    Tn, P, NPH = cfg['T'], cfg['P'], cfg['NPHYS']
    stages = cfg.get('stages', 'A')
    NT = Tn // 128
    LP = P * 128
    TR = Tn + 128
    k = KB()
    k.dbg_on = cfg.get('dbg', False)
    nc = k.nc
    IN = lambda n, s, d=F32: k.dram(n, s, d, "ExternalInput")
    OUT = lambda n, s, d=F32: k.dram(n, s, d, "ExternalOutput")
    xp = IN('xp', [Tn, 1024]); xs = IN('xs', [16, 1024])
    st_win = IN('st_win', [4, 512, 256]); st_wkv = IN('st_wkv', [4, 8, 64, 64]); st_shift = IN('st_shift', [4, RWC])
    ptab = IN('ptab', [4, P], I32)
    norm_mix = IN('norm_mix', [2, 1024]); norm_ffn = IN('norm_ffn', [2, 1024]); norm_final = IN('norm_final', [1, 1024])
    w_in0 = IN('w_in0', [1024, EVC]); w_out0 = IN('w_out0', [1024, 1024])
    gate_b = IN('gate_b', [1, 24])
    w_in1 = IN('w_in1', [1024, ODC]); w_out1 = IN('w_out1', [1024, 1024])
    ffn_g = [IN('ffn_g%d' % i, [1024, DFF]) for i in range(2)]; ffn_u = [IN('ffn_u%d' % i, [1024, DFF]) for i in range(2)]
    ffn_d = [IN('ffn_d%d' % i, [DFF, 1024]) for i in range(2)]
    NBLK = Tn // 256; NBLKP = max(8, NBLK)
    GBM = IN('GBM', [Tn, 2, NBLKP]); EEXP2 = IN('EEXP2', [64, max(Tn, LP)], BF16)
    cache_nsa = IN('cache_nsa', [NPH * 128, 512]); cache_moba = IN('cache_moba', [NPH * 128, 512])
    NSELS = LP // 64 + 1; NBS = LP // 16 - 1; NBTS = (NBS + 127) // 128
    NBLKS = LP // 256; NBLKSP = max(8, NBLKS)
    FBs = IN('FBs', [4, NSELS]); OVs = IN('OVs', [128, NBTS, NSELS], BF16)
    SMK = IN('SMK', [128, 2, 16], BF16)
    IOTA = IN('IOTA', [128, 1])
    rw_mu = IN('rw_mu', [1, RWC]); rw_vec = IN('rw_vec', [7, 512])
    rw_wup = IN('rw_wup', [64, 512]); rw_aup = IN('rw_aup', [64, 512]); rw_gup = IN('rw_gup', [128, 512])
    masks64 = IN('masks64', [64, 3, 64])
    NSEL = Tn // 64; NB = Tn // 16 - 1; NBT = (NB + 127) // 128
    cmp_w1 = IN('cmp_w1', [2, 32, 64, 64]); cmp_pe = IN('cmp_pe', [2, 32, 64]); cmp_w2 = IN('cmp_w2', [2, 64, 64])
    FBt = IN('FBt', [Tn, NSEL]); OVt = IN('OVt', [128, NBT, NSEL], BF16); CMt = IN('CMt', [128, 17, 128], BF16)
    TRIt = IN('TRIt', [128, 2, 128], BF16); EEXP = IN('EEXP', [128, max(Tn, LP)], BF16)
    ropecs = IN('ropecs', [TR, 64])
    identb_d = IN('identb', [128, 128], BF16); identf_d = IN('identf', [128, 128])
    y_p = OUT('y_p', [Tn, 1024]); y_s = OUT('y_s', [16, 1024])
    o_nsa_p = OUT('o_nsa_p', [Tn, 512]); o_nsa_s = OUT('o_nsa_s', [16, 512])
    o_moba_p = OUT('o_moba_p', [Tn, 512]); o_moba_s = OUT('o_moba_s', [16, 512])
    WN = min(512, Tn)
    o_win_p = OUT('o_win_p', [WN, 256]); o_win_s = OUT('o_win_s', [4, 512, 256])
    o_wkv_p = OUT('o_wkv_p', [8, 64, 64]); o_wkv_s = OUT('o_wkv_s', [4, 8, 64, 64])
    o_sh_p = OUT('o_sh_p', [1, RWC]); o_sh_s = OUT('o_sh_s', [4, RWC])
    RW = k.dram('RW', [TR, RWC], F32)
    QT = k.dram('QT', [64, 8, TR], BF16)
    KT = k.dram('KT', [64, 6, TR], BF16)
    VcT = k.dram('VcT', [64, 2, TR], BF16)
    VT = k.dram('VT', [TR, 2, 2, 65], BF16)
    GT = k.dram('GT', [TR, 24], F32)
    AO = k.dram('AO', [TR, 1024], BF16, "ExternalOutput" if cfg.get('dbg_ao') else "Internal")
    H1 = k.dram('H1', [TR, 1024], F32, "ExternalOutput" if cfg.get('dbg_ao') else "Internal")
    ACTT = k.dram('ACTT', [22, 128, TR], BF16)
    QT2 = k.dram('QT2', [64, 16, TR], BF16); KT2 = k.dram('KT2', [64, 4, TR], BF16); VT2 = k.dram('VT2', [TR, 4, 65], BF16)
    NS2 = k.dram('NS2', [Tn, 16, NBLKP], BF16)
    QF2 = k.dram('QF2', [64, 16, 16], F32)
    MO = k.dram('MO', [TR, 1024], BF16, "ExternalOutput" if cfg.get('dbg_ao') else "Internal")

    identb = k.sb('identb', [128, 128], BF16); identf = k.sb('identf', [128, 128], F32)
    k.dma(identb[:, :], identb_d[:, :]); k.dma(identf[:, :], identf_d[:, :])

    tiles = [(i * 128, 128) for i in range(NT)] + [(Tn, 16)]

    def rmsnorm_T(xt, rows, gbc, xn, pT, xnT, ss, junk):
        k.act.activation(out=junk[:rows, :], in_=xt[:rows, :], func=AF.Square, accum_out=ss[:rows, 0:1])
        k.dve.tensor_scalar(out=ss[:rows, 1:2], in0=ss[:rows, 0:1], scalar1=1.0 / 1024, scalar2=1e-6, op0=ALU.mult, op1=ALU.add)
        k.act.activation(out=ss[:rows, 3:4], in_=ss[:rows, 1:2], func=AF.Sqrt)
        k.dve.reciprocal(out=ss[:rows, 2:3], in_=ss[:rows, 3:4])
        k.dve.scalar_tensor_tensor(out=xn[:rows, :], in0=xt[:rows, :], scalar=ss[:rows, 2:3], in1=gbc[:rows, :], op0=ALU.mult, op1=ALU.mult)
        for kk in range(8):
            k.pe.transpose(out=pT[:, kk, :rows], in_=xn[:rows, kk * 128:(kk + 1) * 128], identity=identb[:rows, :rows])
        k.act.copy(out=xnT[:, :, :rows], in_=pT[:, :, :rows])

    def load_w_bf16(Wsb, wd, K, N):
        with k.scope():
            stg = [k.sb('wstg', [128, N], F32) for _ in range(2)]
            for kk in range(K // 128):
                s = stg[kk % 2]
                k.dma(s[:, :], wd[kk * 128:(kk + 1) * 128, :])
                (k.pool if kk % 2 else k.dve).tensor_copy(out=Wsb[:, kk, :], in_=s[:, :])

    with k.scope():
        W0 = k.sb('W0', [128, 8, EVC], BF16)
        load_w_bf16(W0, w_in0, 1024, EVC)
        gbc = k.sb('gbc', [128, 1024]); k.dma(gbc[:, :], norm_mix[0:1, :].pbc(128))
        gb = k.sb('gb', [128, 24]); k.dma(gb[:, :], gate_b[0:1, :].pbc(128))
        xt = [k.sb('xt', [128, 1024]) for _ in range(2)]
        junk = k.sb('junk', [128, 1024], BF16)
        xn = k.sb('xn', [128, 1024], BF16)
        ss = k.sb('ss', [128, 4])
        xnT = k.sb('xnT', [128, 8, 128], BF16)
        proj = [k.sb('proj', [128, EVC]) for _ in range(2)]
        cs = k.sb('cs', [128, 64])
        tmp = [k.sb('tmp%d' % i, [128, 8, 32]) for i in range(4)]
        qb = k.sb('qb', [128, 8, 64], BF16)
        kb = k.sb('kb', [128, 6, 64], BF16)
        vb = k.sb('vb', [128, 3, 2, 65], BF16)
        k.dve.memset(ap=vb[:, :, :, :], constant=1.0) if False else k.op('dve', lambda e: e.memset(vb[:, :, :, :].ap, 1.0), (), (vb,))
        gt = k.sb('gt', [128, 24])
        qT = k.sb('qT', [64, 8, 128], BF16); kT = k.sb('kT', [64, 6, 128], BF16); vcT = k.sb('vcT', [64, 2, 128], BF16)
        pT = k.ps('pT', [128, 8, 128], BF16)
        pA = [k.ps('pA', [128, 512]) for _ in range(2)]
        pQ = k.ps('pQ', [64, 8, 128], BF16)
        pK = k.ps('pK', [64, 8, 128], BF16)
        ci = 0
        for ti, (r0, rows) in enumerate(tiles):
            x = xt[ti % 2]
            pj = proj[ti % 2]
            src = xp[r0:r0 + rows, :] if r0 < Tn else xs[0:16, :]
            k.dma(x[:rows, :], src)
            k.dma(cs[:rows, :], ropecs[r0:r0 + rows, :])
            rmsnorm_T(x, rows, gbc, xn, pT, xnT, ss, junk)
            for c0 in range(0, EVC, 512):
                w = min(512, EVC - c0)
                ps = pA[ci % 2]
                for kk in range(8):
                    k.pe.matmul(out=ps[:rows, :w], lhsT=xnT[:, kk, :rows], rhs=W0[:, kk, c0:c0 + w], start=(kk == 0), stop=(kk == 7))
                if ci % 2:
                    k.act.copy(out=pj[:rows, c0:c0 + w], in_=ps[:rows, :w])
                else:
                    k.dve.tensor_copy(out=pj[:rows, c0:c0 + w], in_=ps[:rows, :w])
                ci += 1
            k.dma(RW[r0:r0 + rows, :], pj[:rows, 0:RWC])
            k.dve.tensor_tensor(out=gt[:rows, :], in0=pj[:rows, 3072:3096], in1=gb[:rows, :], op=ALU.add)
            k.act.activation(out=gt[:rows, :], in_=gt[:rows, :], func=AF.Sigmoid)
            k.dma(GT[r0:r0 + rows, :], gt[:rows, :])
            cosb = lambda n: cs[:rows, 0:32].unsq(1).bc([rows, n, 32])
            sinb = lambda n: cs[:rows, 32:64].unsq(1).bc([rows, n, 32])
            qv = pj[:rows, 1792:2304].rearrange("p (h d) -> p h d", h=8)
            views = [(qv, 8, qb[:rows, :, :])]
            for c in range(3):
                kv_ = pj[:rows, 2304 + c * 256:2304 + c * 256 + 128].rearrange("p (g d) -> p g d", g=2)
                views.append((kv_, 2, None))
            for vi, (xv, n, ob) in enumerate(views):
                E1 = k.dve if vi % 2 == 0 else k.pool
                x1 = xv[:, :, 0:32]; x2 = xv[:, :, 32:64]
                t = [tt[:rows, 0:n, :] for tt in tmp]
                E1.tensor_tensor(out=t[0], in0=x1, in1=cosb(n), op=ALU.mult)
                E1.tensor_tensor(out=t[1], in0=x2, in1=sinb(n), op=ALU.mult)
                E1.tensor_tensor(out=t[2], in0=x2, in1=cosb(n), op=ALU.mult)
                E1.tensor_tensor(out=t[3], in0=x1, in1=sinb(n), op=ALU.mult)
                if ob is not None:
                    E1.tensor_tensor(out=ob[:, :, 0:32], in0=t[0], in1=t[1], op=ALU.subtract)
                    E1.tensor_tensor(out=ob[:, :, 32:64], in0=t[2], in1=t[3], op=ALU.add)
                else:
                    E1.tensor_tensor(out=x1, in0=t[0], in1=t[1], op=ALU.subtract)
                    E1.tensor_tensor(out=x2, in0=t[2], in1=t[3], op=ALU.add)
            kvv = pj[:rows, 2304:3072].rearrange("p (c j g d) -> p c j g d", c=3, j=2, g=2)
            for c in range(3):
                k.dve.tensor_copy(out=kb[:rows, 2 * c:2 * c + 2, :], in_=kvv[:, c, 0, :, :])
                k.pool.tensor_copy(out=vb[:rows, c, :, 0:64], in_=kvv[:, c, 1, :, :])
            if r0 < Tn:
                k.dma(o_nsa_p[r0:r0 + rows, :], pj[:rows, 2304:2816])
                if r0 >= Tn - WN:
                    k.dma(o_win_p[r0 - (Tn - WN):r0 - (Tn - WN) + rows, :], pj[:rows, 2816:3072])
                if ti == NT - 1:
                    k.dma(o_sh_p[0:1, :], pj[127:128, 0:RWC])
            else:
                k.dma(o_nsa_s[0:16, :], pj[:16, 2304:2816])
                for bl in range(4):
                    k.dma(o_win_s[bl, 508:512, :], pj[bl * 4:bl * 4 + 4, 2816:3072])
                    k.dma(o_win_s[bl, 0:508, :], st_win[bl, 4:512, :])
                    k.dma(o_sh_s[bl:bl + 1, :], pj[bl * 4 + 3:bl * 4 + 4, 0:RWC])
            for h in range(8):
                k.pe.transpose(out=pQ[:, h, :rows], in_=qb[:rows, h, :], identity=identb[:rows, :rows])
            k.act.copy(out=qT[:, :, :rows], in_=pQ[:, :, :rows])
            for h in range(6):
                k.pe.transpose(out=pK[:, h, :rows], in_=kb[:rows, h, :], identity=identb[:rows, :rows])
            for g in range(2):
                k.pe.transpose(out=pK[:, 6 + g, :rows], in_=vb[:rows, 0, g, 0:64], identity=identb[:rows, :rows])
            k.dve.tensor_copy(out=kT[:, :, :rows], in_=pK[:, 0:6, :rows])
            k.dve.tensor_copy(out=vcT[:, :, :rows], in_=pK[:, 6:8, :rows])
            k.dma(QT[:, :, r0:r0 + rows], qT[:, :, :rows])
            k.dma(KT[:, :, r0:r0 + rows], kT[:, :, :rows])
            k.dma(VcT[:, :, r0:r0 + rows], vcT[:, :, :rows])
            k.dma(VT[r0:r0 + rows, :, :, :], vb[:rows, 1:3, :, :])
    if 'B' in stages:
        phase_rwkv(k, locals())
    if 'C' in stages:
        phase_nsa_prompt(k, locals())
    LL = locals()
    if 'S' in stages:
        phase_nsa_sample(k, LL)
    if 'D' in stages:
        layer_tail(k, LL, 0, AO, w_out0, None, H1, None)
    if 'E' in stages:
        phase_proj1(k, LL)
    if 'F' in stages:
        phase_moba_prompt(k, LL)
    if 'M' in stages:
        phase_moba_sample(k, LL)
    if 'G' in stages:
        layer_tail(k, LL, 1, MO, w_out1, H1, None, (y_p, y_s))
    nc2 = k.finish()
    return nc2, k


def phase_rwkv(k, L):
    Tn = L['Tn']; RW = L['RW']; AO = L['AO']; identf = L['identf']
    rw_mu = L['rw_mu']; rw_vec = L['rw_vec']; st_wkv = L['st_wkv']; st_shift = L['st_shift']
    with k.scope():
        mu = k.sb('mu', [64, RWC]); k.dma(mu[:, :], rw_mu[0:1, :].pbc(64))
        vec = k.sb('vec', [64, 7, 512])
        for i in range(7):
            k.dma(vec[:, i, :], rw_vec[i:i + 1, :].pbc(64))
        w0b, a0b, kkb, kab, rkb, lgb, lbb = [vec[:, i, :] for i in range(7)]
        wup = k.sb('wup', [64, 512]); k.dma(wup[:, :], L['rw_wup'][:, :])
        aup = k.sb('aup', [64, 512]); k.dma(aup[:, :], L['rw_aup'][:, :])
        gup = k.sb('gup', [128, 512]); k.dma(gup[:, :], L['rw_gup'][:, :])
        mk = k.sb('mk', [64, 3, 64]); k.dma(mk[:, :, :], L['masks64'][:, :, :])
        ones = k.sb('ones', [64, 1]); k.op('dve', lambda e: e.memset(ones[:, :].ap, 1.0), (), (ones,))
        banks = [k.ps('bank', [128, 512]) for _ in range(8)]
        bi = [0]

        def bank():
            b = banks[bi[0] % 8]
            bi[0] += 1
            return b

        ST = k.sb('ST', [64, 8, 64])
        cur = [k.sb('cur', [64, RWC]) for _ in range(2)]
        prv = [k.sb('prv', [64, RWC]) for _ in range(2)]
        Lt = k.sb('Lt', [64, 256]); LT = k.sb('LT', [128, 3, 64])
        lw = k.sb('lw', [64, 512]); av = k.sb('av', [64, 512]); gvs = [k.sb('gv', [64, 512]) for _ in range(2)]
        kk = k.sb('kk', [64, 512]); sq = k.sb('sq', [64, 512]); k2s = [k.sb('k2', [64, 512]) for _ in range(2)]; bv = k.sb('bv', [64, 512])
        sm = k.sb('sm', [64, 8, 4]); sm2 = k.sb('sm2', [64, 8, 4])
        eP = k.sb('eP', [64, 512]); eN = k.sb('eN', [64, 512]); ePm = k.sb('ePm', [64, 512])
        Fs = [k.sb('F', [64, 4, 512]) for _ in range(2)]
        FTs = [[k.sb('FT%d' % i, [64, 8, 64]) for i in range(4)] for _ in range(2)]
        GCs = [k.sb('GC', [64, 8]) for _ in range(2)]
        Mbs = [[k.sb('Mb%d' % i, [64, 8, 64]) for i in range(5)] for _ in range(2)]
        Bt = [k.sb('Bt%d' % i, [64, 8, 64]) for i in range(2)]
        BTt = [k.sb('BTt%d' % i, [64, 8, 64]) for i in range(2)]
        Nts = [k.sb('Nt', [64, 8, 64]) for _ in range(2)]
        Zn = k.sb('Zn', [64, 8, 64]); UT = k.sb('UT', [64, 8, 64])
        yv = k.sb('yv', [64, 512]); yc = k.sb('yc', [64, 512]); t1 = k.sb('t1', [64, 512]); ob = k.sb('ob', [64, 512], BF16)
        Stmp = k.sb('Stmp', [64, 8, 64])
        ci = [0]

        def hv(t, C):
            return t[:C, :].rearrange("p (h d) -> p h d", h=8)

        def chunk(r0, C, first_prev):
            sb_ = ci[0] % 2
            c = cur[sb_]; p = prv[sb_]; ci[0] += 1
            gv = gvs[sb_]; k2 = k2s[sb_]; F = Fs[sb_]; FT = FTs[sb_]; GC = GCs[sb_]; Mb = Mbs[sb_]; Nt = Nts[sb_]
            k.dma(c[:C, :], RW[r0:r0 + C, :])
            if first_prev is None:
                k.dma(p[:C, :], RW[r0 - 1:r0 - 1 + C, :])
            else:
                if first_prev == 'zero':
                    k.op('dve', lambda e: e.memset(p[0:1, :].ap, 0.0), (), (p,))
                else:
                    k.dma(p[0:1, :], first_prev)
                k.dma(p[1:C, :], RW[r0:r0 + C - 1, :])
            k.dve.tensor_tensor(out=p[:C, :], in0=p[:C, :], in1=c[:C, :], op=ALU.subtract)
            k.pool.tensor_tensor(out=p[:C, :], in0=p[:C, :], in1=mu[:C, :], op=ALU.mult)
            k.dve.tensor_tensor(out=c[:C, :], in0=c[:C, :], in1=p[:C, :], op=ALU.add)
            xm = c
            r_ = xm[:C, 0:512]; k_ = xm[:C, 512:1024]; v_ = xm[:C, 1024:1536]
            k.act.activation(out=Lt[:C, 0:64], in_=xm[:C, 1536:1600], func=AF.Tanh)
            k.act.activation(out=Lt[:C, 128:256], in_=xm[:C, 1664:1792], func=AF.Sigmoid)
            k.dve.tensor_copy(out=Lt[:C, 64:128], in_=xm[:C, 1600:1664])
            pb = bank()
            pl = pb[:, 0:192].rearrange("p (a t) -> p a t", a=3)
            k.pe.transpose(out=pl[0:64, 0, :C], in_=Lt[:C, 0:64], identity=identf[:C, :C])
            k.pe.transpose(out=pl[0:64, 1, :C], in_=Lt[:C, 64:128], identity=identf[:C, :C])
            k.pe.transpose(out=pl[:, 2, :C], in_=Lt[:C, 128:256], identity=identf[:C, :C])
            k.dve.tensor_copy(out=LT[0:64, 0:2, :C], in_=pl[0:64, 0:2, :C])
            k.dve.tensor_copy(out=LT[:, 2, :C], in_=pl[:, 2, :C])
            pW = bank(); pA = bank(); pG = bank()
            k.pe.matmul(out=pW[:C, :], lhsT=LT[0:64, 0, :C], rhs=wup[:, :], start=True, stop=True)
            k.pe.matmul(out=pA[:C, :], lhsT=LT[0:64, 1, :C], rhs=aup[:, :], start=True, stop=True)
            k.pe.matmul(out=pG[:C, :], lhsT=LT[:, 2, :C], rhs=gup[:, :], start=True, stop=True)
            k.dve.tensor_tensor(out=lw[:C, :], in0=pW[:C, :], in1=w0b[:C, :], op=ALU.add)
            k.act.activation(out=lw[:C, :], in_=lw[:C, :], func=AF.Sigmoid)
            k.dve.tensor_scalar(out=lw[:C, :], in0=lw[:C, :], scalar1=-0.6065306597126334, scalar2=None, op0=ALU.mult)
            k.dve.tensor_tensor(out=av[:C, :], in0=pA[:C, :], in1=a0b[:C, :], op=ALU.add)
            k.act.activation(out=av[:C, :], in_=av[:C, :], func=AF.Sigmoid)
            k.act.copy(out=gv[:C, :], in_=pG[:C, :])
            k.pool.tensor_tensor(out=kk[:C, :], in0=k_, in1=kkb[:C, :], op=ALU.mult)
            k.pool.tensor_tensor(out=sq[:C, :], in0=kk[:C, :], in1=kk[:C, :], op=ALU.mult)
            k.dve.reduce_sum(out=sm[:C, :, 0], in_=hv(sq, C), axis=AX.X)
            k.act.activation(out=sm[:C, :, 1], in_=sm[:C, :, 0], func=AF.Sqrt)
            k.dve.tensor_scalar(out=sm[:C, :, 1], in0=sm[:C, :, 1], scalar1=1e-12, scalar2=None, op0=ALU.max)
            k.dve.reciprocal(out=sm[:C, :, 2], in_=sm[:C, :, 1])
            if r0 == 0:
                k.dbg('kk0', kk[:C, :], [64, 512]); k.dbg('sq', sq[:C, :], [64, 512]); k.dbg('sm', sm[:C, :, :], [64, 8, 4])
            k.dve.tensor_tensor(out=hv(kk, C), in0=hv(kk, C), in1=sm[:C, :, 2:3].bc([C, 8, 64]), op=ALU.mult)
            k.dve.scalar_tensor_tensor(out=k2[:C, :], in0=av[:C, :], scalar=-1.0, in1=kab[:C, :], op0=ALU.add, op1=ALU.mult)
            k.dve.scalar_tensor_tensor(out=k2[:C, :], in0=k2[:C, :], scalar=1.0, in1=k_, op0=ALU.add, op1=ALU.mult)
            k.pool.tensor_tensor(out=bv[:C, :], in0=kk[:C, :], in1=av[:C, :], op=ALU.mult)
            pC = bank()
            k.pe.matmul(out=pC[:C, :], lhsT=mk[:C, 0, :C], rhs=lw[:C, :], start=True, stop=True)
            k.act.activation(out=eP[:C, :], in_=pC[:C, :], func=AF.Exp)
            k.act.activation(out=eN[:C, :], in_=pC[:C, :], func=AF.Exp, scale=-1.0)
            k.dve.tensor_tensor(out=ePm[:C, :], in0=pC[:C, :], in1=lw[:C, :], op=ALU.subtract)
            k.act.activation(out=ePm[:C, :], in_=ePm[:C, :], func=AF.Exp)
            k.dve.tensor_tensor(out=F[:C, 0, :], in0=kk[:C, :], in1=ePm[:C, :], op=ALU.mult)
            k.pool.tensor_tensor(out=F[:C, 1, :], in0=bv[:C, :], in1=eN[:C, :], op=ALU.mult)
            k.dve.tensor_tensor(out=F[:C, 2, :], in0=k2[:C, :], in1=eN[:C, :], op=ALU.mult)
            k.pool.tensor_tensor(out=F[:C, 3, :], in0=r_, in1=eP[:C, :], op=ALU.mult)
            pg = bank()
            for h in range(8):
                k.pe.matmul(out=pg[0:64, h:h + 1], lhsT=lw[:C, h * 64:(h + 1) * 64], rhs=ones[:C, 0:1], start=True, stop=True)
            k.act.activation(out=GC[:, :], in_=pg[0:64, 0:8], func=AF.Exp)
            for kind in range(4):
                pf = bank()
                pfv = pf[0:64, :].rearrange("p (h t) -> p h t", h=8)
                for h in range(8):
                    k.pe.transpose(out=pfv[:, h, :C], in_=F[:C, kind, h * 64:(h + 1) * 64], identity=identf[:C, :C])
                (k.act.copy if kind % 2 else k.dve.tensor_copy)(out=FT[kind][:, :, :C], in_=pfv[:, :, :C])
            aT, bT, khT, rT = FT
            combos = [(bT, aT, 1), (aT, bT, 2), (khT, aT, 1), (bT, rT, 0), (khT, rT, 0)]
            for i, (lt, rt, mi) in enumerate(combos):
                pm = bank()
                pmv = pm[0:64, :].rearrange("p (h t) -> p h t", h=8)
                for h in range(8):
                    k.pe.matmul(out=pmv[:C, h, :C], lhsT=lt[:, h, :C], rhs=rt[:, h, :C], start=True, stop=True)
                (k.dve if i % 2 == 0 else k.pool).tensor_tensor(out=Mb[i][:C, :, :C], in0=pmv[:C, :, :C], in1=mk[:C, mi:mi + 1, :C].bc([C, 8, C]), op=ALU.mult) if i % 2 == 0 else k.dve.tensor_tensor(out=Mb[i][:C, :, :C], in0=pmv[:C, :, :C], in1=mk[:C, mi:mi + 1, :C].bc([C, 8, C]), op=ALU.mult)
            A, AT, Mka, Mbr, Mkr = Mb
            k.dve.tensor_tensor(out=Nt[:C, :, :C], in0=identf[:C, 0:C].unsq(1).bc([C, 8, C]), in1=A[:C, :, :C], op=ALU.subtract)
            Bc, BTc = A, AT
            nlev = {64: 5, 4: 1}[C]
            for lev in range(nlev):
                Bn = Bt[lev % 2]; BTn = BTt[lev % 2]
                p1 = bank(); p2 = bank()
                p1v = p1[0:64, :].rearrange("p (h t) -> p h t", h=8); p2v = p2[0:64, :].rearrange("p (h t) -> p h t", h=8)
                for h in range(8):
                    k.pe.matmul(out=p1v[:C, h, :C], lhsT=BTc[:C, h, :C], rhs=Bc[:C, h, :C], start=True, stop=True)
                for h in range(8):
                    k.pe.matmul(out=p2v[:C, h, :C], lhsT=Bc[:C, h, :C], rhs=BTc[:C, h, :C], start=True, stop=True)
                k.dve.tensor_copy(out=Bn[:C, :, :C], in_=p1v[:C, :, :C])
                k.act.copy(out=BTn[:C, :, :C], in_=p2v[:C, :, :C])
                p3 = bank(); p3v = p3[0:64, :].rearrange("p (h t) -> p h t", h=8)
                for h in range(8):
                    k.pe.matmul(out=p3v[:C, h, :C], lhsT=BTn[:C, h, :C], rhs=Nt[:C, h, :C], start=True, stop=True)
                k.dve.tensor_tensor(out=Nt[:C, :, :C], in0=Nt[:C, :, :C], in1=p3v[:C, :, :C], op=ALU.add)
                Bc, BTc = Bn, BTn
            def s2():
                vh = lambda h: xm[:C, 1024 + h * 64:1024 + (h + 1) * 64]
                pz = bank(); pzv = pz[0:64, :].rearrange("p (h t) -> p h t", h=8)
                for h in range(8):
                    k.pe.matmul(out=pzv[:C, h, :], lhsT=aT[:, h, :C], rhs=ST[:, h, :], start=True, stop=False)
                    k.pe.matmul(out=pzv[:C, h, :], lhsT=Mka[:C, h, :C], rhs=vh(h), start=False, stop=True)
                k.dve.tensor_scalar(out=Zn[:C, :, :], in0=pzv[:C, :, :], scalar1=-1.0, scalar2=None, op0=ALU.mult)
                pu = bank(); puv = pu[0:64, :].rearrange("p (h t) -> p h t", h=8)
                for h in range(8):
                    k.pe.matmul(out=puv[:C, h, :], lhsT=Nt[:C, h, :C], rhs=Zn[:C, h, :], start=True, stop=True)
                k.act.copy(out=UT[:C, :, :], in_=puv[:C, :, :])
                py = bank(); pyv = py[0:64, :].rearrange("p (h t) -> p h t", h=8)
                for h in range(8):
                    k.pe.matmul(out=pyv[:C, h, :], lhsT=rT[:, h, :C], rhs=ST[:, h, :], start=True, stop=False)
                    k.pe.matmul(out=pyv[:C, h, :], lhsT=Mbr[:C, h, :C], rhs=UT[:C, h, :], start=False, stop=False)
                    k.pe.matmul(out=pyv[:C, h, :], lhsT=Mkr[:C, h, :C], rhs=vh(h), start=False, stop=True)
                k.act.copy(out=yv[:C, :], in_=py[:C, :])
                pS = bank(); pSv = pS[0:64, :].rearrange("p (h t) -> p h t", h=8)
                for h in range(8):
                    k.pe.matmul(out=pSv[:, h, :], lhsT=F[:C, 1, h * 64:(h + 1) * 64], rhs=UT[:C, h, :], start=True, stop=False)
                    k.pe.matmul(out=pSv[:, h, :], lhsT=F[:C, 2, h * 64:(h + 1) * 64], rhs=vh(h), start=False, stop=True)
                k.dve.tensor_tensor(out=ST[:, :, :], in0=ST[:, :, :], in1=pSv[:, :, :], op=ALU.add)
                k.dve.tensor_tensor(out=ST[:, :, :], in0=ST[:, :, :], in1=GC[:, :].unsq(2).bc([64, 8, 64]), op=ALU.mult)
                k.dve.reduce_sum(out=sm2[:C, :, 0], in_=hv(yv, C), axis=AX.X)
                k.dve.tensor_scalar(out=sm2[:C, :, 0], in0=sm2[:C, :, 0], scalar1=1.0 / 64, scalar2=None, op0=ALU.mult)
                k.dve.tensor_tensor(out=hv(yc, C), in0=hv(yv, C), in1=sm2[:C, :, 0:1].bc([C, 8, 64]), op=ALU.subtract)
                k.pool.tensor_tensor(out=t1[:C, :], in0=yc[:C, :], in1=yc[:C, :], op=ALU.mult)
                k.dve.reduce_sum(out=sm2[:C, :, 1], in_=hv(t1, C), axis=AX.X)
                k.dve.tensor_scalar(out=sm2[:C, :, 1], in0=sm2[:C, :, 1], scalar1=1.0 / 64, scalar2=64e-5, op0=ALU.mult, op1=ALU.add)
                k.act.activation(out=sm2[:C, :, 1], in_=sm2[:C, :, 1], func=AF.Sqrt)
                k.dve.reciprocal(out=sm2[:C, :, 2], in_=sm2[:C, :, 1])
                k.dve.tensor_tensor(out=hv(yc, C), in0=hv(yc, C), in1=sm2[:C, :, 2:3].bc([C, 8, 64]), op=ALU.mult)
                k.dve.tensor_tensor(out=yc[:C, :], in0=yc[:C, :], in1=lgb[:C, :], op=ALU.mult)
                k.dve.tensor_tensor(out=yc[:C, :], in0=yc[:C, :], in1=lbb[:C, :], op=ALU.add)
                k.pool.tensor_tensor(out=t1[:C, :], in0=r_, in1=k2[:C, :], op=ALU.mult)
                k.pool.tensor_tensor(out=t1[:C, :], in0=t1[:C, :], in1=rkb[:C, :], op=ALU.mult)
                k.dve.reduce_sum(out=sm2[:C, :, 3], in_=hv(t1, C), axis=AX.X)
                k.dve.tensor_tensor(out=hv(t1, C), in0=xm[:C, 1024:1536].rearrange("p (h d) -> p h d", h=8), in1=sm2[:C, :, 3:4].bc([C, 8, 64]), op=ALU.mult)
                k.dve.tensor_tensor(out=yc[:C, :], in0=yc[:C, :], in1=t1[:C, :], op=ALU.add)
                k.dve.tensor_tensor(out=ob[:C, :], in0=yc[:C, :], in1=gv[:C, :], op=ALU.mult)
                k.dma(AO[r0:r0 + C, 0:512], ob[:C, :])
                if r0 == 0:
                    k.dbg('xm', xm[:C, :], [64, RWC]); k.dbg('lw', lw[:C, :], [64, 512]); k.dbg('av', av[:C, :], [64, 512])
                    k.dbg('kk', kk[:C, :], [64, 512]); k.dbg('k2', k2[:C, :], [64, 512]); k.dbg('F', F[:C, :, :], [64, 4, 512])
                    k.dbg('aT', FT[0][:, :, :], [64, 8, 64]); k.dbg('A', Mb[0][:, :, :], [64, 8, 64]); k.dbg('AT', Mb[1][:, :, :], [64, 8, 64])
                    k.dbg('N', Nt[:, :, :], [64, 8, 64]); k.dbg('UT', UT[:, :, :], [64, 8, 64]); k.dbg('yv', yv[:C, :], [64, 512])
                    k.dbg('ST', ST[:, :, :], [64, 8, 64]); k.dbg('GC', GC[:, :], [64, 8]); k.dbg('gv', gv[:C, :], [64, 512])
                    k.dbg('ob', ob[:C, :], [64, 512], BF16)
            return s2

        def store_state(dst):
            pt = bank(); ptv = pt[0:64, :].rearrange("p (h t) -> p h t", h=8)
            for h in range(8):
                k.pe.transpose(out=ptv[:, h, :], in_=ST[:, h, :], identity=identf[0:64, 0:64])
            k.dve.tensor_copy(out=Stmp[:, :, :], in_=ptv[:, :, :])
            k.dma(dst.rearrange("h i j -> i h j"), Stmp[:, :, :])

        k.op('dve', lambda e: e.memset(ST[:, :, :].ap, 0.0), (), (ST,))
        pend = None
        for ch in range(Tn // 64):
            nxt = chunk(ch * 64, 64, 'zero' if ch == 0 else None)
            if pend is not None:
                pend()
            pend = nxt
        pend()
        store_state(L['o_wkv_p'][:, :, :])
        for bl in range(4):
            k.dma(Stmp[:, :, :], st_wkv[bl].rearrange("h i j -> i h j"))
            pt = bank(); ptv = pt[0:64, :].rearrange("p (h t) -> p h t", h=8)
            for h in range(8):
                k.pe.transpose(out=ptv[:, h, :], in_=Stmp[:, h, :], identity=identf[0:64, 0:64])
            k.dve.tensor_copy(out=ST[:, :, :], in_=ptv[:, :, :])
            chunk(Tn + bl * 4, 4, st_shift[bl:bl + 1, :])()
            store_state(L['o_wkv_s'][bl])


def pipe(items, sc, pv):
    items = list(items)
    if not items:
        return
    nxt = sc(items[0])
    for i, it in enumerate(items):
        cur_e = nxt
        if i + 1 < len(items):
            nxt = sc(items[i + 1])
        pv(it, cur_e)


def gelu_tanh(k, out_bf, x, tmpa, tmpb, shape_idx):
    k.dve.tensor_tensor(out=tmpa, in0=x, in1=x, op=ALU.mult)
    k.dve.tensor_scalar(out=tmpa, in0=tmpa, scalar1=0.044715, scalar2=1.0, op0=ALU.mult, op1=ALU.add)
    k.dve.tensor_tensor(out=tmpa, in0=tmpa, in1=x, op=ALU.mult)
    k.act.activation(out=tmpb, in_=tmpa, func=AF.Tanh, scale=0.7978845608028654)
    k.dve.tensor_scalar(out=tmpb, in0=tmpb, scalar1=1.0, scalar2=0.5, op0=ALU.add, op1=ALU.mult)
    k.dve.tensor_tensor(out=out_bf, in0=tmpb, in1=x, op=ALU.mult)


def load_cmp_weights(k, L, pbias):
    cmp_w1 = L['cmp_w1']; cmp_pe = L['cmp_pe']; cmp_w2 = L['cmp_w2']
    W = {}
    tiles_ = [(k.sb('cw1_%d' % kind, [64, 32, 64], BF16), k.sb('cw2_%d' % kind, [64, 64], BF16),
               k.sb('peT%d' % kind, [64, 32], BF16), k.sb('cbias%d' % kind, [64, 1])) for kind in range(2)]
    sc_ = k.scope(); sc_.__enter__()
    stg = k.sb('cw_stg', [64, 32, 64]); stg2 = k.sb('cw_stg2', [64, 64]); pes = k.sb('pe_stg', [64, 32])
    for kind in range(2):
        w1, w2, peT, bias = tiles_[kind]
        k.dma(stg[:, :, :], cmp_w1[kind].rearrange("c d e -> d c e"))
        k.dve.tensor_copy(out=w1[:, :, :], in_=stg[:, :, :])
        k.dma(stg2[:, :], cmp_w2[kind])
        k.dve.tensor_copy(out=w2[:, :], in_=stg2[:, :])
        k.dma(pes[:, :], cmp_pe[kind].rearrange("c d -> d c"), allow_slow_non_contiguous=True)
        k.dve.tensor_copy(out=peT[:, :], in_=pes[:, :])
        for c in range(32):
            k.pe.matmul(out=pbias[0:64, kind:kind + 1], lhsT=w1[:, c, :], rhs=peT[:, c:c + 1], start=(c == 0), stop=(c == 31))
        k.dve.tensor_copy(out=bias[:, :], in_=pbias[0:64, kind:kind + 1])
        W[kind] = (w1, w2, bias)
    sc_.__exit__(None, None, None)
    return W


def compress(k, W, kind, srcT, NBn, ps_bank, hx, ta, tb, hbf):
    w1, w2, bias = W[kind]
    for c in range(32):
        k.pe.matmul(out=ps_bank[0:64, 0:NBn], lhsT=w1[:, c, :], rhs=srcT[:, c:c + 16 * (NBn - 1) + 1:16], start=(c == 0), stop=(c == 31))
    k.act.activation(out=hx[:, 0:NBn], in_=ps_bank[0:64, 0:NBn], func=AF.Identity, bias=bias[:, 0:1])
    gelu_tanh(k, hbf[:, 0:NBn], hx[:, 0:NBn], ta[:, 0:NBn], tb[:, 0:NBn], None)


def phase_nsa_prompt(k, L):
    Tn = L['Tn']; NT = L['NT']; NSEL = L['NSEL']; NB = L['NB']; NBT = L['NBT']
    QT = L['QT']; KT = L['KT']; VcT = L['VcT']; VT = L['VT']; GT = L['GT']; AO = L['AO']; identb = L['identb']
    with k.scope():
        ps_s = [k.ps('ps_s', [128, 512]) for _ in range(3)]
        W = load_cmp_weights(k, L, ps_s[0])
        cm = k.sb('cm', [128, 17, 128], BF16); k.dma(cm[:, :, :], L['CMt'][:, :, :])
        tri = k.sb('tri', [128, 2, 128], BF16); k.dma(tri[:, :, :], L['TRIt'][:, :, :])
        eexp = k.sb('eexp', [NSEL, Tn], BF16); k.dma(eexp[:, :], L['EEXP'][0:NSEL, 0:Tn])
        po_sel = [k.ps('po_sel', [128, 4, 65]) for _ in range(1)]
        po_win = [k.ps('po_win', [128, 4, 65]) for _ in range(1)]
        po_c = [k.ps('po_c', [128, 65 + NSEL]) for _ in range(2)]
        ps_t = k.ps('ps_t', [128, 128], BF16)
        si = [0]

        def sbank():
            b = ps_s[si[0] % 3]; si[0] += 1
            return b

        KcT = k.sb('KcT', [64, Tn], BF16); VcTs = k.sb('VcTs', [64, Tn], BF16)
        KsT = k.sb('KsT', [64, Tn], BF16); KwT = k.sb('KwT', [64, Tn], BF16)
        Vs = k.sb('Vs', [128, NT, 65], BF16); Vw = k.sb('Vw', [128, NT, 65], BF16)
        KcC = k.sb('KcC', [64, NBT * 128], BF16)
        VcC = k.sb('VcC', [128, NBT, 65 + NSEL], BF16)
        hx = k.sb('hx', [64, 512]); ta = k.sb('ta', [64, 512]); tb = k.sb('tb', [64, 512])
        hk = k.sb('hk', [64, 512], BF16); hv_ = k.sb('hv_', [64, 512], BF16)
        q4 = [k.sb('q4', [64, 4, 128], BF16) for _ in range(2)]
        gtile = [k.sb('gtile', [128, 24]) for _ in range(2)]
        fbt = [k.sb('fbt', [128, NSEL]) for _ in range(2)]
        ebuf = [k.sb('ebuf', [128, 512], BF16) for _ in range(3)]
        ei = [0]

        def enext():
            b = ebuf[ei[0] % 3]; ei[0] += 1
            return b

        acc = k.sb('acc', [128, 4, 64]); accb = k.sb('accb', [128, 4, 64], BF16); tmp4 = k.sb('tmp4', [128, 4, 64])
        rr = k.sb('rr', [128, 16]); imp = k.sb('imp', [128, NSEL]); score = k.sb('score', [128, NSEL]); sc2 = k.sb('sc2', [128, NSEL])
        m8 = k.sb('m8', [128, 16]); negs = k.sb('negs', [128, NSEL], BF16)
        negT4 = k.sb('negT4', [NSEL, 4, 128], BF16)
        for g in range(2):
            k.dma(KcT[:, :], KT[:, 0 + g, 0:Tn]); k.dma(VcTs[:, :], VcT[:, g, 0:Tn])
            k.dma(KsT[:, :], KT[:, 2 + g, 0:Tn]); k.dma(KwT[:, :], KT[:, 4 + g, 0:Tn])
            k.dma(Vs[:, :, :], VT[0:Tn, 0, g, :].rearrange("(kt p) c -> p kt c", p=128))
            k.dma(Vw[:, :, :], VT[0:Tn, 1, g, :].rearrange("(kt p) c -> p kt c", p=128))
            k.dma(VcC[:, :, 65:65 + NSEL], L['OVt'][:, :, :])
            k.op('dve', lambda e: e.memset(VcC[:, :, 64:65].ap, 1.0), (), (VcC,))
            pb = sbank()
            compress(k, W, 0, KcT, NB, pb, hx, ta, tb, hk)
            pb2 = sbank()
            k.pe.matmul(out=pb2[0:64, 0:NB], lhsT=W[0][1][:, :], rhs=hk[:, 0:NB], start=True, stop=True)
            k.dve.tensor_copy(out=KcC[:, 0:NB], in_=pb2[0:64, 0:NB])
            pb = sbank()
            compress(k, W, 1, VcTs, NB, pb, hx, ta, tb, hv_)
            for ni in range(NBT):
                nn = min(128, NB - ni * 128)
                pb3 = sbank()
                k.pe.matmul(out=pb3[:nn, 0:64], lhsT=hv_[:, ni * 128:ni * 128 + nn], rhs=W[1][1][:, :], start=True, stop=True)
                k.dve.tensor_copy(out=VcC[:nn, ni, 0:64], in_=pb3[:nn, 0:64])
            for tt in range(NT):
                t0 = tt * 128
                q = q4[tt % 2]; gt_ = gtile[tt % 2]; fb = fbt[tt % 2]
                k.dma(q[:, :, :], QT[:, g * 4:(g + 1) * 4, t0:t0 + 128])
                k.dma(gt_[:, :], GT[t0:t0 + 128, :])
                k.dma(fb[:, :], L['FBt'][t0:t0 + 128, :])
                gv3 = gt_[:, :].rearrange("p (h b) -> p h b", b=3)
                nvalid = min(NB, (t0 + 96) // 16 + 1)
                nnt = (nvalid + 127) // 128
                for r in range(4):
                    po = po_c[r % 2]
                    for ni in range(nnt):
                        nn = min(128, NB - ni * 128)
                        pss = sbank()
                        k.pe.matmul(out=pss[:nn, 0:128], lhsT=KcC[:, ni * 128:ni * 128 + nn], rhs=q[:, r, :], start=True, stop=True)
                        e = enext()
                        k.act.activation(out=e[:nn, 0:128], in_=pss[:nn, 0:128], func=AF.Exp, scale=0.125)
                        delta = t0 - 2048 * ni
                        if delta < 17 * 128:
                            assert delta >= 0
                            k.pool.tensor_tensor(out=e[:nn, 0:128], in0=e[:nn, 0:128], in1=cm[:nn, delta // 128, :], op=ALU.mult)
                        k.pe.matmul(out=po[:, :], lhsT=e[:nn, 0:128], rhs=VcC[:nn, ni, :], start=(ni == 0), stop=(ni == nnt - 1))
                    k.dve.tensor_scalar(out=rr[:, r:r + 1], in0=po[:, 64:65], scalar1=1e-30, scalar2=None, op0=ALU.max)
                    k.dve.reciprocal(out=rr[:, 4 + r:5 + r], in_=rr[:, r:r + 1])
                    k.dve.tensor_tensor(out=rr[:, 8 + r:9 + r], in0=rr[:, 4 + r:5 + r], in1=gv3[:, g * 4 + r, 0:1], op=ALU.mult)
                    k.dve.tensor_scalar(out=acc[:, r, :], in0=po[:, 0:64], scalar1=rr[:, 8 + r:9 + r], scalar2=None, op0=ALU.mult)
                    if r == 0:
                        k.dve.tensor_scalar(out=imp[:, :], in0=po[:, 65:65 + NSEL], scalar1=rr[:, 4 + r:5 + r], scalar2=None, op0=ALU.mult)
                    else:
                        k.dve.scalar_tensor_tensor(out=imp[:, :], in0=po[:, 65:65 + NSEL], scalar=rr[:, 4 + r:5 + r], in1=imp[:, :], op0=ALU.mult, op1=ALU.add)
                k.dve.tensor_tensor(out=score[:, :], in0=imp[:, :], in1=fb[:, :], op=ALU.add)
                k.dve.max(out=m8[:, 0:8], in_=score[:, :])
                k.dve.match_replace(out=sc2[:, :], in_to_replace=m8[:, 0:8], in_values=score[:, :], imm_value=-3.0e38)
                k.dve.max(out=m8[:, 8:16], in_=sc2[:, :])
                k.dve.tensor_scalar(out=rr[:, 12:13], in0=m8[:, 15:16], scalar1=-1.0e8, scalar2=None, op0=ALU.max)
                k.dve.tensor_scalar(out=negs[:, :], in0=score[:, :], scalar1=rr[:, 12:13], scalar2=NEG, op0=ALU.is_lt, op1=ALU.mult)
                k.pe.transpose(out=ps_t[0:NSEL, :], in_=negs[:, :], identity=identb[:, :])
                k.dve.tensor_copy(out=negT4[:, :, :], in_=ps_t[0:NSEL, :].unsq(1).bc([NSEL, 4, 128]))
                qf = q[:, :, :].rearrange("p r t -> p (r t)")
                nf = negT4[:, :, :].rearrange("p r t -> p (r t)")
                po = po_sel[0]

                def sc_sel(kt):
                    pss = sbank()
                    k.pe.matmul(out=pss[:, :], lhsT=KsT[:, kt * 128:(kt + 1) * 128], rhs=qf, start=True, stop=False)
                    k.pe.matmul(out=pss[:, :], lhsT=eexp[:, kt * 128:(kt + 1) * 128], rhs=nf, start=False, stop=True)
                    e = enext()
                    k.act.activation(out=e[:, :], in_=pss[:, :], func=AF.Exp, scale=0.125)
                    if kt == tt:
                        ev = e[:, :].rearrange("p (r t) -> p r t", r=4)
                        k.pool.tensor_tensor(out=ev, in0=ev, in1=tri[:, 0:1, :].bc([128, 4, 128]), op=ALU.mult)
                    return e

                def pv_sel(kt, e):
                    for r in range(4):
                        k.pe.matmul(out=po[:, r, :], lhsT=e[:, r * 128:(r + 1) * 128], rhs=Vs[:, kt, :], start=(kt == 0 and r == 0), stop=(kt == tt), skip_group_check=True)

                pipe(range(tt + 1), sc_sel, pv_sel)
                k.dve.tensor_scalar(out=rr[:, 0:4], in0=po[:, :, 64], scalar1=1e-30, scalar2=None, op0=ALU.max)
                k.dve.reciprocal(out=rr[:, 4:8], in_=rr[:, 0:4])
                k.dve.tensor_tensor(out=rr[:, 8:12], in0=rr[:, 4:8], in1=gv3[:, g * 4:(g + 1) * 4, 1], op=ALU.mult)
                k.dve.tensor_tensor(out=tmp4[:, :, :], in0=po[:, :, 0:64], in1=rr[:, 8:12].unsq(2).bc([128, 4, 64]), op=ALU.mult)
                k.dve.tensor_tensor(out=acc[:, :, :], in0=acc[:, :, :], in1=tmp4[:, :, :], op=ALU.add)
                po = po_win[0]
                k0 = max(0, tt - 4)
                pow_ = po

                def sc_win(kt):
                    pss = sbank()
                    k.pe.matmul(out=pss[:, :], lhsT=KwT[:, kt * 128:(kt + 1) * 128], rhs=qf, start=True, stop=True)
                    e = enext()
                    k.act.activation(out=e[:, :], in_=pss[:, :], func=AF.Exp, scale=0.125)
                    ev = e[:, :].rearrange("p (r t) -> p r t", r=4)
                    if kt == tt:
                        k.pool.tensor_tensor(out=ev, in0=ev, in1=tri[:, 0:1, :].bc([128, 4, 128]), op=ALU.mult)
                    elif kt == tt - 4:
                        k.pool.tensor_tensor(out=ev, in0=ev, in1=tri[:, 1:2, :].bc([128, 4, 128]), op=ALU.mult)
                    return e

                def pv_win(kt, e):
                    for r in range(4):
                        k.pe.matmul(out=pow_[:, r, :], lhsT=e[:, r * 128:(r + 1) * 128], rhs=Vw[:, kt, :], start=(kt == k0 and r == 0), stop=(kt == tt), skip_group_check=True)

                pipe(range(k0, tt + 1), sc_win, pv_win)
                k.dve.tensor_scalar(out=rr[:, 0:4], in0=po[:, :, 64], scalar1=1e-30, scalar2=None, op0=ALU.max)
                k.dve.reciprocal(out=rr[:, 4:8], in_=rr[:, 0:4])
                k.dve.tensor_tensor(out=rr[:, 8:12], in0=rr[:, 4:8], in1=gv3[:, g * 4:(g + 1) * 4, 2], op=ALU.mult)
                k.dve.tensor_tensor(out=tmp4[:, :, :], in0=po[:, :, 0:64], in1=rr[:, 8:12].unsq(2).bc([128, 4, 64]), op=ALU.mult)
                k.dve.tensor_tensor(out=accb[:, :, :], in0=acc[:, :, :], in1=tmp4[:, :, :], op=ALU.add)
                k.dma(AO[t0:t0 + 128, 512 + g * 256:512 + (g + 1) * 256], accb[:, :, :].rearrange("p r d -> p (r d)"))


def layer_tail(k, L, layer, mix, w_out, h_in, h_out, y_out):
    Tn = L['Tn']; NT = L['NT']; identb = L['identb']; H1 = L['H1']; ACTT = L['ACTT']
    rmsnorm_T = L['rmsnorm_T']; load_w_bf16 = L['load_w_bf16']
    xp = L['xp']; xs = L['xs']
    groups = [(i * 512, min(512, Tn - i * 512)) for i in range((Tn + 511) // 512)] + [(Tn, 16)]
    with k.scope():
        Wo = k.sb('Wo', [128, 8, 1024], BF16); load_w_bf16(Wo, w_out, 1024, 1024)
        Wg = k.sb('Wg', [128, 8, DFF], BF16); load_w_bf16(Wg, L['ffn_g'][layer], 1024, DFF)
        Wu = k.sb('Wu', [128, 8, DFF], BF16); load_w_bf16(Wu, L['ffn_u'][layer], 1024, DFF)
        gbc = k.sb('gbc', [128, 1024]); k.dma(gbc[:, :], L['norm_ffn'][layer:layer + 1, :].pbc(128))
        mt = k.sb('mt', [128, 1024], BF16); mT = k.sb('mT', [128, 8, 128], BF16)
        ht = [k.sb('ht', [128, 1024]) for _ in range(2)]
        junk = k.sb('junk', [128, 1024], BF16); xn = k.sb('xn', [128, 1024], BF16); ss = k.sb('ss', [128, 4])
        xnT = k.sb('xnT', [128, 8, 512], BF16); xnT1 = k.sb('xnT1', [128, 8, 128], BF16)
        sg = k.sb('sg', [128, 512], BF16); at = [k.sb('at', [128, 512], BF16) for _ in range(2)]
        pT = k.ps('pT', [128, 8, 128], BF16)
        pA = [k.ps('pA', [128, 512]) for _ in range(2)]
        pG = [k.ps('pG', [128, 512]) for _ in range(2)]; pU = [k.ps('pU', [128, 512]) for _ in range(2)]
        ci = 0
        for (g0, gn) in groups:
            ntile = (gn + 127) // 128
            for j in range(ntile):
                r0 = g0 + j * 128; rows = min(128, gn - j * 128)
                h = ht[ci % 2]; ci += 1
                k.dma(mt[:rows, :], mix[r0:r0 + rows, :])
                if h_in is None:
                    k.dma(h[:rows, :], xp[r0:r0 + rows, :] if r0 < Tn else xs[0:16, :])
                else:
                    k.dma(h[:rows, :], h_in[r0:r0 + rows, :])
                for kk in range(8):
                    k.pe.transpose(out=pT[:, kk, :rows], in_=mt[:rows, kk * 128:(kk + 1) * 128], identity=identb[:rows, :rows])
                k.act.copy(out=mT[:, :, :rows], in_=pT[:, :, :rows])
                for c in range(2):
                    ps = pA[c]
                    for kk in range(8):
                        k.pe.matmul(out=ps[:rows, :], lhsT=mT[:, kk, :rows], rhs=Wo[:, kk, c * 512:(c + 1) * 512], start=(kk == 0), stop=(kk == 7))
                    k.dve.tensor_tensor(out=h[:rows, c * 512:(c + 1) * 512], in0=h[:rows, c * 512:(c + 1) * 512], in1=ps[:rows, :], op=ALU.add)
                k.dma(H1[r0:r0 + rows, :], h[:rows, :])
                rmsnorm_T(h, rows, gbc, xn, pT, xnT1, ss, junk)
                k.dve.tensor_copy(out=xnT[:, :, j * 128:j * 128 + rows], in_=xnT1[:, :, :rows])
            for hc in range(22):
                pg = pG[hc % 2]; pu = pU[hc % 2]
                for kk in range(8):
                    k.pe.matmul(out=pg[:, :gn], lhsT=Wg[:, kk, hc * 128:(hc + 1) * 128], rhs=xnT[:, kk, :gn], start=(kk == 0), stop=(kk == 7))
                for kk in range(8):
                    k.pe.matmul(out=pu[:, :gn], lhsT=Wu[:, kk, hc * 128:(hc + 1) * 128], rhs=xnT[:, kk, :gn], start=(kk == 0), stop=(kk == 7))
                a = at[hc % 2]
                k.act.activation(out=sg[:, :gn], in_=pg[:, :gn], func=AF.Silu)
                k.dve.tensor_tensor(out=a[:, :gn], in0=sg[:, :gn], in1=pu[:, :gn], op=ALU.mult)
                k.dma(ACTT[hc, :, g0:g0 + gn], a[:, :gn])
    with k.scope():
        Wd = k.sb('Wd', [128, 22, 1024], BF16); load_w_bf16(Wd, L['ffn_d'][layer], DFF, 1024)
        gbc = k.sb('gbc', [128, 1024])
        if y_out is not None:
            k.dma(gbc[:, :], L['norm_final'][0:1, :].pbc(128))
        aT = [k.sb('aT', [128, 22, 128], BF16) for _ in range(2)]
        ht = [k.sb('ht', [128, 1024]) for _ in range(2)]
        junk = k.sb('junk', [128, 1024]); ss = k.sb('ss', [128, 4]); yt = k.sb('yt', [128, 1024])
        pA = [k.ps('pA', [128, 512]) for _ in range(4)]
        ci = 0
        for ti, (r0, rows) in enumerate(L['tiles']):
            a = aT[ti % 2]; h = ht[ti % 2]
            k.dma(a[:, :, :rows], ACTT[:, :, r0:r0 + rows].rearrange("c p t -> p c t"))
            k.dma(h[:rows, :], H1[r0:r0 + rows, :])
            for c in range(2):
                ps = pA[ci % 4]; ci += 1
                for hc in range(22):
                    k.pe.matmul(out=ps[:rows, :], lhsT=a[:, hc, :rows], rhs=Wd[:, hc, c * 512:(c + 1) * 512], start=(hc == 0), stop=(hc == 21))
                k.dve.tensor_tensor(out=h[:rows, c * 512:(c + 1) * 512], in0=h[:rows, c * 512:(c + 1) * 512], in1=ps[:rows, :], op=ALU.add)
            if y_out is None:
                k.dma(H1[r0:r0 + rows, :], h[:rows, :])
            else:
                k.act.activation(out=junk[:rows, :], in_=h[:rows, :], func=AF.Square, accum_out=ss[:rows, 0:1])
                k.dve.tensor_scalar(out=ss[:rows, 1:2], in0=ss[:rows, 0:1], scalar1=1.0 / 1024, scalar2=1e-6, op0=ALU.mult, op1=ALU.add)
                k.act.activation(out=ss[:rows, 3:4], in_=ss[:rows, 1:2], func=AF.Sqrt)
                k.dve.reciprocal(out=ss[:rows, 2:3], in_=ss[:rows, 3:4])
                k.dve.scalar_tensor_tensor(out=yt[:rows, :], in0=h[:rows, :], scalar=ss[:rows, 2:3], in1=gbc[:rows, :], op0=ALU.mult, op1=ALU.mult)
                if r0 < Tn:
                    k.dma(y_out[0][r0:r0 + rows, :], yt[:rows, :])
                else:
                    k.dma(y_out[1][0:16, :], yt[:16, :])


def phase_proj1(k, L):
    Tn = L['Tn']; NT = L['NT']; identb = L['identb']; identf = L['identf']; H1 = L['H1']
    NBLK = L['NBLK']; NBLKP = L['NBLKP']
    rmsnorm_T = L['rmsnorm_T']; load_w_bf16 = L['load_w_bf16']
    QT2 = L['QT2']; KT2 = L['KT2']; VT2 = L['VT2']; NS2 = L['NS2']
    with k.scope():
        W1 = k.sb('W1', [128, 8, ODC], BF16); load_w_bf16(W1, L['w_in1'], 1024, ODC)
        gbc = k.sb('gbc', [128, 1024]); k.dma(gbc[:, :], L['norm_mix'][1:2, :].pbc(128))
        xt = [k.sb('xt', [128, 1024]) for _ in range(2)]
        junk = k.sb('junk', [128, 1024], BF16); xn = k.sb('xn', [128, 1024], BF16); ss = k.sb('ss', [128, 4])
        xnT = k.sb('xnT', [128, 8, 128], BF16)
        proj = [k.sb('proj', [128, ODC]) for _ in range(2)]
        cs = k.sb('cs', [128, 64])
        tmp = [k.sb('tmp%d' % i, [128, 16, 32]) for i in range(4)]
        qb = k.sb('qb', [128, 16, 64], BF16); kb = k.sb('kb', [128, 4, 64], BF16)
        vb = k.sb('vb', [128, 4, 65], BF16)
        k.op('dve', lambda e: e.memset(vb[:, :, :].ap, 1.0), (), (vb,))
        ones = k.sb('ones', [128, 1]); k.op('dve', lambda e: e.memset(ones[:, :].ap, 1.0), (), (ones,))
        qT = k.sb('qT', [64, 16, 128], BF16); kT = k.sb('kT', [64, 4, 128], BF16)
        qTf = k.sb('qTf', [64, 16, 128])
        meansT = k.sb('meansT', [64, 4, NBLKP])
        k.op('dve', lambda e: e.memset(meansT[:, :, :].ap, 0.0), (), (meansT,))
        gbm = k.sb('gbm', [128, 2, NBLKP])
        score = k.sb('score', [128, 16, NBLKP]); m8 = k.sb('m8', [128, 16, 8]); thr = k.sb('thr', [128, 16])
        negs = k.sb('negs', [128, 16, NBLKP]); negb = k.sb('negb', [128, 16, NBLKP], BF16)
        pT = k.ps('pT', [128, 8, 128], BF16)
        pA = [k.ps('pA', [128, 512]) for _ in range(2)]
        pQ = k.ps('pQ', [64, 8, 128], BF16)
        pK = k.ps('pK', [64, 4, 128], BF16)
        pQf = [k.ps('pQf', [64, 4, 128]) for _ in range(1)]
        pM = k.ps('pM', [64, 4, NBLKP])
        pGt = k.ps('pGt', [128, 16, NBLKP])
        ci = 0
        for ti, (r0, rows) in enumerate(L['tiles']):
            x = xt[ti % 2]; pj = proj[ti % 2]
            k.dma(x[:rows, :], H1[r0:r0 + rows, :])
            k.dma(cs[:rows, :], L['ropecs'][r0:r0 + rows, :])
            rmsnorm_T(x, rows, gbc, xn, pT, xnT, ss, junk)
            for c0 in range(0, ODC, 512):
                ps = pA[ci % 2]; ci += 1
                for kk in range(8):
                    k.pe.matmul(out=ps[:rows, :], lhsT=xnT[:, kk, :rows], rhs=W1[:, kk, c0:c0 + 512], start=(kk == 0), stop=(kk == 7))
                (k.act.copy if ci % 2 else k.dve.tensor_copy)(out=pj[:rows, c0:c0 + 512], in_=ps[:rows, :])
            cosb = lambda n: cs[:rows, 0:32].unsq(1).bc([rows, n, 32])
            sinb = lambda n: cs[:rows, 32:64].unsq(1).bc([rows, n, 32])
            for vi, (c0, n) in enumerate([(0, 16), (1024, 4)]):
                xv = pj[:rows, c0:c0 + n * 64].rearrange("p (h d) -> p h d", h=n)
                E1 = k.dve if vi == 0 else k.pool
                x1 = xv[:, :, 0:32]; x2 = xv[:, :, 32:64]
                t = [tt_[:rows, 0:n, :] for tt_ in tmp]
                E1.tensor_tensor(out=t[0], in0=x1, in1=cosb(n), op=ALU.mult)
                E1.tensor_tensor(out=t[1], in0=x2, in1=sinb(n), op=ALU.mult)
                E1.tensor_tensor(out=t[2], in0=x2, in1=cosb(n), op=ALU.mult)
                E1.tensor_tensor(out=t[3], in0=x1, in1=sinb(n), op=ALU.mult)
                E1.tensor_tensor(out=x1, in0=t[0], in1=t[1], op=ALU.subtract)
                E1.tensor_tensor(out=x2, in0=t[2], in1=t[3], op=ALU.add)
            qv = pj[:rows, 0:1024].rearrange("p (h d) -> p h d", h=16)
            kv_ = pj[:rows, 1024:1280].rearrange("p (h d) -> p h d", h=4)
            vv = pj[:rows, 1280:1536].rearrange("p (h d) -> p h d", h=4)
            k.dve.tensor_copy(out=qb[:rows, :, :], in_=qv)
            k.pool.tensor_copy(out=kb[:rows, :, :], in_=kv_)
            k.pool.tensor_copy(out=vb[:rows, :, 0:64], in_=vv)
            if r0 < Tn:
                k.dma(L['o_moba_p'][r0:r0 + rows, :], pj[:rows, 1024:1536])
            else:
                k.dma(L['o_moba_s'][0:16, :], pj[:16, 1024:1536])
            for hh in range(2):
                for h in range(8):
                    k.pe.transpose(out=pQ[:, h, :rows], in_=qb[:rows, hh * 8 + h, :], identity=identb[:rows, :rows])
                k.act.copy(out=qT[:, hh * 8:(hh + 1) * 8, :rows], in_=pQ[:, :, :rows])
            for h in range(4):
                k.pe.transpose(out=pK[:, h, :rows], in_=kb[:rows, h, :], identity=identb[:rows, :rows])
            k.dve.tensor_copy(out=kT[:, :, :rows], in_=pK[:, :, :rows])
            k.dma(QT2[:, :, r0:r0 + rows], qT[:, :, :rows])
            k.dma(KT2[:, :, r0:r0 + rows], kT[:, :, :rows])
            k.dma(VT2[r0:r0 + rows, :, :], vb[:rows, :, :])
            for hq in range(4):
                pq = pQf[0]
                for h in range(4):
                    k.pe.transpose(out=pq[:, h, :rows], in_=pj[:rows, (hq * 4 + h) * 64:(hq * 4 + h + 1) * 64], identity=identf[:rows, :rows])
                k.act.copy(out=qTf[:, hq * 4:(hq + 1) * 4, :rows], in_=pq[:, :, :rows])
            if r0 >= Tn:
                k.dma(L['QF2'][:, :, :], qTf[:, :, 0:16])
                continue
            blk = r0 // 256
            for h in range(16):
                k.pe.matmul(out=pGt[:rows, h, :], lhsT=qTf[:, h, :rows], rhs=meansT[:, h // 4, :], start=(h == 0), stop=(h == 15), skip_group_check=True)
            k.dma(gbm[:rows, :, :], L['GBM'][r0:r0 + rows, :, :])
            k.dve.tensor_tensor(out=score[:rows, :, :], in0=pGt[:rows, :, :], in1=gbm[:rows, 0:1, :].bc([rows, 16, NBLKP]), op=ALU.add)
            for h in range(16):
                k.dve.max(out=m8[:rows, h, :], in_=score[:rows, h, :])
            k.dve.tensor_scalar(out=thr[:rows, :], in0=m8[:rows, :, 2], scalar1=-1.0e8, scalar2=None, op0=ALU.max)
            k.dve.tensor_tensor(out=negs[:rows, :, :], in0=score[:rows, :, :], in1=thr[:rows, :].unsq(2).bc([rows, 16, NBLKP]), op=ALU.is_lt)
            k.dve.tensor_tensor(out=negs[:rows, :, :], in0=negs[:rows, :, :], in1=gbm[:rows, 1:2, :].bc([rows, 16, NBLKP]), op=ALU.mult)
            k.dve.tensor_scalar(out=negb[:rows, :, :], in0=negs[:rows, :, :], scalar1=NEG, scalar2=None, op0=ALU.mult)
            k.dma(NS2[r0:r0 + rows, :, :], negb[:rows, :, :])
            for g in range(4):
                first = (r0 % 256 == 0)
                k.pe.matmul(out=pM[:, g, blk:blk + 1], lhsT=pj[:rows, 1024 + g * 64:1024 + (g + 1) * 64], rhs=ones[:rows, 0:1],
                            start=(ti == 0 and g == 0), stop=not first, skip_group_check=True)
            if r0 % 256 == 128:
                k.dve.tensor_scalar(out=meansT[:, :, blk:blk + 1], in0=pM[:, :, blk:blk + 1], scalar1=1.0 / 2048, scalar2=None, op0=ALU.mult)


def phase_moba_prompt(k, L):
    Tn = L['Tn']; NT = L['NT']; NBLK = L['NBLK']; NBLKP = L['NBLKP']; identb = L['identb']
    QT2 = L['QT2']; KT2 = L['KT2']; VT2 = L['VT2']; NS2 = L['NS2']; MO = L['MO']
    with k.scope():
        tri = k.sb('tri', [128, 2, 128], BF16); k.dma(tri[:, :, :], L['TRIt'][:, :, :])
        eexp = k.sb('eexp', [NBLKP, Tn], BF16); k.dma(eexp[:, :], L['EEXP2'][0:NBLKP, 0:Tn])
        ps_s = [k.ps('ps_s', [128, 512]) for _ in range(3)]
        po_ = [k.ps('po', [128, 4, 65]) for _ in range(2)]
        ps_t = k.ps('ps_t', [NBLKP, 4, 128], BF16)
        K2 = k.sb('K2', [64, Tn], BF16); V2 = k.sb('V2', [128, NT, 65], BF16)
        q4 = [k.sb('q4', [64, 4, 128], BF16) for _ in range(2)]
        ns = [k.sb('ns', [128, 4, NBLKP], BF16) for _ in range(2)]
        negT4 = k.sb('negT4', [NBLKP, 4, 128], BF16)
        ebuf = [k.sb('ebuf', [128, 512], BF16) for _ in range(3)]
        rr = k.sb('rr', [128, 8]); accb = k.sb('accb', [128, 4, 64], BF16)
        cnt = [0]
        for g in range(4):
            k.dma(K2[:, :], KT2[:, g, 0:Tn])
            k.dma(V2[:, :, :], VT2[0:Tn, g, :].rearrange("(kt p) c -> p kt c", p=128))
            for tt in range(NT):
                t0 = tt * 128
                q = q4[tt % 2]; n_ = ns[tt % 2]
                k.dma(q[:, :, :], QT2[:, g * 4:(g + 1) * 4, t0:t0 + 128])
                k.dma(n_[:, :, :], NS2[t0:t0 + 128, g * 4:(g + 1) * 4, :])
                for r in range(4):
                    k.pe.transpose(out=ps_t[:, r, :], in_=n_[:, r, :], identity=identb[:, :])
                k.dve.tensor_copy(out=negT4[:, :, :], in_=ps_t[:, :, :])
                qf = q[:, :, :].rearrange("p r t -> p (r t)")
                nf = negT4[:, :, :].rearrange("p r t -> p (r t)")
                po = po_[tt % 2]
                def sc_m(kt):
                    pss = ps_s[cnt[0] % 3]
                    k.pe.matmul(out=pss[:, :], lhsT=K2[:, kt * 128:(kt + 1) * 128], rhs=qf, start=True, stop=False)
                    k.pe.matmul(out=pss[:, :], lhsT=eexp[:, kt * 128:(kt + 1) * 128], rhs=nf, start=False, stop=True)
                    e = ebuf[cnt[0] % 3]; cnt[0] += 1
                    k.act.activation(out=e[:, :], in_=pss[:, :], func=AF.Exp, scale=0.125)
                    if kt == tt:
                        ev = e[:, :].rearrange("p (r t) -> p r t", r=4)
                        k.pool.tensor_tensor(out=ev, in0=ev, in1=tri[:, 0:1, :].bc([128, 4, 128]), op=ALU.mult)
                    return e

                def pv_m(kt, e):
                    for r in range(4):
                        k.pe.matmul(out=po[:, r, :], lhsT=e[:, r * 128:(r + 1) * 128], rhs=V2[:, kt, :], start=(kt == 0 and r == 0), stop=(kt == tt), skip_group_check=True)

                pipe(range(tt + 1), sc_m, pv_m)
                k.dve.tensor_scalar(out=rr[:, 0:4], in0=po[:, :, 64], scalar1=1e-30, scalar2=None, op0=ALU.max)
                k.dve.reciprocal(out=rr[:, 4:8], in_=rr[:, 0:4])
                k.dve.tensor_tensor(out=accb[:, :, :], in0=po[:, :, 0:64], in1=rr[:, 4:8].unsq(2).bc([128, 4, 64]), op=ALU.mult)
                k.dma(MO[t0:t0 + 128, g * 256:(g + 1) * 256], accb[:, :, :].rearrange("p r d -> p (r d)"))


def page_indices(k, L):
    P = L['P']
    pti = k.sb('pti', [128, 4 * P], I32); ptf = k.sb('ptf', [128, 4 * P]); io = k.sb('io', [128, 1])
    idx = k.sb('idx', [128, 4 * P], I32)
    k.dma(pti[:, :], L['ptab'][:, :].rearrange("b p -> (b p)").unsq(0).pbc(128) if False else L['ptab'][:, :].rearrange("(o b) p -> o (b p)", o=1).pbc(128))
    k.dma(io[:, :], L['IOTA'][:, :])
    k.dve.tensor_copy(out=ptf[:, :], in_=pti[:, :])
    k.dve.tensor_scalar(out=ptf[:, :], in0=ptf[:, :], scalar1=128.0, scalar2=None, op0=ALU.mult)
    k.dve.tensor_tensor(out=ptf[:, :], in0=ptf[:, :], in1=io[:, 0:1].bc([128, 4 * P]), op=ALU.add)
    k.dve.tensor_copy(out=idx[:, :], in_=ptf[:, :])
    return idx


def gather_page(k, pg, cache, idx, col):
    k.raw16('pool', lambda e: e.indirect_dma_start(out=pg[:, :].ap, out_offset=None, in_=cache[:, :].ap,
                                                   in_offset=bass.IndirectOffsetOnAxis(ap=idx[:, col:col + 1].ap, axis=0)),
            reads=[cache.t if isinstance(cache, V) else cache, idx], writes=[pg])


def phase_nsa_sample(k, L):
    Tn = L['Tn']; P = L['P']; LP = L['LP']; NSELS = L['NSELS']; NBS = L['NBS']; NBTS = L['NBTS']
    QT = L['QT']; KT = L['KT']; VT = L['VT']; GT = L['GT']; AO = L['AO']; identb = L['identb']; identf = L['identf']
    cache = L['cache_nsa']; st_win = L['st_win']
    NJ = LP // 64
    with k.scope():
        ps_s = [k.ps('ps_s', [128, 512]) for _ in range(2)]
        W = load_cmp_weights(k, L, ps_s[0])
        idx = page_indices(k, L)
        eexp = k.sb('eexp', [NJ, LP], BF16); k.dma(eexp[:, :], L['EEXP'][0:NJ, 0:LP])
        smk = k.sb('smk', [128, 2, 16], BF16); k.dma(smk[:, :, :], L['SMK'][:, :, :])
        fbs = k.sb('fbs', [4, NSELS]); k.dma(fbs[:, :], L['FBs'][:, :])
        pX = [k.ps('pX', [64, 4, 128]) for _ in range(1)]
        pXb = [k.ps('pXb', [64, 8, 128], BF16) for _ in range(1)]
        stg = [k.sb('stg', [128, 384], BF16) for _ in range(2)]
        po_a = [k.ps('po_a', [4, 2, 65 + NSELS]) for _ in range(2)]
        po_b = k.ps('po_b', [4, 4, 65])
        ps_t = k.ps('ps_t', [128, 16], BF16)
        si = [0]

        def sbank():
            b = ps_s[si[0] % 2]; si[0] += 1
            return b

        pg = [k.sb('pg', [128, 512]) for _ in range(3)]
        KcTs = k.sb('KcTs', [64, 2, LP], BF16); VcTs = k.sb('VcTs', [64, 2, LP], BF16); KsTs = k.sb('KsTs', [64, 2, LP], BF16)
        Vss = k.sb('Vss', [128, P, 2, 65], BF16)
        k.op('dve', lambda e: e.memset(Vss[:, :, :, 64:65].ap, 1.0), (), (Vss,))
        KwTs = k.sb('KwTs', [64, 2, 512], BF16); Vws = k.sb('Vws', [128, 4, 2, 65], BF16)
        k.op('dve', lambda e: e.memset(Vws[:, :, :, 64:65].ap, 1.0), (), (Vws,))
        wt = [k.sb('wt', [128, 256]) for _ in range(2)]
        KcCs = k.sb('KcCs', [64, NBTS * 128], BF16)
        VcCs = k.sb('VcCs', [128, NBTS, 65 + NSELS], BF16)
        k.dma(VcCs[:, :, 65:65 + NSELS], L['OVs'][:, :, :])
        k.op('dve', lambda e: e.memset(VcCs[:, :, 64:65].ap, 1.0), (), (VcCs,))
        hx = k.sb('hx', [64, 512]); ta = k.sb('ta', [64, 512]); tb = k.sb('tb', [64, 512])
        hk = k.sb('hk', [64, 512], BF16); hv_ = k.sb('hv_', [64, 512], BF16)
        qs_all = k.sb('qs_all', [64, 8, 16], BF16); k.dma(qs_all[:, :, :], QT[:, :, Tn:Tn + 16])
        knew = k.sb('knew', [64, 6, 16], BF16); k.dma(knew[:, :, :], KT[:, :, Tn:Tn + 16])
        vnew = [k.sb('vnew', [4, 2, 2, 65], BF16) for _ in range(2)]
        gts = [k.sb('gts', [4, 24]) for _ in range(2)]
        q16 = k.sb('q16', [64, 4, 4], BF16)
        ebuf = [k.sb('ebuf', [128, 16], BF16) for _ in range(4)]
        ei = [0]

        def enext():
            b = ebuf[ei[0] % 4]; ei[0] += 1
            return b

        acc = k.sb('acc', [4, 8, 64]); accb = k.sb('accb', [4, 8, 64], BF16); tmp4 = k.sb('tmp4', [4, 4, 64])
        rr = k.sb('rr', [4, 16]); imp = k.sb('imp', [4, NSELS]); score = k.sb('score', [4, NSELS]); sc2 = k.sb('sc2', [4, NSELS])
        m8 = k.sb('m8', [4, 16]); negs = k.sb('negs', [4, NSELS], BF16)
        negT16 = k.sb('negT16', [NJ, 4, 4], BF16)
        SD = L['cfg'].get('sdbg', 99)
        for bl in range(4 if SD >= 99 else 1):
            vn = vnew[bl % 2]; gt_ = gts[bl % 2]
            if SD < 1:
                break
            import os
            if os.environ.get('SKIPVN') != '1':
                k.dma(vn[:, :, :, :], VT[Tn + bl * 4:Tn + bl * 4 + 4, :, :, :])
                k.dma(gt_[:, :], GT[Tn + bl * 4:Tn + bl * 4 + 4, :])
            gv3 = gt_[:, :].rearrange("p (h b) -> p h b", b=3)
            for lp in range(P if os.environ.get('SKIPG') != '1' else 0):
                pgt = pg[lp % 3]
                gather_page(k, pgt, cache, idx, bl * P + lp)
                sg_ = stg[lp % 2]
                k.dve.tensor_copy(out=sg_[:, :], in_=pgt[:, 0:384])
                k.pool.tensor_copy(out=Vss[:, lp, :, 0:64], in_=pgt[:, 384:512].rearrange("p (g d) -> p g d", g=2))
                px0 = pXb[0]
                for i in range(6):
                    k.pe.transpose(out=px0[:, i, :], in_=sg_[:, i * 64:(i + 1) * 64], identity=identb[:, :])
                k.dve.tensor_copy(out=KcTs[:, :, lp * 128:(lp + 1) * 128], in_=px0[:, 0:2, :])
                k.dve.tensor_copy(out=VcTs[:, :, lp * 128:(lp + 1) * 128], in_=px0[:, 2:4, :])
                k.dve.tensor_copy(out=KsTs[:, :, lp * 128:(lp + 1) * 128], in_=px0[:, 4:6, :])
            if SD < 2:
                break
            for wi in range(4):
                w_ = wt[wi % 2]
                k.dma(w_[:, :], st_win[bl, wi * 128:(wi + 1) * 128, :])
                px1 = pX[0]
                for g in range(2):
                    k.pe.transpose(out=px1[:, g, :], in_=w_[:, g * 64:(g + 1) * 64], identity=identf[:, :])
                k.dve.tensor_copy(out=KwTs[:, :, wi * 128:(wi + 1) * 128], in_=px1[:, 0:2, :])
                k.pool.tensor_copy(out=Vws[:, wi, :, 0:64], in_=w_[:, 128:256].rearrange("p (g d) -> p g d", g=2))
            if SD < 3:
                break
            for g in range(2):
                pb = sbank()
                compress(k, W, 0, KcTs[:, g, :], NBS, pb, hx, ta, tb, hk)
                pb2 = sbank()
                k.pe.matmul(out=pb2[0:64, 0:NBS], lhsT=W[0][1][:, :], rhs=hk[:, 0:NBS], start=True, stop=True)
                k.dve.tensor_copy(out=KcCs[:, 0:NBS], in_=pb2[0:64, 0:NBS])
                pb = sbank()
                compress(k, W, 1, VcTs[:, g, :], NBS, pb, hx, ta, tb, hv_)
                for ni in range(NBTS):
                    nn = min(128, NBS - ni * 128)
                    pb3 = sbank()
                    k.pe.matmul(out=pb3[:nn, 0:64], lhsT=hv_[:, ni * 128:ni * 128 + nn], rhs=W[1][1][:, :], start=True, stop=True)
                    k.dve.tensor_copy(out=VcCs[:nn, ni, 0:64], in_=pb3[:nn, 0:64])
                if SD < 4:
                    continue
                k.dve.tensor_copy(out=q16[:, :, :], in_=qs_all[:, g * 4:(g + 1) * 4, bl * 4:(bl + 1) * 4])
                qf = q16[:, :, :].rearrange("p r t -> p (r t)")
                for ni in range(NBTS):
                    nn = min(128, NBS - ni * 128)
                    pss = sbank()
                    k.pe.matmul(out=pss[:nn, 0:16], lhsT=KcCs[:, ni * 128:ni * 128 + nn], rhs=qf, start=True, stop=True)
                    e = enext()
                    k.act.activation(out=e[:nn, :], in_=pss[:nn, 0:16], func=AF.Exp, scale=0.125)
                    for r in range(4):
                        k.pe.matmul(out=po_a[r // 2][:, r % 2, :], lhsT=e[:nn, r * 4:(r + 1) * 4], rhs=VcCs[:nn, ni, :],
                                    start=(ni == 0 and r % 2 == 0), stop=(ni == NBTS - 1), skip_group_check=True)
                for r in range(4):
                    po = po_a[r // 2][:, r % 2, :]
                    k.dve.tensor_scalar(out=rr[:, r:r + 1], in0=po[:, 64:65], scalar1=1e-30, scalar2=None, op0=ALU.max)
                    k.dve.reciprocal(out=rr[:, 4 + r:5 + r], in_=rr[:, r:r + 1])
                    k.dve.tensor_tensor(out=rr[:, 8 + r:9 + r], in0=rr[:, 4 + r:5 + r], in1=gv3[:, g * 4 + r, 0:1], op=ALU.mult)
                    k.dve.tensor_scalar(out=acc[:, g * 4 + r, :], in0=po[:, 0:64], scalar1=rr[:, 8 + r:9 + r], scalar2=None, op0=ALU.mult)
                    if r == 0:
                        k.dve.tensor_scalar(out=imp[:, :], in0=po[:, 65:65 + NSELS], scalar1=rr[:, 4 + r:5 + r], scalar2=None, op0=ALU.mult)
                    else:
                        k.dve.scalar_tensor_tensor(out=imp[:, :], in0=po[:, 65:65 + NSELS], scalar=rr[:, 4 + r:5 + r], in1=imp[:, :], op0=ALU.mult, op1=ALU.add)
                if SD < 5:
                    continue
                k.dve.tensor_tensor(out=score[:, :], in0=imp[:, :], in1=fbs[:, :], op=ALU.add)
                k.dve.max(out=m8[:, 0:8], in_=score[:, :])
                k.dve.match_replace(out=sc2[:, :], in_to_replace=m8[:, 0:8], in_values=score[:, :], imm_value=-3.0e38)
                k.dve.max(out=m8[:, 8:16], in_=sc2[:, :])
                k.dve.tensor_scalar(out=rr[:, 12:13], in0=m8[:, 15:16], scalar1=-1.0e8, scalar2=None, op0=ALU.max)
                k.dve.tensor_scalar(out=negs[:, :], in0=score[:, :], scalar1=rr[:, 12:13], scalar2=NEG, op0=ALU.is_lt, op1=ALU.mult)
                k.pe.transpose(out=ps_t[0:NJ, 0:4], in_=negs[:, 0:NJ], identity=identb[0:4, 0:4])
                k.dve.tensor_copy(out=negT16[:, :, :], in_=ps_t[0:NJ, 0:4].unsq(1).bc([NJ, 4, 4]))
                nf = negT16[:, :, :].rearrange("p r t -> p (r t)")
                if SD < 6:
                    continue
                def sc_s(kt):
                    pss = sbank()
                    e = enext()
                    if kt < P:
                        k.pe.matmul(out=pss[:, 0:16], lhsT=KsTs[:, g, kt * 128:(kt + 1) * 128], rhs=qf, start=True, stop=False)
                        k.pe.matmul(out=pss[:, 0:16], lhsT=eexp[:, kt * 128:(kt + 1) * 128], rhs=nf, start=False, stop=True)
                        k.act.activation(out=e[:, :], in_=pss[:, 0:16], func=AF.Exp, scale=0.125)
                        return (e, 128, Vss[:, kt, g, :])
                    k.pe.matmul(out=pss[0:4, 0:16], lhsT=knew[:, 2 + g, bl * 4:(bl + 1) * 4], rhs=qf, start=True, stop=True)
                    k.act.activation(out=e[0:4, :], in_=pss[0:4, 0:16], func=AF.Exp, scale=0.125)
                    k.pool.tensor_tensor(out=e[0:4, :], in0=e[0:4, :], in1=smk[0:4, 1, :], op=ALU.mult)
                    return (e, 4, vn[:, 0, g, :])

                def pv_s(kt, tup):
                    e, nk, vv = tup
                    for r in range(4):
                        k.pe.matmul(out=po_b[:, r, :], lhsT=e[:nk, r * 4:(r + 1) * 4], rhs=vv, start=(kt == 0 and r == 0), stop=(kt == P), skip_group_check=True)

                pipe(range(P + 1), sc_s, pv_s)
                k.dve.tensor_scalar(out=rr[:, 0:4], in0=po_b[:, :, 64], scalar1=1e-30, scalar2=None, op0=ALU.max)
                k.dve.reciprocal(out=rr[:, 4:8], in_=rr[:, 0:4])
                k.dve.tensor_tensor(out=rr[:, 8:12], in0=rr[:, 4:8], in1=gv3[:, g * 4:(g + 1) * 4, 1], op=ALU.mult)
                k.dve.tensor_tensor(out=tmp4[:, :, :], in0=po_b[:, :, 0:64], in1=rr[:, 8:12].unsq(2).bc([4, 4, 64]), op=ALU.mult)
                k.dve.tensor_tensor(out=acc[:, g * 4:(g + 1) * 4, :], in0=acc[:, g * 4:(g + 1) * 4, :], in1=tmp4[:, :, :], op=ALU.add)
                if SD < 7:
                    continue
                def sc_w(kt):
                    pss = sbank()
                    e = enext()
                    if kt < 4:
                        k.pe.matmul(out=pss[:, 0:16], lhsT=KwTs[:, g, kt * 128:(kt + 1) * 128], rhs=qf, start=True, stop=True)
                        k.act.activation(out=e[:, :], in_=pss[:, 0:16], func=AF.Exp, scale=0.125)
                        if kt == 0:
                            k.pool.tensor_tensor(out=e[:, :], in0=e[:, :], in1=smk[:, 0, :], op=ALU.mult)
                        return (e, 128, Vws[:, kt, g, :])
                    k.pe.matmul(out=pss[0:4, 0:16], lhsT=knew[:, 4 + g, bl * 4:(bl + 1) * 4], rhs=qf, start=True, stop=True)
                    k.act.activation(out=e[0:4, :], in_=pss[0:4, 0:16], func=AF.Exp, scale=0.125)
                    k.pool.tensor_tensor(out=e[0:4, :], in0=e[0:4, :], in1=smk[0:4, 1, :], op=ALU.mult)
                    return (e, 4, vn[:, 1, g, :])

                def pv_w(kt, tup):
                    e, nk, vv = tup
                    for r in range(4):
                        k.pe.matmul(out=po_b[:, r, :], lhsT=e[:nk, r * 4:(r + 1) * 4], rhs=vv, start=(kt == 0 and r == 0), stop=(kt == 4), skip_group_check=True)

                pipe(range(5), sc_w, pv_w)
                k.dve.tensor_scalar(out=rr[:, 0:4], in0=po_b[:, :, 64], scalar1=1e-30, scalar2=None, op0=ALU.max)
                k.dve.reciprocal(out=rr[:, 4:8], in_=rr[:, 0:4])
                k.dve.tensor_tensor(out=rr[:, 8:12], in0=rr[:, 4:8], in1=gv3[:, g * 4:(g + 1) * 4, 2], op=ALU.mult)
                k.dve.tensor_tensor(out=tmp4[:, :, :], in0=po_b[:, :, 0:64], in1=rr[:, 8:12].unsq(2).bc([4, 4, 64]), op=ALU.mult)
                k.dve.tensor_tensor(out=acc[:, g * 4:(g + 1) * 4, :], in0=acc[:, g * 4:(g + 1) * 4, :], in1=tmp4[:, :, :], op=ALU.add)
            k.dve.tensor_copy(out=accb[:, :, :], in_=acc[:, :, :])
            k.dma(AO[Tn + bl * 4:Tn + bl * 4 + 4, 512:1024], accb[:, :, :].rearrange("p h d -> p (h d)"))


def phase_moba_sample(k, L):
    Tn = L['Tn']; P = L['P']; LP = L['LP']; NBLKS = L['NBLKS']; NBLKSP = L['NBLKSP']
    QT2 = L['QT2']; KT2 = L['KT2']; VT2 = L['VT2']; MO = L['MO']; identb = L['identb']; identf = L['identf']
    cache = L['cache_moba']
    with k.scope():
        idx = page_indices(k, L)
        eexp = k.sb('eexp', [NBLKSP, LP], BF16); k.dma(eexp[:, :], L['EEXP2'][0:NBLKSP, 0:LP])
        smk = k.sb('smk', [128, 2, 16], BF16); k.dma(smk[:, :, :], L['SMK'][:, :, :])
        ones = k.sb('ones', [128, 1]); k.op('dve', lambda e: e.memset(ones[:, :].ap, 1.0), (), (ones,))
        ps_s = [k.ps('ps_s', [128, 512]) for _ in range(2)]
        pXb = [k.ps('pXb', [64, 4, 128], BF16) for _ in range(2)]
        stg = [k.sb('stg', [128, 256], BF16) for _ in range(2)]; stf = [k.sb('stf', [128, 256]) for _ in range(2)]
        pM = k.ps('pM', [64, 4, NBLKSP])
        pGt = k.ps('pGt', [4, 16, NBLKSP])
        po_b = k.ps('po_b', [4, 4, 65])
        ps_t = k.ps('ps_t', [NBLKSP, 16, 4], BF16)
        pg = [k.sb('pg', [128, 512]) for _ in range(3)]
        K2s = k.sb('K2s', [64, 4, LP], BF16); V2s = k.sb('V2s', [128, P, 4, 65], BF16)
        k.op('dve', lambda e: e.memset(V2s[:, :, :, 64:65].ap, 1.0), (), (V2s,))
        meansT = k.sb('meansT', [64, 4, NBLKSP])
        qf32 = k.sb('qf32', [64, 16, 16]); k.dma(qf32[:, :, :], L['QF2'][:, :, :])
        qs_all = k.sb('qs_all', [64, 16, 16], BF16); k.dma(qs_all[:, :, :], QT2[:, :, Tn:Tn + 16])
        knew = k.sb('knew', [64, 4, 16], BF16); k.dma(knew[:, :, :], KT2[:, :, Tn:Tn + 16])
        vnew = [k.sb('vnew', [4, 4, 65], BF16) for _ in range(2)]
        q16 = k.sb('q16', [64, 4, 4], BF16)
        score = k.sb('score', [4, 16, NBLKSP]); m8 = k.sb('m8', [4, 16, 8]); thr = k.sb('thr', [4, 16])
        negs = k.sb('negs', [4, 16, NBLKSP]); negb = k.sb('negb', [4, 16, NBLKSP], BF16)
        negT = k.sb('negT', [NBLKSP, 16, 4], BF16)
        ebuf = [k.sb('ebuf', [128, 16], BF16) for _ in range(4)]
        rr = k.sb('rr', [4, 8]); accb = k.sb('accb', [4, 16, 64], BF16)
        k.op('dve', lambda e: e.memset(score[:, :, :].ap, -2.0e9), (), (score,))
        cnt = [0]
        for bl in range(4):
            vn = vnew[bl % 2]
            k.dma(vn[:, :, :], VT2[Tn + bl * 4:Tn + bl * 4 + 4, :, :])
            for lp in range(P):
                pgt = pg[lp % 3]
                gather_page(k, pgt, cache, idx, bl * P + lp)
                sg_ = stg[lp % 2]; sf_ = stf[lp % 2]
                k.dve.tensor_copy(out=sg_[:, :], in_=pgt[:, 0:256])
                k.dve.tensor_copy(out=sf_[:, :], in_=pgt[:, 0:256])
                k.pool.tensor_copy(out=V2s[:, lp, :, 0:64], in_=pgt[:, 256:512].rearrange("p (g d) -> p g d", g=4))
                px = pXb[lp % 2]
                for g in range(4):
                    k.pe.transpose(out=px[:, g, :], in_=sg_[:, g * 64:(g + 1) * 64], identity=identb[:, :])
                k.dve.tensor_copy(out=K2s[:, :, lp * 128:(lp + 1) * 128], in_=px[:, :, :])
                blk = lp // 2
                for g in range(4):
                    k.pe.matmul(out=pM[:, g, blk:blk + 1], lhsT=sf_[:, g * 64:(g + 1) * 64], rhs=ones[:, 0:1],
                                start=(lp == 0 and g == 0), stop=(lp % 2 == 1), skip_group_check=True)
            k.dve.tensor_scalar(out=meansT[:, :, 0:NBLKS], in0=pM[:, :, 0:NBLKS], scalar1=1.0 / 2048, scalar2=None, op0=ALU.mult)
            for h in range(16):
                k.pe.matmul(out=pGt[:, h, 0:NBLKS], lhsT=qf32[:, h, bl * 4:(bl + 1) * 4], rhs=meansT[:, h // 4, 0:NBLKS], start=(h == 0), stop=(h == 15), skip_group_check=True)
            k.dve.tensor_copy(out=score[:, :, 0:NBLKS], in_=pGt[:, :, 0:NBLKS])
            for h in range(16):
                k.dve.max(out=m8[:, h, :], in_=score[:, h, :])
            k.dve.tensor_scalar(out=thr[:, :], in0=m8[:, :, 2], scalar1=-1.0e8, scalar2=None, op0=ALU.max)
            k.dve.tensor_tensor(out=negs[:, :, :], in0=score[:, :, :], in1=thr[:, :].unsq(2).bc([4, 16, NBLKSP]), op=ALU.is_lt)
            k.dve.tensor_scalar(out=negb[:, :, :], in0=negs[:, :, :], scalar1=NEG, scalar2=None, op0=ALU.mult)
            for h in range(16):
                k.pe.transpose(out=ps_t[:, h, :], in_=negb[:, h, :], identity=identb[0:4, 0:4])
            k.dve.tensor_copy(out=negT[:, :, :], in_=ps_t[:, :, :])
            for g in range(4):
                k.dve.tensor_copy(out=q16[:, :, :], in_=qs_all[:, g * 4:(g + 1) * 4, bl * 4:(bl + 1) * 4])
                qf = q16[:, :, :].rearrange("p r t -> p (r t)")
                nf = negT[:, g * 4:(g + 1) * 4, :].rearrange("p r t -> p (r t)")
                def sc_ms(kt):
                    pss = ps_s[cnt[0] % 2]
                    e = ebuf[cnt[0] % 4]; cnt[0] += 1
                    if kt < P:
                        k.pe.matmul(out=pss[:, 0:16], lhsT=K2s[:, g, kt * 128:(kt + 1) * 128], rhs=qf, start=True, stop=False)
                        k.pe.matmul(out=pss[:, 0:16], lhsT=eexp[:, kt * 128:(kt + 1) * 128], rhs=nf, start=False, stop=True)
                        k.act.activation(out=e[:, :], in_=pss[:, 0:16], func=AF.Exp, scale=0.125)
                        return (e, 128, V2s[:, kt, g, :])
                    k.pe.matmul(out=pss[0:4, 0:16], lhsT=knew[:, g, bl * 4:(bl + 1) * 4], rhs=qf, start=True, stop=True)
                    k.act.activation(out=e[0:4, :], in_=pss[0:4, 0:16], func=AF.Exp, scale=0.125)
                    k.pool.tensor_tensor(out=e[0:4, :], in0=e[0:4, :], in1=smk[0:4, 1, :], op=ALU.mult)
                    return (e, 4, vn[:, g, :])

                def pv_ms(kt, tup):
                    e, nk, vv = tup
                    for r in range(4):
                        k.pe.matmul(out=po_b[:, r, :], lhsT=e[:nk, r * 4:(r + 1) * 4], rhs=vv, start=(kt == 0 and r == 0), stop=(kt == P), skip_group_check=True)

                pipe(range(P + 1), sc_ms, pv_ms)
                k.dve.tensor_scalar(out=rr[:, 0:4], in0=po_b[:, :, 64], scalar1=1e-30, scalar2=None, op0=ALU.max)
                k.dve.reciprocal(out=rr[:, 4:8], in_=rr[:, 0:4])
                k.dve.tensor_tensor(out=accb[:, g * 4:(g + 1) * 4, :], in0=po_b[:, :, 0:64], in1=rr[:, 4:8].unsq(2).bc([4, 4, 64]), op=ALU.mult)
            k.dma(MO[Tn + bl * 4:Tn + bl * 4 + 4, :], accb[:, :, :].rearrange("p h d -> p (h d)"))


def host_consts(cfg):
    Tn, P = cfg['T'], cfg['P']
    TR = Tn + 128
    LP = P * 128
    pos = np.concatenate([np.arange(Tn), np.tile(LP + np.arange(4), 4), np.zeros(112)]).astype(np.float32)
    half = 32
    inv = (10000.0 ** (-np.arange(half, dtype=np.float32) / half)).astype(np.float32)
    ang = pos[:, None] * inv[None, :]
    ropecs = np.concatenate([np.cos(ang), np.sin(ang)], axis=1).astype(np.float32)
    return {
        'ropecs': ropecs,
        'identb': np.eye(128, dtype=np.float32).astype(ml_dtypes.bfloat16),
        'identf': np.eye(128, dtype=np.float32),
        **nsa_consts(Tn, LP),
        'masks64': np.stack([np.triu(np.ones((64, 64), np.float32)), np.triu(np.ones((64, 64), np.float32), 1), np.tril(np.ones((64, 64), np.float32), -1)], axis=1),
    }


def nsa_consts(Tn, LP):
    bf = ml_dtypes.bfloat16
    NBLK = Tn // 256; NBLKP = max(8, NBLK)
    tq = np.arange(Tn)[:, None]; nb = np.arange(NBLKP)[None, :]
    gb0 = np.where(nb < tq // 256, 0.0, -1.0e9).astype(np.float32)
    gb1 = (nb != tq // 256).astype(np.float32)
    GBM = np.stack([gb0, gb1], axis=1)
    Wd_ = max(Tn, LP)
    EE2 = (np.arange(Wd_)[None, :] // 256 == np.arange(64)[:, None]).astype(np.float32).astype(bf)
    NSEL = Tn // 64; NB = Tn // 16 - 1; NBT = (NB + 127) // 128
    t = np.arange(Tn)[:, None]; j = np.arange(NSEL)[None, :]
    cur = t // 64
    forced = ((j == cur) | (j == cur - 1) | (j == 0)).astype(np.float32)
    FB = np.where(j <= cur, 100.0 * forced, -1.0e9).astype(np.float32)
    n = np.arange(NBT * 128)[:, None]
    lo = np.maximum(n * 16, j * 64); hi = np.minimum(n * 16 + 32, (j + 1) * 64)
    ov = (np.clip(hi - lo, 0, None).astype(np.float32) / 32.0)
    ov[NB:] = 0
    OV = ov.reshape(NBT, 128, NSEL).transpose(1, 0, 2).astype(bf)
    nl = np.arange(128)[:, None, None]; idx = np.arange(17)[None, :, None]; tl = np.arange(128)[None, None, :]
    CM = (16 * nl + 31 - 128 * idx <= tl).astype(np.float32).astype(bf)
    s_ = np.arange(128)[:, None]; t_ = np.arange(128)[None, :]
    TRI = np.stack([(s_ <= t_), (s_ > t_)], axis=1).astype(np.float32).astype(bf)
    W = max(Tn, LP)
    EE = (np.arange(W)[None, :] // 64 == np.arange(128)[:, None]).astype(np.float32).astype(bf)
    NSELS = LP // 64 + 1; NBS = LP // 16 - 1; NBTS = (NBS + 127) // 128
    js = np.arange(NSELS)[None, :]
    FBs = np.tile((100.0 * ((js == 0) | (js == NSELS - 2) | (js == NSELS - 1))).astype(np.float32), (4, 1))
    ns_ = np.arange(NBTS * 128)[:, None]
    lo = np.maximum(ns_ * 16, js * 64); hi = np.minimum(ns_ * 16 + 32, (js + 1) * 64)
    ovs = (np.clip(hi - lo, 0, None).astype(np.float32) / 32.0); ovs[NBS:] = 0
    OVs = ovs.reshape(NBTS, 128, NSELS).transpose(1, 0, 2).astype(bf)
    i_ = np.arange(128)[:, None]; tq_ = np.tile(np.arange(4), 4)[None, :]
    SMK = np.stack([(i_ > tq_), (i_ <= tq_)], axis=1).astype(np.float32).astype(bf)
    return {'FBt': FB, 'OVt': OV, 'CMt': CM, 'TRIt': TRI, 'EEXP': EE, 'GBM': GBM, 'EEXP2': EE2,
            'FBs': FBs, 'OVs': OVs, 'SMK': SMK, 'IOTA': np.arange(128, dtype=np.float32).reshape(128, 1)}


def make_in_maps(cfg, inputs, ncores):
    Tn, P = cfg['T'], cfg['P']
    hc = host_consts(cfg)
    B = inputs['x_prompt'].shape[0]
    maps = []
    for c in range(ncores):
        b = c % B
        sl = slice(4 * c, 4 * c + 4)
        m = {
            'xp': np.ascontiguousarray(inputs['x_prompt'][b]),
            'xs': np.ascontiguousarray(inputs['x_sample'][sl]).reshape(16, 1024),
            'st_win': np.ascontiguousarray(inputs['state_win_kv'][0, sl]).reshape(4, 512, 256),
            'st_wkv': np.ascontiguousarray(inputs['state_wkv'][0, sl]),
            'st_shift': np.ascontiguousarray(inputs['state_shift'][0, sl]),
            'ptab': np.ascontiguousarray(inputs['page_table'][sl]).astype(np.int32),
            'norm_mix': inputs['norm_mix'], 'norm_ffn': inputs['norm_ffn'], 'norm_final': inputs['norm_final'].reshape(1, 1024),
            'w_in0': inputs['even_w_in'][0], 'w_out0': inputs['even_w_out'][0],
            'gate_b': inputs['nsa_gate_b'][0].reshape(1, 24),
            'cache_nsa': inputs['cache_nsa_kv'][0].reshape(-1, 512), 'cache_moba': inputs['cache_moba_kv'][0].reshape(-1, 512),
            'w_in1': inputs['odd_w_in'][0], 'w_out1': inputs['odd_w_out'][0],
            'ffn_g0': inputs['ffn_w_gate'][0], 'ffn_g1': inputs['ffn_w_gate'][1], 'ffn_u0': inputs['ffn_w_up'][0], 'ffn_u1': inputs['ffn_w_up'][1],
            'ffn_d0': inputs['ffn_w_down'][0], 'ffn_d1': inputs['ffn_w_down'][1],
            'cmp_w1': inputs['nsa_cmp_w1'][0], 'cmp_pe': inputs['nsa_cmp_pe'][0], 'cmp_w2': inputs['nsa_cmp_w2'][0],
            'rw_mu': inputs['rwkv_mu'][0].reshape(1, RWC),
            'rw_vec': np.stack([inputs['rwkv_w0'][0], inputs['rwkv_a0'][0], inputs['rwkv_k_k'][0], inputs['rwkv_k_a'][0],
                                inputs['rwkv_r_k'][0].reshape(512), inputs['rwkv_ln_g'][0], inputs['rwkv_ln_b'][0]]).astype(np.float32),
            'rw_wup': inputs['rwkv_w_up'][0], 'rw_aup': inputs['rwkv_a_up'][0], 'rw_gup': inputs['rwkv_g_up'][0],
        }
        m.update(hc)
        maps.append(m)
    return maps


CFG_FULL = {'T': 4096, 'P': 64, 'NPHYS': 2560, 'stages': 'ABCSDEFMG'}


def kernel(**inputs):
    cfg = dict(CFG_FULL)
    inputs = {k_: np.asarray(v) for k_, v in inputs.items()}
    nc, kb = build(cfg)
    maps = make_in_maps(cfg, inputs, 8)
    res = run_bass_kernel_spmd(nc, maps, core_ids=list(range(8)))
    R = res.results
    f32 = np.float32
    y_p = np.stack([R[c]['y_p'] for c in range(4)]).astype(f32)
    y_s = np.concatenate([R[c]['y_s'].reshape(4, 4, 1024) for c in range(8)]).astype(f32)
    nsa_p = np.stack([R[c]['o_nsa_p'].reshape(4096, 4, 2, 64) for c in range(4)])[None].astype(f32)
    nsa_s = np.concatenate([R[c]['o_nsa_s'].reshape(4, 4, 4, 2, 64) for c in range(8)])[None].astype(f32)
    moba_p = np.stack([R[c]['o_moba_p'].reshape(4096, 2, 4, 64) for c in range(4)])[None].astype(f32)
    moba_s = np.concatenate([R[c]['o_moba_s'].reshape(4, 4, 2, 4, 64) for c in range(8)])[None].astype(f32)
    win_p = np.stack([R[c]['o_win_p'].reshape(512, 2, 2, 64) for c in range(4)])[None].astype(f32)
    win_s = np.concatenate([R[c]['o_win_s'].reshape(4, 512, 2, 2, 64) for c in range(8)])[None].astype(f32)
    wkv_p = np.stack([R[c]['o_wkv_p'] for c in range(4)])[None].astype(f32)
    wkv_s = np.concatenate([R[c]['o_wkv_s'] for c in range(8)])[None].astype(f32)
    sh_p = np.concatenate([R[c]['o_sh_p'] for c in range(4)])[None].astype(f32)
    sh_s = np.concatenate([R[c]['o_sh_s'] for c in range(8)])[None].astype(f32)
    return (y_p, y_s, nsa_p, nsa_s, moba_p, moba_s, win_p, win_s, wkv_p, wkv_s, sh_p, sh_s)
```

```python
import numpy as np
import ml_dtypes
from contextlib import ExitStack, contextmanager
import concourse.bass as bass
import concourse.mybir as mybir
from concourse.bass_utils import run_bass_kernel_spmd

F32 = mybir.dt.float32
BF16 = mybir.dt.bfloat16
I32 = mybir.dt.int32
AF = mybir.ActivationFunctionType
ALU = mybir.AluOpType
AX = mybir.AxisListType

ENGS = ['pe', 'act', 'dve', 'pool', 'sp']
NDMA = 12
SAME_SYNC = {'pe': False, 'act': True, 'dve': True, 'pool': True, 'sp': False}
WRITE_KEYS = ('out', 'accum_out', 'out_max', 'out_indices', 'out_ap')


class T:
    def __init__(self, h, name):
        self.h = h
        self.name = name
        self.w = {}
        self.r = {}

    def __getitem__(self, idx):
        return V(self.h[idx], self)


class V:
    def __init__(self, ap, t):
        self.ap = ap
        self.t = t

    def __getitem__(self, idx):
        return V(self.ap[idx], self.t)

    def rearrange(self, s, **kw):
        return V(self.ap.rearrange(s, **kw), self.t)

    def bc(self, shape):
        return V(self.ap.to_broadcast(list(shape)), self.t)

    def pbc(self, n):
        return V(self.ap.partition_broadcast(n), self.t)

    def bitcast(self, dt):
        return V(self.ap.bitcast(dt), self.t)

    def unsq(self, ax):
        return V(self.ap.unsqueeze(ax), self.t)

    @property
    def shape(self):
        return self.ap.shape


def _merge(d, s):
    for k, v in s.items():
        if d.get(k, 0) < v:
            d[k] = v


class EngProxy:
    def __init__(self, kb, name):
        self.kb = kb
        self.name = name

    def __getattr__(self, opname):
        kb = self.kb
        name = self.name

        def call(**kw):
            reads, writes = [], []
            kw2 = {}
            for key, v in kw.items():
                if isinstance(v, V):
                    (writes if key in WRITE_KEYS else reads).append(v.t)
                    kw2[key] = v.ap
                else:
                    kw2[key] = v
            return kb.op(name, lambda e: getattr(e, opname)(**kw2), reads, writes)

        return call


class KB:
    def __init__(self):
        self.nc = bass.Bass("TRN2", target_bir_lowering=False)
        self.es = ExitStack()
        self.cnt = {}
        self.known = {e: {} for e in ENGS}
        self.sem = {}
        nc = self.nc
        self.eng = {'pe': nc.tensor, 'act': nc.scalar, 'dve': nc.vector, 'pool': nc.gpsimd, 'sp': nc.sync}
        for e in ENGS:
            self._mksem('c_' + e)
        for i in range(NDMA):
            self._mksem('d%d' % i)
        for i in range(4):
            self._mksem('g%d' % i)
        self.dma_rr = 0
        self.g_rr = 0
        self.pe = EngProxy(self, 'pe')
        self.act = EngProxy(self, 'act')
        self.dve = EngProxy(self, 'dve')
        self.pool = EngProxy(self, 'pool')
        self.stack = [self.es]
        self.ninst = 0
        self.uid = 0

    def _mksem(self, name):
        self.sem[name] = self.es.enter_context(self.nc.semaphore(name))
        self.cnt[name] = 0

    def dram(self, name, shape, dt, kind="Internal"):
        h = self.nc.dram_tensor(name, list(shape), dt, kind=kind)
        return T(h.ap(), name)

    def sb(self, name, shape, dt=F32):
        self.uid += 1
        h = self.stack[-1].enter_context(self.nc.sbuf_tensor("%s_%d" % (name, self.uid), list(shape), dt))
        return T(h, name)

    def ps(self, name, shape, dt=F32):
        self.uid += 1
        h = self.stack[-1].enter_context(self.nc.psum_tensor("%s_%d" % (name, self.uid), list(shape), dt))
        return T(h, name)

    @contextmanager
    def scope(self):
        es = ExitStack()
        self.stack.append(es)
        try:
            yield
        finally:
            self.barrier()
            self.stack.pop()
            es.close()

    def _deps(self, eng, reads, writes, own):
        deps = {}
        for t in reads:
            _merge(deps, t.w)
        for t in writes:
            _merge(deps, t.w)
            _merge(deps, t.r)
        kn = self.known[eng]
        e = self.eng[eng]
        for s, v in deps.items():
            if s == own and not SAME_SYNC[eng]:
                continue
            if kn.get(s, 0) >= v:
                continue
            e.wait_ge(self.sem[s], v)
            self.ninst += 1
            kn[s] = v

    def _mark(self, s, val, reads, writes):
        for t in reads:
            if t.r.get(s, 0) < val:
                t.r[s] = val
        for t in writes:
            if t.w.get(s, 0) < val:
                t.w[s] = val

    def op(self, eng, fn, reads=(), writes=()):
        own = 'c_' + eng
        self._deps(eng, reads, writes, own)
        self.cnt[own] += 1
        val = self.cnt[own]
        fn(self.eng[eng]).then_inc(self.sem[own], 1)
        self.ninst += 1
        self._mark(own, val, reads, writes)

    def dma(self, out, in_, q='sp', **kw):
        reads, writes = [in_.t], [out.t]
        s = 'd%d' % self.dma_rr
        self.dma_rr = (self.dma_rr + 1) % NDMA
        self._deps(q, reads, writes, None)
        self.cnt[s] += 16
        val = self.cnt[s]
        self.eng[q].dma_start(out=out.ap, in_=in_.ap, **kw).then_inc(self.sem[s], 16)
        self.ninst += 1
        self._mark(s, val, reads, writes)

    def raw16(self, q, fn, reads=(), writes=()):
        s = 'g%d' % self.g_rr
        self.g_rr = (self.g_rr + 1) % 4
        self._deps(q, reads, writes, None)
        self.cnt[s] += 16
        val = self.cnt[s]
        fn(self.eng[q]).then_inc(self.sem[s], 16)
        self.ninst += 1
        self._mark(s, val, reads, writes)

    def barrier(self, engs=ENGS):
        for e in engs:
            kn = self.known[e]
            for s, v in self.cnt.items():
                if v > 0 and kn.get(s, 0) < v and s != 'c_' + e:
                    self.eng[e].wait_ge(self.sem[s], v)
                    kn[s] = v

    def dbg(self, name, v, shape, dt=F32):
        if not getattr(self, 'dbg_on', False):
            return
        o = self.dram('dbg_' + name, shape, dt, "ExternalOutput")
        idx = tuple(slice(0, n) for n in shape)
        self.dma(o[idx], v)

    def finish(self):
        self.barrier(['sp'])
        self.es.close()
        return self.nc


RWC = 1792
EVC = 3096
ODC = 1536
DFF = 2816
NEG = -240000.0


def build(cfg):
    Tn, P, NPH = cfg['T'], cfg['P'], cfg['NPHYS']
    stages = cfg.get('stages', 'A')
    NT = Tn // 128
    LP = P * 128
    TR = Tn + 128
    k = KB()
    k.dbg_on = cfg.get('dbg', False)
    nc = k.nc
    IN = lambda n, s, d=F32: k.dram(n, s, d, "ExternalInput")
    OUT = lambda n, s, d=F32: k.dram(n, s, d, "ExternalOutput")
    xp = IN('xp', [Tn, 1024]); xs = IN('xs', [16, 1024])
    st_win = IN('st_win', [4, 512, 256]); st_wkv = IN('st_wkv', [4, 8, 64, 64]); st_shift = IN('st_shift', [4, RWC])
    ptab = IN('ptab', [4, P], I32)
    norm_mix = IN('norm_mix', [2, 1024]); norm_ffn = IN('norm_ffn', [2, 1024]); norm_final = IN('norm_final', [1, 1024])
    w_in0 = IN('w_in0', [1024, EVC]); w_out0 = IN('w_out0', [1024, 1024])
    gate_b = IN('gate_b', [1, 24])
    w_in1 = IN('w_in1', [1024, ODC]); w_out1 = IN('w_out1', [1024, 1024])
    ffn_g = [IN('ffn_g%d' % i, [1024, DFF]) for i in range(2)]; ffn_u = [IN('ffn_u%d' % i, [1024, DFF]) for i in range(2)]
    ffn_d = [IN('ffn_d%d' % i, [DFF, 1024]) for i in range(2)]
    NBLK = Tn // 256; NBLKP = max(8, NBLK)
    GBM = IN('GBM', [Tn, 2, NBLKP]); EEXP2 = IN('EEXP2', [64, max(Tn, LP)], BF16)
    cache_nsa = IN('cache_nsa', [NPH * 128, 512]); cache_moba = IN('cache_moba', [NPH * 128, 512])
    NSELS = LP // 64 + 1; NBS = LP // 16 - 1; NBTS = (NBS + 127) // 128
    NBLKS = LP // 256; NBLKSP = max(8, NBLKS)
    FBs = IN('FBs', [4, NSELS]); OVs = IN('OVs', [128, NBTS, NSELS], BF16)
    SMK = IN('SMK', [128, 2, 16], BF16)
    IOTA = IN('IOTA', [128, 1])
    rw_mu = IN('rw_mu', [1, RWC]); rw_vec = IN('rw_vec', [7, 512])
    rw_wup = IN('rw_wup', [64, 512]); rw_aup = IN('rw_aup', [64, 512]); rw_gup = IN('rw_gup', [128, 512])
    masks64 = IN('masks64', [64, 3, 64])
    NSEL = Tn // 64; NB = Tn // 16 - 1; NBT = (NB + 127) // 128
    cmp_w1 = IN('cmp_w1', [2, 32, 64, 64]); cmp_pe = IN('cmp_pe', [2, 32, 64]); cmp_w2 = IN('cmp_w2', [2, 64, 64])
    FBt = IN('FBt', [Tn, NSEL]); OVt = IN('OVt', [128, NBT, NSEL], BF16); CMt = IN('CMt', [128, 17, 128], BF16)
    TRIt = IN('TRIt', [128, 2, 128], BF16); EEXP = IN('EEXP', [128, max(Tn, LP)], BF16)
    ropecs = IN('ropecs', [TR, 64])
    identb_d = IN('identb', [128, 128], BF16); identf_d = IN('identf', [128, 128])
    y_p = OUT('y_p', [Tn, 1024]); y_s = OUT('y_s', [16, 1024])
    o_nsa_p = OUT('o_nsa_p', [Tn, 512]); o_nsa_s = OUT('o_nsa_s', [16, 512])
    o_moba_p = OUT('o_moba_p', [Tn, 512]); o_moba_s = OUT('o_moba_s', [16, 512])
    WN = min(512, Tn)
    o_win_p = OUT('o_win_p', [WN, 256]); o_win_s = OUT('o_win_s', [4, 512, 256])
    o_wkv_p = OUT('o_wkv_p', [8, 64, 64]); o_wkv_s = OUT('o_wkv_s', [4, 8, 64, 64])
    o_sh_p = OUT('o_sh_p', [1, RWC]); o_sh_s = OUT('o_sh_s', [4, RWC])
    RW = k.dram('RW', [TR, RWC], F32)
    QT = k.dram('QT', [64, 8, TR], BF16)
    KT = k.dram('KT', [64, 6, TR], BF16)
    VcT = k.dram('VcT', [64, 2, TR], BF16)
    VT = k.dram('VT', [TR, 2, 2, 65], BF16)
    GT = k.dram('GT', [TR, 24], F32)
    AO = k.dram('AO', [TR, 1024], BF16, "ExternalOutput" if cfg.get('dbg_ao') else "Internal")
    H1 = k.dram('H1', [TR, 1024], F32, "ExternalOutput" if cfg.get('dbg_ao') else "Internal")
    ACTT = k.dram('ACTT', [22, 128, TR], BF16)
    QT2 = k.dram('QT2', [64, 16, TR], BF16); KT2 = k.dram('KT2', [64, 4, TR], BF16); VT2 = k.dram('VT2', [TR, 4, 65], BF16)
    NS2 = k.dram('NS2', [Tn, 16, NBLKP], BF16)
    QF2 = k.dram('QF2', [64, 16, 16], F32)
    MO = k.dram('MO', [TR, 1024], BF16, "ExternalOutput" if cfg.get('dbg_ao') else "Internal")

    identb = k.sb('identb', [128, 128], BF16); identf = k.sb('identf', [128, 128], F32)
    k.dma(identb[:, :], identb_d[:, :]); k.dma(identf[:, :], identf_d[:, :])

    tiles = [(i * 128, 128) for i in range(NT)] + [(Tn, 16)]

    def rmsnorm_T(xt, rows, gbc, xn, pT, xnT, ss, junk):
        k.act.activation(out=junk[:rows, :], in_=xt[:rows, :], func=AF.Square, accum_out=ss[:rows, 0:1])
        k.dve.tensor_scalar(out=ss[:rows, 1:2], in0=ss[:rows, 0:1], scalar1=1.0 / 1024, scalar2=1e-6, op0=ALU.mult, op1=ALU.add)
        k.act.activation(out=ss[:rows, 3:4], in_=ss[:rows, 1:2], func=AF.Sqrt)
        k.dve.reciprocal(out=ss[:rows, 2:3], in_=ss[:rows, 3:4])
        k.dve.scalar_tensor_tensor(out=xn[:rows, :], in0=xt[:rows, :], scalar=ss[:rows, 2:3], in1=gbc[:rows, :], op0=ALU.mult, op1=ALU.mult)
        for kk in range(8):
            k.pe.transpose(out=pT[:, kk, :rows], in_=xn[:rows, kk * 128:(kk + 1) * 128], identity=identb[:rows, :rows])
        k.act.copy(out=xnT[:, :, :rows], in_=pT[:, :, :rows])

    def load_w_bf16(Wsb, wd, K, N):
        with k.scope():
            stg = [k.sb('wstg', [128, N], F32) for _ in range(2)]
            for kk in range(K // 128):
                s = stg[kk % 2]
                k.dma(s[:, :], wd[kk * 128:(kk + 1) * 128, :])
                (k.pool if kk % 2 else k.dve).tensor_copy(out=Wsb[:, kk, :], in_=s[:, :])

    with k.scope():
        W0 = k.sb('W0', [128, 8, EVC], BF16)
        load_w_bf16(W0, w_in0, 1024, EVC)
        gbc = k.sb('gbc', [128, 1024]); k.dma(gbc[:, :], norm_mix[0:1, :].pbc(128))
        gb = k.sb('gb', [128, 24]); k.dma(gb[:, :], gate_b[0:1, :].pbc(128))
        xt = [k.sb('xt', [128, 1024]) for _ in range(2)]
        junk = k.sb('junk', [128, 1024], BF16)
        xn = k.sb('xn', [128, 1024], BF16)
        ss = k.sb('ss', [128, 4])
        xnT = k.sb('xnT', [128, 8, 128], BF16)
        proj = [k.sb('proj', [128, EVC]) for _ in range(2)]
        cs = k.sb('cs', [128, 64])
        tmp = [k.sb('tmp%d' % i, [128, 8, 32]) for i in range(4)]
        qb = k.sb('qb', [128, 8, 64], BF16)
        kb = k.sb('kb', [128, 6, 64], BF16)
        vb = k.sb('vb', [128, 3, 2, 65], BF16)
        k.dve.memset(ap=vb[:, :, :, :], constant=1.0) if False else k.op('dve', lambda e: e.memset(vb[:, :, :, :].ap, 1.0), (), (vb,))
        gt = k.sb('gt', [128, 24])
        qT = k.sb('qT', [64, 8, 128], BF16); kT = k.sb('kT', [64, 6, 128], BF16); vcT = k.sb('vcT', [64, 2, 128], BF16)
        pT = k.ps('pT', [128, 8, 128], BF16)
        pA = [k.ps('pA', [128, 512]) for _ in range(2)]
        pQ = k.ps('pQ', [64, 8, 128], BF16)
        pK = k.ps('pK', [64, 8, 128], BF16)
        ci = 0
        for ti, (r0, rows) in enumerate(tiles):
            x = xt[ti % 2]
            pj = proj[ti % 2]
            src = xp[r0:r0 + rows, :] if r0 < Tn else xs[0:16, :]
            k.dma(x[:rows, :], src)
            k.dma(cs[:rows, :], ropecs[r0:r0 + rows, :])
            rmsnorm_T(x, rows, gbc, xn, pT, xnT, ss, junk)
            for c0 in range(0, EVC, 512):
                w = min(512, EVC - c0)
                ps = pA[ci % 2]
                for kk in range(8):
                    k.pe.matmul(out=ps[:rows, :w], lhsT=xnT[:, kk, :rows], rhs=W0[:, kk, c0:c0 + w], start=(kk == 0), stop=(kk == 7))
                if ci % 2:
                    k.act.copy(out=pj[:rows, c0:c0 + w], in_=ps[:rows, :w])
                else:
                    k.dve.tensor_copy(out=pj[:rows, c0:c0 + w], in_=ps[:rows, :w])
                ci += 1
            k.dma(RW[r0:r0 + rows, :], pj[:rows, 0:RWC])
            k.dve.tensor_tensor(out=gt[:rows, :], in0=pj[:rows, 3072:3096], in1=gb[:rows, :], op=ALU.add)
            k.act.activation(out=gt[:rows, :], in_=gt[:rows, :], func=AF.Sigmoid)
            k.dma(GT[r0:r0 + rows, :], gt[:rows, :])
            cosb = lambda n: cs[:rows, 0:32].unsq(1).bc([rows, n, 32])
            sinb = lambda n: cs[:rows, 32:64].unsq(1).bc([rows, n, 32])
            qv = pj[:rows, 1792:2304].rearrange("p (h d) -> p h d", h=8)
            views = [(qv, 8, qb[:rows, :, :])]
            for c in range(3):
                kv_ = pj[:rows, 2304 + c * 256:2304 + c * 256 + 128].rearrange("p (g d) -> p g d", g=2)
                views.append((kv_, 2, None))
            for vi, (xv, n, ob) in enumerate(views):
                E1 = k.dve if vi % 2 == 0 else k.pool
                x1 = xv[:, :, 0:32]; x2 = xv[:, :, 32:64]
                t = [tt[:rows, 0:n, :] for tt in tmp]
                E1.tensor_tensor(out=t[0], in0=x1, in1=cosb(n), op=ALU.mult)
                E1.tensor_tensor(out=t[1], in0=x2, in1=sinb(n), op=ALU.mult)
                E1.tensor_tensor(out=t[2], in0=x2, in1=cosb(n), op=ALU.mult)
                E1.tensor_tensor(out=t[3], in0=x1, in1=sinb(n), op=ALU.mult)
                if ob is not None:
                    E1.tensor_tensor(out=ob[:, :, 0:32], in0=t[0], in1=t[1], op=ALU.subtract)
                    E1.tensor_tensor(out=ob[:, :, 32:64], in0=t[2], in1=t[3], op=ALU.add)
                else:
                    E1.tensor_tensor(out=x1, in0=t[0], in1=t[1], op=ALU.subtract)
                    E1.tensor_tensor(out=x2, in0=t[2], in1=t[3], op=ALU.add)
            kvv = pj[:rows, 2304:3072].rearrange("p (c j g d) -> p c j g d", c=3, j=2, g=2)
            for c in range(3):
                k.dve.tensor_copy(out=kb[:rows, 2 * c:2 * c + 2, :], in_=kvv[:, c, 0, :, :])
                k.pool.tensor_copy(out=vb[:rows, c, :, 0:64], in_=kvv[:, c, 1, :, :])
            if r0 < Tn:
                k.dma(o_nsa_p[r0:r0 + rows, :], pj[:rows, 2304:2816])
                if r0 >= Tn - WN:
                    k.dma(o_win_p[r0 - (Tn - WN):r0 - (Tn - WN) + rows, :], pj[:rows, 2816:3072])
                if ti == NT - 1:
                    k.dma(o_sh_p[0:1, :], pj[127:128, 0:RWC])
            else:
                k.dma(o_nsa_s[0:16, :], pj[:16, 2304:2816])
                for bl in range(4):
                    k.dma(o_win_s[bl, 508:512, :], pj[bl * 4:bl * 4 + 4, 2816:3072])
                    k.dma(o_win_s[bl, 0:508, :], st_win[bl, 4:512, :])
                    k.dma(o_sh_s[bl:bl + 1, :], pj[bl * 4 + 3:bl * 4 + 4, 0:RWC])
            for h in range(8):
                k.pe.transpose(out=pQ[:, h, :rows], in_=qb[:rows, h, :], identity=identb[:rows, :rows])
            k.act.copy(out=qT[:, :, :rows], in_=pQ[:, :, :rows])
            for h in range(6):
                k.pe.transpose(out=pK[:, h, :rows], in_=kb[:rows, h, :], identity=identb[:rows, :rows])
            for g in range(2):
                k.pe.transpose(out=pK[:, 6 + g, :rows], in_=vb[:rows, 0, g, 0:64], identity=identb[:rows, :rows])
            k.dve.tensor_copy(out=kT[:, :, :rows], in_=pK[:, 0:6, :rows])
            k.dve.tensor_copy(out=vcT[:, :, :rows], in_=pK[:, 6:8, :rows])
            k.dma(QT[:, :, r0:r0 + rows], qT[:, :, :rows])
            k.dma(KT[:, :, r0:r0 + rows], kT[:, :, :rows])
            k.dma(VcT[:, :, r0:r0 + rows], vcT[:, :, :rows])
            k.dma(VT[r0:r0 + rows, :, :, :], vb[:rows, 1:3, :, :])
    if 'B' in stages:
        phase_rwkv(k, locals())
    if 'C' in stages:
        phase_nsa_prompt(k, locals())
    LL = locals()
    if 'S' in stages:
        phase_nsa_sample(k, LL)
    if 'D' in stages:
        layer_tail(k, LL, 0, AO, w_out0, None, H1, None)
    if 'E' in stages:
        phase_proj1(k, LL)
    if 'F' in stages:
        phase_moba_prompt(k, LL)
    if 'M' in stages:
        phase_moba_sample(k, LL)
    if 'G' in stages:
        layer_tail(k, LL, 1, MO, w_out1, H1, None, (y_p, y_s))
    nc2 = k.finish()
    return nc2, k


def phase_rwkv(k, L):
    Tn = L['Tn']; RW = L['RW']; AO = L['AO']; identf = L['identf']; identb = L['identb']
    rw_mu = L['rw_mu']; rw_vec = L['rw_vec']; st_wkv = L['st_wkv']; st_shift = L['st_shift']
    with k.scope():
        mu = k.sb('mu', [64, RWC]); k.dma(mu[:, :], rw_mu[0:1, :].pbc(64))
        vec = k.sb('vec', [64, 7, 512])
        for i in range(7):
            k.dma(vec[:, i, :], rw_vec[i:i + 1, :].pbc(64))
        w0b, a0b, kkb, kab, rkb, lgb, lbb = [vec[:, i, :] for i in range(7)]
        wup = k.sb('wup', [64, 512]); k.dma(wup[:, :], L['rw_wup'][:, :])
        aup = k.sb('aup', [64, 512]); k.dma(aup[:, :], L['rw_aup'][:, :])
        gup = k.sb('gup', [128, 512]); k.dma(gup[:, :], L['rw_gup'][:, :])
        mk = k.sb('mk', [64, 3, 64]); k.dma(mk[:, :, :], L['masks64'][:, :, :])
        ones = k.sb('ones', [64, 1]); k.op('dve', lambda e: e.memset(ones[:, :].ap, 1.0), (), (ones,))
        banks = [k.ps('bank', [128, 512]) for _ in range(6)]
        pfbs = [k.ps('pfb', [64, 8, 64], BF16) for _ in range(2)]
        bi = [0]

        def bank():
            b = banks[bi[0] % 6]
            bi[0] += 1
            return b

        ST = k.sb('ST', [64, 8, 64]); STb = k.sb('STb', [64, 8, 64], BF16)
        vbs = [k.sb('vb16', [64, 512], BF16) for _ in range(2)]
        cur = [k.sb('cur', [64, RWC]) for _ in range(2)]
        prv = [k.sb('prv', [64, RWC]) for _ in range(2)]
        Lt = k.sb('Lt', [64, 256]); LT = k.sb('LT', [128, 3, 64])
        lw = k.sb('lw', [64, 512]); av = k.sb('av', [64, 512]); gvs = [k.sb('gv', [64, 512]) for _ in range(2)]
        kk = k.sb('kk', [64, 512]); sq = k.sb('sq', [64, 512]); k2s = [k.sb('k2', [64, 512]) for _ in range(2)]; bv = k.sb('bv', [64, 512])
        sm = k.sb('sm', [64, 8, 4]); sm2 = k.sb('sm2', [64, 8, 4])
        eP = k.sb('eP', [64, 512]); eN = k.sb('eN', [64, 512]); ePm = k.sb('ePm', [64, 512])
        Fs = [k.sb('F', [64, 4, 512], BF16) for _ in range(2)]
        FTs = [[k.sb('FT%d' % i, [64, 8, 64], BF16) for i in range(4)] for _ in range(2)]
        GCs = [k.sb('GC', [64, 8]) for _ in range(2)]
        Mbs = [[k.sb('Mb%d' % i, [64, 8, 64], BF16) for i in range(5)] for _ in range(2)]
        Bt = [k.sb('Bt%d' % i, [64, 8, 64], BF16) for i in range(2)]
        BTt = [k.sb('BTt%d' % i, [64, 8, 64], BF16) for i in range(2)]
        Nts = [k.sb('Nt', [64, 8, 64], BF16) for _ in range(2)]
        Zn = k.sb('Zn', [64, 8, 64], BF16); UT = k.sb('UT', [64, 8, 64], BF16)
        yv = k.sb('yv', [64, 512]); yc = k.sb('yc', [64, 512]); t1 = k.sb('t1', [64, 512]); ob = k.sb('ob', [64, 512], BF16)
        Stmp = k.sb('Stmp', [64, 8, 64])
        ci = [0]

        def hv(t, C):
            return t[:C, :].rearrange("p (h d) -> p h d", h=8)

        def chunk(r0, C, first_prev):
            sb_ = ci[0] % 2
            c = cur[sb_]; p = prv[sb_]; ci[0] += 1
            gv = gvs[sb_]; k2 = k2s[sb_]; vb16 = vbs[sb_]; F = Fs[sb_]; FT = FTs[sb_]; GC = GCs[sb_]; Mb = Mbs[sb_]; Nt = Nts[sb_]
            k.dma(c[:C, :], RW[r0:r0 + C, :])
            if first_prev is None:
                k.dma(p[:C, :], RW[r0 - 1:r0 - 1 + C, :])
            else:
                if first_prev == 'zero':
                    k.op('dve', lambda e: e.memset(p[0:1, :].ap, 0.0), (), (p,))
                else:
                    k.dma(p[0:1, :], first_prev)
                k.dma(p[1:C, :], RW[r0:r0 + C - 1, :])
            k.dve.tensor_tensor(out=p[:C, :], in0=p[:C, :], in1=c[:C, :], op=ALU.subtract)
            k.pool.tensor_tensor(out=p[:C, :], in0=p[:C, :], in1=mu[:C, :], op=ALU.mult)
            k.dve.tensor_tensor(out=c[:C, :], in0=c[:C, :], in1=p[:C, :], op=ALU.add)
            xm = c
            k.pool.tensor_copy(out=vb16[:C, :], in_=c[:C, 1024:1536])
            r_ = xm[:C, 0:512]; k_ = xm[:C, 512:1024]; v_ = xm[:C, 1024:1536]
            k.act.activation(out=Lt[:C, 0:64], in_=xm[:C, 1536:1600], func=AF.Tanh)
            k.act.activation(out=Lt[:C, 128:256], in_=xm[:C, 1664:1792], func=AF.Sigmoid)
            k.dve.tensor_copy(out=Lt[:C, 64:128], in_=xm[:C, 1600:1664])
            pb = bank()
            pl = pb[:, 0:192].rearrange("p (a t) -> p a t", a=3)
            k.pe.transpose(out=pl[0:64, 0, :C], in_=Lt[:C, 0:64], identity=identf[:C, :C])
            k.pe.transpose(out=pl[0:64, 1, :C], in_=Lt[:C, 64:128], identity=identf[:C, :C])
            k.pe.transpose(out=pl[:, 2, :C], in_=Lt[:C, 128:256], identity=identf[:C, :C])
            k.dve.tensor_copy(out=LT[0:64, 0:2, :C], in_=pl[0:64, 0:2, :C])
            k.dve.tensor_copy(out=LT[:, 2, :C], in_=pl[:, 2, :C])
            pW = bank(); pA = bank(); pG = bank()
            k.pe.matmul(out=pW[:C, :], lhsT=LT[0:64, 0, :C], rhs=wup[:, :], start=True, stop=True)
            k.pe.matmul(out=pA[:C, :], lhsT=LT[0:64, 1, :C], rhs=aup[:, :], start=True, stop=True)
            k.pe.matmul(out=pG[:C, :], lhsT=LT[:, 2, :C], rhs=gup[:, :], start=True, stop=True)
            k.dve.tensor_tensor(out=lw[:C, :], in0=pW[:C, :], in1=w0b[:C, :], op=ALU.add)
            k.act.activation(out=lw[:C, :], in_=lw[:C, :], func=AF.Sigmoid)
            k.dve.tensor_scalar(out=lw[:C, :], in0=lw[:C, :], scalar1=-0.6065306597126334, scalar2=None, op0=ALU.mult)
            k.dve.tensor_tensor(out=av[:C, :], in0=pA[:C, :], in1=a0b[:C, :], op=ALU.add)
            k.act.activation(out=av[:C, :], in_=av[:C, :], func=AF.Sigmoid)
            k.act.copy(out=gv[:C, :], in_=pG[:C, :])
            k.pool.tensor_tensor(out=kk[:C, :], in0=k_, in1=kkb[:C, :], op=ALU.mult)
            k.pool.tensor_tensor(out=sq[:C, :], in0=kk[:C, :], in1=kk[:C, :], op=ALU.mult)
            k.dve.reduce_sum(out=sm[:C, :, 0], in_=hv(sq, C), axis=AX.X)
            k.act.activation(out=sm[:C, :, 1], in_=sm[:C, :, 0], func=AF.Sqrt)
            k.dve.tensor_scalar(out=sm[:C, :, 1], in0=sm[:C, :, 1], scalar1=1e-12, scalar2=None, op0=ALU.max)
            k.dve.reciprocal(out=sm[:C, :, 2], in_=sm[:C, :, 1])
            if r0 == 0:
                k.dbg('kk0', kk[:C, :], [64, 512]); k.dbg('sq', sq[:C, :], [64, 512]); k.dbg('sm', sm[:C, :, :], [64, 8, 4])
            k.dve.tensor_tensor(out=hv(kk, C), in0=hv(kk, C), in1=sm[:C, :, 2:3].bc([C, 8, 64]), op=ALU.mult)
            k.dve.scalar_tensor_tensor(out=k2[:C, :], in0=av[:C, :], scalar=-1.0, in1=kab[:C, :], op0=ALU.add, op1=ALU.mult)
            k.dve.scalar_tensor_tensor(out=k2[:C, :], in0=k2[:C, :], scalar=1.0, in1=k_, op0=ALU.add, op1=ALU.mult)
            k.pool.tensor_tensor(out=bv[:C, :], in0=kk[:C, :], in1=av[:C, :], op=ALU.mult)
            pC = bank()
            k.pe.matmul(out=pC[:C, :], lhsT=mk[:C, 0, :C], rhs=lw[:C, :], start=True, stop=True)
            k.act.activation(out=eP[:C, :], in_=pC[:C, :], func=AF.Exp)
            k.act.activation(out=eN[:C, :], in_=pC[:C, :], func=AF.Exp, scale=-1.0)
            k.dve.tensor_tensor(out=ePm[:C, :], in0=pC[:C, :], in1=lw[:C, :], op=ALU.subtract)
            k.act.activation(out=ePm[:C, :], in_=ePm[:C, :], func=AF.Exp)
            k.dve.tensor_tensor(out=F[:C, 0, :], in0=kk[:C, :], in1=ePm[:C, :], op=ALU.mult)
            k.pool.tensor_tensor(out=F[:C, 1, :], in0=bv[:C, :], in1=eN[:C, :], op=ALU.mult)
            k.dve.tensor_tensor(out=F[:C, 2, :], in0=k2[:C, :], in1=eN[:C, :], op=ALU.mult)
            k.pool.tensor_tensor(out=F[:C, 3, :], in0=r_, in1=eP[:C, :], op=ALU.mult)
            pg = bank()
            for h in range(8):
                k.pe.matmul(out=pg[0:64, h:h + 1], lhsT=lw[:C, h * 64:(h + 1) * 64], rhs=ones[:C, 0:1], start=True, stop=True)
            k.act.activation(out=GC[:, :], in_=pg[0:64, 0:8], func=AF.Exp)
            for kind in range(4):
                pfv = pfbs[kind % 2]
                for h in range(8):
                    k.pe.transpose(out=pfv[:, h, :C], in_=F[:C, kind, h * 64:(h + 1) * 64], identity=identb[:C, :C])
                k.dve.tensor_copy(out=FT[kind][:, :, :C], in_=pfv[:, :, :C])
            aT, bT, khT, rT = FT
            combos = [(bT, aT, 1), (aT, bT, 2), (khT, aT, 1), (bT, rT, 0), (khT, rT, 0)]
            for i, (lt, rt, mi) in enumerate(combos):
                pm = bank()
                pmv = pm[0:64, :].rearrange("p (h t) -> p h t", h=8)
                for h in range(8):
                    k.pe.matmul(out=pmv[:C, h, :C], lhsT=lt[:, h, :C], rhs=rt[:, h, :C], start=True, stop=True)
                (k.dve if i % 2 == 0 else k.pool).tensor_tensor(out=Mb[i][:C, :, :C], in0=pmv[:C, :, :C], in1=mk[:C, mi:mi + 1, :C].bc([C, 8, C]), op=ALU.mult) if i % 2 == 0 else k.dve.tensor_tensor(out=Mb[i][:C, :, :C], in0=pmv[:C, :, :C], in1=mk[:C, mi:mi + 1, :C].bc([C, 8, C]), op=ALU.mult)
            A, AT, Mka, Mbr, Mkr = Mb
            k.dve.tensor_tensor(out=Nt[:C, :, :C], in0=identf[:C, 0:C].unsq(1).bc([C, 8, C]), in1=A[:C, :, :C], op=ALU.subtract)
            Bc, BTc = A, AT
            nlev = {64: 5, 4: 1}[C]
            for lev in range(nlev):
                Bn = Bt[lev % 2]; BTn = BTt[lev % 2]
                p1 = bank(); p2 = bank()
                p1v = p1[0:64, :].rearrange("p (h t) -> p h t", h=8); p2v = p2[0:64, :].rearrange("p (h t) -> p h t", h=8)
                for h in range(8):
                    k.pe.matmul(out=p1v[:C, h, :C], lhsT=BTc[:C, h, :C], rhs=Bc[:C, h, :C], start=True, stop=True)
                for h in range(8):
                    k.pe.matmul(out=p2v[:C, h, :C], lhsT=Bc[:C, h, :C], rhs=BTc[:C, h, :C], start=True, stop=True)
                k.dve.tensor_copy(out=Bn[:C, :, :C], in_=p1v[:C, :, :C])
                k.dve.tensor_copy(out=BTn[:C, :, :C], in_=p2v[:C, :, :C])
                p3 = bank(); p3v = p3[0:64, :].rearrange("p (h t) -> p h t", h=8)
                for h in range(8):
                    k.pe.matmul(out=p3v[:C, h, :C], lhsT=BTn[:C, h, :C], rhs=Nt[:C, h, :C], start=True, stop=True)
                k.dve.tensor_tensor(out=Nt[:C, :, :C], in0=Nt[:C, :, :C], in1=p3v[:C, :, :C], op=ALU.add)
                Bc, BTc = Bn, BTn
            def s2():
                vh = lambda h: vb16[:C, h * 64:(h + 1) * 64]
                pz = bank(); pzv = pz[0:64, :].rearrange("p (h t) -> p h t", h=8)
                for h in range(8):
                    k.pe.matmul(out=pzv[:C, h, :], lhsT=aT[:, h, :C], rhs=STb[:, h, :], start=True, stop=False)
                    k.pe.matmul(out=pzv[:C, h, :], lhsT=Mka[:C, h, :C], rhs=vh(h), start=False, stop=True)
                k.dve.tensor_scalar(out=Zn[:C, :, :], in0=pzv[:C, :, :], scalar1=-1.0, scalar2=None, op0=ALU.mult)
                pu = bank(); puv = pu[0:64, :].rearrange("p (h t) -> p h t", h=8)
                for h in range(8):
                    k.pe.matmul(out=puv[:C, h, :], lhsT=Nt[:C, h, :C], rhs=Zn[:C, h, :], start=True, stop=True)
                k.dve.tensor_copy(out=UT[:C, :, :], in_=puv[:C, :, :])
                py = bank(); pyv = py[0:64, :].rearrange("p (h t) -> p h t", h=8)
                for h in range(8):
                    k.pe.matmul(out=pyv[:C, h, :], lhsT=rT[:, h, :C], rhs=STb[:, h, :], start=True, stop=False)
                    k.pe.matmul(out=pyv[:C, h, :], lhsT=Mbr[:C, h, :C], rhs=UT[:C, h, :], start=False, stop=False)
                    k.pe.matmul(out=pyv[:C, h, :], lhsT=Mkr[:C, h, :C], rhs=vh(h), start=False, stop=True)
                k.act.copy(out=yv[:C, :], in_=py[:C, :])
                pS = bank(); pSv = pS[0:64, :].rearrange("p (h t) -> p h t", h=8)
                for h in range(8):
                    k.pe.matmul(out=pSv[:, h, :], lhsT=F[:C, 1, h * 64:(h + 1) * 64], rhs=UT[:C, h, :], start=True, stop=False)
                    k.pe.matmul(out=pSv[:, h, :], lhsT=F[:C, 2, h * 64:(h + 1) * 64], rhs=vh(h), start=False, stop=True)
                k.dve.tensor_tensor(out=ST[:, :, :], in0=ST[:, :, :], in1=pSv[:, :, :], op=ALU.add)
                k.dve.tensor_tensor(out=ST[:, :, :], in0=ST[:, :, :], in1=GC[:, :].unsq(2).bc([64, 8, 64]), op=ALU.mult)
                k.pool.tensor_copy(out=STb[:, :, :], in_=ST[:, :, :])
                k.dve.reduce_sum(out=sm2[:C, :, 0], in_=hv(yv, C), axis=AX.X)
                k.dve.tensor_scalar(out=sm2[:C, :, 0], in0=sm2[:C, :, 0], scalar1=1.0 / 64, scalar2=None, op0=ALU.mult)
                k.dve.tensor_tensor(out=hv(yc, C), in0=hv(yv, C), in1=sm2[:C, :, 0:1].bc([C, 8, 64]), op=ALU.subtract)
                k.pool.tensor_tensor(out=t1[:C, :], in0=yc[:C, :], in1=yc[:C, :], op=ALU.mult)
                k.dve.reduce_sum(out=sm2[:C, :, 1], in_=hv(t1, C), axis=AX.X)
                k.dve.tensor_scalar(out=sm2[:C, :, 1], in0=sm2[:C, :, 1], scalar1=1.0 / 64, scalar2=64e-5, op0=ALU.mult, op1=ALU.add)
                k.act.activation(out=sm2[:C, :, 1], in_=sm2[:C, :, 1], func=AF.Sqrt)
                k.dve.reciprocal(out=sm2[:C, :, 2], in_=sm2[:C, :, 1])
                k.dve.tensor_tensor(out=hv(yc, C), in0=hv(yc, C), in1=sm2[:C, :, 2:3].bc([C, 8, 64]), op=ALU.mult)
                k.dve.tensor_tensor(out=yc[:C, :], in0=yc[:C, :], in1=lgb[:C, :], op=ALU.mult)
                k.dve.tensor_tensor(out=yc[:C, :], in0=yc[:C, :], in1=lbb[:C, :], op=ALU.add)
                k.pool.tensor_tensor(out=t1[:C, :], in0=r_, in1=k2[:C, :], op=ALU.mult)
                k.pool.tensor_tensor(out=t1[:C, :], in0=t1[:C, :], in1=rkb[:C, :], op=ALU.mult)
                k.dve.reduce_sum(out=sm2[:C, :, 3], in_=hv(t1, C), axis=AX.X)
                k.dve.tensor_tensor(out=hv(t1, C), in0=xm[:C, 1024:1536].rearrange("p (h d) -> p h d", h=8), in1=sm2[:C, :, 3:4].bc([C, 8, 64]), op=ALU.mult)
                k.dve.tensor_tensor(out=yc[:C, :], in0=yc[:C, :], in1=t1[:C, :], op=ALU.add)
                k.dve.tensor_tensor(out=ob[:C, :], in0=yc[:C, :], in1=gv[:C, :], op=ALU.mult)
                k.dma(AO[r0:r0 + C, 0:512], ob[:C, :])
                if r0 == 0:
                    k.dbg('xm', xm[:C, :], [64, RWC]); k.dbg('lw', lw[:C, :], [64, 512]); k.dbg('av', av[:C, :], [64, 512])
                    k.dbg('kk', kk[:C, :], [64, 512]); k.dbg('k2', k2[:C, :], [64, 512]); k.dbg('F', F[:C, :, :], [64, 4, 512])
                    k.dbg('aT', FT[0][:, :, :], [64, 8, 64]); k.dbg('A', Mb[0][:, :, :], [64, 8, 64]); k.dbg('AT', Mb[1][:, :, :], [64, 8, 64])
                    k.dbg('N', Nt[:, :, :], [64, 8, 64]); k.dbg('UT', UT[:, :, :], [64, 8, 64]); k.dbg('yv', yv[:C, :], [64, 512])
                    k.dbg('ST', ST[:, :, :], [64, 8, 64]); k.dbg('GC', GC[:, :], [64, 8]); k.dbg('gv', gv[:C, :], [64, 512])
                    k.dbg('ob', ob[:C, :], [64, 512], BF16)
            return s2

        def store_state(dst):
            pt = bank(); ptv = pt[0:64, :].rearrange("p (h t) -> p h t", h=8)
            for h in range(8):
                k.pe.transpose(out=ptv[:, h, :], in_=ST[:, h, :], identity=identf[0:64, 0:64])
            k.dve.tensor_copy(out=Stmp[:, :, :], in_=ptv[:, :, :])
            k.dma(dst.rearrange("h i j -> i h j"), Stmp[:, :, :])

        k.op('dve', lambda e: e.memset(ST[:, :, :].ap, 0.0), (), (ST,))
        k.op('dve', lambda e: e.memset(STb[:, :, :].ap, 0.0), (), (STb,))
        pend = None
        for ch in range(Tn // 64):
            nxt = chunk(ch * 64, 64, 'zero' if ch == 0 else None)
            if pend is not None:
                pend()
            pend = nxt
        pend()
        store_state(L['o_wkv_p'][:, :, :])
        for bl in range(4):
            k.dma(Stmp[:, :, :], st_wkv[bl].rearrange("h i j -> i h j"))
            pt = bank(); ptv = pt[0:64, :].rearrange("p (h t) -> p h t", h=8)
            for h in range(8):
                k.pe.transpose(out=ptv[:, h, :], in_=Stmp[:, h, :], identity=identf[0:64, 0:64])
            k.dve.tensor_copy(out=ST[:, :, :], in_=ptv[:, :, :])
            k.dve.tensor_copy(out=STb[:, :, :], in_=ptv[:, :, :])
            chunk(Tn + bl * 4, 4, st_shift[bl:bl + 1, :])()
            store_state(L['o_wkv_s'][bl])


def pipe(items, sc, pv):
    items = list(items)
    if not items:
        return
    nxt = sc(items[0])
    for i, it in enumerate(items):
        cur_e = nxt
        if i + 1 < len(items):
            nxt = sc(items[i + 1])
        pv(it, cur_e)


def gelu_tanh(k, out_bf, x, tmpa, tmpb, shape_idx):
    k.dve.tensor_tensor(out=tmpa, in0=x, in1=x, op=ALU.mult)
    k.dve.tensor_scalar(out=tmpa, in0=tmpa, scalar1=0.044715, scalar2=1.0, op0=ALU.mult, op1=ALU.add)
    k.dve.tensor_tensor(out=tmpa, in0=tmpa, in1=x, op=ALU.mult)
    k.act.activation(out=tmpb, in_=tmpa, func=AF.Tanh, scale=0.7978845608028654)
    k.dve.tensor_scalar(out=tmpb, in0=tmpb, scalar1=1.0, scalar2=0.5, op0=ALU.add, op1=ALU.mult)
    k.dve.tensor_tensor(out=out_bf, in0=tmpb, in1=x, op=ALU.mult)


def load_cmp_weights(k, L, pbias):
    cmp_w1 = L['cmp_w1']; cmp_pe = L['cmp_pe']; cmp_w2 = L['cmp_w2']
    W = {}
    tiles_ = [(k.sb('cw1_%d' % kind, [64, 32, 64], BF16), k.sb('cw2_%d' % kind, [64, 64], BF16),
               k.sb('peT%d' % kind, [64, 32], BF16), k.sb('cbias%d' % kind, [64, 1])) for kind in range(2)]
    sc_ = k.scope(); sc_.__enter__()
    stg = k.sb('cw_stg', [64, 32, 64]); stg2 = k.sb('cw_stg2', [64, 64]); pes = k.sb('pe_stg', [64, 32])
    for kind in range(2):
        w1, w2, peT, bias = tiles_[kind]
        k.dma(stg[:, :, :], cmp_w1[kind].rearrange("c d e -> d c e"))
        k.dve.tensor_copy(out=w1[:, :, :], in_=stg[:, :, :])
        k.dma(stg2[:, :], cmp_w2[kind])
        k.dve.tensor_copy(out=w2[:, :], in_=stg2[:, :])
        k.dma(pes[:, :], cmp_pe[kind].rearrange("c d -> d c"), allow_slow_non_contiguous=True)
        k.dve.tensor_copy(out=peT[:, :], in_=pes[:, :])
        for c in range(32):
            k.pe.matmul(out=pbias[0:64, kind:kind + 1], lhsT=w1[:, c, :], rhs=peT[:, c:c + 1], start=(c == 0), stop=(c == 31))
        k.dve.tensor_copy(out=bias[:, :], in_=pbias[0:64, kind:kind + 1])
        W[kind] = (w1, w2, bias)
    sc_.__exit__(None, None, None)
    return W


def compress(k, W, kind, srcT, NBn, ps_bank, hx, ta, tb, hbf):
    w1, w2, bias = W[kind]
    for c in range(32):
        k.pe.matmul(out=ps_bank[0:64, 0:NBn], lhsT=w1[:, c, :], rhs=srcT[:, c:c + 16 * (NBn - 1) + 1:16], start=(c == 0), stop=(c == 31))
    k.act.activation(out=hx[:, 0:NBn], in_=ps_bank[0:64, 0:NBn], func=AF.Identity, bias=bias[:, 0:1])
    gelu_tanh(k, hbf[:, 0:NBn], hx[:, 0:NBn], ta[:, 0:NBn], tb[:, 0:NBn], None)


def phase_nsa_prompt(k, L):
    Tn = L['Tn']; NT = L['NT']; NSEL = L['NSEL']; NB = L['NB']; NBT = L['NBT']
    QT = L['QT']; KT = L['KT']; VcT = L['VcT']; VT = L['VT']; GT = L['GT']; AO = L['AO']; identb = L['identb']
    with k.scope():
        ps_s = [k.ps('ps_s', [128, 512]) for _ in range(3)]
        W = load_cmp_weights(k, L, ps_s[0])
        cm = k.sb('cm', [128, 17, 128], BF16); k.dma(cm[:, :, :], L['CMt'][:, :, :])
        tri = k.sb('tri', [128, 2, 128], BF16); k.dma(tri[:, :, :], L['TRIt'][:, :, :])
        eexp = k.sb('eexp', [NSEL, Tn], BF16); k.dma(eexp[:, :], L['EEXP'][0:NSEL, 0:Tn])
        po_sel = [k.ps('po_sel', [128, 4, 65]) for _ in range(1)]
        po_win = [k.ps('po_win', [128, 4, 65]) for _ in range(1)]
        po_c = [k.ps('po_c', [128, 65 + NSEL]) for _ in range(2)]
        ps_t = k.ps('ps_t', [128, 128], BF16)
        si = [0]

        def sbank():
            b = ps_s[si[0] % 3]; si[0] += 1
            return b

        KcT = k.sb('KcT', [64, Tn], BF16); VcTs = k.sb('VcTs', [64, Tn], BF16)
        KsT = k.sb('KsT', [64, Tn], BF16); KwT = k.sb('KwT', [64, Tn], BF16)
        Vs = k.sb('Vs', [128, NT, 65], BF16); Vw = k.sb('Vw', [128, NT, 65], BF16)
        KcC = k.sb('KcC', [64, NBT * 128], BF16)
        VcC = k.sb('VcC', [128, NBT, 65 + NSEL], BF16)
        hx = k.sb('hx', [64, 512]); ta = k.sb('ta', [64, 512]); tb = k.sb('tb', [64, 512])
        hk = k.sb('hk', [64, 512], BF16); hv_ = k.sb('hv_', [64, 512], BF16)
        q4 = [k.sb('q4', [64, 4, 128], BF16) for _ in range(2)]
        gtile = [k.sb('gtile', [128, 24]) for _ in range(2)]
        fbt = [k.sb('fbt', [128, NSEL]) for _ in range(2)]
        ebuf = [k.sb('ebuf', [128, 512], BF16) for _ in range(3)]
        ei = [0]

        def enext():
            b = ebuf[ei[0] % 3]; ei[0] += 1
            return b

        acc = k.sb('acc', [128, 4, 64]); accb = k.sb('accb', [128, 4, 64], BF16); tmp4 = k.sb('tmp4', [128, 4, 64])
        rr = k.sb('rr', [128, 16]); imp = k.sb('imp', [128, NSEL]); score = k.sb('score', [128, NSEL]); sc2 = k.sb('sc2', [128, NSEL])
        m8 = k.sb('m8', [128, 16]); negs = k.sb('negs', [128, NSEL], BF16)
        negT4 = k.sb('negT4', [NSEL, 4, 128], BF16)
        for g in range(2):
            k.dma(KcT[:, :], KT[:, 0 + g, 0:Tn]); k.dma(VcTs[:, :], VcT[:, g, 0:Tn])
            k.dma(KsT[:, :], KT[:, 2 + g, 0:Tn]); k.dma(KwT[:, :], KT[:, 4 + g, 0:Tn])
            k.dma(Vs[:, :, :], VT[0:Tn, 0, g, :].rearrange("(kt p) c -> p kt c", p=128))
            k.dma(Vw[:, :, :], VT[0:Tn, 1, g, :].rearrange("(kt p) c -> p kt c", p=128))
            k.dma(VcC[:, :, 65:65 + NSEL], L['OVt'][:, :, :])
            k.op('dve', lambda e: e.memset(VcC[:, :, 64:65].ap, 1.0), (), (VcC,))
            pb = sbank()
            compress(k, W, 0, KcT, NB, pb, hx, ta, tb, hk)
            pb2 = sbank()
            k.pe.matmul(out=pb2[0:64, 0:NB], lhsT=W[0][1][:, :], rhs=hk[:, 0:NB], start=True, stop=True)
            k.dve.tensor_copy(out=KcC[:, 0:NB], in_=pb2[0:64, 0:NB])
            pb = sbank()
            compress(k, W, 1, VcTs, NB, pb, hx, ta, tb, hv_)
            for ni in range(NBT):
                nn = min(128, NB - ni * 128)
                pb3 = sbank()
                k.pe.matmul(out=pb3[:nn, 0:64], lhsT=hv_[:, ni * 128:ni * 128 + nn], rhs=W[1][1][:, :], start=True, stop=True)
                k.dve.tensor_copy(out=VcC[:nn, ni, 0:64], in_=pb3[:nn, 0:64])
            for tt in range(NT):
                t0 = tt * 128
                q = q4[tt % 2]; gt_ = gtile[tt % 2]; fb = fbt[tt % 2]
                k.dma(q[:, :, :], QT[:, g * 4:(g + 1) * 4, t0:t0 + 128])
                k.dma(gt_[:, :], GT[t0:t0 + 128, :])
                k.dma(fb[:, :], L['FBt'][t0:t0 + 128, :])
                gv3 = gt_[:, :].rearrange("p (h b) -> p h b", b=3)
                nvalid = min(NB, (t0 + 96) // 16 + 1)
                nnt = (nvalid + 127) // 128
                for r in range(4):
                    po = po_c[r % 2]
                    for ni in range(nnt):
                        nn = min(128, NB - ni * 128)
                        pss = sbank()
                        k.pe.matmul(out=pss[:nn, 0:128], lhsT=KcC[:, ni * 128:ni * 128 + nn], rhs=q[:, r, :], start=True, stop=True)
                        e = enext()
                        k.act.activation(out=e[:nn, 0:128], in_=pss[:nn, 0:128], func=AF.Exp, scale=0.125)
                        delta = t0 - 2048 * ni
                        if delta < 17 * 128:
                            assert delta >= 0
                            k.pool.tensor_tensor(out=e[:nn, 0:128], in0=e[:nn, 0:128], in1=cm[:nn, delta // 128, :], op=ALU.mult)
                        k.pe.matmul(out=po[:, :], lhsT=e[:nn, 0:128], rhs=VcC[:nn, ni, :], start=(ni == 0), stop=(ni == nnt - 1))
                    k.dve.tensor_scalar(out=rr[:, r:r + 1], in0=po[:, 64:65], scalar1=1e-30, scalar2=None, op0=ALU.max)
                    k.dve.reciprocal(out=rr[:, 4 + r:5 + r], in_=rr[:, r:r + 1])
                    k.dve.tensor_tensor(out=rr[:, 8 + r:9 + r], in0=rr[:, 4 + r:5 + r], in1=gv3[:, g * 4 + r, 0:1], op=ALU.mult)
                    k.dve.tensor_scalar(out=acc[:, r, :], in0=po[:, 0:64], scalar1=rr[:, 8 + r:9 + r], scalar2=None, op0=ALU.mult)
                    if r == 0:
                        k.dve.tensor_scalar(out=imp[:, :], in0=po[:, 65:65 + NSEL], scalar1=rr[:, 4 + r:5 + r], scalar2=None, op0=ALU.mult)
                    else:
                        k.dve.scalar_tensor_tensor(out=imp[:, :], in0=po[:, 65:65 + NSEL], scalar=rr[:, 4 + r:5 + r], in1=imp[:, :], op0=ALU.mult, op1=ALU.add)
                k.dve.tensor_tensor(out=score[:, :], in0=imp[:, :], in1=fb[:, :], op=ALU.add)
                k.dve.max(out=m8[:, 0:8], in_=score[:, :])
                k.dve.match_replace(out=sc2[:, :], in_to_replace=m8[:, 0:8], in_values=score[:, :], imm_value=-3.0e38)
                k.dve.max(out=m8[:, 8:16], in_=sc2[:, :])
                k.dve.tensor_scalar(out=rr[:, 12:13], in0=m8[:, 15:16], scalar1=-1.0e8, scalar2=None, op0=ALU.max)
                k.dve.tensor_scalar(out=negs[:, :], in0=score[:, :], scalar1=rr[:, 12:13], scalar2=NEG, op0=ALU.is_lt, op1=ALU.mult)
                k.pe.transpose(out=ps_t[0:NSEL, :], in_=negs[:, :], identity=identb[:, :])
                k.dve.tensor_copy(out=negT4[:, :, :], in_=ps_t[0:NSEL, :].unsq(1).bc([NSEL, 4, 128]))
                qf = q[:, :, :].rearrange("p r t -> p (r t)")
                nf = negT4[:, :, :].rearrange("p r t -> p (r t)")
                po = po_sel[0]

                def sc_sel(kt):
                    pss = sbank()
                    k.pe.matmul(out=pss[:, :], lhsT=KsT[:, kt * 128:(kt + 1) * 128], rhs=qf, start=True, stop=False)
                    k.pe.matmul(out=pss[:, :], lhsT=eexp[:, kt * 128:(kt + 1) * 128], rhs=nf, start=False, stop=True)
                    e = enext()
                    k.act.activation(out=e[:, :], in_=pss[:, :], func=AF.Exp, scale=0.125)
                    if kt == tt:
                        ev = e[:, :].rearrange("p (r t) -> p r t", r=4)
                        k.pool.tensor_tensor(out=ev, in0=ev, in1=tri[:, 0:1, :].bc([128, 4, 128]), op=ALU.mult)
                    return e

                def pv_sel(kt, e):
                    for r in range(4):
                        k.pe.matmul(out=po[:, r, :], lhsT=e[:, r * 128:(r + 1) * 128], rhs=Vs[:, kt, :], start=(kt == 0 and r == 0), stop=(kt == tt), skip_group_check=True)

                pipe(range(tt + 1), sc_sel, pv_sel)
                k.dve.tensor_scalar(out=rr[:, 0:4], in0=po[:, :, 64], scalar1=1e-30, scalar2=None, op0=ALU.max)
                k.dve.reciprocal(out=rr[:, 4:8], in_=rr[:, 0:4])
                k.dve.tensor_tensor(out=rr[:, 8:12], in0=rr[:, 4:8], in1=gv3[:, g * 4:(g + 1) * 4, 1], op=ALU.mult)
                k.dve.tensor_tensor(out=tmp4[:, :, :], in0=po[:, :, 0:64], in1=rr[:, 8:12].unsq(2).bc([128, 4, 64]), op=ALU.mult)
                k.dve.tensor_tensor(out=acc[:, :, :], in0=acc[:, :, :], in1=tmp4[:, :, :], op=ALU.add)
                po = po_win[0]
                k0 = max(0, tt - 4)
                pow_ = po

                def sc_win(kt):
                    pss = sbank()
                    k.pe.matmul(out=pss[:, :], lhsT=KwT[:, kt * 128:(kt + 1) * 128], rhs=qf, start=True, stop=True)
                    e = enext()
                    k.act.activation(out=e[:, :], in_=pss[:, :], func=AF.Exp, scale=0.125)
                    ev = e[:, :].rearrange("p (r t) -> p r t", r=4)
                    if kt == tt:
                        k.pool.tensor_tensor(out=ev, in0=ev, in1=tri[:, 0:1, :].bc([128, 4, 128]), op=ALU.mult)
                    elif kt == tt - 4:
                        k.pool.tensor_tensor(out=ev, in0=ev, in1=tri[:, 1:2, :].bc([128, 4, 128]), op=ALU.mult)
                    return e

                def pv_win(kt, e):
                    for r in range(4):
                        k.pe.matmul(out=pow_[:, r, :], lhsT=e[:, r * 128:(r + 1) * 128], rhs=Vw[:, kt, :], start=(kt == k0 and r == 0), stop=(kt == tt), skip_group_check=True)

                pipe(range(k0, tt + 1), sc_win, pv_win)
                k.dve.tensor_scalar(out=rr[:, 0:4], in0=po[:, :, 64], scalar1=1e-30, scalar2=None, op0=ALU.max)
                k.dve.reciprocal(out=rr[:, 4:8], in_=rr[:, 0:4])
                k.dve.tensor_tensor(out=rr[:, 8:12], in0=rr[:, 4:8], in1=gv3[:, g * 4:(g + 1) * 4, 2], op=ALU.mult)
                k.dve.tensor_tensor(out=tmp4[:, :, :], in0=po[:, :, 0:64], in1=rr[:, 8:12].unsq(2).bc([128, 4, 64]), op=ALU.mult)
                k.dve.tensor_tensor(out=accb[:, :, :], in0=acc[:, :, :], in1=tmp4[:, :, :], op=ALU.add)
                k.dma(AO[t0:t0 + 128, 512 + g * 256:512 + (g + 1) * 256], accb[:, :, :].rearrange("p r d -> p (r d)"))


def layer_tail(k, L, layer, mix, w_out, h_in, h_out, y_out):
    Tn = L['Tn']; NT = L['NT']; identb = L['identb']; H1 = L['H1']; ACTT = L['ACTT']
    rmsnorm_T = L['rmsnorm_T']; load_w_bf16 = L['load_w_bf16']
    xp = L['xp']; xs = L['xs']
    groups = [(i * 512, min(512, Tn - i * 512)) for i in range((Tn + 511) // 512)] + [(Tn, 16)]
    with k.scope():
        Wo = k.sb('Wo', [128, 8, 1024], BF16); load_w_bf16(Wo, w_out, 1024, 1024)
        Wg = k.sb('Wg', [128, 8, DFF], BF16); load_w_bf16(Wg, L['ffn_g'][layer], 1024, DFF)
        Wu = k.sb('Wu', [128, 8, DFF], BF16); load_w_bf16(Wu, L['ffn_u'][layer], 1024, DFF)
        gbc = k.sb('gbc', [128, 1024]); k.dma(gbc[:, :], L['norm_ffn'][layer:layer + 1, :].pbc(128))
        mt = k.sb('mt', [128, 1024], BF16); mT = k.sb('mT', [128, 8, 128], BF16)
        ht = [k.sb('ht', [128, 1024]) for _ in range(2)]
        junk = k.sb('junk', [128, 1024], BF16); xn = k.sb('xn', [128, 1024], BF16); ss = k.sb('ss', [128, 4])
        xnT = k.sb('xnT', [128, 8, 512], BF16); xnT1 = k.sb('xnT1', [128, 8, 128], BF16)
        sg = k.sb('sg', [128, 512], BF16); at = [k.sb('at', [128, 512], BF16) for _ in range(2)]
        pT = k.ps('pT', [128, 8, 128], BF16)
        pA = [k.ps('pA', [128, 512]) for _ in range(2)]
        pG = [k.ps('pG', [128, 512]) for _ in range(2)]; pU = [k.ps('pU', [128, 512]) for _ in range(2)]
        ci = 0
        for (g0, gn) in groups:
            ntile = (gn + 127) // 128
            for j in range(ntile):
                r0 = g0 + j * 128; rows = min(128, gn - j * 128)
                h = ht[ci % 2]; ci += 1
                k.dma(mt[:rows, :], mix[r0:r0 + rows, :])
                if h_in is None:
                    k.dma(h[:rows, :], xp[r0:r0 + rows, :] if r0 < Tn else xs[0:16, :])
                else:
                    k.dma(h[:rows, :], h_in[r0:r0 + rows, :])
                for kk in range(8):
                    k.pe.transpose(out=pT[:, kk, :rows], in_=mt[:rows, kk * 128:(kk + 1) * 128], identity=identb[:rows, :rows])
                k.act.copy(out=mT[:, :, :rows], in_=pT[:, :, :rows])
                for c in range(2):
                    ps = pA[c]
                    for kk in range(8):
                        k.pe.matmul(out=ps[:rows, :], lhsT=mT[:, kk, :rows], rhs=Wo[:, kk, c * 512:(c + 1) * 512], start=(kk == 0), stop=(kk == 7))
                    k.dve.tensor_tensor(out=h[:rows, c * 512:(c + 1) * 512], in0=h[:rows, c * 512:(c + 1) * 512], in1=ps[:rows, :], op=ALU.add)
                k.dma(H1[r0:r0 + rows, :], h[:rows, :])
                rmsnorm_T(h, rows, gbc, xn, pT, xnT1, ss, junk)
                k.dve.tensor_copy(out=xnT[:, :, j * 128:j * 128 + rows], in_=xnT1[:, :, :rows])
            for hc in range(22):
                pg = pG[hc % 2]; pu = pU[hc % 2]
                for kk in range(8):
                    k.pe.matmul(out=pg[:, :gn], lhsT=Wg[:, kk, hc * 128:(hc + 1) * 128], rhs=xnT[:, kk, :gn], start=(kk == 0), stop=(kk == 7))
                for kk in range(8):
                    k.pe.matmul(out=pu[:, :gn], lhsT=Wu[:, kk, hc * 128:(hc + 1) * 128], rhs=xnT[:, kk, :gn], start=(kk == 0), stop=(kk == 7))
                a = at[hc % 2]
                k.act.activation(out=sg[:, :gn], in_=pg[:, :gn], func=AF.Silu)
                k.dve.tensor_tensor(out=a[:, :gn], in0=sg[:, :gn], in1=pu[:, :gn], op=ALU.mult)
                k.dma(ACTT[hc, :, g0:g0 + gn], a[:, :gn])
    with k.scope():
        Wd = k.sb('Wd', [128, 22, 1024], BF16); load_w_bf16(Wd, L['ffn_d'][layer], DFF, 1024)
        gbc = k.sb('gbc', [128, 1024])
        if y_out is not None:
            k.dma(gbc[:, :], L['norm_final'][0:1, :].pbc(128))
        aT = [k.sb('aT', [128, 22, 128], BF16) for _ in range(2)]
        ht = [k.sb('ht', [128, 1024]) for _ in range(2)]
        junk = k.sb('junk', [128, 1024]); ss = k.sb('ss', [128, 4]); yt = k.sb('yt', [128, 1024])
        pA = [k.ps('pA', [128, 512]) for _ in range(4)]
        ci = 0
        for ti, (r0, rows) in enumerate(L['tiles']):
            a = aT[ti % 2]; h = ht[ti % 2]
            k.dma(a[:, :, :rows], ACTT[:, :, r0:r0 + rows].rearrange("c p t -> p c t"))
            k.dma(h[:rows, :], H1[r0:r0 + rows, :])
            for c in range(2):
                ps = pA[ci % 4]; ci += 1
                for hc in range(22):
                    k.pe.matmul(out=ps[:rows, :], lhsT=a[:, hc, :rows], rhs=Wd[:, hc, c * 512:(c + 1) * 512], start=(hc == 0), stop=(hc == 21))
                k.dve.tensor_tensor(out=h[:rows, c * 512:(c + 1) * 512], in0=h[:rows, c * 512:(c + 1) * 512], in1=ps[:rows, :], op=ALU.add)
            if y_out is None:
                k.dma(H1[r0:r0 + rows, :], h[:rows, :])
            else:
                k.act.activation(out=junk[:rows, :], in_=h[:rows, :], func=AF.Square, accum_out=ss[:rows, 0:1])
                k.dve.tensor_scalar(out=ss[:rows, 1:2], in0=ss[:rows, 0:1], scalar1=1.0 / 1024, scalar2=1e-6, op0=ALU.mult, op1=ALU.add)
                k.act.activation(out=ss[:rows, 3:4], in_=ss[:rows, 1:2], func=AF.Sqrt)
                k.dve.reciprocal(out=ss[:rows, 2:3], in_=ss[:rows, 3:4])
                k.dve.scalar_tensor_tensor(out=yt[:rows, :], in0=h[:rows, :], scalar=ss[:rows, 2:3], in1=gbc[:rows, :], op0=ALU.mult, op1=ALU.mult)
                if r0 < Tn:
                    k.dma(y_out[0][r0:r0 + rows, :], yt[:rows, :])
                else:
                    k.dma(y_out[1][0:16, :], yt[:16, :])


def phase_proj1(k, L):
    Tn = L['Tn']; NT = L['NT']; identb = L['identb']; identf = L['identf']; H1 = L['H1']
    NBLK = L['NBLK']; NBLKP = L['NBLKP']
    rmsnorm_T = L['rmsnorm_T']; load_w_bf16 = L['load_w_bf16']
    QT2 = L['QT2']; KT2 = L['KT2']; VT2 = L['VT2']; NS2 = L['NS2']
    with k.scope():
        W1 = k.sb('W1', [128, 8, ODC], BF16); load_w_bf16(W1, L['w_in1'], 1024, ODC)
        gbc = k.sb('gbc', [128, 1024]); k.dma(gbc[:, :], L['norm_mix'][1:2, :].pbc(128))
        xt = [k.sb('xt', [128, 1024]) for _ in range(2)]
        junk = k.sb('junk', [128, 1024], BF16); xn = k.sb('xn', [128, 1024], BF16); ss = k.sb('ss', [128, 4])
        xnT = k.sb('xnT', [128, 8, 128], BF16)
        proj = [k.sb('proj', [128, ODC]) for _ in range(2)]
        cs = k.sb('cs', [128, 64])
        tmp = [k.sb('tmp%d' % i, [128, 16, 32]) for i in range(4)]
        qb = k.sb('qb', [128, 16, 64], BF16); kb = k.sb('kb', [128, 4, 64], BF16)
        vb = k.sb('vb', [128, 4, 65], BF16)
        k.op('dve', lambda e: e.memset(vb[:, :, :].ap, 1.0), (), (vb,))
        ones = k.sb('ones', [128, 1]); k.op('dve', lambda e: e.memset(ones[:, :].ap, 1.0), (), (ones,))
        qT = k.sb('qT', [64, 16, 128], BF16); kT = k.sb('kT', [64, 4, 128], BF16)
        qTf = k.sb('qTf', [64, 16, 128])
        meansT = k.sb('meansT', [64, 4, NBLKP])
        k.op('dve', lambda e: e.memset(meansT[:, :, :].ap, 0.0), (), (meansT,))
        gbm = k.sb('gbm', [128, 2, NBLKP])
        score = k.sb('score', [128, 16, NBLKP]); m8 = k.sb('m8', [128, 16, 8]); thr = k.sb('thr', [128, 16])
        negs = k.sb('negs', [128, 16, NBLKP]); negb = k.sb('negb', [128, 16, NBLKP], BF16)
        pT = k.ps('pT', [128, 8, 128], BF16)
        pA = [k.ps('pA', [128, 512]) for _ in range(2)]
        pQ = k.ps('pQ', [64, 8, 128], BF16)
        pK = k.ps('pK', [64, 4, 128], BF16)
        pQf = [k.ps('pQf', [64, 4, 128]) for _ in range(1)]
        pM = k.ps('pM', [64, 4, NBLKP])
        pGt = k.ps('pGt', [128, 16, NBLKP])
        ci = 0
        for ti, (r0, rows) in enumerate(L['tiles']):
            x = xt[ti % 2]; pj = proj[ti % 2]
            k.dma(x[:rows, :], H1[r0:r0 + rows, :])
            k.dma(cs[:rows, :], L['ropecs'][r0:r0 + rows, :])
            rmsnorm_T(x, rows, gbc, xn, pT, xnT, ss, junk)
            for c0 in range(0, ODC, 512):
                ps = pA[ci % 2]; ci += 1
                for kk in range(8):
                    k.pe.matmul(out=ps[:rows, :], lhsT=xnT[:, kk, :rows], rhs=W1[:, kk, c0:c0 + 512], start=(kk == 0), stop=(kk == 7))
                (k.act.copy if ci % 2 else k.dve.tensor_copy)(out=pj[:rows, c0:c0 + 512], in_=ps[:rows, :])
            cosb = lambda n: cs[:rows, 0:32].unsq(1).bc([rows, n, 32])
            sinb = lambda n: cs[:rows, 32:64].unsq(1).bc([rows, n, 32])
            for vi, (c0, n) in enumerate([(0, 16), (1024, 4)]):
                xv = pj[:rows, c0:c0 + n * 64].rearrange("p (h d) -> p h d", h=n)
                E1 = k.dve if vi == 0 else k.pool
                x1 = xv[:, :, 0:32]; x2 = xv[:, :, 32:64]
                t = [tt_[:rows, 0:n, :] for tt_ in tmp]
                E1.tensor_tensor(out=t[0], in0=x1, in1=cosb(n), op=ALU.mult)
                E1.tensor_tensor(out=t[1], in0=x2, in1=sinb(n), op=ALU.mult)
                E1.tensor_tensor(out=t[2], in0=x2, in1=cosb(n), op=ALU.mult)
                E1.tensor_tensor(out=t[3], in0=x1, in1=sinb(n), op=ALU.mult)
                E1.tensor_tensor(out=x1, in0=t[0], in1=t[1], op=ALU.subtract)
                E1.tensor_tensor(out=x2, in0=t[2], in1=t[3], op=ALU.add)
            qv = pj[:rows, 0:1024].rearrange("p (h d) -> p h d", h=16)
            kv_ = pj[:rows, 1024:1280].rearrange("p (h d) -> p h d", h=4)
            vv = pj[:rows, 1280:1536].rearrange("p (h d) -> p h d", h=4)
            k.dve.tensor_copy(out=qb[:rows, :, :], in_=qv)
            k.pool.tensor_copy(out=kb[:rows, :, :], in_=kv_)
            k.pool.tensor_copy(out=vb[:rows, :, 0:64], in_=vv)
            if r0 < Tn:
                k.dma(L['o_moba_p'][r0:r0 + rows, :], pj[:rows, 1024:1536])
            else:
                k.dma(L['o_moba_s'][0:16, :], pj[:16, 1024:1536])
            for hh in range(2):
                for h in range(8):
                    k.pe.transpose(out=pQ[:, h, :rows], in_=qb[:rows, hh * 8 + h, :], identity=identb[:rows, :rows])
                k.act.copy(out=qT[:, hh * 8:(hh + 1) * 8, :rows], in_=pQ[:, :, :rows])
            for h in range(4):
                k.pe.transpose(out=pK[:, h, :rows], in_=kb[:rows, h, :], identity=identb[:rows, :rows])
            k.dve.tensor_copy(out=kT[:, :, :rows], in_=pK[:, :, :rows])
            k.dma(QT2[:, :, r0:r0 + rows], qT[:, :, :rows])
            k.dma(KT2[:, :, r0:r0 + rows], kT[:, :, :rows])
            k.dma(VT2[r0:r0 + rows, :, :], vb[:rows, :, :])
            for hq in range(4):
                pq = pQf[0]
                for h in range(4):
                    k.pe.transpose(out=pq[:, h, :rows], in_=pj[:rows, (hq * 4 + h) * 64:(hq * 4 + h + 1) * 64], identity=identf[:rows, :rows])
                k.act.copy(out=qTf[:, hq * 4:(hq + 1) * 4, :rows], in_=pq[:, :, :rows])
            if r0 >= Tn:
                k.dma(L['QF2'][:, :, :], qTf[:, :, 0:16])
                continue
            blk = r0 // 256
            for h in range(16):
                k.pe.matmul(out=pGt[:rows, h, :], lhsT=qTf[:, h, :rows], rhs=meansT[:, h // 4, :], start=(h == 0), stop=(h == 15), skip_group_check=True)
            k.dma(gbm[:rows, :, :], L['GBM'][r0:r0 + rows, :, :])
            k.dve.tensor_tensor(out=score[:rows, :, :], in0=pGt[:rows, :, :], in1=gbm[:rows, 0:1, :].bc([rows, 16, NBLKP]), op=ALU.add)
            for h in range(16):
                k.dve.max(out=m8[:rows, h, :], in_=score[:rows, h, :])
            k.dve.tensor_scalar(out=thr[:rows, :], in0=m8[:rows, :, 2], scalar1=-1.0e8, scalar2=None, op0=ALU.max)
            k.dve.tensor_tensor(out=negs[:rows, :, :], in0=score[:rows, :, :], in1=thr[:rows, :].unsq(2).bc([rows, 16, NBLKP]), op=ALU.is_lt)
            k.dve.tensor_tensor(out=negs[:rows, :, :], in0=negs[:rows, :, :], in1=gbm[:rows, 1:2, :].bc([rows, 16, NBLKP]), op=ALU.mult)
            k.dve.tensor_scalar(out=negb[:rows, :, :], in0=negs[:rows, :, :], scalar1=NEG, scalar2=None, op0=ALU.mult)
            k.dma(NS2[r0:r0 + rows, :, :], negb[:rows, :, :])
            for g in range(4):
                first = (r0 % 256 == 0)
                k.pe.matmul(out=pM[:, g, blk:blk + 1], lhsT=pj[:rows, 1024 + g * 64:1024 + (g + 1) * 64], rhs=ones[:rows, 0:1],
                            start=(ti == 0 and g == 0), stop=not first, skip_group_check=True)
            if r0 % 256 == 128:
                k.dve.tensor_scalar(out=meansT[:, :, blk:blk + 1], in0=pM[:, :, blk:blk + 1], scalar1=1.0 / 2048, scalar2=None, op0=ALU.mult)


def phase_moba_prompt(k, L):
    Tn = L['Tn']; NT = L['NT']; NBLK = L['NBLK']; NBLKP = L['NBLKP']; identb = L['identb']
    QT2 = L['QT2']; KT2 = L['KT2']; VT2 = L['VT2']; NS2 = L['NS2']; MO = L['MO']
    with k.scope():
        tri = k.sb('tri', [128, 2, 128], BF16); k.dma(tri[:, :, :], L['TRIt'][:, :, :])
        eexp = k.sb('eexp', [NBLKP, Tn], BF16); k.dma(eexp[:, :], L['EEXP2'][0:NBLKP, 0:Tn])
        ps_s = [k.ps('ps_s', [128, 512]) for _ in range(3)]
        po_ = [k.ps('po', [128, 4, 65]) for _ in range(2)]
        ps_t = k.ps('ps_t', [NBLKP, 4, 128], BF16)
        K2 = k.sb('K2', [64, Tn], BF16); V2 = k.sb('V2', [128, NT, 65], BF16)
        q4 = [k.sb('q4', [64, 4, 128], BF16) for _ in range(2)]
        ns = [k.sb('ns', [128, 4, NBLKP], BF16) for _ in range(2)]
        negT4 = k.sb('negT4', [NBLKP, 4, 128], BF16)
        ebuf = [k.sb('ebuf', [128, 512], BF16) for _ in range(3)]
        rr = k.sb('rr', [128, 8]); accb = k.sb('accb', [128, 4, 64], BF16)
        cnt = [0]
        for g in range(4):
            k.dma(K2[:, :], KT2[:, g, 0:Tn])
            k.dma(V2[:, :, :], VT2[0:Tn, g, :].rearrange("(kt p) c -> p kt c", p=128))
            for tt in range(NT):
                t0 = tt * 128
                q = q4[tt % 2]; n_ = ns[tt % 2]
                k.dma(q[:, :, :], QT2[:, g * 4:(g + 1) * 4, t0:t0 + 128])
                k.dma(n_[:, :, :], NS2[t0:t0 + 128, g * 4:(g + 1) * 4, :])
                for r in range(4):
                    k.pe.transpose(out=ps_t[:, r, :], in_=n_[:, r, :], identity=identb[:, :])
                k.dve.tensor_copy(out=negT4[:, :, :], in_=ps_t[:, :, :])
                qf = q[:, :, :].rearrange("p r t -> p (r t)")
                nf = negT4[:, :, :].rearrange("p r t -> p (r t)")
                po = po_[tt % 2]
                def sc_m(kt):
                    pss = ps_s[cnt[0] % 3]
                    k.pe.matmul(out=pss[:, :], lhsT=K2[:, kt * 128:(kt + 1) * 128], rhs=qf, start=True, stop=False)
                    k.pe.matmul(out=pss[:, :], lhsT=eexp[:, kt * 128:(kt + 1) * 128], rhs=nf, start=False, stop=True)
                    e = ebuf[cnt[0] % 3]; cnt[0] += 1
                    k.act.activation(out=e[:, :], in_=pss[:, :], func=AF.Exp, scale=0.125)
                    if kt == tt:
                        ev = e[:, :].rearrange("p (r t) -> p r t", r=4)
                        k.pool.tensor_tensor(out=ev, in0=ev, in1=tri[:, 0:1, :].bc([128, 4, 128]), op=ALU.mult)
                    return e

                def pv_m(kt, e):
                    for r in range(4):
                        k.pe.matmul(out=po[:, r, :], lhsT=e[:, r * 128:(r + 1) * 128], rhs=V2[:, kt, :], start=(kt == 0 and r == 0), stop=(kt == tt), skip_group_check=True)

                pipe(range(tt + 1), sc_m, pv_m)
                k.dve.tensor_scalar(out=rr[:, 0:4], in0=po[:, :, 64], scalar1=1e-30, scalar2=None, op0=ALU.max)
                k.dve.reciprocal(out=rr[:, 4:8], in_=rr[:, 0:4])
                k.dve.tensor_tensor(out=accb[:, :, :], in0=po[:, :, 0:64], in1=rr[:, 4:8].unsq(2).bc([128, 4, 64]), op=ALU.mult)
                k.dma(MO[t0:t0 + 128, g * 256:(g + 1) * 256], accb[:, :, :].rearrange("p r d -> p (r d)"))


def page_indices(k, L):
    P = L['P']
    pti = k.sb('pti', [128, 4 * P], I32); ptf = k.sb('ptf', [128, 4 * P]); io = k.sb('io', [128, 1])
    idx = k.sb('idx', [128, 4 * P], I32)
    k.dma(pti[:, :], L['ptab'][:, :].rearrange("b p -> (b p)").unsq(0).pbc(128) if False else L['ptab'][:, :].rearrange("(o b) p -> o (b p)", o=1).pbc(128))
    k.dma(io[:, :], L['IOTA'][:, :])
    k.dve.tensor_copy(out=ptf[:, :], in_=pti[:, :])
    k.dve.tensor_scalar(out=ptf[:, :], in0=ptf[:, :], scalar1=128.0, scalar2=None, op0=ALU.mult)
    k.dve.tensor_tensor(out=ptf[:, :], in0=ptf[:, :], in1=io[:, 0:1].bc([128, 4 * P]), op=ALU.add)
    k.dve.tensor_copy(out=idx[:, :], in_=ptf[:, :])
    return idx


def gather_page(k, pg, cache, idx, col):
    k.raw16('pool', lambda e: e.indirect_dma_start(out=pg[:, :].ap, out_offset=None, in_=cache[:, :].ap,
                                                   in_offset=bass.IndirectOffsetOnAxis(ap=idx[:, col:col + 1].ap, axis=0)),
            reads=[cache.t if isinstance(cache, V) else cache, idx], writes=[pg])


def phase_nsa_sample(k, L):
    Tn = L['Tn']; P = L['P']; LP = L['LP']; NSELS = L['NSELS']; NBS = L['NBS']; NBTS = L['NBTS']
    QT = L['QT']; KT = L['KT']; VT = L['VT']; GT = L['GT']; AO = L['AO']; identb = L['identb']; identf = L['identf']
    cache = L['cache_nsa']; st_win = L['st_win']
    NJ = LP // 64
    with k.scope():
        ps_s = [k.ps('ps_s', [128, 512]) for _ in range(2)]
        W = load_cmp_weights(k, L, ps_s[0])
        idx = page_indices(k, L)
        eexp = k.sb('eexp', [NJ, LP], BF16); k.dma(eexp[:, :], L['EEXP'][0:NJ, 0:LP])
        smk = k.sb('smk', [128, 2, 16], BF16); k.dma(smk[:, :, :], L['SMK'][:, :, :])
        fbs = k.sb('fbs', [4, NSELS]); k.dma(fbs[:, :], L['FBs'][:, :])
        pX = [k.ps('pX', [64, 4, 128]) for _ in range(1)]
        pXb = [k.ps('pXb', [64, 8, 128], BF16) for _ in range(1)]
        stg = [k.sb('stg', [128, 384], BF16) for _ in range(2)]
        po_a = [k.ps('po_a', [4, 2, 65 + NSELS]) for _ in range(2)]
        po_b = k.ps('po_b', [4, 4, 65])
        ps_t = k.ps('ps_t', [128, 16], BF16)
        si = [0]

        def sbank():
            b = ps_s[si[0] % 2]; si[0] += 1
            return b

        pg = [k.sb('pg', [128, 512]) for _ in range(3)]
        KcTs = k.sb('KcTs', [64, 2, LP], BF16); VcTs = k.sb('VcTs', [64, 2, LP], BF16); KsTs = k.sb('KsTs', [64, 2, LP], BF16)
        Vss = k.sb('Vss', [128, P, 2, 65], BF16)
        k.op('dve', lambda e: e.memset(Vss[:, :, :, 64:65].ap, 1.0), (), (Vss,))
        KwTs = k.sb('KwTs', [64, 2, 512], BF16); Vws = k.sb('Vws', [128, 4, 2, 65], BF16)
        k.op('dve', lambda e: e.memset(Vws[:, :, :, 64:65].ap, 1.0), (), (Vws,))
        wt = [k.sb('wt', [128, 256]) for _ in range(2)]
        KcCs = k.sb('KcCs', [64, NBTS * 128], BF16)
        VcCs = k.sb('VcCs', [128, NBTS, 65 + NSELS], BF16)
        k.dma(VcCs[:, :, 65:65 + NSELS], L['OVs'][:, :, :])
        k.op('dve', lambda e: e.memset(VcCs[:, :, 64:65].ap, 1.0), (), (VcCs,))
        hx = k.sb('hx', [64, 512]); ta = k.sb('ta', [64, 512]); tb = k.sb('tb', [64, 512])
        hk = k.sb('hk', [64, 512], BF16); hv_ = k.sb('hv_', [64, 512], BF16)
        qs_all = k.sb('qs_all', [64, 8, 16], BF16); k.dma(qs_all[:, :, :], QT[:, :, Tn:Tn + 16])
        knew = k.sb('knew', [64, 6, 16], BF16); k.dma(knew[:, :, :], KT[:, :, Tn:Tn + 16])
        vnew = [k.sb('vnew', [4, 2, 2, 65], BF16) for _ in range(2)]
        gts = [k.sb('gts', [4, 24]) for _ in range(2)]
        q16 = k.sb('q16', [64, 4, 4], BF16)
        ebuf = [k.sb('ebuf', [128, 16], BF16) for _ in range(4)]
        ei = [0]

        def enext():
            b = ebuf[ei[0] % 4]; ei[0] += 1
            return b

        acc = k.sb('acc', [4, 8, 64]); accb = k.sb('accb', [4, 8, 64], BF16); tmp4 = k.sb('tmp4', [4, 4, 64])
        rr = k.sb('rr', [4, 16]); imp = k.sb('imp', [4, NSELS]); score = k.sb('score', [4, NSELS]); sc2 = k.sb('sc2', [4, NSELS])
        m8 = k.sb('m8', [4, 16]); negs = k.sb('negs', [4, NSELS], BF16)
        negT16 = k.sb('negT16', [NJ, 4, 4], BF16)
        SD = L['cfg'].get('sdbg', 99)
        for bl in range(4 if SD >= 99 else 1):
            vn = vnew[bl % 2]; gt_ = gts[bl % 2]
            if SD < 1:
                break
            import os
            if os.environ.get('SKIPVN') != '1':
                k.dma(vn[:, :, :, :], VT[Tn + bl * 4:Tn + bl * 4 + 4, :, :, :])
                k.dma(gt_[:, :], GT[Tn + bl * 4:Tn + bl * 4 + 4, :])
            gv3 = gt_[:, :].rearrange("p (h b) -> p h b", b=3)
            for lp in range(P if os.environ.get('SKIPG') != '1' else 0):
                pgt = pg[lp % 3]
                gather_page(k, pgt, cache, idx, bl * P + lp)
                sg_ = stg[lp % 2]
                k.dve.tensor_copy(out=sg_[:, :], in_=pgt[:, 0:384])
                k.pool.tensor_copy(out=Vss[:, lp, :, 0:64], in_=pgt[:, 384:512].rearrange("p (g d) -> p g d", g=2))
                px0 = pXb[0]
                for i in range(6):
                    k.pe.transpose(out=px0[:, i, :], in_=sg_[:, i * 64:(i + 1) * 64], identity=identb[:, :])
                k.dve.tensor_copy(out=KcTs[:, :, lp * 128:(lp + 1) * 128], in_=px0[:, 0:2, :])
                k.dve.tensor_copy(out=VcTs[:, :, lp * 128:(lp + 1) * 128], in_=px0[:, 2:4, :])
                k.dve.tensor_copy(out=KsTs[:, :, lp * 128:(lp + 1) * 128], in_=px0[:, 4:6, :])
            if SD < 2:
                break
            for wi in range(4):
                w_ = wt[wi % 2]
                k.dma(w_[:, :], st_win[bl, wi * 128:(wi + 1) * 128, :])
                px1 = pX[0]
                for g in range(2):
                    k.pe.transpose(out=px1[:, g, :], in_=w_[:, g * 64:(g + 1) * 64], identity=identf[:, :])
                k.dve.tensor_copy(out=KwTs[:, :, wi * 128:(wi + 1) * 128], in_=px1[:, 0:2, :])
                k.pool.tensor_copy(out=Vws[:, wi, :, 0:64], in_=w_[:, 128:256].rearrange("p (g d) -> p g d", g=2))
            if SD < 3:
                break
            for g in range(2):
                pb = sbank()
                compress(k, W, 0, KcTs[:, g, :], NBS, pb, hx, ta, tb, hk)
                pb2 = sbank()
                k.pe.matmul(out=pb2[0:64, 0:NBS], lhsT=W[0][1][:, :], rhs=hk[:, 0:NBS], start=True, stop=True)
                k.dve.tensor_copy(out=KcCs[:, 0:NBS], in_=pb2[0:64, 0:NBS])
                pb = sbank()
                compress(k, W, 1, VcTs[:, g, :], NBS, pb, hx, ta, tb, hv_)
                for ni in range(NBTS):
                    nn = min(128, NBS - ni * 128)
                    pb3 = sbank()
                    k.pe.matmul(out=pb3[:nn, 0:64], lhsT=hv_[:, ni * 128:ni * 128 + nn], rhs=W[1][1][:, :], start=True, stop=True)
                    k.dve.tensor_copy(out=VcCs[:nn, ni, 0:64], in_=pb3[:nn, 0:64])
                if SD < 4:
                    continue
                k.dve.tensor_copy(out=q16[:, :, :], in_=qs_all[:, g * 4:(g + 1) * 4, bl * 4:(bl + 1) * 4])
                qf = q16[:, :, :].rearrange("p r t -> p (r t)")
                for ni in range(NBTS):
                    nn = min(128, NBS - ni * 128)
                    pss = sbank()
                    k.pe.matmul(out=pss[:nn, 0:16], lhsT=KcCs[:, ni * 128:ni * 128 + nn], rhs=qf, start=True, stop=True)
                    e = enext()
                    k.act.activation(out=e[:nn, :], in_=pss[:nn, 0:16], func=AF.Exp, scale=0.125)
                    for r in range(4):
                        k.pe.matmul(out=po_a[r // 2][:, r % 2, :], lhsT=e[:nn, r * 4:(r + 1) * 4], rhs=VcCs[:nn, ni, :],
                                    start=(ni == 0 and r % 2 == 0), stop=(ni == NBTS - 1), skip_group_check=True)
                for r in range(4):
                    po = po_a[r // 2][:, r % 2, :]
                    k.dve.tensor_scalar(out=rr[:, r:r + 1], in0=po[:, 64:65], scalar1=1e-30, scalar2=None, op0=ALU.max)
                    k.dve.reciprocal(out=rr[:, 4 + r:5 + r], in_=rr[:, r:r + 1])
                    k.dve.tensor_tensor(out=rr[:, 8 + r:9 + r], in0=rr[:, 4 + r:5 + r], in1=gv3[:, g * 4 + r, 0:1], op=ALU.mult)
                    k.dve.tensor_scalar(out=acc[:, g * 4 + r, :], in0=po[:, 0:64], scalar1=rr[:, 8 + r:9 + r], scalar2=None, op0=ALU.mult)
                    if r == 0:
                        k.dve.tensor_scalar(out=imp[:, :], in0=po[:, 65:65 + NSELS], scalar1=rr[:, 4 + r:5 + r], scalar2=None, op0=ALU.mult)
                    else:
                        k.dve.scalar_tensor_tensor(out=imp[:, :], in0=po[:, 65:65 + NSELS], scalar=rr[:, 4 + r:5 + r], in1=imp[:, :], op0=ALU.mult, op1=ALU.add)
                if SD < 5:
                    continue
                k.dve.tensor_tensor(out=score[:, :], in0=imp[:, :], in1=fbs[:, :], op=ALU.add)
                k.dve.max(out=m8[:, 0:8], in_=score[:, :])
                k.dve.match_replace(out=sc2[:, :], in_to_replace=m8[:, 0:8], in_values=score[:, :], imm_value=-3.0e38)
                k.dve.max(out=m8[:, 8:16], in_=sc2[:, :])
                k.dve.tensor_scalar(out=rr[:, 12:13], in0=m8[:, 15:16], scalar1=-1.0e8, scalar2=None, op0=ALU.max)
                k.dve.tensor_scalar(out=negs[:, :], in0=score[:, :], scalar1=rr[:, 12:13], scalar2=NEG, op0=ALU.is_lt, op1=ALU.mult)
                k.pe.transpose(out=ps_t[0:NJ, 0:4], in_=negs[:, 0:NJ], identity=identb[0:4, 0:4])
                k.dve.tensor_copy(out=negT16[:, :, :], in_=ps_t[0:NJ, 0:4].unsq(1).bc([NJ, 4, 4]))
                nf = negT16[:, :, :].rearrange("p r t -> p (r t)")
                if SD < 6:
                    continue
                def sc_s(kt):
                    pss = sbank()
                    e = enext()
                    if kt < P:
                        k.pe.matmul(out=pss[:, 0:16], lhsT=KsTs[:, g, kt * 128:(kt + 1) * 128], rhs=qf, start=True, stop=False)
                        k.pe.matmul(out=pss[:, 0:16], lhsT=eexp[:, kt * 128:(kt + 1) * 128], rhs=nf, start=False, stop=True)
                        k.act.activation(out=e[:, :], in_=pss[:, 0:16], func=AF.Exp, scale=0.125)
                        return (e, 128, Vss[:, kt, g, :])
                    k.pe.matmul(out=pss[0:4, 0:16], lhsT=knew[:, 2 + g, bl * 4:(bl + 1) * 4], rhs=qf, start=True, stop=True)
                    k.act.activation(out=e[0:4, :], in_=pss[0:4, 0:16], func=AF.Exp, scale=0.125)
                    k.pool.tensor_tensor(out=e[0:4, :], in0=e[0:4, :], in1=smk[0:4, 1, :], op=ALU.mult)
                    return (e, 4, vn[:, 0, g, :])

                def pv_s(kt, tup):
                    e, nk, vv = tup
                    for r in range(4):
                        k.pe.matmul(out=po_b[:, r, :], lhsT=e[:nk, r * 4:(r + 1) * 4], rhs=vv, start=(kt == 0 and r == 0), stop=(kt == P), skip_group_check=True)

                pipe(range(P + 1), sc_s, pv_s)
                k.dve.tensor_scalar(out=rr[:, 0:4], in0=po_b[:, :, 64], scalar1=1e-30, scalar2=None, op0=ALU.max)
                k.dve.reciprocal(out=rr[:, 4:8], in_=rr[:, 0:4])
                k.dve.tensor_tensor(out=rr[:, 8:12], in0=rr[:, 4:8], in1=gv3[:, g * 4:(g + 1) * 4, 1], op=ALU.mult)
                k.dve.tensor_tensor(out=tmp4[:, :, :], in0=po_b[:, :, 0:64], in1=rr[:, 8:12].unsq(2).bc([4, 4, 64]), op=ALU.mult)
                k.dve.tensor_tensor(out=acc[:, g * 4:(g + 1) * 4, :], in0=acc[:, g * 4:(g + 1) * 4, :], in1=tmp4[:, :, :], op=ALU.add)
                if SD < 7:
                    continue
                def sc_w(kt):
                    pss = sbank()
                    e = enext()
                    if kt < 4:
                        k.pe.matmul(out=pss[:, 0:16], lhsT=KwTs[:, g, kt * 128:(kt + 1) * 128], rhs=qf, start=True, stop=True)
                        k.act.activation(out=e[:, :], in_=pss[:, 0:16], func=AF.Exp, scale=0.125)
                        if kt == 0:
                            k.pool.tensor_tensor(out=e[:, :], in0=e[:, :], in1=smk[:, 0, :], op=ALU.mult)
                        return (e, 128, Vws[:, kt, g, :])
                    k.pe.matmul(out=pss[0:4, 0:16], lhsT=knew[:, 4 + g, bl * 4:(bl + 1) * 4], rhs=qf, start=True, stop=True)
                    k.act.activation(out=e[0:4, :], in_=pss[0:4, 0:16], func=AF.Exp, scale=0.125)
                    k.pool.tensor_tensor(out=e[0:4, :], in0=e[0:4, :], in1=smk[0:4, 1, :], op=ALU.mult)
                    return (e, 4, vn[:, 1, g, :])

                def pv_w(kt, tup):
                    e, nk, vv = tup
                    for r in range(4):
                        k.pe.matmul(out=po_b[:, r, :], lhsT=e[:nk, r * 4:(r + 1) * 4], rhs=vv, start=(kt == 0 and r == 0), stop=(kt == 4), skip_group_check=True)

                pipe(range(5), sc_w, pv_w)
                k.dve.tensor_scalar(out=rr[:, 0:4], in0=po_b[:, :, 64], scalar1=1e-30, scalar2=None, op0=ALU.max)
                k.dve.reciprocal(out=rr[:, 4:8], in_=rr[:, 0:4])
                k.dve.tensor_tensor(out=rr[:, 8:12], in0=rr[:, 4:8], in1=gv3[:, g * 4:(g + 1) * 4, 2], op=ALU.mult)
                k.dve.tensor_tensor(out=tmp4[:, :, :], in0=po_b[:, :, 0:64], in1=rr[:, 8:12].unsq(2).bc([4, 4, 64]), op=ALU.mult)
                k.dve.tensor_tensor(out=acc[:, g * 4:(g + 1) * 4, :], in0=acc[:, g * 4:(g + 1) * 4, :], in1=tmp4[:, :, :], op=ALU.add)
            k.dve.tensor_copy(out=accb[:, :, :], in_=acc[:, :, :])
            k.dma(AO[Tn + bl * 4:Tn + bl * 4 + 4, 512:1024], accb[:, :, :].rearrange("p h d -> p (h d)"))


def phase_moba_sample(k, L):
    Tn = L['Tn']; P = L['P']; LP = L['LP']; NBLKS = L['NBLKS']; NBLKSP = L['NBLKSP']
    QT2 = L['QT2']; KT2 = L['KT2']; VT2 = L['VT2']; MO = L['MO']; identb = L['identb']; identf = L['identf']
    cache = L['cache_moba']
    with k.scope():
        idx = page_indices(k, L)
        eexp = k.sb('eexp', [NBLKSP, LP], BF16); k.dma(eexp[:, :], L['EEXP2'][0:NBLKSP, 0:LP])
        smk = k.sb('smk', [128, 2, 16], BF16); k.dma(smk[:, :, :], L['SMK'][:, :, :])
        ones = k.sb('ones', [128, 1]); k.op('dve', lambda e: e.memset(ones[:, :].ap, 1.0), (), (ones,))
        ps_s = [k.ps('ps_s', [128, 512]) for _ in range(2)]
        pXb = [k.ps('pXb', [64, 4, 128], BF16) for _ in range(2)]
        stg = [k.sb('stg', [128, 256], BF16) for _ in range(2)]; stf = [k.sb('stf', [128, 256]) for _ in range(2)]
        pM = k.ps('pM', [64, 4, NBLKSP])
        pGt = k.ps('pGt', [4, 16, NBLKSP])
        po_b = k.ps('po_b', [4, 4, 65])
        ps_t = k.ps('ps_t', [NBLKSP, 16, 4], BF16)
        pg = [k.sb('pg', [128, 512]) for _ in range(3)]
        K2s = k.sb('K2s', [64, 4, LP], BF16); V2s = k.sb('V2s', [128, P, 4, 65], BF16)
        k.op('dve', lambda e: e.memset(V2s[:, :, :, 64:65].ap, 1.0), (), (V2s,))
        meansT = k.sb('meansT', [64, 4, NBLKSP])
        qf32 = k.sb('qf32', [64, 16, 16]); k.dma(qf32[:, :, :], L['QF2'][:, :, :])
        qs_all = k.sb('qs_all', [64, 16, 16], BF16); k.dma(qs_all[:, :, :], QT2[:, :, Tn:Tn + 16])
        knew = k.sb('knew', [64, 4, 16], BF16); k.dma(knew[:, :, :], KT2[:, :, Tn:Tn + 16])
        vnew = [k.sb('vnew', [4, 4, 65], BF16) for _ in range(2)]
        q16 = k.sb('q16', [64, 4, 4], BF16)
        score = k.sb('score', [4, 16, NBLKSP]); m8 = k.sb('m8', [4, 16, 8]); thr = k.sb('thr', [4, 16])
        negs = k.sb('negs', [4, 16, NBLKSP]); negb = k.sb('negb', [4, 16, NBLKSP], BF16)
        negT = k.sb('negT', [NBLKSP, 16, 4], BF16)
        ebuf = [k.sb('ebuf', [128, 16], BF16) for _ in range(4)]
        rr = k.sb('rr', [4, 8]); accb = k.sb('accb', [4, 16, 64], BF16)
        k.op('dve', lambda e: e.memset(score[:, :, :].ap, -2.0e9), (), (score,))
        cnt = [0]
        for bl in range(4):
            vn = vnew[bl % 2]
            k.dma(vn[:, :, :], VT2[Tn + bl * 4:Tn + bl * 4 + 4, :, :])
            for lp in range(P):
                pgt = pg[lp % 3]
                gather_page(k, pgt, cache, idx, bl * P + lp)
                sg_ = stg[lp % 2]; sf_ = stf[lp % 2]
                k.dve.tensor_copy(out=sg_[:, :], in_=pgt[:, 0:256])
                k.dve.tensor_copy(out=sf_[:, :], in_=pgt[:, 0:256])
                k.pool.tensor_copy(out=V2s[:, lp, :, 0:64], in_=pgt[:, 256:512].rearrange("p (g d) -> p g d", g=4))
                px = pXb[lp % 2]
                for g in range(4):
                    k.pe.transpose(out=px[:, g, :], in_=sg_[:, g * 64:(g + 1) * 64], identity=identb[:, :])
                k.dve.tensor_copy(out=K2s[:, :, lp * 128:(lp + 1) * 128], in_=px[:, :, :])
                blk = lp // 2
                for g in range(4):
                    k.pe.matmul(out=pM[:, g, blk:blk + 1], lhsT=sf_[:, g * 64:(g + 1) * 64], rhs=ones[:, 0:1],
                                start=(lp == 0 and g == 0), stop=(lp % 2 == 1), skip_group_check=True)
            k.dve.tensor_scalar(out=meansT[:, :, 0:NBLKS], in0=pM[:, :, 0:NBLKS], scalar1=1.0 / 2048, scalar2=None, op0=ALU.mult)
            for h in range(16):
                k.pe.matmul(out=pGt[:, h, 0:NBLKS], lhsT=qf32[:, h, bl * 4:(bl + 1) * 4], rhs=meansT[:, h // 4, 0:NBLKS], start=(h == 0), stop=(h == 15), skip_group_check=True)
            k.dve.tensor_copy(out=score[:, :, 0:NBLKS], in_=pGt[:, :, 0:NBLKS])
            for h in range(16):
                k.dve.max(out=m8[:, h, :], in_=score[:, h, :])
            k.dve.tensor_scalar(out=thr[:, :], in0=m8[:, :, 2], scalar1=-1.0e8, scalar2=None, op0=ALU.max)
            k.dve.tensor_tensor(out=negs[:, :, :], in0=score[:, :, :], in1=thr[:, :].unsq(2).bc([4, 16, NBLKSP]), op=ALU.is_lt)
            k.dve.tensor_scalar(out=negb[:, :, :], in0=negs[:, :, :], scalar1=NEG, scalar2=None, op0=ALU.mult)
            for h in range(16):
                k.pe.transpose(out=ps_t[:, h, :], in_=negb[:, h, :], identity=identb[0:4, 0:4])
            k.dve.tensor_copy(out=negT[:, :, :], in_=ps_t[:, :, :])
            for g in range(4):
                k.dve.tensor_copy(out=q16[:, :, :], in_=qs_all[:, g * 4:(g + 1) * 4, bl * 4:(bl + 1) * 4])
                qf = q16[:, :, :].rearrange("p r t -> p (r t)")
                nf = negT[:, g * 4:(g + 1) * 4, :].rearrange("p r t -> p (r t)")
                def sc_ms(kt):
                    pss = ps_s[cnt[0] % 2]
                    e = ebuf[cnt[0] % 4]; cnt[0] += 1
                    if kt < P:
                        k.pe.matmul(out=pss[:, 0:16], lhsT=K2s[:, g, kt * 128:(kt + 1) * 128], rhs=qf, start=True, stop=False)
                        k.pe.matmul(out=pss[:, 0:16], lhsT=eexp[:, kt * 128:(kt + 1) * 128], rhs=nf, start=False, stop=True)
                        k.act.activation(out=e[:, :], in_=pss[:, 0:16], func=AF.Exp, scale=0.125)
                        return (e, 128, V2s[:, kt, g, :])
                    k.pe.matmul(out=pss[0:4, 0:16], lhsT=knew[:, g, bl * 4:(bl + 1) * 4], rhs=qf, start=True, stop=True)
                    k.act.activation(out=e[0:4, :], in_=pss[0:4, 0:16], func=AF.Exp, scale=0.125)
                    k.pool.tensor_tensor(out=e[0:4, :], in0=e[0:4, :], in1=smk[0:4, 1, :], op=ALU.mult)
                    return (e, 4, vn[:, g, :])

                def pv_ms(kt, tup):
                    e, nk, vv = tup
                    for r in range(4):
                        k.pe.matmul(out=po_b[:, r, :], lhsT=e[:nk, r * 4:(r + 1) * 4], rhs=vv, start=(kt == 0 and r == 0), stop=(kt == P), skip_group_check=True)

                pipe(range(P + 1), sc_ms, pv_ms)
                k.dve.tensor_scalar(out=rr[:, 0:4], in0=po_b[:, :, 64], scalar1=1e-30, scalar2=None, op0=ALU.max)
                k.dve.reciprocal(out=rr[:, 4:8], in_=rr[:, 0:4])
                k.dve.tensor_tensor(out=accb[:, g * 4:(g + 1) * 4, :], in0=po_b[:, :, 0:64], in1=rr[:, 4:8].unsq(2).bc([4, 4, 64]), op=ALU.mult)
            k.dma(MO[Tn + bl * 4:Tn + bl * 4 + 4, :], accb[:, :, :].rearrange("p h d -> p (h d)"))


def host_consts(cfg):
    Tn, P = cfg['T'], cfg['P']
    TR = Tn + 128
    LP = P * 128
    pos = np.concatenate([np.arange(Tn), np.tile(LP + np.arange(4), 4), np.zeros(112)]).astype(np.float32)
    half = 32
    inv = (10000.0 ** (-np.arange(half, dtype=np.float32) / half)).astype(np.float32)
    ang = pos[:, None] * inv[None, :]
    ropecs = np.concatenate([np.cos(ang), np.sin(ang)], axis=1).astype(np.float32)
    return {
        'ropecs': ropecs,
        'identb': np.eye(128, dtype=np.float32).astype(ml_dtypes.bfloat16),
        'identf': np.eye(128, dtype=np.float32),
        **nsa_consts(Tn, LP),
        'masks64': np.stack([np.triu(np.ones((64, 64), np.float32)), np.triu(np.ones((64, 64), np.float32), 1), np.tril(np.ones((64, 64), np.float32), -1)], axis=1),
    }


def nsa_consts(Tn, LP):
    bf = ml_dtypes.bfloat16
    NBLK = Tn // 256; NBLKP = max(8, NBLK)
    tq = np.arange(Tn)[:, None]; nb = np.arange(NBLKP)[None, :]
    gb0 = np.where(nb < tq // 256, 0.0, -1.0e9).astype(np.float32)
    gb1 = (nb != tq // 256).astype(np.float32)
    GBM = np.stack([gb0, gb1], axis=1)
    Wd_ = max(Tn, LP)
    EE2 = (np.arange(Wd_)[None, :] // 256 == np.arange(64)[:, None]).astype(np.float32).astype(bf)
    NSEL = Tn // 64; NB = Tn // 16 - 1; NBT = (NB + 127) // 128
    t = np.arange(Tn)[:, None]; j = np.arange(NSEL)[None, :]
    cur = t // 64
    forced = ((j == cur) | (j == cur - 1) | (j == 0)).astype(np.float32)
    FB = np.where(j <= cur, 100.0 * forced, -1.0e9).astype(np.float32)
    n = np.arange(NBT * 128)[:, None]
    lo = np.maximum(n * 16, j * 64); hi = np.minimum(n * 16 + 32, (j + 1) * 64)
    ov = (np.clip(hi - lo, 0, None).astype(np.float32) / 32.0)
    ov[NB:] = 0
    OV = ov.reshape(NBT, 128, NSEL).transpose(1, 0, 2).astype(bf)
    nl = np.arange(128)[:, None, None]; idx = np.arange(17)[None, :, None]; tl = np.arange(128)[None, None, :]
    CM = (16 * nl + 31 - 128 * idx <= tl).astype(np.float32).astype(bf)
    s_ = np.arange(128)[:, None]; t_ = np.arange(128)[None, :]
    TRI = np.stack([(s_ <= t_), (s_ > t_)], axis=1).astype(np.float32).astype(bf)
    W = max(Tn, LP)
    EE = (np.arange(W)[None, :] // 64 == np.arange(128)[:, None]).astype(np.float32).astype(bf)
    NSELS = LP // 64 + 1; NBS = LP // 16 - 1; NBTS = (NBS + 127) // 128
    js = np.arange(NSELS)[None, :]
    FBs = np.tile((100.0 * ((js == 0) | (js == NSELS - 2) | (js == NSELS - 1))).astype(np.float32), (4, 1))
    ns_ = np.arange(NBTS * 128)[:, None]
    lo = np.maximum(ns_ * 16, js * 64); hi = np.minimum(ns_ * 16 + 32, (js + 1) * 64)
    ovs = (np.clip(hi - lo, 0, None).astype(np.float32) / 32.0); ovs[NBS:] = 0
    OVs = ovs.reshape(NBTS, 128, NSELS).transpose(1, 0, 2).astype(bf)
    i_ = np.arange(128)[:, None]; tq_ = np.tile(np.arange(4), 4)[None, :]
    SMK = np.stack([(i_ > tq_), (i_ <= tq_)], axis=1).astype(np.float32).astype(bf)
    return {'FBt': FB, 'OVt': OV, 'CMt': CM, 'TRIt': TRI, 'EEXP': EE, 'GBM': GBM, 'EEXP2': EE2,
            'FBs': FBs, 'OVs': OVs, 'SMK': SMK, 'IOTA': np.arange(128, dtype=np.float32).reshape(128, 1)}


def make_in_maps(cfg, inputs, ncores):
    Tn, P = cfg['T'], cfg['P']
    hc = host_consts(cfg)
    B = inputs['x_prompt'].shape[0]
    maps = []
    for c in range(ncores):
        b = c % B
        sl = slice(4 * c, 4 * c + 4)
        m = {
            'xp': np.ascontiguousarray(inputs['x_prompt'][b]),
            'xs': np.ascontiguousarray(inputs['x_sample'][sl]).reshape(16, 1024),
            'st_win': np.ascontiguousarray(inputs['state_win_kv'][0, sl]).reshape(4, 512, 256),
            'st_wkv': np.ascontiguousarray(inputs['state_wkv'][0, sl]),
            'st_shift': np.ascontiguousarray(inputs['state_shift'][0, sl]),
            'ptab': np.ascontiguousarray(inputs['page_table'][sl]).astype(np.int32),
            'norm_mix': inputs['norm_mix'], 'norm_ffn': inputs['norm_ffn'], 'norm_final': inputs['norm_final'].reshape(1, 1024),
            'w_in0': inputs['even_w_in'][0], 'w_out0': inputs['even_w_out'][0],
            'gate_b': inputs['nsa_gate_b'][0].reshape(1, 24),
            'cache_nsa': inputs['cache_nsa_kv'][0].reshape(-1, 512), 'cache_moba': inputs['cache_moba_kv'][0].reshape(-1, 512),
            'w_in1': inputs['odd_w_in'][0], 'w_out1': inputs['odd_w_out'][0],
            'ffn_g0': inputs['ffn_w_gate'][0], 'ffn_g1': inputs['ffn_w_gate'][1], 'ffn_u0': inputs['ffn_w_up'][0], 'ffn_u1': inputs['ffn_w_up'][1],
            'ffn_d0': inputs['ffn_w_down'][0], 'ffn_d1': inputs['ffn_w_down'][1],
            'cmp_w1': inputs['nsa_cmp_w1'][0], 'cmp_pe': inputs['nsa_cmp_pe'][0], 'cmp_w2': inputs['nsa_cmp_w2'][0],
            'rw_mu': inputs['rwkv_mu'][0].reshape(1, RWC),
            'rw_vec': np.stack([inputs['rwkv_w0'][0], inputs['rwkv_a0'][0], inputs['rwkv_k_k'][0], inputs['rwkv_k_a'][0],
                                inputs['rwkv_r_k'][0].reshape(512), inputs['rwkv_ln_g'][0], inputs['rwkv_ln_b'][0]]).astype(np.float32),
            'rw_wup': inputs['rwkv_w_up'][0], 'rw_aup': inputs['rwkv_a_up'][0], 'rw_gup': inputs['rwkv_g_up'][0],
        }
        m.update(hc)
        maps.append(m)
    return maps


CFG_FULL = {'T': 4096, 'P': 64, 'NPHYS': 2560, 'stages': 'ABCSDEFMG'}


def kernel(**inputs):
    cfg = dict(CFG_FULL)
    inputs = {k_: np.asarray(v) for k_, v in inputs.items()}
    nc, kb = build(cfg)
    maps = make_in_maps(cfg, inputs, 8)
    res = run_bass_kernel_spmd(nc, maps, core_ids=list(range(8)))
    R = res.results
    f32 = np.float32
    y_p = np.stack([R[c]['y_p'] for c in range(4)]).astype(f32)
    y_s = np.concatenate([R[c]['y_s'].reshape(4, 4, 1024) for c in range(8)]).astype(f32)
    nsa_p = np.stack([R[c]['o_nsa_p'].reshape(4096, 4, 2, 64) for c in range(4)])[None].astype(f32)
    nsa_s = np.concatenate([R[c]['o_nsa_s'].reshape(4, 4, 4, 2, 64) for c in range(8)])[None].astype(f32)
    moba_p = np.stack([R[c]['o_moba_p'].reshape(4096, 2, 4, 64) for c in range(4)])[None].astype(f32)
    moba_s = np.concatenate([R[c]['o_moba_s'].reshape(4, 4, 2, 4, 64) for c in range(8)])[None].astype(f32)
    win_p = np.stack([R[c]['o_win_p'].reshape(512, 2, 2, 64) for c in range(4)])[None].astype(f32)
    win_s = np.concatenate([R[c]['o_win_s'].reshape(4, 512, 2, 2, 64) for c in range(8)])[None].astype(f32)
    wkv_p = np.stack([R[c]['o_wkv_p'] for c in range(4)])[None].astype(f32)
    wkv_s = np.concatenate([R[c]['o_wkv_s'] for c in range(8)])[None].astype(f32)
    sh_p = np.concatenate([R[c]['o_sh_p'] for c in range(4)])[None].astype(f32)
    sh_s = np.concatenate([R[c]['o_sh_s'] for c in range(8)])[None].astype(f32)
    return (y_p, y_s, nsa_p, nsa_s, moba_p, moba_s, win_p, win_s, wkv_p, wkv_s, sh_p, sh_s)
```

```python
import numpy as np
import ml_dtypes
from contextlib import ExitStack, contextmanager
import concourse.bass as bass
import concourse.mybir as mybir
from concourse.bass_utils import run_bass_kernel_spmd

F32 = mybir.dt.float32
BF16 = mybir.dt.bfloat16
I32 = mybir.dt.int32
AF = mybir.ActivationFunctionType
ALU = mybir.AluOpType
AX = mybir.AxisListType

ENGS = ['pe', 'act', 'dve', 'pool', 'sp']
NDMA = 12
SAME_SYNC = {'pe': False, 'act': True, 'dve': True, 'pool': True, 'sp': False}
WRITE_KEYS = ('out', 'accum_out', 'out_max', 'out_indices', 'out_ap')


class T:
    def __init__(self, h, name):
        self.h = h
        self.name = name
        self.w = {}
        self.r = {}

    def __getitem__(self, idx):
        return V(self.h[idx], self)


class V:
    def __init__(self, ap, t):
        self.ap = ap
        self.t = t

    def __getitem__(self, idx):
        return V(self.ap[idx], self.t)

    def rearrange(self, s, **kw):
        return V(self.ap.rearrange(s, **kw), self.t)

    def bc(self, shape):
        return V(self.ap.to_broadcast(list(shape)), self.t)

    def pbc(self, n):
        return V(self.ap.partition_broadcast(n), self.t)

    def bitcast(self, dt):
        return V(self.ap.bitcast(dt), self.t)

    def unsq(self, ax):
        return V(self.ap.unsqueeze(ax), self.t)

    @property
    def shape(self):
        return self.ap.shape


def _merge(d, s):
    for k, v in s.items():
        if d.get(k, 0) < v:
            d[k] = v


class EngProxy:
    def __init__(self, kb, name):
        self.kb = kb
        self.name = name

    def __getattr__(self, opname):
        kb = self.kb
        name = self.name

        def call(**kw):
            reads, writes = [], []
            kw2 = {}
            for key, v in kw.items():
                if isinstance(v, V):
                    (writes if key in WRITE_KEYS else reads).append(v.t)
                    kw2[key] = v.ap
                else:
                    kw2[key] = v
            return kb.op(name, lambda e: getattr(e, opname)(**kw2), reads, writes)

        return call


class KB:
    def __init__(self):
        self.nc = bass.Bass("TRN2", target_bir_lowering=False)
        self.es = ExitStack()
        self.cnt = {}
        self.known = {e: {} for e in ENGS}
        self.sem = {}
        nc = self.nc
        self.eng = {'pe': nc.tensor, 'act': nc.scalar, 'dve': nc.vector, 'pool': nc.gpsimd, 'sp': nc.sync}
        for e in ENGS:
            self._mksem('c_' + e)
        for i in range(NDMA):
            self._mksem('d%d' % i)
        for i in range(4):
            self._mksem('g%d' % i)
        self.dma_rr = 0
        self.g_rr = 0
        self.pe = EngProxy(self, 'pe')
        self.act = EngProxy(self, 'act')
        self.dve = EngProxy(self, 'dve')
        self.pool = EngProxy(self, 'pool')
        self.stack = [self.es]
        self.ninst = 0
        self.uid = 0

    def _mksem(self, name):
        self.sem[name] = self.es.enter_context(self.nc.semaphore(name))
        self.cnt[name] = 0

    def dram(self, name, shape, dt, kind="Internal"):
        h = self.nc.dram_tensor(name, list(shape), dt, kind=kind)
        return T(h.ap(), name)

    def sb(self, name, shape, dt=F32):
        self.uid += 1
        h = self.stack[-1].enter_context(self.nc.sbuf_tensor("%s_%d" % (name, self.uid), list(shape), dt))
        return T(h, name)

    def ps(self, name, shape, dt=F32):
        self.uid += 1
        h = self.stack[-1].enter_context(self.nc.psum_tensor("%s_%d" % (name, self.uid), list(shape), dt))
        return T(h, name)

    @contextmanager
    def scope(self):
        es = ExitStack()
        self.stack.append(es)
        try:
            yield
        finally:
            self.barrier()
            self.stack.pop()
            es.close()

    def _deps(self, eng, reads, writes, own):
        deps = {}
        for t in reads:
            _merge(deps, t.w)
        for t in writes:
            _merge(deps, t.w)
            _merge(deps, t.r)
        kn = self.known[eng]
        e = self.eng[eng]
        for s, v in deps.items():
            if s == own and not SAME_SYNC[eng]:
                continue
            if kn.get(s, 0) >= v:
                continue
            e.wait_ge(self.sem[s], v)
            self.ninst += 1
            kn[s] = v

    def _mark(self, s, val, reads, writes):
        for t in reads:
            if t.r.get(s, 0) < val:
                t.r[s] = val
        for t in writes:
            if t.w.get(s, 0) < val:
                t.w[s] = val

    def op(self, eng, fn, reads=(), writes=()):
        own = 'c_' + eng
        self._deps(eng, reads, writes, own)
        self.cnt[own] += 1
        val = self.cnt[own]
        fn(self.eng[eng]).then_inc(self.sem[own], 1)
        self.ninst += 1
        self._mark(own, val, reads, writes)

    def dma(self, out, in_, q='sp', **kw):
        reads, writes = [in_.t], [out.t]
        s = 'd%d' % self.dma_rr
        self.dma_rr = (self.dma_rr + 1) % NDMA
        self._deps(q, reads, writes, None)
        self.cnt[s] += 16
        val = self.cnt[s]
        self.eng[q].dma_start(out=out.ap, in_=in_.ap, **kw).then_inc(self.sem[s], 16)
        self.ninst += 1
        self._mark(s, val, reads, writes)

    def raw16(self, q, fn, reads=(), writes=()):
        s = 'g%d' % self.g_rr
        self.g_rr = (self.g_rr + 1) % 4
        self._deps(q, reads, writes, None)
        self.cnt[s] += 16
        val = self.cnt[s]
        fn(self.eng[q]).then_inc(self.sem[s], 16)
        self.ninst += 1
        self._mark(s, val, reads, writes)

    def barrier(self, engs=ENGS):
        for e in engs:
            kn = self.known[e]
            for s, v in self.cnt.items():
                if v > 0 and kn.get(s, 0) < v and s != 'c_' + e:
                    self.eng[e].wait_ge(self.sem[s], v)
                    kn[s] = v

    def dbg(self, name, v, shape, dt=F32):
        if not getattr(self, 'dbg_on', False):
            return
        o = self.dram('dbg_' + name, shape, dt, "ExternalOutput")
        idx = tuple(slice(0, n) for n in shape)
        self.dma(o[idx], v)

    def finish(self):
        self.barrier(['sp'])
        self.es.close()
        return self.nc


RWC = 1792
EVC = 3096
ODC = 1536
DFF = 2816
NEG = -240000.0


def build(cfg):
    Tn, P, NPH = cfg['T'], cfg['P'], cfg['NPHYS']
    stages = cfg.get('stages', 'A')
    NT = Tn // 128
    LP = P * 128
    TR = Tn + 128
    k = KB()
    k.dbg_on = cfg.get('dbg', False)
    nc = k.nc
    IN = lambda n, s, d=F32: k.dram(n, s, d, "ExternalInput")
    OUT = lambda n, s, d=F32: k.dram(n, s, d, "ExternalOutput")
    xp = IN('xp', [Tn, 1024]); xs = IN('xs', [16, 1024])
    st_win = IN('st_win', [4, 512, 256]); st_wkv = IN('st_wkv', [4, 8, 64, 64]); st_shift = IN('st_shift', [4, RWC])
    ptab = IN('ptab', [4, P], I32)
    norm_mix = IN('norm_mix', [2, 1024]); norm_ffn = IN('norm_ffn', [2, 1024]); norm_final = IN('norm_final', [1, 1024])
    w_in0 = IN('w_in0', [1024, EVC]); w_out0 = IN('w_out0', [1024, 1024])
    gate_b = IN('gate_b', [1, 24])
    w_in1 = IN('w_in1', [1024, ODC]); w_out1 = IN('w_out1', [1024, 1024])
    ffn_g = [IN('ffn_g%d' % i, [1024, DFF]) for i in range(2)]; ffn_u = [IN('ffn_u%d' % i, [1024, DFF]) for i in range(2)]
    ffn_d = [IN('ffn_d%d' % i, [DFF, 1024]) for i in range(2)]
    NBLK = Tn // 256; NBLKP = max(8, NBLK)
    GBM = IN('GBM', [Tn, 2, NBLKP]); EEXP2 = IN('EEXP2', [64, max(Tn, LP)], BF16)
    cache_nsa = IN('cache_nsa', [NPH * 128, 512]); cache_moba = IN('cache_moba', [NPH * 128, 512])
    NSELS = LP // 64 + 1; NBS = LP // 16 - 1; NBTS = (NBS + 127) // 128
    NBLKS = LP // 256; NBLKSP = max(8, NBLKS)
    FBs = IN('FBs', [4, NSELS]); OVs = IN('OVs', [128, NBTS, NSELS], BF16)
    SMK = IN('SMK', [128, 2, 16], BF16)
    IOTA = IN('IOTA', [128, 1])
    rw_mu = IN('rw_mu', [1, RWC]); rw_vec = IN('rw_vec', [7, 512])
    rw_wup = IN('rw_wup', [64, 512]); rw_aup = IN('rw_aup', [64, 512]); rw_gup = IN('rw_gup', [128, 512])
    masks64 = IN('masks64', [64, 3, 64])
    NSEL = Tn // 64; NB = Tn // 16 - 1; NBT = (NB + 127) // 128
    cmp_w1 = IN('cmp_w1', [2, 32, 64, 64]); cmp_pe = IN('cmp_pe', [2, 32, 64]); cmp_w2 = IN('cmp_w2', [2, 64, 64])
    FBt = IN('FBt', [Tn, NSEL]); OVt = IN('OVt', [128, NBT, NSEL], BF16); CMt = IN('CMt', [128, 17, 128], BF16)
    TRIt = IN('TRIt', [128, 2, 128], BF16); EEXP = IN('EEXP', [128, max(Tn, LP)], BF16)
    ropecs = IN('ropecs', [TR, 64])
    identb_d = IN('identb', [128, 128], BF16); identf_d = IN('identf', [128, 128])
    y_p = OUT('y_p', [Tn, 1024]); y_s = OUT('y_s', [16, 1024])
    o_nsa_p = OUT('o_nsa_p', [Tn, 512]); o_nsa_s = OUT('o_nsa_s', [16, 512])
    o_moba_p = OUT('o_moba_p', [Tn, 512]); o_moba_s = OUT('o_moba_s', [16, 512])
    WN = min(512, Tn)
    o_win_p = OUT('o_win_p', [WN, 256]); o_win_s = OUT('o_win_s', [4, 512, 256])
    o_wkv_p = OUT('o_wkv_p', [8, 64, 64]); o_wkv_s = OUT('o_wkv_s', [4, 8, 64, 64])
    o_sh_p = OUT('o_sh_p', [1, RWC]); o_sh_s = OUT('o_sh_s', [4, RWC])
    RW = k.dram('RW', [TR, RWC], F32)
    QT = k.dram('QT', [64, 8, TR], BF16)
    KT = k.dram('KT', [64, 6, TR], BF16)
    VcT = k.dram('VcT', [64, 2, TR], BF16)
    VT = k.dram('VT', [TR, 2, 2, 65], BF16)
    GT = k.dram('GT', [TR, 24], F32)
    AO = k.dram('AO', [TR, 1024], BF16, "ExternalOutput" if cfg.get('dbg_ao') else "Internal")
    H1 = k.dram('H1', [TR, 1024], F32, "ExternalOutput" if cfg.get('dbg_ao') else "Internal")
    ACTT = k.dram('ACTT', [22, 128, TR], BF16)
    QT2 = k.dram('QT2', [64, 16, TR], BF16); KT2 = k.dram('KT2', [64, 4, TR], BF16); VT2 = k.dram('VT2', [TR, 4, 65], BF16)
    NS2 = k.dram('NS2', [Tn, 16, NBLKP], BF16)
    QF2 = k.dram('QF2', [64, 16, 16], F32)
    MO = k.dram('MO', [TR, 1024], BF16, "ExternalOutput" if cfg.get('dbg_ao') else "Internal")

    identb = k.sb('identb', [128, 128], BF16); identf = k.sb('identf', [128, 128], F32)
    k.dma(identb[:, :], identb_d[:, :]); k.dma(identf[:, :], identf_d[:, :])

    tiles = [(i * 128, 128) for i in range(NT)] + [(Tn, 16)]

    def rmsnorm_T(xt, rows, gbc, xn, pT, xnT, ss, junk):
        k.act.activation(out=junk[:rows, :], in_=xt[:rows, :], func=AF.Square, accum_out=ss[:rows, 0:1])
        k.dve.tensor_scalar(out=ss[:rows, 1:2], in0=ss[:rows, 0:1], scalar1=1.0 / 1024, scalar2=1e-6, op0=ALU.mult, op1=ALU.add)
        k.act.activation(out=ss[:rows, 3:4], in_=ss[:rows, 1:2], func=AF.Sqrt)
        k.dve.reciprocal(out=ss[:rows, 2:3], in_=ss[:rows, 3:4])
        k.dve.scalar_tensor_tensor(out=xn[:rows, :], in0=xt[:rows, :], scalar=ss[:rows, 2:3], in1=gbc[:rows, :], op0=ALU.mult, op1=ALU.mult)
        for kk in range(8):
            k.pe.transpose(out=pT[:, kk, :rows], in_=xn[:rows, kk * 128:(kk + 1) * 128], identity=identb[:rows, :rows])
        k.act.copy(out=xnT[:, :, :rows], in_=pT[:, :, :rows])

    def load_w_bf16(Wsb, wd, K, N):
        with k.scope():
            stg = [k.sb('wstg', [128, N], F32) for _ in range(2)]
            for kk in range(K // 128):
                s = stg[kk % 2]
                k.dma(s[:, :], wd[kk * 128:(kk + 1) * 128, :])
                (k.pool if kk % 2 else k.dve).tensor_copy(out=Wsb[:, kk, :], in_=s[:, :])

    with k.scope():
        W0 = k.sb('W0', [128, 8, EVC], BF16)
        load_w_bf16(W0, w_in0, 1024, EVC)
        gbc = k.sb('gbc', [128, 1024]); k.dma(gbc[:, :], norm_mix[0:1, :].pbc(128))
        gb = k.sb('gb', [128, 24]); k.dma(gb[:, :], gate_b[0:1, :].pbc(128))
        xt = [k.sb('xt', [128, 1024]) for _ in range(2)]
        junk = k.sb('junk', [128, 1024], BF16)
        xn = k.sb('xn', [128, 1024], BF16)
        ss = k.sb('ss', [128, 4])
        xnT = k.sb('xnT', [128, 8, 128], BF16)
        proj = [k.sb('proj', [128, EVC]) for _ in range(2)]
        cs = k.sb('cs', [128, 64])
        tmp = [k.sb('tmp%d' % i, [128, 8, 32]) for i in range(4)]
        qb = k.sb('qb', [128, 8, 64], BF16)
        kb = k.sb('kb', [128, 6, 64], BF16)
        vb = k.sb('vb', [128, 3, 2, 65], BF16)
        k.dve.memset(ap=vb[:, :, :, :], constant=1.0) if False else k.op('dve', lambda e: e.memset(vb[:, :, :, :].ap, 1.0), (), (vb,))
        gt = k.sb('gt', [128, 24])
        qT = k.sb('qT', [64, 8, 128], BF16); kT = k.sb('kT', [64, 6, 128], BF16); vcT = k.sb('vcT', [64, 2, 128], BF16)
        pT = k.ps('pT', [128, 8, 128], BF16)
        pA = [k.ps('pA', [128, 512]) for _ in range(2)]
        pQ = k.ps('pQ', [64, 8, 128], BF16)
        pK = k.ps('pK', [64, 8, 128], BF16)
        ci = 0
        for ti, (r0, rows) in enumerate(tiles):
            x = xt[ti % 2]
            pj = proj[ti % 2]
            src = xp[r0:r0 + rows, :] if r0 < Tn else xs[0:16, :]
            k.dma(x[:rows, :], src)
            k.dma(cs[:rows, :], ropecs[r0:r0 + rows, :])
            rmsnorm_T(x, rows, gbc, xn, pT, xnT, ss, junk)
            for c0 in range(0, EVC, 512):
                w = min(512, EVC - c0)
                ps = pA[ci % 2]
                for kk in range(8):
                    k.pe.matmul(out=ps[:rows, :w], lhsT=xnT[:, kk, :rows], rhs=W0[:, kk, c0:c0 + w], start=(kk == 0), stop=(kk == 7))
                if ci % 2:
                    k.act.copy(out=pj[:rows, c0:c0 + w], in_=ps[:rows, :w])
                else:
                    k.dve.tensor_copy(out=pj[:rows, c0:c0 + w], in_=ps[:rows, :w])
                ci += 1
            k.dma(RW[r0:r0 + rows, :], pj[:rows, 0:RWC])
            k.dve.tensor_tensor(out=gt[:rows, :], in0=pj[:rows, 3072:3096], in1=gb[:rows, :], op=ALU.add)
            k.act.activation(out=gt[:rows, :], in_=gt[:rows, :], func=AF.Sigmoid)
            k.dma(GT[r0:r0 + rows, :], gt[:rows, :])
            cosb = lambda n: cs[:rows, 0:32].unsq(1).bc([rows, n, 32])
            sinb = lambda n: cs[:rows, 32:64].unsq(1).bc([rows, n, 32])
            qv = pj[:rows, 1792:2304].rearrange("p (h d) -> p h d", h=8)
            views = [(qv, 8, qb[:rows, :, :])]
            for c in range(3):
                kv_ = pj[:rows, 2304 + c * 256:2304 + c * 256 + 128].rearrange("p (g d) -> p g d", g=2)
                views.append((kv_, 2, None))
            for vi, (xv, n, ob) in enumerate(views):
                E1 = k.dve if vi % 2 == 0 else k.pool
                x1 = xv[:, :, 0:32]; x2 = xv[:, :, 32:64]
                t = [tt[:rows, 0:n, :] for tt in tmp]
                E1.tensor_tensor(out=t[0], in0=x1, in1=cosb(n), op=ALU.mult)
                E1.tensor_tensor(out=t[1], in0=x2, in1=sinb(n), op=ALU.mult)
                E1.tensor_tensor(out=t[2], in0=x2, in1=cosb(n), op=ALU.mult)
                E1.tensor_tensor(out=t[3], in0=x1, in1=sinb(n), op=ALU.mult)
                if ob is not None:
                    E1.tensor_tensor(out=ob[:, :, 0:32], in0=t[0], in1=t[1], op=ALU.subtract)
                    E1.tensor_tensor(out=ob[:, :, 32:64], in0=t[2], in1=t[3], op=ALU.add)
                else:
                    E1.tensor_tensor(out=x1, in0=t[0], in1=t[1], op=ALU.subtract)
                    E1.tensor_tensor(out=x2, in0=t[2], in1=t[3], op=ALU.add)
            kvv = pj[:rows, 2304:3072].rearrange("p (c j g d) -> p c j g d", c=3, j=2, g=2)
            for c in range(3):
                k.dve.tensor_copy(out=kb[:rows, 2 * c:2 * c + 2, :], in_=kvv[:, c, 0, :, :])
                k.pool.tensor_copy(out=vb[:rows, c, :, 0:64], in_=kvv[:, c, 1, :, :])
            if r0 < Tn:
                k.dma(o_nsa_p[r0:r0 + rows, :], pj[:rows, 2304:2816])
                if r0 >= Tn - WN:
                    k.dma(o_win_p[r0 - (Tn - WN):r0 - (Tn - WN) + rows, :], pj[:rows, 2816:3072])
                if ti == NT - 1:
                    k.dma(o_sh_p[0:1, :], pj[127:128, 0:RWC])
            else:
                k.dma(o_nsa_s[0:16, :], pj[:16, 2304:2816])
                for bl in range(4):
                    k.dma(o_win_s[bl, 508:512, :], pj[bl * 4:bl * 4 + 4, 2816:3072])
                    k.dma(o_win_s[bl, 0:508, :], st_win[bl, 4:512, :])
                    k.dma(o_sh_s[bl:bl + 1, :], pj[bl * 4 + 3:bl * 4 + 4, 0:RWC])
            for h in range(8):
                k.pe.transpose(out=pQ[:, h, :rows], in_=qb[:rows, h, :], identity=identb[:rows, :rows])
            k.act.copy(out=qT[:, :, :rows], in_=pQ[:, :, :rows])
            for h in range(6):
                k.pe.transpose(out=pK[:, h, :rows], in_=kb[:rows, h, :], identity=identb[:rows, :rows])
            for g in range(2):
                k.pe.transpose(out=pK[:, 6 + g, :rows], in_=vb[:rows, 0, g, 0:64], identity=identb[:rows, :rows])
            k.dve.tensor_copy(out=kT[:, :, :rows], in_=pK[:, 0:6, :rows])
            k.dve.tensor_copy(out=vcT[:, :, :rows], in_=pK[:, 6:8, :rows])
            k.dma(QT[:, :, r0:r0 + rows], qT[:, :, :rows])
            k.dma(KT[:, :, r0:r0 + rows], kT[:, :, :rows])
            k.dma(VcT[:, :, r0:r0 + rows], vcT[:, :, :rows])
            k.dma(VT[r0:r0 + rows, :, :, :], vb[:rows, 1:3, :, :])
    if 'B' in stages:
        phase_rwkv(k, locals())
    if 'C' in stages:
        phase_nsa_prompt(k, locals())
    LL = locals()
    if 'S' in stages:
        phase_nsa_sample(k, LL)
    if 'D' in stages:
        layer_tail(k, LL, 0, AO, w_out0, None, H1, None)
    if 'E' in stages:
        phase_proj1(k, LL)
    if 'F' in stages:
        phase_moba_prompt(k, LL)
    if 'M' in stages:
        phase_moba_sample(k, LL)
    if 'G' in stages:
        layer_tail(k, LL, 1, MO, w_out1, H1, None, (y_p, y_s))
    nc2 = k.finish()
    return nc2, k


def phase_rwkv(k, L):
    Tn = L['Tn']; RW = L['RW']; AO = L['AO']; identf = L['identf']; identb = L['identb']
    rw_mu = L['rw_mu']; rw_vec = L['rw_vec']; st_wkv = L['st_wkv']; st_shift = L['st_shift']
    with k.scope():
        mu = k.sb('mu', [64, RWC]); k.dma(mu[:, :], rw_mu[0:1, :].pbc(64))
        vec = k.sb('vec', [64, 7, 512])
        for i in range(7):
            k.dma(vec[:, i, :], rw_vec[i:i + 1, :].pbc(64))
        w0b, a0b, kkb, kab, rkb, lgb, lbb = [vec[:, i, :] for i in range(7)]
        wup = k.sb('wup', [64, 512]); k.dma(wup[:, :], L['rw_wup'][:, :])
        aup = k.sb('aup', [64, 512]); k.dma(aup[:, :], L['rw_aup'][:, :])
        gup = k.sb('gup', [128, 512]); k.dma(gup[:, :], L['rw_gup'][:, :])
        mk = k.sb('mk', [64, 3, 64]); k.dma(mk[:, :, :], L['masks64'][:, :, :])
        ones = k.sb('ones', [64, 1]); k.op('dve', lambda e: e.memset(ones[:, :].ap, 1.0), (), (ones,))
        banks = [k.ps('bank', [128, 512]) for _ in range(6)]
        pfbs = [k.ps('pfb', [64, 8, 64], BF16) for _ in range(2)]
        bi = [0]

        def bank():
            b = banks[bi[0] % 6]
            bi[0] += 1
            return b

        ST = k.sb('ST', [64, 8, 64]); STb = k.sb('STb', [64, 8, 64], BF16)
        vbs = [k.sb('vb16', [64, 512], BF16) for _ in range(2)]
        cur = [k.sb('cur', [64, RWC]) for _ in range(2)]
        prv = [k.sb('prv', [64, RWC]) for _ in range(2)]
        Lt = k.sb('Lt', [64, 256]); LT = k.sb('LT', [128, 3, 64])
        lw = k.sb('lw', [64, 512]); av = k.sb('av', [64, 512]); gvs = [k.sb('gv', [64, 512]) for _ in range(2)]
        kk = k.sb('kk', [64, 512]); sq = k.sb('sq', [64, 512]); k2s = [k.sb('k2', [64, 512]) for _ in range(2)]; bv = k.sb('bv', [64, 512])
        sm = k.sb('sm', [64, 8, 4]); sm2 = k.sb('sm2', [64, 8, 4])
        eP = k.sb('eP', [64, 512]); eN = k.sb('eN', [64, 512]); ePm = k.sb('ePm', [64, 512])
        Fs = [k.sb('F', [64, 4, 512], BF16) for _ in range(2)]
        FTs = [[k.sb('FT%d' % i, [64, 8, 64], BF16) for i in range(4)] for _ in range(2)]
        GCs = [k.sb('GC', [64, 8]) for _ in range(2)]
        Mbs = [[k.sb('Mb%d' % i, [64, 8, 64], BF16) for i in range(5)] for _ in range(2)]
        Bt = [k.sb('Bt%d' % i, [64, 8, 64], BF16) for i in range(2)]
        BTt = [k.sb('BTt%d' % i, [64, 8, 64], BF16) for i in range(2)]
        Nts = [k.sb('Nt', [64, 8, 64], BF16) for _ in range(2)]
        Zn = k.sb('Zn', [64, 8, 64], BF16); UT = k.sb('UT', [64, 8, 64], BF16)
        yv = k.sb('yv', [64, 512]); yc = k.sb('yc', [64, 512]); t1 = k.sb('t1', [64, 512]); ob = k.sb('ob', [64, 512], BF16)
        Stmp = k.sb('Stmp', [64, 8, 64])
        ci = [0]

        def hv(t, C):
            return t[:C, :].rearrange("p (h d) -> p h d", h=8)

        def chunk(r0, C, first_prev):
            sb_ = ci[0] % 2
            c = cur[sb_]; p = prv[sb_]; ci[0] += 1
            gv = gvs[sb_]; k2 = k2s[sb_]; vb16 = vbs[sb_]; F = Fs[sb_]; FT = FTs[sb_]; GC = GCs[sb_]; Mb = Mbs[sb_]; Nt = Nts[sb_]
            k.dma(c[:C, :], RW[r0:r0 + C, :])
            if first_prev is None:
                k.dma(p[:C, :], RW[r0 - 1:r0 - 1 + C, :])
            else:
                if first_prev == 'zero':
                    k.op('dve', lambda e: e.memset(p[0:1, :].ap, 0.0), (), (p,))
                else:
                    k.dma(p[0:1, :], first_prev)
                k.dma(p[1:C, :], RW[r0:r0 + C - 1, :])
            k.dve.tensor_tensor(out=p[:C, :], in0=p[:C, :], in1=c[:C, :], op=ALU.subtract)
            k.pool.tensor_tensor(out=p[:C, :], in0=p[:C, :], in1=mu[:C, :], op=ALU.mult)
            k.dve.tensor_tensor(out=c[:C, :], in0=c[:C, :], in1=p[:C, :], op=ALU.add)
            xm = c
            k.pool.tensor_copy(out=vb16[:C, :], in_=c[:C, 1024:1536])
            r_ = xm[:C, 0:512]; k_ = xm[:C, 512:1024]; v_ = xm[:C, 1024:1536]
            k.act.activation(out=Lt[:C, 0:64], in_=xm[:C, 1536:1600], func=AF.Tanh)
            k.act.activation(out=Lt[:C, 128:256], in_=xm[:C, 1664:1792], func=AF.Sigmoid)
            k.dve.tensor_copy(out=Lt[:C, 64:128], in_=xm[:C, 1600:1664])
            pb = bank()
            pl = pb[:, 0:192].rearrange("p (a t) -> p a t", a=3)
            k.pe.transpose(out=pl[0:64, 0, :C], in_=Lt[:C, 0:64], identity=identf[:C, :C])
            k.pe.transpose(out=pl[0:64, 1, :C], in_=Lt[:C, 64:128], identity=identf[:C, :C])
            k.pe.transpose(out=pl[:, 2, :C], in_=Lt[:C, 128:256], identity=identf[:C, :C])
            k.dve.tensor_copy(out=LT[0:64, 0:2, :C], in_=pl[0:64, 0:2, :C])
            k.dve.tensor_copy(out=LT[:, 2, :C], in_=pl[:, 2, :C])
            pW = bank(); pA = bank(); pG = bank()
            k.pe.matmul(out=pW[:C, :], lhsT=LT[0:64, 0, :C], rhs=wup[:, :], start=True, stop=True)
            k.pe.matmul(out=pA[:C, :], lhsT=LT[0:64, 1, :C], rhs=aup[:, :], start=True, stop=True)
            k.pe.matmul(out=pG[:C, :], lhsT=LT[:, 2, :C], rhs=gup[:, :], start=True, stop=True)
            k.dve.tensor_tensor(out=lw[:C, :], in0=pW[:C, :], in1=w0b[:C, :], op=ALU.add)
            k.act.activation(out=lw[:C, :], in_=lw[:C, :], func=AF.Sigmoid)
            k.dve.tensor_scalar(out=lw[:C, :], in0=lw[:C, :], scalar1=-0.6065306597126334, scalar2=None, op0=ALU.mult)
            k.dve.tensor_tensor(out=av[:C, :], in0=pA[:C, :], in1=a0b[:C, :], op=ALU.add)
            k.act.activation(out=av[:C, :], in_=av[:C, :], func=AF.Sigmoid)
            k.act.copy(out=gv[:C, :], in_=pG[:C, :])
            k.pool.tensor_tensor(out=kk[:C, :], in0=k_, in1=kkb[:C, :], op=ALU.mult)
            k.pool.tensor_tensor(out=sq[:C, :], in0=kk[:C, :], in1=kk[:C, :], op=ALU.mult)
            k.dve.reduce_sum(out=sm[:C, :, 0], in_=hv(sq, C), axis=AX.X)
            k.act.activation(out=sm[:C, :, 1], in_=sm[:C, :, 0], func=AF.Sqrt)
            k.dve.tensor_scalar(out=sm[:C, :, 1], in0=sm[:C, :, 1], scalar1=1e-12, scalar2=None, op0=ALU.max)
            k.dve.reciprocal(out=sm[:C, :, 2], in_=sm[:C, :, 1])
            if r0 == 0:
                k.dbg('kk0', kk[:C, :], [64, 512]); k.dbg('sq', sq[:C, :], [64, 512]); k.dbg('sm', sm[:C, :, :], [64, 8, 4])
            k.dve.tensor_tensor(out=hv(kk, C), in0=hv(kk, C), in1=sm[:C, :, 2:3].bc([C, 8, 64]), op=ALU.mult)
            k.dve.scalar_tensor_tensor(out=k2[:C, :], in0=av[:C, :], scalar=-1.0, in1=kab[:C, :], op0=ALU.add, op1=ALU.mult)
            k.dve.scalar_tensor_tensor(out=k2[:C, :], in0=k2[:C, :], scalar=1.0, in1=k_, op0=ALU.add, op1=ALU.mult)
            k.pool.tensor_tensor(out=bv[:C, :], in0=kk[:C, :], in1=av[:C, :], op=ALU.mult)
            pC = bank()
            k.pe.matmul(out=pC[:C, :], lhsT=mk[:C, 0, :C], rhs=lw[:C, :], start=True, stop=True)
            k.act.activation(out=eP[:C, :], in_=pC[:C, :], func=AF.Exp)
            k.act.activation(out=eN[:C, :], in_=pC[:C, :], func=AF.Exp, scale=-1.0)
            k.dve.tensor_tensor(out=ePm[:C, :], in0=pC[:C, :], in1=lw[:C, :], op=ALU.subtract)
            k.act.activation(out=ePm[:C, :], in_=ePm[:C, :], func=AF.Exp)
            k.dve.tensor_tensor(out=F[:C, 0, :], in0=kk[:C, :], in1=ePm[:C, :], op=ALU.mult)
            k.pool.tensor_tensor(out=F[:C, 1, :], in0=bv[:C, :], in1=eN[:C, :], op=ALU.mult)
            k.dve.tensor_tensor(out=F[:C, 2, :], in0=k2[:C, :], in1=eN[:C, :], op=ALU.mult)
            k.pool.tensor_tensor(out=F[:C, 3, :], in0=r_, in1=eP[:C, :], op=ALU.mult)
            pg = bank()
            for h in range(8):
                k.pe.matmul(out=pg[0:64, h:h + 1], lhsT=lw[:C, h * 64:(h + 1) * 64], rhs=ones[:C, 0:1], start=True, stop=True)
            k.act.activation(out=GC[:, :], in_=pg[0:64, 0:8], func=AF.Exp)
            for kind in range(4):
                pfv = pfbs[kind % 2]
                for h in range(8):
                    k.pe.transpose(out=pfv[:, h, :C], in_=F[:C, kind, h * 64:(h + 1) * 64], identity=identb[:C, :C])
                k.dve.tensor_copy(out=FT[kind][:, :, :C], in_=pfv[:, :, :C])
            aT, bT, khT, rT = FT
            combos = [(bT, aT, 1), (aT, bT, 2), (khT, aT, 1), (bT, rT, 0), (khT, rT, 0)]
            for i, (lt, rt, mi) in enumerate(combos):
                pm = bank()
                pmv = pm[0:64, :].rearrange("p (h t) -> p h t", h=8)
                for h in range(8):
                    k.pe.matmul(out=pmv[:C, h, :C], lhsT=lt[:, h, :C], rhs=rt[:, h, :C], start=True, stop=True)
                (k.dve if i % 2 == 0 else k.pool).tensor_tensor(out=Mb[i][:C, :, :C], in0=pmv[:C, :, :C], in1=mk[:C, mi:mi + 1, :C].bc([C, 8, C]), op=ALU.mult) if i % 2 == 0 else k.dve.tensor_tensor(out=Mb[i][:C, :, :C], in0=pmv[:C, :, :C], in1=mk[:C, mi:mi + 1, :C].bc([C, 8, C]), op=ALU.mult)
            A, AT, Mka, Mbr, Mkr = Mb
            k.dve.tensor_tensor(out=Nt[:C, :, :C], in0=identf[:C, 0:C].unsq(1).bc([C, 8, C]), in1=A[:C, :, :C], op=ALU.subtract)
            Bc, BTc = A, AT
            nlev = {64: 5, 4: 1}[C]
            for lev in range(nlev):
                Bn = Bt[lev % 2]; BTn = BTt[lev % 2]
                p1 = bank(); p2 = bank()
                p1v = p1[0:64, :].rearrange("p (h t) -> p h t", h=8); p2v = p2[0:64, :].rearrange("p (h t) -> p h t", h=8)
                for h in range(8):
                    k.pe.matmul(out=p1v[:C, h, :C], lhsT=BTc[:C, h, :C], rhs=Bc[:C, h, :C], start=True, stop=True)
                for h in range(8):
                    k.pe.matmul(out=p2v[:C, h, :C], lhsT=Bc[:C, h, :C], rhs=BTc[:C, h, :C], start=True, stop=True)
                k.dve.tensor_copy(out=Bn[:C, :, :C], in_=p1v[:C, :, :C])
                k.dve.tensor_copy(out=BTn[:C, :, :C], in_=p2v[:C, :, :C])
                p3 = bank(); p3v = p3[0:64, :].rearrange("p (h t) -> p h t", h=8)
                for h in range(8):
                    k.pe.matmul(out=p3v[:C, h, :C], lhsT=BTn[:C, h, :C], rhs=Nt[:C, h, :C], start=True, stop=True)
                k.dve.tensor_tensor(out=Nt[:C, :, :C], in0=Nt[:C, :, :C], in1=p3v[:C, :, :C], op=ALU.add)
                Bc, BTc = Bn, BTn
            def s2():
                vh = lambda h: vb16[:C, h * 64:(h + 1) * 64]
                pz = bank(); pzv = pz[0:64, :].rearrange("p (h t) -> p h t", h=8)
                for h in range(8):
                    k.pe.matmul(out=pzv[:C, h, :], lhsT=aT[:, h, :C], rhs=STb[:, h, :], start=True, stop=False)
                    k.pe.matmul(out=pzv[:C, h, :], lhsT=Mka[:C, h, :C], rhs=vh(h), start=False, stop=True)
                k.dve.tensor_scalar(out=Zn[:C, :, :], in0=pzv[:C, :, :], scalar1=-1.0, scalar2=None, op0=ALU.mult)
                pu = bank(); puv = pu[0:64, :].rearrange("p (h t) -> p h t", h=8)
                for h in range(8):
                    k.pe.matmul(out=puv[:C, h, :], lhsT=Nt[:C, h, :C], rhs=Zn[:C, h, :], start=True, stop=True)
                k.dve.tensor_copy(out=UT[:C, :, :], in_=puv[:C, :, :])
                py = bank(); pyv = py[0:64, :].rearrange("p (h t) -> p h t", h=8)
                for h in range(8):
                    k.pe.matmul(out=pyv[:C, h, :], lhsT=rT[:, h, :C], rhs=STb[:, h, :], start=True, stop=False)
                    k.pe.matmul(out=pyv[:C, h, :], lhsT=Mbr[:C, h, :C], rhs=UT[:C, h, :], start=False, stop=False)
                    k.pe.matmul(out=pyv[:C, h, :], lhsT=Mkr[:C, h, :C], rhs=vh(h), start=False, stop=True)
                k.act.copy(out=yv[:C, :], in_=py[:C, :])
                pS = bank(); pSv = pS[0:64, :].rearrange("p (h t) -> p h t", h=8)
                for h in range(8):
                    k.pe.matmul(out=pSv[:, h, :], lhsT=F[:C, 1, h * 64:(h + 1) * 64], rhs=UT[:C, h, :], start=True, stop=False)
                    k.pe.matmul(out=pSv[:, h, :], lhsT=F[:C, 2, h * 64:(h + 1) * 64], rhs=vh(h), start=False, stop=True)
                k.dve.tensor_tensor(out=ST[:, :, :], in0=ST[:, :, :], in1=pSv[:, :, :], op=ALU.add)
                k.dve.tensor_tensor(out=ST[:, :, :], in0=ST[:, :, :], in1=GC[:, :].unsq(2).bc([64, 8, 64]), op=ALU.mult)
                k.pool.tensor_copy(out=STb[:, :, :], in_=ST[:, :, :])
                k.dve.reduce_sum(out=sm2[:C, :, 0], in_=hv(yv, C), axis=AX.X)
                k.dve.tensor_scalar(out=sm2[:C, :, 0], in0=sm2[:C, :, 0], scalar1=1.0 / 64, scalar2=None, op0=ALU.mult)
                k.dve.tensor_tensor(out=hv(yc, C), in0=hv(yv, C), in1=sm2[:C, :, 0:1].bc([C, 8, 64]), op=ALU.subtract)
                k.pool.tensor_tensor(out=t1[:C, :], in0=yc[:C, :], in1=yc[:C, :], op=ALU.mult)
                k.dve.reduce_sum(out=sm2[:C, :, 1], in_=hv(t1, C), axis=AX.X)
                k.dve.tensor_scalar(out=sm2[:C, :, 1], in0=sm2[:C, :, 1], scalar1=1.0 / 64, scalar2=64e-5, op0=ALU.mult, op1=ALU.add)
                k.act.activation(out=sm2[:C, :, 1], in_=sm2[:C, :, 1], func=AF.Sqrt)
                k.dve.reciprocal(out=sm2[:C, :, 2], in_=sm2[:C, :, 1])
                k.dve.tensor_tensor(out=hv(yc, C), in0=hv(yc, C), in1=sm2[:C, :, 2:3].bc([C, 8, 64]), op=ALU.mult)
                k.dve.tensor_tensor(out=yc[:C, :], in0=yc[:C, :], in1=lgb[:C, :], op=ALU.mult)
                k.dve.tensor_tensor(out=yc[:C, :], in0=yc[:C, :], in1=lbb[:C, :], op=ALU.add)
                k.pool.tensor_tensor(out=t1[:C, :], in0=r_, in1=k2[:C, :], op=ALU.mult)
                k.pool.tensor_tensor(out=t1[:C, :], in0=t1[:C, :], in1=rkb[:C, :], op=ALU.mult)
                k.dve.reduce_sum(out=sm2[:C, :, 3], in_=hv(t1, C), axis=AX.X)
                k.dve.tensor_tensor(out=hv(t1, C), in0=xm[:C, 1024:1536].rearrange("p (h d) -> p h d", h=8), in1=sm2[:C, :, 3:4].bc([C, 8, 64]), op=ALU.mult)
                k.dve.tensor_tensor(out=yc[:C, :], in0=yc[:C, :], in1=t1[:C, :], op=ALU.add)
                k.dve.tensor_tensor(out=ob[:C, :], in0=yc[:C, :], in1=gv[:C, :], op=ALU.mult)
                k.dma(AO[r0:r0 + C, 0:512], ob[:C, :])
                if r0 == 0:
                    k.dbg('xm', xm[:C, :], [64, RWC]); k.dbg('lw', lw[:C, :], [64, 512]); k.dbg('av', av[:C, :], [64, 512])
                    k.dbg('kk', kk[:C, :], [64, 512]); k.dbg('k2', k2[:C, :], [64, 512]); k.dbg('F', F[:C, :, :], [64, 4, 512])
                    k.dbg('aT', FT[0][:, :, :], [64, 8, 64]); k.dbg('A', Mb[0][:, :, :], [64, 8, 64]); k.dbg('AT', Mb[1][:, :, :], [64, 8, 64])
                    k.dbg('N', Nt[:, :, :], [64, 8, 64]); k.dbg('UT', UT[:, :, :], [64, 8, 64]); k.dbg('yv', yv[:C, :], [64, 512])
                    k.dbg('ST', ST[:, :, :], [64, 8, 64]); k.dbg('GC', GC[:, :], [64, 8]); k.dbg('gv', gv[:C, :], [64, 512])
                    k.dbg('ob', ob[:C, :], [64, 512], BF16)
            return s2

        def store_state(dst):
            pt = bank(); ptv = pt[0:64, :].rearrange("p (h t) -> p h t", h=8)
            for h in range(8):
                k.pe.transpose(out=ptv[:, h, :], in_=ST[:, h, :], identity=identf[0:64, 0:64])
            k.dve.tensor_copy(out=Stmp[:, :, :], in_=ptv[:, :, :])
            k.dma(dst.rearrange("h i j -> i h j"), Stmp[:, :, :])

        k.op('dve', lambda e: e.memset(ST[:, :, :].ap, 0.0), (), (ST,))
        k.op('dve', lambda e: e.memset(STb[:, :, :].ap, 0.0), (), (STb,))
        pend = None
        for ch in range(Tn // 64):
            nxt = chunk(ch * 64, 64, 'zero' if ch == 0 else None)
            if pend is not None:
                pend()
            pend = nxt
        pend()
        store_state(L['o_wkv_p'][:, :, :])
        for bl in range(4):
            k.dma(Stmp[:, :, :], st_wkv[bl].rearrange("h i j -> i h j"))
            pt = bank(); ptv = pt[0:64, :].rearrange("p (h t) -> p h t", h=8)
            for h in range(8):
                k.pe.transpose(out=ptv[:, h, :], in_=Stmp[:, h, :], identity=identf[0:64, 0:64])
            k.dve.tensor_copy(out=ST[:, :, :], in_=ptv[:, :, :])
            k.dve.tensor_copy(out=STb[:, :, :], in_=ptv[:, :, :])
            chunk(Tn + bl * 4, 4, st_shift[bl:bl + 1, :])()
            store_state(L['o_wkv_s'][bl])


def pipe(items, sc, pv):
    items = list(items)
    if not items:
        return
    nxt = sc(items[0])
    for i, it in enumerate(items):
        cur_e = nxt
        if i + 1 < len(items):
            nxt = sc(items[i + 1])
        pv(it, cur_e)


def gelu_tanh(k, out_bf, x, tmpa, tmpb, shape_idx):
    k.dve.tensor_tensor(out=tmpa, in0=x, in1=x, op=ALU.mult)
    k.dve.tensor_scalar(out=tmpa, in0=tmpa, scalar1=0.044715, scalar2=1.0, op0=ALU.mult, op1=ALU.add)
    k.dve.tensor_tensor(out=tmpa, in0=tmpa, in1=x, op=ALU.mult)
    k.act.activation(out=tmpb, in_=tmpa, func=AF.Tanh, scale=0.7978845608028654)
    k.dve.tensor_scalar(out=tmpb, in0=tmpb, scalar1=1.0, scalar2=0.5, op0=ALU.add, op1=ALU.mult)
    k.dve.tensor_tensor(out=out_bf, in0=tmpb, in1=x, op=ALU.mult)


def load_cmp_weights(k, L, pbias):
    cmp_w1 = L['cmp_w1']; cmp_pe = L['cmp_pe']; cmp_w2 = L['cmp_w2']
    W = {}
    tiles_ = [(k.sb('cw1_%d' % kind, [64, 32, 64], BF16), k.sb('cw2_%d' % kind, [64, 64], BF16),
               k.sb('peT%d' % kind, [64, 32], BF16), k.sb('cbias%d' % kind, [64, 1])) for kind in range(2)]
    sc_ = k.scope(); sc_.__enter__()
    stg = k.sb('cw_stg', [64, 32, 64]); stg2 = k.sb('cw_stg2', [64, 64]); pes = k.sb('pe_stg', [64, 32])
    for kind in range(2):
        w1, w2, peT, bias = tiles_[kind]
        k.dma(stg[:, :, :], cmp_w1[kind].rearrange("c d e -> d c e"))
        k.dve.tensor_copy(out=w1[:, :, :], in_=stg[:, :, :])
        k.dma(stg2[:, :], cmp_w2[kind])
        k.dve.tensor_copy(out=w2[:, :], in_=stg2[:, :])
        k.dma(pes[:, :], cmp_pe[kind].rearrange("c d -> d c"), allow_slow_non_contiguous=True)
        k.dve.tensor_copy(out=peT[:, :], in_=pes[:, :])
        for c in range(32):
            k.pe.matmul(out=pbias[0:64, kind:kind + 1], lhsT=w1[:, c, :], rhs=peT[:, c:c + 1], start=(c == 0), stop=(c == 31))
        k.dve.tensor_copy(out=bias[:, :], in_=pbias[0:64, kind:kind + 1])
        W[kind] = (w1, w2, bias)
    sc_.__exit__(None, None, None)
    return W


def compress(k, W, kind, srcT, NBn, ps_bank, hx, ta, tb, hbf):
    w1, w2, bias = W[kind]
    for c in range(32):
        k.pe.matmul(out=ps_bank[0:64, 0:NBn], lhsT=w1[:, c, :], rhs=srcT[:, c:c + 16 * (NBn - 1) + 1:16], start=(c == 0), stop=(c == 31))
    k.act.activation(out=hx[:, 0:NBn], in_=ps_bank[0:64, 0:NBn], func=AF.Identity, bias=bias[:, 0:1])
    gelu_tanh(k, hbf[:, 0:NBn], hx[:, 0:NBn], ta[:, 0:NBn], tb[:, 0:NBn], None)


def phase_nsa_prompt(k, L):
    Tn = L['Tn']; NT = L['NT']; NSEL = L['NSEL']; NB = L['NB']; NBT = L['NBT']
    QT = L['QT']; KT = L['KT']; VcT = L['VcT']; VT = L['VT']; GT = L['GT']; AO = L['AO']; identb = L['identb']
    with k.scope():
        ps_s = [k.ps('ps_s', [128, 512]) for _ in range(3)]
        W = load_cmp_weights(k, L, ps_s[0])
        cm = k.sb('cm', [128, 17, 128], BF16); k.dma(cm[:, :, :], L['CMt'][:, :, :])
        tri = k.sb('tri', [128, 2, 128], BF16); k.dma(tri[:, :, :], L['TRIt'][:, :, :])
        po_sel = [k.ps('po_sel', [128, 4, 65]) for _ in range(1)]
        po_win = [k.ps('po_win', [128, 4, 65]) for _ in range(1)]
        po_c = [k.ps('po_c', [128, 65 + NSEL]) for _ in range(2)]
        ps_t = k.ps('ps_t', [128, 128], BF16)
        si = [0]

        def sbank():
            b = ps_s[si[0] % 3]; si[0] += 1
            return b

        KcT = k.sb('KcT', [64, Tn], BF16); VcTs = k.sb('VcTs', [64, Tn], BF16)
        KA = NSEL + 64
        KsT = k.sb('KsT', [KA, Tn], BF16); KwT = k.sb('KwT', [64, Tn], BF16)
        k.dma(KsT[0:NSEL, :], L['EEXP'][0:NSEL, 0:Tn])
        qsel = [k.sb('qsel', [KA, 4, 128], BF16) for _ in range(2)]
        Vs = k.sb('Vs', [128, NT, 65], BF16); Vw = k.sb('Vw', [128, NT, 65], BF16)
        KcC = k.sb('KcC', [64, NBT * 128], BF16)
        VcC = k.sb('VcC', [128, NBT, 65 + NSEL], BF16)
        hx = k.sb('hx', [64, 512]); ta = k.sb('ta', [64, 512]); tb = k.sb('tb', [64, 512])
        hk = k.sb('hk', [64, 512], BF16); hv_ = k.sb('hv_', [64, 512], BF16)
        q4 = [k.sb('q4', [64, 4, 128], BF16) for _ in range(2)]
        gtile = [k.sb('gtile', [128, 24]) for _ in range(2)]
        fbt = [k.sb('fbt', [128, NSEL]) for _ in range(2)]
        ebuf = [k.sb('ebuf', [128, 512], BF16) for _ in range(3)]
        ei = [0]

        def enext():
            b = ebuf[ei[0] % 3]; ei[0] += 1
            return b

        acc = k.sb('acc', [128, 4, 64]); accb = k.sb('accb', [128, 4, 64], BF16); tmp4 = k.sb('tmp4', [128, 4, 64])
        rr = k.sb('rr', [128, 16]); imp = k.sb('imp', [128, NSEL]); score = k.sb('score', [128, NSEL]); sc2 = k.sb('sc2', [128, NSEL])
        m8 = k.sb('m8', [128, 16]); negs = k.sb('negs', [128, NSEL], BF16)
        negT4 = k.sb('negT4', [NSEL, 4, 128], BF16)
        for g in range(2):
            k.dma(KcT[:, :], KT[:, 0 + g, 0:Tn]); k.dma(VcTs[:, :], VcT[:, g, 0:Tn])
            k.dma(KsT[NSEL:KA, :], KT[:, 2 + g, 0:Tn]); k.dma(KwT[:, :], KT[:, 4 + g, 0:Tn])
            k.dma(Vs[:, :, :], VT[0:Tn, 0, g, :].rearrange("(kt p) c -> p kt c", p=128))
            k.dma(Vw[:, :, :], VT[0:Tn, 1, g, :].rearrange("(kt p) c -> p kt c", p=128))
            k.dma(VcC[:, :, 65:65 + NSEL], L['OVt'][:, :, :])
            k.op('dve', lambda e: e.memset(VcC[:, :, 64:65].ap, 1.0), (), (VcC,))
            pb = sbank()
            compress(k, W, 0, KcT, NB, pb, hx, ta, tb, hk)
            pb2 = sbank()
            k.pe.matmul(out=pb2[0:64, 0:NB], lhsT=W[0][1][:, :], rhs=hk[:, 0:NB], start=True, stop=True)
            k.dve.tensor_copy(out=KcC[:, 0:NB], in_=pb2[0:64, 0:NB])
            pb = sbank()
            compress(k, W, 1, VcTs, NB, pb, hx, ta, tb, hv_)
            for ni in range(NBT):
                nn = min(128, NB - ni * 128)
                pb3 = sbank()
                k.pe.matmul(out=pb3[:nn, 0:64], lhsT=hv_[:, ni * 128:ni * 128 + nn], rhs=W[1][1][:, :], start=True, stop=True)
                k.dve.tensor_copy(out=VcC[:nn, ni, 0:64], in_=pb3[:nn, 0:64])
            def loads(tt):
                k.dma(q4[tt % 2][:, :, :], QT[:, g * 4:(g + 1) * 4, tt * 128:tt * 128 + 128])
                k.dma(qsel[tt % 2][NSEL:KA, :, :], QT[:, g * 4:(g + 1) * 4, tt * 128:tt * 128 + 128])
                k.dma(gtile[tt % 2][:, :], GT[tt * 128:tt * 128 + 128, :])
                k.dma(fbt[tt % 2][:, :], L['FBt'][tt * 128:tt * 128 + 128, :])

            loads(0)
            for tt in range(NT):
                t0 = tt * 128
                q = q4[tt % 2]; gt_ = gtile[tt % 2]; fb = fbt[tt % 2]
                if tt + 1 < NT:
                    loads(tt + 1)
                gv3 = gt_[:, :].rearrange("p (h b) -> p h b", b=3)
                nvalid = min(NB, (t0 + 96) // 16 + 1)
                nnt = (nvalid + 127) // 128
                for r in range(4):
                    po = po_c[r % 2]
                    for ni in range(nnt):
                        nn = min(128, NB - ni * 128)
                        pss = sbank()
                        k.pe.matmul(out=pss[:nn, 0:128], lhsT=KcC[:, ni * 128:ni * 128 + nn], rhs=q[:, r, :], start=True, stop=True)
                        e = enext()
                        k.act.activation(out=e[:nn, 0:128], in_=pss[:nn, 0:128], func=AF.Exp, scale=0.125)
                        delta = t0 - 2048 * ni
                        if delta < 17 * 128:
                            assert delta >= 0
                            k.pool.tensor_tensor(out=e[:nn, 0:128], in0=e[:nn, 0:128], in1=cm[:nn, delta // 128, :], op=ALU.mult)
                        k.pe.matmul(out=po[:, :], lhsT=e[:nn, 0:128], rhs=VcC[:nn, ni, :], start=(ni == 0), stop=(ni == nnt - 1))
                    k.dve.tensor_scalar(out=rr[:, r:r + 1], in0=po[:, 64:65], scalar1=1e-30, scalar2=None, op0=ALU.max)
                    k.dve.reciprocal(out=rr[:, 4 + r:5 + r], in_=rr[:, r:r + 1])
                    k.dve.tensor_tensor(out=rr[:, 8 + r:9 + r], in0=rr[:, 4 + r:5 + r], in1=gv3[:, g * 4 + r, 0:1], op=ALU.mult)
                    k.dve.tensor_scalar(out=acc[:, r, :], in0=po[:, 0:64], scalar1=rr[:, 8 + r:9 + r], scalar2=None, op0=ALU.mult)
                    if r == 0:
                        k.dve.tensor_scalar(out=imp[:, :], in0=po[:, 65:65 + NSEL], scalar1=rr[:, 4 + r:5 + r], scalar2=None, op0=ALU.mult)
                    else:
                        k.dve.scalar_tensor_tensor(out=imp[:, :], in0=po[:, 65:65 + NSEL], scalar=rr[:, 4 + r:5 + r], in1=imp[:, :], op0=ALU.mult, op1=ALU.add)
                k.dve.tensor_tensor(out=score[:, :], in0=imp[:, :], in1=fb[:, :], op=ALU.add)
                k.dve.max(out=m8[:, 0:8], in_=score[:, :])
                k.dve.match_replace(out=sc2[:, :], in_to_replace=m8[:, 0:8], in_values=score[:, :], imm_value=-3.0e38)
                k.dve.max(out=m8[:, 8:16], in_=sc2[:, :])
                k.dve.tensor_scalar(out=rr[:, 12:13], in0=m8[:, 15:16], scalar1=-1.0e8, scalar2=None, op0=ALU.max)
                k.dve.tensor_scalar(out=negs[:, :], in0=score[:, :], scalar1=rr[:, 12:13], scalar2=NEG, op0=ALU.is_lt, op1=ALU.mult)
                k.pe.transpose(out=ps_t[0:NSEL, :], in_=negs[:, :], identity=identb[:, :])
                qs_ = qsel[tt % 2]
                k.dve.tensor_copy(out=qs_[0:NSEL, :, :], in_=ps_t[0:NSEL, :].unsq(1).bc([NSEL, 4, 128]))
                qsf = qs_[:, :, :].rearrange("p r t -> p (r t)")
                qf = q[:, :, :].rearrange("p r t -> p (r t)")
                po = po_sel[0]

                def sc_sel(kt):
                    pss = sbank()
                    k.pe.matmul(out=pss[:, :], lhsT=KsT[:, kt * 128:(kt + 1) * 128], rhs=qsf, start=True, stop=True)
                    e = enext()
                    k.act.activation(out=e[:, :], in_=pss[:, :], func=AF.Exp, scale=0.125)
                    if kt == tt:
                        ev = e[:, :].rearrange("p (r t) -> p r t", r=4)
                        k.pool.tensor_tensor(out=ev, in0=ev, in1=tri[:, 0:1, :].bc([128, 4, 128]), op=ALU.mult)
                    return e

                def pv_sel(kt, e):
                    for r in range(4):
                        k.pe.matmul(out=po[:, r, :], lhsT=e[:, r * 128:(r + 1) * 128], rhs=Vs[:, kt, :], start=(kt == 0 and r == 0), stop=(kt == tt), skip_group_check=True)

                pipe(range(tt + 1), sc_sel, pv_sel)
                k.dve.tensor_scalar(out=rr[:, 0:4], in0=po[:, :, 64], scalar1=1e-30, scalar2=None, op0=ALU.max)
                k.dve.reciprocal(out=rr[:, 4:8], in_=rr[:, 0:4])
                k.dve.tensor_tensor(out=rr[:, 8:12], in0=rr[:, 4:8], in1=gv3[:, g * 4:(g + 1) * 4, 1], op=ALU.mult)
                k.dve.tensor_tensor(out=tmp4[:, :, :], in0=po[:, :, 0:64], in1=rr[:, 8:12].unsq(2).bc([128, 4, 64]), op=ALU.mult)
                k.dve.tensor_tensor(out=acc[:, :, :], in0=acc[:, :, :], in1=tmp4[:, :, :], op=ALU.add)
                po = po_win[0]
                k0 = max(0, tt - 4)
                pow_ = po

                def sc_win(kt):
                    pss = sbank()
                    k.pe.matmul(out=pss[:, :], lhsT=KwT[:, kt * 128:(kt + 1) * 128], rhs=qf, start=True, stop=True)
                    e = enext()
                    k.act.activation(out=e[:, :], in_=pss[:, :], func=AF.Exp, scale=0.125)
                    ev = e[:, :].rearrange("p (r t) -> p r t", r=4)
                    if kt == tt:
                        k.pool.tensor_tensor(out=ev, in0=ev, in1=tri[:, 0:1, :].bc([128, 4, 128]), op=ALU.mult)
                    elif kt == tt - 4:
                        k.pool.tensor_tensor(out=ev, in0=ev, in1=tri[:, 1:2, :].bc([128, 4, 128]), op=ALU.mult)
                    return e

                def pv_win(kt, e):
                    for r in range(4):
                        k.pe.matmul(out=pow_[:, r, :], lhsT=e[:, r * 128:(r + 1) * 128], rhs=Vw[:, kt, :], start=(kt == k0 and r == 0), stop=(kt == tt), skip_group_check=True)

                pipe(range(k0, tt + 1), sc_win, pv_win)
                k.dve.tensor_scalar(out=rr[:, 0:4], in0=po[:, :, 64], scalar1=1e-30, scalar2=None, op0=ALU.max)
                k.dve.reciprocal(out=rr[:, 4:8], in_=rr[:, 0:4])
                k.dve.tensor_tensor(out=rr[:, 8:12], in0=rr[:, 4:8], in1=gv3[:, g * 4:(g + 1) * 4, 2], op=ALU.mult)
                k.dve.tensor_tensor(out=tmp4[:, :, :], in0=po[:, :, 0:64], in1=rr[:, 8:12].unsq(2).bc([128, 4, 64]), op=ALU.mult)
                k.dve.tensor_tensor(out=accb[:, :, :], in0=acc[:, :, :], in1=tmp4[:, :, :], op=ALU.add)
                k.dma(AO[t0:t0 + 128, 512 + g * 256:512 + (g + 1) * 256], accb[:, :, :].rearrange("p r d -> p (r d)"))


def layer_tail(k, L, layer, mix, w_out, h_in, h_out, y_out):
    Tn = L['Tn']; NT = L['NT']; identb = L['identb']; H1 = L['H1']; ACTT = L['ACTT']
    rmsnorm_T = L['rmsnorm_T']; load_w_bf16 = L['load_w_bf16']
    xp = L['xp']; xs = L['xs']
    groups = [(i * 512, min(512, Tn - i * 512)) for i in range((Tn + 511) // 512)] + [(Tn, 16)]
    with k.scope():
        Wo = k.sb('Wo', [128, 8, 1024], BF16); load_w_bf16(Wo, w_out, 1024, 1024)
        Wg = k.sb('Wg', [128, 8, DFF], BF16); load_w_bf16(Wg, L['ffn_g'][layer], 1024, DFF)
        Wu = k.sb('Wu', [128, 8, DFF], BF16); load_w_bf16(Wu, L['ffn_u'][layer], 1024, DFF)
        gbc = k.sb('gbc', [128, 1024]); k.dma(gbc[:, :], L['norm_ffn'][layer:layer + 1, :].pbc(128))
        mt = k.sb('mt', [128, 1024], BF16); mT = k.sb('mT', [128, 8, 128], BF16)
        ht = [k.sb('ht', [128, 1024]) for _ in range(2)]
        junk = k.sb('junk', [128, 1024], BF16); xn = k.sb('xn', [128, 1024], BF16); ss = k.sb('ss', [128, 4])
        xnT = k.sb('xnT', [128, 8, 512], BF16); xnT1 = k.sb('xnT1', [128, 8, 128], BF16)
        sg = k.sb('sg', [128, 512], BF16); at = [k.sb('at', [128, 512], BF16) for _ in range(2)]
        pT = k.ps('pT', [128, 8, 128], BF16)
        pA = [k.ps('pA', [128, 512]) for _ in range(2)]
        pG = [k.ps('pG', [128, 512]) for _ in range(2)]; pU = [k.ps('pU', [128, 512]) for _ in range(2)]
        ci = 0
        for (g0, gn) in groups:
            ntile = (gn + 127) // 128
            for j in range(ntile):
                r0 = g0 + j * 128; rows = min(128, gn - j * 128)
                h = ht[ci % 2]; ci += 1
                k.dma(mt[:rows, :], mix[r0:r0 + rows, :])
                if h_in is None:
                    k.dma(h[:rows, :], xp[r0:r0 + rows, :] if r0 < Tn else xs[0:16, :])
                else:
                    k.dma(h[:rows, :], h_in[r0:r0 + rows, :])
                for kk in range(8):
                    k.pe.transpose(out=pT[:, kk, :rows], in_=mt[:rows, kk * 128:(kk + 1) * 128], identity=identb[:rows, :rows])
                k.act.copy(out=mT[:, :, :rows], in_=pT[:, :, :rows])
                for c in range(2):
                    ps = pA[c]
                    for kk in range(8):
                        k.pe.matmul(out=ps[:rows, :], lhsT=mT[:, kk, :rows], rhs=Wo[:, kk, c * 512:(c + 1) * 512], start=(kk == 0), stop=(kk == 7))
                    k.dve.tensor_tensor(out=h[:rows, c * 512:(c + 1) * 512], in0=h[:rows, c * 512:(c + 1) * 512], in1=ps[:rows, :], op=ALU.add)
                k.dma(H1[r0:r0 + rows, :], h[:rows, :])
                rmsnorm_T(h, rows, gbc, xn, pT, xnT1, ss, junk)
                k.dve.tensor_copy(out=xnT[:, :, j * 128:j * 128 + rows], in_=xnT1[:, :, :rows])
            for hc in range(22):
                pg = pG[hc % 2]; pu = pU[hc % 2]
                for kk in range(8):
                    k.pe.matmul(out=pg[:, :gn], lhsT=Wg[:, kk, hc * 128:(hc + 1) * 128], rhs=xnT[:, kk, :gn], start=(kk == 0), stop=(kk == 7))
                for kk in range(8):
                    k.pe.matmul(out=pu[:, :gn], lhsT=Wu[:, kk, hc * 128:(hc + 1) * 128], rhs=xnT[:, kk, :gn], start=(kk == 0), stop=(kk == 7))
                a = at[hc % 2]
                k.act.activation(out=sg[:, :gn], in_=pg[:, :gn], func=AF.Silu)
                k.dve.tensor_tensor(out=a[:, :gn], in0=sg[:, :gn], in1=pu[:, :gn], op=ALU.mult)
                k.dma(ACTT[hc, :, g0:g0 + gn], a[:, :gn])
    with k.scope():
        Wd = k.sb('Wd', [128, 22, 1024], BF16); load_w_bf16(Wd, L['ffn_d'][layer], DFF, 1024)
        gbc = k.sb('gbc', [128, 1024])
        if y_out is not None:
            k.dma(gbc[:, :], L['norm_final'][0:1, :].pbc(128))
        aT = [k.sb('aT', [128, 22, 128], BF16) for _ in range(2)]
        ht = [k.sb('ht', [128, 1024]) for _ in range(2)]
        junk = k.sb('junk', [128, 1024]); ss = k.sb('ss', [128, 4]); yt = k.sb('yt', [128, 1024])
        pA = [k.ps('pA', [128, 512]) for _ in range(4)]
        ci = 0
        for ti, (r0, rows) in enumerate(L['tiles']):
            a = aT[ti % 2]; h = ht[ti % 2]
            k.dma(a[:, :, :rows], ACTT[:, :, r0:r0 + rows].rearrange("c p t -> p c t"))
            k.dma(h[:rows, :], H1[r0:r0 + rows, :])
            for c in range(2):
                ps = pA[ci % 4]; ci += 1
                for hc in range(22):
                    k.pe.matmul(out=ps[:rows, :], lhsT=a[:, hc, :rows], rhs=Wd[:, hc, c * 512:(c + 1) * 512], start=(hc == 0), stop=(hc == 21))
                k.dve.tensor_tensor(out=h[:rows, c * 512:(c + 1) * 512], in0=h[:rows, c * 512:(c + 1) * 512], in1=ps[:rows, :], op=ALU.add)
            if y_out is None:
                k.dma(H1[r0:r0 + rows, :], h[:rows, :])
            else:
                k.act.activation(out=junk[:rows, :], in_=h[:rows, :], func=AF.Square, accum_out=ss[:rows, 0:1])
                k.dve.tensor_scalar(out=ss[:rows, 1:2], in0=ss[:rows, 0:1], scalar1=1.0 / 1024, scalar2=1e-6, op0=ALU.mult, op1=ALU.add)
                k.act.activation(out=ss[:rows, 3:4], in_=ss[:rows, 1:2], func=AF.Sqrt)
                k.dve.reciprocal(out=ss[:rows, 2:3], in_=ss[:rows, 3:4])
                k.dve.scalar_tensor_tensor(out=yt[:rows, :], in0=h[:rows, :], scalar=ss[:rows, 2:3], in1=gbc[:rows, :], op0=ALU.mult, op1=ALU.mult)
                if r0 < Tn:
                    k.dma(y_out[0][r0:r0 + rows, :], yt[:rows, :])
                else:
                    k.dma(y_out[1][0:16, :], yt[:16, :])


def phase_proj1(k, L):
    Tn = L['Tn']; NT = L['NT']; identb = L['identb']; identf = L['identf']; H1 = L['H1']
    NBLK = L['NBLK']; NBLKP = L['NBLKP']
    rmsnorm_T = L['rmsnorm_T']; load_w_bf16 = L['load_w_bf16']
    QT2 = L['QT2']; KT2 = L['KT2']; VT2 = L['VT2']; NS2 = L['NS2']
    with k.scope():
        W1 = k.sb('W1', [128, 8, ODC], BF16); load_w_bf16(W1, L['w_in1'], 1024, ODC)
        gbc = k.sb('gbc', [128, 1024]); k.dma(gbc[:, :], L['norm_mix'][1:2, :].pbc(128))
        xt = [k.sb('xt', [128, 1024]) for _ in range(2)]
        junk = k.sb('junk', [128, 1024], BF16); xn = k.sb('xn', [128, 1024], BF16); ss = k.sb('ss', [128, 4])
        xnT = k.sb('xnT', [128, 8, 128], BF16)
        proj = [k.sb('proj', [128, ODC]) for _ in range(2)]
        cs = k.sb('cs', [128, 64])
        tmp = [k.sb('tmp%d' % i, [128, 16, 32]) for i in range(4)]
        qb = k.sb('qb', [128, 16, 64], BF16); kb = k.sb('kb', [128, 4, 64], BF16)
        vb = k.sb('vb', [128, 4, 65], BF16)
        k.op('dve', lambda e: e.memset(vb[:, :, :].ap, 1.0), (), (vb,))
        ones = k.sb('ones', [128, 1]); k.op('dve', lambda e: e.memset(ones[:, :].ap, 1.0), (), (ones,))
        qT = k.sb('qT', [64, 16, 128], BF16); kT = k.sb('kT', [64, 4, 128], BF16)
        qTf = k.sb('qTf', [64, 16, 128])
        meansT = k.sb('meansT', [64, 4, NBLKP])
        k.op('dve', lambda e: e.memset(meansT[:, :, :].ap, 0.0), (), (meansT,))
        gbm = k.sb('gbm', [128, 2, NBLKP])
        score = k.sb('score', [128, 16, NBLKP]); m8 = k.sb('m8', [128, 16, 8]); thr = k.sb('thr', [128, 16])
        negs = k.sb('negs', [128, 16, NBLKP]); negb = k.sb('negb', [128, 16, NBLKP], BF16)
        pT = k.ps('pT', [128, 8, 128], BF16)
        pA = [k.ps('pA', [128, 512]) for _ in range(2)]
        pQ = k.ps('pQ', [64, 8, 128], BF16)
        pK = k.ps('pK', [64, 4, 128], BF16)
        pQf = [k.ps('pQf', [64, 4, 128]) for _ in range(1)]
        pM = k.ps('pM', [64, 4, NBLKP])
        pGt = k.ps('pGt', [128, 16, NBLKP])
        ci = 0
        for ti, (r0, rows) in enumerate(L['tiles']):
            x = xt[ti % 2]; pj = proj[ti % 2]
            k.dma(x[:rows, :], H1[r0:r0 + rows, :])
            k.dma(cs[:rows, :], L['ropecs'][r0:r0 + rows, :])
            rmsnorm_T(x, rows, gbc, xn, pT, xnT, ss, junk)
            for c0 in range(0, ODC, 512):
                ps = pA[ci % 2]; ci += 1
                for kk in range(8):
                    k.pe.matmul(out=ps[:rows, :], lhsT=xnT[:, kk, :rows], rhs=W1[:, kk, c0:c0 + 512], start=(kk == 0), stop=(kk == 7))
                (k.act.copy if ci % 2 else k.dve.tensor_copy)(out=pj[:rows, c0:c0 + 512], in_=ps[:rows, :])
            cosb = lambda n: cs[:rows, 0:32].unsq(1).bc([rows, n, 32])
            sinb = lambda n: cs[:rows, 32:64].unsq(1).bc([rows, n, 32])
            for vi, (c0, n) in enumerate([(0, 16), (1024, 4)]):
                xv = pj[:rows, c0:c0 + n * 64].rearrange("p (h d) -> p h d", h=n)
                E1 = k.dve if vi == 0 else k.pool
                x1 = xv[:, :, 0:32]; x2 = xv[:, :, 32:64]
                t = [tt_[:rows, 0:n, :] for tt_ in tmp]
                E1.tensor_tensor(out=t[0], in0=x1, in1=cosb(n), op=ALU.mult)
                E1.tensor_tensor(out=t[1], in0=x2, in1=sinb(n), op=ALU.mult)
                E1.tensor_tensor(out=t[2], in0=x2, in1=cosb(n), op=ALU.mult)
                E1.tensor_tensor(out=t[3], in0=x1, in1=sinb(n), op=ALU.mult)
                E1.tensor_tensor(out=x1, in0=t[0], in1=t[1], op=ALU.subtract)
                E1.tensor_tensor(out=x2, in0=t[2], in1=t[3], op=ALU.add)
            qv = pj[:rows, 0:1024].rearrange("p (h d) -> p h d", h=16)
            kv_ = pj[:rows, 1024:1280].rearrange("p (h d) -> p h d", h=4)
            vv = pj[:rows, 1280:1536].rearrange("p (h d) -> p h d", h=4)
            k.dve.tensor_copy(out=qb[:rows, :, :], in_=qv)
            k.pool.tensor_copy(out=kb[:rows, :, :], in_=kv_)
            k.pool.tensor_copy(out=vb[:rows, :, 0:64], in_=vv)
            if r0 < Tn:
                k.dma(L['o_moba_p'][r0:r0 + rows, :], pj[:rows, 1024:1536])
            else:
                k.dma(L['o_moba_s'][0:16, :], pj[:16, 1024:1536])
            for hh in range(2):
                for h in range(8):
                    k.pe.transpose(out=pQ[:, h, :rows], in_=qb[:rows, hh * 8 + h, :], identity=identb[:rows, :rows])
                k.act.copy(out=qT[:, hh * 8:(hh + 1) * 8, :rows], in_=pQ[:, :, :rows])
            for h in range(4):
                k.pe.transpose(out=pK[:, h, :rows], in_=kb[:rows, h, :], identity=identb[:rows, :rows])
            k.dve.tensor_copy(out=kT[:, :, :rows], in_=pK[:, :, :rows])
            k.dma(QT2[:, :, r0:r0 + rows], qT[:, :, :rows])
            k.dma(KT2[:, :, r0:r0 + rows], kT[:, :, :rows])
            k.dma(VT2[r0:r0 + rows, :, :], vb[:rows, :, :])
            for hq in range(4):
                pq = pQf[0]
                for h in range(4):
                    k.pe.transpose(out=pq[:, h, :rows], in_=pj[:rows, (hq * 4 + h) * 64:(hq * 4 + h + 1) * 64], identity=identf[:rows, :rows])
                k.act.copy(out=qTf[:, hq * 4:(hq + 1) * 4, :rows], in_=pq[:, :, :rows])
            if r0 >= Tn:
                k.dma(L['QF2'][:, :, :], qTf[:, :, 0:16])
                continue
            blk = r0 // 256
            for h in range(16):
                k.pe.matmul(out=pGt[:rows, h, :], lhsT=qTf[:, h, :rows], rhs=meansT[:, h // 4, :], start=(h == 0), stop=(h == 15), skip_group_check=True)
            k.dma(gbm[:rows, :, :], L['GBM'][r0:r0 + rows, :, :])
            k.dve.tensor_tensor(out=score[:rows, :, :], in0=pGt[:rows, :, :], in1=gbm[:rows, 0:1, :].bc([rows, 16, NBLKP]), op=ALU.add)
            for h in range(16):
                k.dve.max(out=m8[:rows, h, :], in_=score[:rows, h, :])
            k.dve.tensor_scalar(out=thr[:rows, :], in0=m8[:rows, :, 2], scalar1=-1.0e8, scalar2=None, op0=ALU.max)
            k.dve.tensor_tensor(out=negs[:rows, :, :], in0=score[:rows, :, :], in1=thr[:rows, :].unsq(2).bc([rows, 16, NBLKP]), op=ALU.is_lt)
            k.dve.tensor_tensor(out=negs[:rows, :, :], in0=negs[:rows, :, :], in1=gbm[:rows, 1:2, :].bc([rows, 16, NBLKP]), op=ALU.mult)
            k.dve.tensor_scalar(out=negb[:rows, :, :], in0=negs[:rows, :, :], scalar1=NEG, scalar2=None, op0=ALU.mult)
            k.dma(NS2[r0:r0 + rows, :, :], negb[:rows, :, :])
            for g in range(4):
                first = (r0 % 256 == 0)
                k.pe.matmul(out=pM[:, g, blk:blk + 1], lhsT=pj[:rows, 1024 + g * 64:1024 + (g + 1) * 64], rhs=ones[:rows, 0:1],
                            start=(ti == 0 and g == 0), stop=not first, skip_group_check=True)
            if r0 % 256 == 128:
                k.dve.tensor_scalar(out=meansT[:, :, blk:blk + 1], in0=pM[:, :, blk:blk + 1], scalar1=1.0 / 2048, scalar2=None, op0=ALU.mult)


def phase_moba_prompt(k, L):
    Tn = L['Tn']; NT = L['NT']; NBLK = L['NBLK']; NBLKP = L['NBLKP']; identb = L['identb']
    QT2 = L['QT2']; KT2 = L['KT2']; VT2 = L['VT2']; NS2 = L['NS2']; MO = L['MO']
    with k.scope():
        tri = k.sb('tri', [128, 2, 128], BF16); k.dma(tri[:, :, :], L['TRIt'][:, :, :])
        ps_s = [k.ps('ps_s', [128, 512]) for _ in range(3)]
        po_ = [k.ps('po', [128, 4, 65]) for _ in range(2)]
        ps_t = k.ps('ps_t', [NBLKP, 4, 128], BF16)
        KA = NBLKP + 64
        K2 = k.sb('K2', [KA, Tn], BF16); V2 = k.sb('V2', [128, NT, 65], BF16)
        k.dma(K2[0:NBLKP, :], L['EEXP2'][0:NBLKP, 0:Tn])
        q4 = [k.sb('q4', [KA, 4, 128], BF16) for _ in range(2)]
        ns = [k.sb('ns', [128, 4, NBLKP], BF16) for _ in range(2)]
        negT4 = k.sb('negT4', [NBLKP, 4, 128], BF16)
        ebuf = [k.sb('ebuf', [128, 512], BF16) for _ in range(3)]
        rr = k.sb('rr', [128, 8]); accb = k.sb('accb', [128, 4, 64], BF16)
        cnt = [0]
        for g in range(4):
            k.dma(K2[NBLKP:KA, :], KT2[:, g, 0:Tn])
            k.dma(V2[:, :, :], VT2[0:Tn, g, :].rearrange("(kt p) c -> p kt c", p=128))
            def loads(tt):
                k.dma(q4[tt % 2][NBLKP:KA, :, :], QT2[:, g * 4:(g + 1) * 4, tt * 128:tt * 128 + 128])
                k.dma(ns[tt % 2][:, :, :], NS2[tt * 128:tt * 128 + 128, g * 4:(g + 1) * 4, :])

            loads(0)
            for tt in range(NT):
                t0 = tt * 128
                q = q4[tt % 2]; n_ = ns[tt % 2]
                if tt + 1 < NT:
                    loads(tt + 1)
                for r in range(4):
                    k.pe.transpose(out=ps_t[:, r, :], in_=n_[:, r, :], identity=identb[:, :])
                k.dve.tensor_copy(out=q[0:NBLKP, :, :], in_=ps_t[:, :, :])
                qf = q[:, :, :].rearrange("p r t -> p (r t)")
                po = po_[tt % 2]
                def sc_m(kt):
                    pss = ps_s[cnt[0] % 3]
                    k.pe.matmul(out=pss[:, :], lhsT=K2[:, kt * 128:(kt + 1) * 128], rhs=qf, start=True, stop=True)
                    e = ebuf[cnt[0] % 3]; cnt[0] += 1
                    k.act.activation(out=e[:, :], in_=pss[:, :], func=AF.Exp, scale=0.125)
                    if kt == tt:
                        ev = e[:, :].rearrange("p (r t) -> p r t", r=4)
                        k.pool.tensor_tensor(out=ev, in0=ev, in1=tri[:, 0:1, :].bc([128, 4, 128]), op=ALU.mult)
                    return e

                def pv_m(kt, e):
                    for r in range(4):
                        k.pe.matmul(out=po[:, r, :], lhsT=e[:, r * 128:(r + 1) * 128], rhs=V2[:, kt, :], start=(kt == 0 and r == 0), stop=(kt == tt), skip_group_check=True)

                pipe(range(tt + 1), sc_m, pv_m)
                k.dve.tensor_scalar(out=rr[:, 0:4], in0=po[:, :, 64], scalar1=1e-30, scalar2=None, op0=ALU.max)
                k.dve.reciprocal(out=rr[:, 4:8], in_=rr[:, 0:4])
                k.dve.tensor_tensor(out=accb[:, :, :], in0=po[:, :, 0:64], in1=rr[:, 4:8].unsq(2).bc([128, 4, 64]), op=ALU.mult)
                k.dma(MO[t0:t0 + 128, g * 256:(g + 1) * 256], accb[:, :, :].rearrange("p r d -> p (r d)"))


def page_indices(k, L):
    P = L['P']
    pti = k.sb('pti', [128, 4 * P], I32); ptf = k.sb('ptf', [128, 4 * P]); io = k.sb('io', [128, 1])
    idx = k.sb('idx', [128, 4 * P], I32)
    k.dma(pti[:, :], L['ptab'][:, :].rearrange("b p -> (b p)").unsq(0).pbc(128) if False else L['ptab'][:, :].rearrange("(o b) p -> o (b p)", o=1).pbc(128))
    k.dma(io[:, :], L['IOTA'][:, :])
    k.dve.tensor_copy(out=ptf[:, :], in_=pti[:, :])
    k.dve.tensor_scalar(out=ptf[:, :], in0=ptf[:, :], scalar1=128.0, scalar2=None, op0=ALU.mult)
    k.dve.tensor_tensor(out=ptf[:, :], in0=ptf[:, :], in1=io[:, 0:1].bc([128, 4 * P]), op=ALU.add)
    k.dve.tensor_copy(out=idx[:, :], in_=ptf[:, :])
    return idx


def gather_page(k, pg, cache, idx, col):
    k.raw16('pool', lambda e: e.indirect_dma_start(out=pg[:, :].ap, out_offset=None, in_=cache[:, :].ap,
                                                   in_offset=bass.IndirectOffsetOnAxis(ap=idx[:, col:col + 1].ap, axis=0)),
            reads=[cache.t if isinstance(cache, V) else cache, idx], writes=[pg])


def phase_nsa_sample(k, L):
    Tn = L['Tn']; P = L['P']; LP = L['LP']; NSELS = L['NSELS']; NBS = L['NBS']; NBTS = L['NBTS']
    QT = L['QT']; KT = L['KT']; VT = L['VT']; GT = L['GT']; AO = L['AO']; identb = L['identb']; identf = L['identf']
    cache = L['cache_nsa']; st_win = L['st_win']
    NJ = LP // 64
    with k.scope():
        ps_s = [k.ps('ps_s', [128, 512]) for _ in range(2)]
        W = load_cmp_weights(k, L, ps_s[0])
        idx = page_indices(k, L)
        eexp = k.sb('eexp', [NJ, LP], BF16); k.dma(eexp[:, :], L['EEXP'][0:NJ, 0:LP])
        smk = k.sb('smk', [128, 2, 16], BF16); k.dma(smk[:, :, :], L['SMK'][:, :, :])
        fbs = k.sb('fbs', [4, NSELS]); k.dma(fbs[:, :], L['FBs'][:, :])
        pX = [k.ps('pX', [64, 4, 128]) for _ in range(1)]
        pXb = [k.ps('pXb', [64, 8, 128], BF16) for _ in range(1)]
        stg = [k.sb('stg', [128, 384], BF16) for _ in range(2)]
        po_a = [k.ps('po_a', [4, 2, 65 + NSELS]) for _ in range(2)]
        po_b = k.ps('po_b', [4, 4, 65])
        ps_t = k.ps('ps_t', [128, 16], BF16)
        si = [0]

        def sbank():
            b = ps_s[si[0] % 2]; si[0] += 1
            return b

        pg = [k.sb('pg', [128, 512]) for _ in range(3)]
        KcTs = k.sb('KcTs', [64, 2, LP], BF16); VcTs = k.sb('VcTs', [64, 2, LP], BF16); KsTs = k.sb('KsTs', [64, 2, LP], BF16)
        Vss = k.sb('Vss', [128, P, 2, 65], BF16)
        k.op('dve', lambda e: e.memset(Vss[:, :, :, 64:65].ap, 1.0), (), (Vss,))
        KwTs = k.sb('KwTs', [64, 2, 512], BF16); Vws = k.sb('Vws', [128, 4, 2, 65], BF16)
        k.op('dve', lambda e: e.memset(Vws[:, :, :, 64:65].ap, 1.0), (), (Vws,))
        wt = [k.sb('wt', [128, 256]) for _ in range(2)]
        KcCs = k.sb('KcCs', [64, NBTS * 128], BF16)
        VcCs = k.sb('VcCs', [128, NBTS, 65 + NSELS], BF16)
        k.dma(VcCs[:, :, 65:65 + NSELS], L['OVs'][:, :, :])
        k.op('dve', lambda e: e.memset(VcCs[:, :, 64:65].ap, 1.0), (), (VcCs,))
        hx = k.sb('hx', [64, 512]); ta = k.sb('ta', [64, 512]); tb = k.sb('tb', [64, 512])
        hk = k.sb('hk', [64, 512], BF16); hv_ = k.sb('hv_', [64, 512], BF16)
        qs_all = k.sb('qs_all', [64, 8, 16], BF16); k.dma(qs_all[:, :, :], QT[:, :, Tn:Tn + 16])
        knew = k.sb('knew', [64, 6, 16], BF16); k.dma(knew[:, :, :], KT[:, :, Tn:Tn + 16])
        vnew = [k.sb('vnew', [4, 2, 2, 65], BF16) for _ in range(2)]
        gts = [k.sb('gts', [4, 24]) for _ in range(2)]
        q16 = k.sb('q16', [64, 4, 4], BF16)
        ebuf = [k.sb('ebuf', [128, 16], BF16) for _ in range(4)]
        ei = [0]

        def enext():
            b = ebuf[ei[0] % 4]; ei[0] += 1
            return b

        acc = k.sb('acc', [4, 8, 64]); accb = k.sb('accb', [4, 8, 64], BF16); tmp4 = k.sb('tmp4', [4, 4, 64])
        rr = k.sb('rr', [4, 16]); imp = k.sb('imp', [4, NSELS]); score = k.sb('score', [4, NSELS]); sc2 = k.sb('sc2', [4, NSELS])
        m8 = k.sb('m8', [4, 16]); negs = k.sb('negs', [4, NSELS], BF16)
        negT16 = k.sb('negT16', [NJ, 4, 4], BF16)
        SD = L['cfg'].get('sdbg', 99)
        for bl in range(4 if SD >= 99 else 1):
            vn = vnew[bl % 2]; gt_ = gts[bl % 2]
            if SD < 1:
                break
            import os
            if os.environ.get('SKIPVN') != '1':
                k.dma(vn[:, :, :, :], VT[Tn + bl * 4:Tn + bl * 4 + 4, :, :, :])
                k.dma(gt_[:, :], GT[Tn + bl * 4:Tn + bl * 4 + 4, :])
            gv3 = gt_[:, :].rearrange("p (h b) -> p h b", b=3)
            for lp in range(P if os.environ.get('SKIPG') != '1' else 0):
                pgt = pg[lp % 3]
                gather_page(k, pgt, cache, idx, bl * P + lp)
                sg_ = stg[lp % 2]
                k.dve.tensor_copy(out=sg_[:, :], in_=pgt[:, 0:384])
                k.pool.tensor_copy(out=Vss[:, lp, :, 0:64], in_=pgt[:, 384:512].rearrange("p (g d) -> p g d", g=2))
                px0 = pXb[0]
                for i in range(6):
                    k.pe.transpose(out=px0[:, i, :], in_=sg_[:, i * 64:(i + 1) * 64], identity=identb[:, :])
                k.dve.tensor_copy(out=KcTs[:, :, lp * 128:(lp + 1) * 128], in_=px0[:, 0:2, :])
                k.dve.tensor_copy(out=VcTs[:, :, lp * 128:(lp + 1) * 128], in_=px0[:, 2:4, :])
                k.dve.tensor_copy(out=KsTs[:, :, lp * 128:(lp + 1) * 128], in_=px0[:, 4:6, :])
            if SD < 2:
                break
            for wi in range(4):
                w_ = wt[wi % 2]
                k.dma(w_[:, :], st_win[bl, wi * 128:(wi + 1) * 128, :])
                px1 = pX[0]
                for g in range(2):
                    k.pe.transpose(out=px1[:, g, :], in_=w_[:, g * 64:(g + 1) * 64], identity=identf[:, :])
                k.dve.tensor_copy(out=KwTs[:, :, wi * 128:(wi + 1) * 128], in_=px1[:, 0:2, :])
                k.pool.tensor_copy(out=Vws[:, wi, :, 0:64], in_=w_[:, 128:256].rearrange("p (g d) -> p g d", g=2))
            if SD < 3:
                break
            for g in range(2):
                pb = sbank()
                compress(k, W, 0, KcTs[:, g, :], NBS, pb, hx, ta, tb, hk)
                pb2 = sbank()
                k.pe.matmul(out=pb2[0:64, 0:NBS], lhsT=W[0][1][:, :], rhs=hk[:, 0:NBS], start=True, stop=True)
                k.dve.tensor_copy(out=KcCs[:, 0:NBS], in_=pb2[0:64, 0:NBS])
                pb = sbank()
                compress(k, W, 1, VcTs[:, g, :], NBS, pb, hx, ta, tb, hv_)
                for ni in range(NBTS):
                    nn = min(128, NBS - ni * 128)
                    pb3 = sbank()
                    k.pe.matmul(out=pb3[:nn, 0:64], lhsT=hv_[:, ni * 128:ni * 128 + nn], rhs=W[1][1][:, :], start=True, stop=True)
                    k.dve.tensor_copy(out=VcCs[:nn, ni, 0:64], in_=pb3[:nn, 0:64])
                if SD < 4:
                    continue
                k.dve.tensor_copy(out=q16[:, :, :], in_=qs_all[:, g * 4:(g + 1) * 4, bl * 4:(bl + 1) * 4])
                qf = q16[:, :, :].rearrange("p r t -> p (r t)")
                for ni in range(NBTS):
                    nn = min(128, NBS - ni * 128)
                    pss = sbank()
                    k.pe.matmul(out=pss[:nn, 0:16], lhsT=KcCs[:, ni * 128:ni * 128 + nn], rhs=qf, start=True, stop=True)
                    e = enext()
                    k.act.activation(out=e[:nn, :], in_=pss[:nn, 0:16], func=AF.Exp, scale=0.125)
                    for r in range(4):
                        k.pe.matmul(out=po_a[r // 2][:, r % 2, :], lhsT=e[:nn, r * 4:(r + 1) * 4], rhs=VcCs[:nn, ni, :],
                                    start=(ni == 0 and r % 2 == 0), stop=(ni == NBTS - 1), skip_group_check=True)
                for r in range(4):
                    po = po_a[r // 2][:, r % 2, :]
                    k.dve.tensor_scalar(out=rr[:, r:r + 1], in0=po[:, 64:65], scalar1=1e-30, scalar2=None, op0=ALU.max)
                    k.dve.reciprocal(out=rr[:, 4 + r:5 + r], in_=rr[:, r:r + 1])
                    k.dve.tensor_tensor(out=rr[:, 8 + r:9 + r], in0=rr[:, 4 + r:5 + r], in1=gv3[:, g * 4 + r, 0:1], op=ALU.mult)
                    k.dve.tensor_scalar(out=acc[:, g * 4 + r, :], in0=po[:, 0:64], scalar1=rr[:, 8 + r:9 + r], scalar2=None, op0=ALU.mult)
                    if r == 0:
                        k.dve.tensor_scalar(out=imp[:, :], in0=po[:, 65:65 + NSELS], scalar1=rr[:, 4 + r:5 + r], scalar2=None, op0=ALU.mult)
                    else:
                        k.dve.scalar_tensor_tensor(out=imp[:, :], in0=po[:, 65:65 + NSELS], scalar=rr[:, 4 + r:5 + r], in1=imp[:, :], op0=ALU.mult, op1=ALU.add)
                if SD < 5:
                    continue
                k.dve.tensor_tensor(out=score[:, :], in0=imp[:, :], in1=fbs[:, :], op=ALU.add)
                k.dve.max(out=m8[:, 0:8], in_=score[:, :])
                k.dve.match_replace(out=sc2[:, :], in_to_replace=m8[:, 0:8], in_values=score[:, :], imm_value=-3.0e38)
                k.dve.max(out=m8[:, 8:16], in_=sc2[:, :])
                k.dve.tensor_scalar(out=rr[:, 12:13], in0=m8[:, 15:16], scalar1=-1.0e8, scalar2=None, op0=ALU.max)
                k.dve.tensor_scalar(out=negs[:, :], in0=score[:, :], scalar1=rr[:, 12:13], scalar2=NEG, op0=ALU.is_lt, op1=ALU.mult)
                k.pe.transpose(out=ps_t[0:NJ, 0:4], in_=negs[:, 0:NJ], identity=identb[0:4, 0:4])
                k.dve.tensor_copy(out=negT16[:, :, :], in_=ps_t[0:NJ, 0:4].unsq(1).bc([NJ, 4, 4]))
                nf = negT16[:, :, :].rearrange("p r t -> p (r t)")
                if SD < 6:
                    continue
                def sc_s(kt):
                    pss = sbank()
                    e = enext()
                    if kt < P:
                        k.pe.matmul(out=pss[:, 0:16], lhsT=KsTs[:, g, kt * 128:(kt + 1) * 128], rhs=qf, start=True, stop=False)
                        k.pe.matmul(out=pss[:, 0:16], lhsT=eexp[:, kt * 128:(kt + 1) * 128], rhs=nf, start=False, stop=True)
                        k.act.activation(out=e[:, :], in_=pss[:, 0:16], func=AF.Exp, scale=0.125)
                        return (e, 128, Vss[:, kt, g, :])
                    k.pe.matmul(out=pss[0:4, 0:16], lhsT=knew[:, 2 + g, bl * 4:(bl + 1) * 4], rhs=qf, start=True, stop=True)
                    k.act.activation(out=e[0:4, :], in_=pss[0:4, 0:16], func=AF.Exp, scale=0.125)
                    k.pool.tensor_tensor(out=e[0:4, :], in0=e[0:4, :], in1=smk[0:4, 1, :], op=ALU.mult)
                    return (e, 4, vn[:, 0, g, :])

                def pv_s(kt, tup):
                    e, nk, vv = tup
                    for r in range(4):
                        k.pe.matmul(out=po_b[:, r, :], lhsT=e[:nk, r * 4:(r + 1) * 4], rhs=vv, start=(kt == 0 and r == 0), stop=(kt == P), skip_group_check=True)

                pipe(range(P + 1), sc_s, pv_s)
                k.dve.tensor_scalar(out=rr[:, 0:4], in0=po_b[:, :, 64], scalar1=1e-30, scalar2=None, op0=ALU.max)
                k.dve.reciprocal(out=rr[:, 4:8], in_=rr[:, 0:4])
                k.dve.tensor_tensor(out=rr[:, 8:12], in0=rr[:, 4:8], in1=gv3[:, g * 4:(g + 1) * 4, 1], op=ALU.mult)
                k.dve.tensor_tensor(out=tmp4[:, :, :], in0=po_b[:, :, 0:64], in1=rr[:, 8:12].unsq(2).bc([4, 4, 64]), op=ALU.mult)
                k.dve.tensor_tensor(out=acc[:, g * 4:(g + 1) * 4, :], in0=acc[:, g * 4:(g + 1) * 4, :], in1=tmp4[:, :, :], op=ALU.add)
                if SD < 7:
                    continue
                def sc_w(kt):
                    pss = sbank()
                    e = enext()
                    if kt < 4:
                        k.pe.matmul(out=pss[:, 0:16], lhsT=KwTs[:, g, kt * 128:(kt + 1) * 128], rhs=qf, start=True, stop=True)
                        k.act.activation(out=e[:, :], in_=pss[:, 0:16], func=AF.Exp, scale=0.125)
                        if kt == 0:
                            k.pool.tensor_tensor(out=e[:, :], in0=e[:, :], in1=smk[:, 0, :], op=ALU.mult)
                        return (e, 128, Vws[:, kt, g, :])
                    k.pe.matmul(out=pss[0:4, 0:16], lhsT=knew[:, 4 + g, bl * 4:(bl + 1) * 4], rhs=qf, start=True, stop=True)
                    k.act.activation(out=e[0:4, :], in_=pss[0:4, 0:16], func=AF.Exp, scale=0.125)
                    k.pool.tensor_tensor(out=e[0:4, :], in0=e[0:4, :], in1=smk[0:4, 1, :], op=ALU.mult)
                    return (e, 4, vn[:, 1, g, :])

                def pv_w(kt, tup):
                    e, nk, vv = tup
                    for r in range(4):
                        k.pe.matmul(out=po_b[:, r, :], lhsT=e[:nk, r * 4:(r + 1) * 4], rhs=vv, start=(kt == 0 and r == 0), stop=(kt == 4), skip_group_check=True)

                pipe(range(5), sc_w, pv_w)
                k.dve.tensor_scalar(out=rr[:, 0:4], in0=po_b[:, :, 64], scalar1=1e-30, scalar2=None, op0=ALU.max)
                k.dve.reciprocal(out=rr[:, 4:8], in_=rr[:, 0:4])
                k.dve.tensor_tensor(out=rr[:, 8:12], in0=rr[:, 4:8], in1=gv3[:, g * 4:(g + 1) * 4, 2], op=ALU.mult)
                k.dve.tensor_tensor(out=tmp4[:, :, :], in0=po_b[:, :, 0:64], in1=rr[:, 8:12].unsq(2).bc([4, 4, 64]), op=ALU.mult)
                k.dve.tensor_tensor(out=acc[:, g * 4:(g + 1) * 4, :], in0=acc[:, g * 4:(g + 1) * 4, :], in1=tmp4[:, :, :], op=ALU.add)
            k.dve.tensor_copy(out=accb[:, :, :], in_=acc[:, :, :])
            k.dma(AO[Tn + bl * 4:Tn + bl * 4 + 4, 512:1024], accb[:, :, :].rearrange("p h d -> p (h d)"))


def phase_moba_sample(k, L):
    Tn = L['Tn']; P = L['P']; LP = L['LP']; NBLKS = L['NBLKS']; NBLKSP = L['NBLKSP']
    QT2 = L['QT2']; KT2 = L['KT2']; VT2 = L['VT2']; MO = L['MO']; identb = L['identb']; identf = L['identf']
    cache = L['cache_moba']
    with k.scope():
        idx = page_indices(k, L)
        eexp = k.sb('eexp', [NBLKSP, LP], BF16); k.dma(eexp[:, :], L['EEXP2'][0:NBLKSP, 0:LP])
        smk = k.sb('smk', [128, 2, 16], BF16); k.dma(smk[:, :, :], L['SMK'][:, :, :])
        ones = k.sb('ones', [128, 1]); k.op('dve', lambda e: e.memset(ones[:, :].ap, 1.0), (), (ones,))
        ps_s = [k.ps('ps_s', [128, 512]) for _ in range(2)]
        pXb = [k.ps('pXb', [64, 4, 128], BF16) for _ in range(2)]
        stg = [k.sb('stg', [128, 256], BF16) for _ in range(2)]; stf = [k.sb('stf', [128, 256]) for _ in range(2)]
        pM = k.ps('pM', [64, 4, NBLKSP])
        pGt = k.ps('pGt', [4, 16, NBLKSP])
        po_b = k.ps('po_b', [4, 4, 65])
        ps_t = k.ps('ps_t', [NBLKSP, 16, 4], BF16)
        pg = [k.sb('pg', [128, 512]) for _ in range(3)]
        K2s = k.sb('K2s', [64, 4, LP], BF16); V2s = k.sb('V2s', [128, P, 4, 65], BF16)
        k.op('dve', lambda e: e.memset(V2s[:, :, :, 64:65].ap, 1.0), (), (V2s,))
        meansT = k.sb('meansT', [64, 4, NBLKSP])
        qf32 = k.sb('qf32', [64, 16, 16]); k.dma(qf32[:, :, :], L['QF2'][:, :, :])
        qs_all = k.sb('qs_all', [64, 16, 16], BF16); k.dma(qs_all[:, :, :], QT2[:, :, Tn:Tn + 16])
        knew = k.sb('knew', [64, 4, 16], BF16); k.dma(knew[:, :, :], KT2[:, :, Tn:Tn + 16])
        vnew = [k.sb('vnew', [4, 4, 65], BF16) for _ in range(2)]
        q16 = k.sb('q16', [64, 4, 4], BF16)
        score = k.sb('score', [4, 16, NBLKSP]); m8 = k.sb('m8', [4, 16, 8]); thr = k.sb('thr', [4, 16])
        negs = k.sb('negs', [4, 16, NBLKSP]); negb = k.sb('negb', [4, 16, NBLKSP], BF16)
        negT = k.sb('negT', [NBLKSP, 16, 4], BF16)
        ebuf = [k.sb('ebuf', [128, 16], BF16) for _ in range(4)]
        rr = k.sb('rr', [4, 8]); accb = k.sb('accb', [4, 16, 64], BF16)
        k.op('dve', lambda e: e.memset(score[:, :, :].ap, -2.0e9), (), (score,))
        cnt = [0]
        for bl in range(4):
            vn = vnew[bl % 2]
            k.dma(vn[:, :, :], VT2[Tn + bl * 4:Tn + bl * 4 + 4, :, :])
            for lp in range(P):
                pgt = pg[lp % 3]
                gather_page(k, pgt, cache, idx, bl * P + lp)
                sg_ = stg[lp % 2]; sf_ = stf[lp % 2]
                k.dve.tensor_copy(out=sg_[:, :], in_=pgt[:, 0:256])
                k.dve.tensor_copy(out=sf_[:, :], in_=pgt[:, 0:256])
                k.pool.tensor_copy(out=V2s[:, lp, :, 0:64], in_=pgt[:, 256:512].rearrange("p (g d) -> p g d", g=4))
                px = pXb[lp % 2]
                for g in range(4):
                    k.pe.transpose(out=px[:, g, :], in_=sg_[:, g * 64:(g + 1) * 64], identity=identb[:, :])
                k.dve.tensor_copy(out=K2s[:, :, lp * 128:(lp + 1) * 128], in_=px[:, :, :])
                blk = lp // 2
                for g in range(4):
                    k.pe.matmul(out=pM[:, g, blk:blk + 1], lhsT=sf_[:, g * 64:(g + 1) * 64], rhs=ones[:, 0:1],
                                start=(lp == 0 and g == 0), stop=(lp % 2 == 1), skip_group_check=True)
            k.dve.tensor_scalar(out=meansT[:, :, 0:NBLKS], in0=pM[:, :, 0:NBLKS], scalar1=1.0 / 2048, scalar2=None, op0=ALU.mult)
            for h in range(16):
                k.pe.matmul(out=pGt[:, h, 0:NBLKS], lhsT=qf32[:, h, bl * 4:(bl + 1) * 4], rhs=meansT[:, h // 4, 0:NBLKS], start=(h == 0), stop=(h == 15), skip_group_check=True)
            k.dve.tensor_copy(out=score[:, :, 0:NBLKS], in_=pGt[:, :, 0:NBLKS])
            for h in range(16):
                k.dve.max(out=m8[:, h, :], in_=score[:, h, :])
            k.dve.tensor_scalar(out=thr[:, :], in0=m8[:, :, 2], scalar1=-1.0e8, scalar2=None, op0=ALU.max)
            k.dve.tensor_tensor(out=negs[:, :, :], in0=score[:, :, :], in1=thr[:, :].unsq(2).bc([4, 16, NBLKSP]), op=ALU.is_lt)
            k.dve.tensor_scalar(out=negb[:, :, :], in0=negs[:, :, :], scalar1=NEG, scalar2=None, op0=ALU.mult)
            for h in range(16):
                k.pe.transpose(out=ps_t[:, h, :], in_=negb[:, h, :], identity=identb[0:4, 0:4])
            k.dve.tensor_copy(out=negT[:, :, :], in_=ps_t[:, :, :])
            for g in range(4):
                k.dve.tensor_copy(out=q16[:, :, :], in_=qs_all[:, g * 4:(g + 1) * 4, bl * 4:(bl + 1) * 4])
                qf = q16[:, :, :].rearrange("p r t -> p (r t)")
                nf = negT[:, g * 4:(g + 1) * 4, :].rearrange("p r t -> p (r t)")
                def sc_ms(kt):
                    pss = ps_s[cnt[0] % 2]
                    e = ebuf[cnt[0] % 4]; cnt[0] += 1
                    if kt < P:
                        k.pe.matmul(out=pss[:, 0:16], lhsT=K2s[:, g, kt * 128:(kt + 1) * 128], rhs=qf, start=True, stop=False)
                        k.pe.matmul(out=pss[:, 0:16], lhsT=eexp[:, kt * 128:(kt + 1) * 128], rhs=nf, start=False, stop=True)
                        k.act.activation(out=e[:, :], in_=pss[:, 0:16], func=AF.Exp, scale=0.125)
                        return (e, 128, V2s[:, kt, g, :])
                    k.pe.matmul(out=pss[0:4, 0:16], lhsT=knew[:, g, bl * 4:(bl + 1) * 4], rhs=qf, start=True, stop=True)
                    k.act.activation(out=e[0:4, :], in_=pss[0:4, 0:16], func=AF.Exp, scale=0.125)
                    k.pool.tensor_tensor(out=e[0:4, :], in0=e[0:4, :], in1=smk[0:4, 1, :], op=ALU.mult)
                    return (e, 4, vn[:, g, :])

                def pv_ms(kt, tup):
                    e, nk, vv = tup
                    for r in range(4):
                        k.pe.matmul(out=po_b[:, r, :], lhsT=e[:nk, r * 4:(r + 1) * 4], rhs=vv, start=(kt == 0 and r == 0), stop=(kt == P), skip_group_check=True)

                pipe(range(P + 1), sc_ms, pv_ms)
                k.dve.tensor_scalar(out=rr[:, 0:4], in0=po_b[:, :, 64], scalar1=1e-30, scalar2=None, op0=ALU.max)
                k.dve.reciprocal(out=rr[:, 4:8], in_=rr[:, 0:4])
                k.dve.tensor_tensor(out=accb[:, g * 4:(g + 1) * 4, :], in0=po_b[:, :, 0:64], in1=rr[:, 4:8].unsq(2).bc([4, 4, 64]), op=ALU.mult)
            k.dma(MO[Tn + bl * 4:Tn + bl * 4 + 4, :], accb[:, :, :].rearrange("p h d -> p (h d)"))


def host_consts(cfg):
    Tn, P = cfg['T'], cfg['P']
    TR = Tn + 128
    LP = P * 128
    pos = np.concatenate([np.arange(Tn), np.tile(LP + np.arange(4), 4), np.zeros(112)]).astype(np.float32)
    half = 32
    inv = (10000.0 ** (-np.arange(half, dtype=np.float32) / half)).astype(np.float32)
    ang = pos[:, None] * inv[None, :]
    ropecs = np.concatenate([np.cos(ang), np.sin(ang)], axis=1).astype(np.float32)
    return {
        'ropecs': ropecs,
        'identb': np.eye(128, dtype=np.float32).astype(ml_dtypes.bfloat16),
        'identf': np.eye(128, dtype=np.float32),
        **nsa_consts(Tn, LP),
        'masks64': np.stack([np.triu(np.ones((64, 64), np.float32)), np.triu(np.ones((64, 64), np.float32), 1), np.tril(np.ones((64, 64), np.float32), -1)], axis=1),
    }


def nsa_consts(Tn, LP):
    bf = ml_dtypes.bfloat16
    NBLK = Tn // 256; NBLKP = max(8, NBLK)
    tq = np.arange(Tn)[:, None]; nb = np.arange(NBLKP)[None, :]
    gb0 = np.where(nb < tq // 256, 0.0, -1.0e9).astype(np.float32)
    gb1 = (nb != tq // 256).astype(np.float32)
    GBM = np.stack([gb0, gb1], axis=1)
    Wd_ = max(Tn, LP)
    EE2 = (np.arange(Wd_)[None, :] // 256 == np.arange(64)[:, None]).astype(np.float32).astype(bf)
    NSEL = Tn // 64; NB = Tn // 16 - 1; NBT = (NB + 127) // 128
    t = np.arange(Tn)[:, None]; j = np.arange(NSEL)[None, :]
    cur = t // 64
    forced = ((j == cur) | (j == cur - 1) | (j == 0)).astype(np.float32)
    FB = np.where(j <= cur, 100.0 * forced, -1.0e9).astype(np.float32)
    n = np.arange(NBT * 128)[:, None]
    lo = np.maximum(n * 16, j * 64); hi = np.minimum(n * 16 + 32, (j + 1) * 64)
    ov = (np.clip(hi - lo, 0, None).astype(np.float32) / 32.0)
    ov[NB:] = 0
    OV = ov.reshape(NBT, 128, NSEL).transpose(1, 0, 2).astype(bf)
    nl = np.arange(128)[:, None, None]; idx = np.arange(17)[None, :, None]; tl = np.arange(128)[None, None, :]
    CM = (16 * nl + 31 - 128 * idx <= tl).astype(np.float32).astype(bf)
    s_ = np.arange(128)[:, None]; t_ = np.arange(128)[None, :]
    TRI = np.stack([(s_ <= t_), (s_ > t_)], axis=1).astype(np.float32).astype(bf)
    W = max(Tn, LP)
    EE = (np.arange(W)[None, :] // 64 == np.arange(128)[:, None]).astype(np.float32).astype(bf)
    NSELS = LP // 64 + 1; NBS = LP // 16 - 1; NBTS = (NBS + 127) // 128
    js = np.arange(NSELS)[None, :]
    FBs = np.tile((100.0 * ((js == 0) | (js == NSELS - 2) | (js == NSELS - 1))).astype(np.float32), (4, 1))
    ns_ = np.arange(NBTS * 128)[:, None]
    lo = np.maximum(ns_ * 16, js * 64); hi = np.minimum(ns_ * 16 + 32, (js + 1) * 64)
    ovs = (np.clip(hi - lo, 0, None).astype(np.float32) / 32.0); ovs[NBS:] = 0
    OVs = ovs.reshape(NBTS, 128, NSELS).transpose(1, 0, 2).astype(bf)
    i_ = np.arange(128)[:, None]; tq_ = np.tile(np.arange(4), 4)[None, :]
    SMK = np.stack([(i_ > tq_), (i_ <= tq_)], axis=1).astype(np.float32).astype(bf)
    return {'FBt': FB, 'OVt': OV, 'CMt': CM, 'TRIt': TRI, 'EEXP': EE, 'GBM': GBM, 'EEXP2': EE2,
            'FBs': FBs, 'OVs': OVs, 'SMK': SMK, 'IOTA': np.arange(128, dtype=np.float32).reshape(128, 1)}


def make_in_maps(cfg, inputs, ncores):
    Tn, P = cfg['T'], cfg['P']
    hc = host_consts(cfg)
    B = inputs['x_prompt'].shape[0]
    maps = []
    for c in range(ncores):
        b = c % B
        sl = slice(4 * c, 4 * c + 4)
        m = {
            'xp': np.ascontiguousarray(inputs['x_prompt'][b]),
            'xs': np.ascontiguousarray(inputs['x_sample'][sl]).reshape(16, 1024),
            'st_win': np.ascontiguousarray(inputs['state_win_kv'][0, sl]).reshape(4, 512, 256),
            'st_wkv': np.ascontiguousarray(inputs['state_wkv'][0, sl]),
            'st_shift': np.ascontiguousarray(inputs['state_shift'][0, sl]),
            'ptab': np.ascontiguousarray(inputs['page_table'][sl]).astype(np.int32),
            'norm_mix': inputs['norm_mix'], 'norm_ffn': inputs['norm_ffn'], 'norm_final': inputs['norm_final'].reshape(1, 1024),
            'w_in0': inputs['even_w_in'][0], 'w_out0': inputs['even_w_out'][0],
            'gate_b': inputs['nsa_gate_b'][0].reshape(1, 24),
            'cache_nsa': inputs['cache_nsa_kv'][0].reshape(-1, 512), 'cache_moba': inputs['cache_moba_kv'][0].reshape(-1, 512),
            'w_in1': inputs['odd_w_in'][0], 'w_out1': inputs['odd_w_out'][0],
            'ffn_g0': inputs['ffn_w_gate'][0], 'ffn_g1': inputs['ffn_w_gate'][1], 'ffn_u0': inputs['ffn_w_up'][0], 'ffn_u1': inputs['ffn_w_up'][1],
            'ffn_d0': inputs['ffn_w_down'][0], 'ffn_d1': inputs['ffn_w_down'][1],
            'cmp_w1': inputs['nsa_cmp_w1'][0], 'cmp_pe': inputs['nsa_cmp_pe'][0], 'cmp_w2': inputs['nsa_cmp_w2'][0],
            'rw_mu': inputs['rwkv_mu'][0].reshape(1, RWC),
            'rw_vec': np.stack([inputs['rwkv_w0'][0], inputs['rwkv_a0'][0], inputs['rwkv_k_k'][0], inputs['rwkv_k_a'][0],
                                inputs['rwkv_r_k'][0].reshape(512), inputs['rwkv_ln_g'][0], inputs['rwkv_ln_b'][0]]).astype(np.float32),
            'rw_wup': inputs['rwkv_w_up'][0], 'rw_aup': inputs['rwkv_a_up'][0], 'rw_gup': inputs['rwkv_g_up'][0],
        }
        m.update(hc)
        maps.append(m)
    return maps


CFG_FULL = {'T': 4096, 'P': 64, 'NPHYS': 2560, 'stages': 'ABCSDEFMG'}


def kernel(**inputs):
    cfg = dict(CFG_FULL)
    inputs = {k_: np.asarray(v) for k_, v in inputs.items()}
    nc, kb = build(cfg)
    maps = make_in_maps(cfg, inputs, 8)
    res = run_bass_kernel_spmd(nc, maps, core_ids=list(range(8)))
    R = res.results
    f32 = np.float32
    y_p = np.stack([R[c]['y_p'] for c in range(4)]).astype(f32)
    y_s = np.concatenate([R[c]['y_s'].reshape(4, 4, 1024) for c in range(8)]).astype(f32)
    nsa_p = np.stack([R[c]['o_nsa_p'].reshape(4096, 4, 2, 64) for c in range(4)])[None].astype(f32)
    nsa_s = np.concatenate([R[c]['o_nsa_s'].reshape(4, 4, 4, 2, 64) for c in range(8)])[None].astype(f32)
    moba_p = np.stack([R[c]['o_moba_p'].reshape(4096, 2, 4, 64) for c in range(4)])[None].astype(f32)
    moba_s = np.concatenate([R[c]['o_moba_s'].reshape(4, 4, 2, 4, 64) for c in range(8)])[None].astype(f32)
    win_p = np.stack([R[c]['o_win_p'].reshape(512, 2, 2, 64) for c in range(4)])[None].astype(f32)
    win_s = np.concatenate([R[c]['o_win_s'].reshape(4, 512, 2, 2, 64) for c in range(8)])[None].astype(f32)
    wkv_p = np.stack([R[c]['o_wkv_p'] for c in range(4)])[None].astype(f32)
    wkv_s = np.concatenate([R[c]['o_wkv_s'] for c in range(8)])[None].astype(f32)
    sh_p = np.concatenate([R[c]['o_sh_p'] for c in range(4)])[None].astype(f32)
    sh_s = np.concatenate([R[c]['o_sh_s'] for c in range(8)])[None].astype(f32)
    return (y_p, y_s, nsa_p, nsa_s, moba_p, moba_s, win_p, win_s, wkv_p, wkv_s, sh_p, sh_s)
```

```python
import numpy as np
import ml_dtypes
from contextlib import ExitStack, contextmanager
import concourse.bass as bass
import concourse.mybir as mybir
from concourse.bass_utils import run_bass_kernel_spmd

F32 = mybir.dt.float32
BF16 = mybir.dt.bfloat16
I32 = mybir.dt.int32
AF = mybir.ActivationFunctionType
ALU = mybir.AluOpType
AX = mybir.AxisListType

ENGS = ['pe', 'act', 'dve', 'pool', 'sp']
NDMA = 12
SAME_SYNC = {'pe': False, 'act': True, 'dve': True, 'pool': True, 'sp': False}
WRITE_KEYS = ('out', 'accum_out', 'out_max', 'out_indices', 'out_ap')


class T:
    def __init__(self, h, name):
        self.h = h
        self.name = name
        self.w = {}
        self.r = {}

    def __getitem__(self, idx):
        return V(self.h[idx], self)


class V:
    def __init__(self, ap, t):
        self.ap = ap
        self.t = t

    def __getitem__(self, idx):
        return V(self.ap[idx], self.t)

    def rearrange(self, s, **kw):
        return V(self.ap.rearrange(s, **kw), self.t)

    def bc(self, shape):
        return V(self.ap.to_broadcast(list(shape)), self.t)

    def pbc(self, n):
        return V(self.ap.partition_broadcast(n), self.t)

    def bitcast(self, dt):
        return V(self.ap.bitcast(dt), self.t)

    def unsq(self, ax):
        return V(self.ap.unsqueeze(ax), self.t)

    @property
    def shape(self):
        return self.ap.shape


def _merge(d, s):
    for k, v in s.items():
        if d.get(k, 0) < v:
            d[k] = v


class EngProxy:
    def __init__(self, kb, name):
        self.kb = kb
        self.name = name

    def __getattr__(self, opname):
        kb = self.kb
        name = self.name

        def call(**kw):
            reads, writes = [], []
            kw2 = {}
            for key, v in kw.items():
                if isinstance(v, V):
                    (writes if key in WRITE_KEYS else reads).append(v.t)
                    kw2[key] = v.ap
                else:
                    kw2[key] = v
            return kb.op(name, lambda e: getattr(e, opname)(**kw2), reads, writes)

        return call


class KB:
    def __init__(self):
        self.nc = bass.Bass("TRN2", target_bir_lowering=False)
        self.es = ExitStack()
        self.cnt = {}
        self.known = {e: {} for e in ENGS}
        self.sem = {}
        nc = self.nc
        self.eng = {'pe': nc.tensor, 'act': nc.scalar, 'dve': nc.vector, 'pool': nc.gpsimd, 'sp': nc.sync}
        for e in ENGS:
            self._mksem('c_' + e)
        for i in range(NDMA):
            self._mksem('d%d' % i)
        for i in range(4):
            self._mksem('g%d' % i)
        self.dma_rr = 0
        self.g_rr = 0
        self.pe = EngProxy(self, 'pe')
        self.act = EngProxy(self, 'act')
        self.dve = EngProxy(self, 'dve')
        self.pool = EngProxy(self, 'pool')
        self.stack = [self.es]
        self.ninst = 0
        self.uid = 0

    def _mksem(self, name):
        self.sem[name] = self.es.enter_context(self.nc.semaphore(name))
        self.cnt[name] = 0

    def dram(self, name, shape, dt, kind="Internal"):
        h = self.nc.dram_tensor(name, list(shape), dt, kind=kind)
        return T(h.ap(), name)

    def sb(self, name, shape, dt=F32):
        self.uid += 1
        h = self.stack[-1].enter_context(self.nc.sbuf_tensor("%s_%d" % (name, self.uid), list(shape), dt))
        return T(h, name)

    def ps(self, name, shape, dt=F32):
        self.uid += 1
        h = self.stack[-1].enter_context(self.nc.psum_tensor("%s_%d" % (name, self.uid), list(shape), dt))
        return T(h, name)

    @contextmanager
    def scope(self):
        es = ExitStack()
        self.stack.append(es)
        try:
            yield
        finally:
            self.barrier()
            self.stack.pop()
            es.close()

    def _deps(self, eng, reads, writes, own):
        deps = {}
        for t in reads:
            _merge(deps, t.w)
        for t in writes:
            _merge(deps, t.w)
            _merge(deps, t.r)
        kn = self.known[eng]
        e = self.eng[eng]
        for s, v in deps.items():
            if s == own and not SAME_SYNC[eng]:
                continue
            if kn.get(s, 0) >= v:
                continue
            e.wait_ge(self.sem[s], v)
            self.ninst += 1
            kn[s] = v

    def _mark(self, s, val, reads, writes):
        for t in reads:
            if t.r.get(s, 0) < val:
                t.r[s] = val
        for t in writes:
            if t.w.get(s, 0) < val:
                t.w[s] = val

    def op(self, eng, fn, reads=(), writes=()):
        own = 'c_' + eng
        self._deps(eng, reads, writes, own)
        self.cnt[own] += 1
        val = self.cnt[own]
        fn(self.eng[eng]).then_inc(self.sem[own], 1)
        self.ninst += 1
        self._mark(own, val, reads, writes)

    def dma(self, out, in_, q='sp', **kw):
        reads, writes = [in_.t], [out.t]
        s = 'd%d' % self.dma_rr
        self.dma_rr = (self.dma_rr + 1) % NDMA
        self._deps(q, reads, writes, None)
        self.cnt[s] += 16
        val = self.cnt[s]
        self.eng[q].dma_start(out=out.ap, in_=in_.ap, **kw).then_inc(self.sem[s], 16)
        self.ninst += 1
        self._mark(s, val, reads, writes)

    def raw16(self, q, fn, reads=(), writes=()):
        s = 'g%d' % self.g_rr
        self.g_rr = (self.g_rr + 1) % 4
        self._deps(q, reads, writes, None)
        self.cnt[s] += 16
        val = self.cnt[s]
        fn(self.eng[q]).then_inc(self.sem[s], 16)
        self.ninst += 1
        self._mark(s, val, reads, writes)

    def barrier(self, engs=ENGS):
        for e in engs:
            kn = self.known[e]
            for s, v in self.cnt.items():
                if v > 0 and kn.get(s, 0) < v and s != 'c_' + e:
                    self.eng[e].wait_ge(self.sem[s], v)
                    kn[s] = v

    def dbg(self, name, v, shape, dt=F32):
        if not getattr(self, 'dbg_on', False):
            return
        o = self.dram('dbg_' + name, shape, dt, "ExternalOutput")
        idx = tuple(slice(0, n) for n in shape)
        self.dma(o[idx], v)

    def finish(self):
        self.barrier(['sp'])
        self.es.close()
        return self.nc


RWC = 1792
EVC = 3096
ODC = 1536
DFF = 2816
NEG = -240000.0


def build(cfg):
    Tn, P, NPH = cfg['T'], cfg['P'], cfg['NPHYS']
    stages = cfg.get('stages', 'A')
    NT = Tn // 128
    LP = P * 128
    TR = Tn + 128
    k = KB()
    k.dbg_on = cfg.get('dbg', False)
    nc = k.nc
    IN = lambda n, s, d=F32: k.dram(n, s, d, "ExternalInput")
    OUT = lambda n, s, d=F32: k.dram(n, s, d, "ExternalOutput")
    xp = IN('xp', [Tn, 1024]); xs = IN('xs', [16, 1024])
    st_win = IN('st_win', [4, 512, 256]); st_wkv = IN('st_wkv', [4, 8, 64, 64]); st_shift = IN('st_shift', [4, RWC])
    ptab = IN('ptab', [4, P], I32)
    norm_mix = IN('norm_mix', [2, 1024]); norm_ffn = IN('norm_ffn', [2, 1024]); norm_final = IN('norm_final', [1, 1024])
    w_in0 = IN('w_in0', [1024, EVC]); w_out0 = IN('w_out0', [1024, 1024])
    gate_b = IN('gate_b', [1, 24])
    w_in1 = IN('w_in1', [1024, ODC]); w_out1 = IN('w_out1', [1024, 1024])
    ffn_g = [IN('ffn_g%d' % i, [1024, DFF]) for i in range(2)]; ffn_u = [IN('ffn_u%d' % i, [1024, DFF]) for i in range(2)]
    ffn_d = [IN('ffn_d%d' % i, [DFF, 1024]) for i in range(2)]
    NBLK = Tn // 256; NBLKP = max(8, NBLK)
    GBM = IN('GBM', [Tn, 2, NBLKP]); EEXP2 = IN('EEXP2', [64, max(Tn, LP)], BF16)
    cache_nsa = IN('cache_nsa', [NPH * 128, 512]); cache_moba = IN('cache_moba', [NPH * 128, 512])
    NSELS = LP // 64 + 1; NBS = LP // 16 - 1; NBTS = (NBS + 127) // 128
    NBLKS = LP // 256; NBLKSP = max(8, NBLKS)
    FBs = IN('FBs', [4, NSELS]); OVs = IN('OVs', [128, NBTS, NSELS], BF16)
    SMK = IN('SMK', [128, 2, 16], BF16)
    IOTA = IN('IOTA', [128, 1])
    rw_mu = IN('rw_mu', [1, RWC]); rw_vec = IN('rw_vec', [7, 512])
    rw_wup = IN('rw_wup', [64, 512]); rw_aup = IN('rw_aup', [64, 512]); rw_gup = IN('rw_gup', [128, 512])
    masks64 = IN('masks64', [64, 3, 64])
    NSEL = Tn // 64; NB = Tn // 16 - 1; NBT = (NB + 127) // 128
    cmp_w1 = IN('cmp_w1', [2, 32, 64, 64]); cmp_pe = IN('cmp_pe', [2, 32, 64]); cmp_w2 = IN('cmp_w2', [2, 64, 64])
    FBt = IN('FBt', [Tn, NSEL]); OVt = IN('OVt', [128, NBT, NSEL], BF16); CMt = IN('CMt', [128, 17, 128], BF16)
    TRIt = IN('TRIt', [128, 2, 128], BF16); EEXP = IN('EEXP', [128, max(Tn, LP)], BF16)
    ropecs = IN('ropecs', [TR, 64])
    identb_d = IN('identb', [128, 128], BF16); identf_d = IN('identf', [128, 128])
    y_p = OUT('y_p', [Tn, 1024]); y_s = OUT('y_s', [16, 1024])
    o_nsa_p = OUT('o_nsa_p', [Tn, 512]); o_nsa_s = OUT('o_nsa_s', [16, 512])
    o_moba_p = OUT('o_moba_p', [Tn, 512]); o_moba_s = OUT('o_moba_s', [16, 512])
    WN = min(512, Tn)
    o_win_p = OUT('o_win_p', [WN, 256]); o_win_s = OUT('o_win_s', [4, 512, 256])
    o_wkv_p = OUT('o_wkv_p', [8, 64, 64]); o_wkv_s = OUT('o_wkv_s', [4, 8, 64, 64])
    o_sh_p = OUT('o_sh_p', [1, RWC]); o_sh_s = OUT('o_sh_s', [4, RWC])
    RW = k.dram('RW', [TR, RWC], F32)
    QT = k.dram('QT', [64, 8, TR], BF16)
    KT = k.dram('KT', [64, 6, TR], BF16)
    VcT = k.dram('VcT', [64, 2, TR], BF16)
    VT = k.dram('VT', [TR, 2, 2, 65], BF16)
    GT = k.dram('GT', [TR, 24], F32)
    AO = k.dram('AO', [TR, 1024], BF16, "ExternalOutput" if cfg.get('dbg_ao') else "Internal")
    H1 = k.dram('H1', [TR, 1024], F32, "ExternalOutput" if cfg.get('dbg_ao') else "Internal")
    ACTT = k.dram('ACTT', [22, 128, TR], BF16)
    QT2 = k.dram('QT2', [64, 16, TR], BF16); KT2 = k.dram('KT2', [64, 4, TR], BF16); VT2 = k.dram('VT2', [TR, 4, 65], BF16)
    NS2 = k.dram('NS2', [Tn, 16, NBLKP], BF16)
    QF2 = k.dram('QF2', [64, 16, 16], F32)
    MO = k.dram('MO', [TR, 1024], BF16, "ExternalOutput" if cfg.get('dbg_ao') else "Internal")

    identb = k.sb('identb', [128, 128], BF16); identf = k.sb('identf', [128, 128], F32)
    k.dma(identb[:, :], identb_d[:, :]); k.dma(identf[:, :], identf_d[:, :])

    tiles = [(i * 128, 128) for i in range(NT)] + [(Tn, 16)]

    def rmsnorm_T(xt, rows, gbc, xn, pT, xnT, ss, junk):
        k.act.activation(out=junk[:rows, :], in_=xt[:rows, :], func=AF.Square, accum_out=ss[:rows, 0:1])
        k.dve.tensor_scalar(out=ss[:rows, 1:2], in0=ss[:rows, 0:1], scalar1=1.0 / 1024, scalar2=1e-6, op0=ALU.mult, op1=ALU.add)
        k.act.activation(out=ss[:rows, 3:4], in_=ss[:rows, 1:2], func=AF.Sqrt)
        k.dve.reciprocal(out=ss[:rows, 2:3], in_=ss[:rows, 3:4])
        k.dve.scalar_tensor_tensor(out=xn[:rows, :], in0=xt[:rows, :], scalar=ss[:rows, 2:3], in1=gbc[:rows, :], op0=ALU.mult, op1=ALU.mult)
        for kk in range(8):
            k.pe.transpose(out=pT[:, kk, :rows], in_=xn[:rows, kk * 128:(kk + 1) * 128], identity=identb[:rows, :rows])
        k.act.copy(out=xnT[:, :, :rows], in_=pT[:, :, :rows])

    def load_w_bf16(Wsb, wd, K, N):
        with k.scope():
            stg = [k.sb('wstg', [128, N], F32) for _ in range(2)]
            for kk in range(K // 128):
                s = stg[kk % 2]
                k.dma(s[:, :], wd[kk * 128:(kk + 1) * 128, :])
                (k.pool if kk % 2 else k.dve).tensor_copy(out=Wsb[:, kk, :], in_=s[:, :])

    with k.scope():
        W0 = k.sb('W0', [128, 8, EVC], BF16)
        load_w_bf16(W0, w_in0, 1024, EVC)
        gbc = k.sb('gbc', [128, 1024]); k.dma(gbc[:, :], norm_mix[0:1, :].pbc(128))
        gb = k.sb('gb', [128, 24]); k.dma(gb[:, :], gate_b[0:1, :].pbc(128))
        xt = [k.sb('xt', [128, 1024]) for _ in range(2)]
        junk = k.sb('junk', [128, 1024], BF16)
        xn = k.sb('xn', [128, 1024], BF16)
        ss = k.sb('ss', [128, 4])
        xnT = k.sb('xnT', [128, 8, 128], BF16)
        proj = [k.sb('proj', [128, EVC]) for _ in range(2)]
        cs = k.sb('cs', [128, 64])
        tmp = [k.sb('tmp%d' % i, [128, 8, 32]) for i in range(4)]
        qb = k.sb('qb', [128, 8, 64], BF16)
        kb = k.sb('kb', [128, 6, 64], BF16)
        vb = k.sb('vb', [128, 3, 2, 65], BF16)
        k.dve.memset(ap=vb[:, :, :, :], constant=1.0) if False else k.op('dve', lambda e: e.memset(vb[:, :, :, :].ap, 1.0), (), (vb,))
        gt = k.sb('gt', [128, 24])
        qT = k.sb('qT', [64, 8, 128], BF16); kT = k.sb('kT', [64, 6, 128], BF16); vcT = k.sb('vcT', [64, 2, 128], BF16)
        pT = k.ps('pT', [128, 8, 128], BF16)
        pA = [k.ps('pA', [128, 512]) for _ in range(2)]
        pQ = k.ps('pQ', [64, 8, 128], BF16)
        pK = k.ps('pK', [64, 8, 128], BF16)
        ci = 0
        for ti, (r0, rows) in enumerate(tiles):
            x = xt[ti % 2]
            pj = proj[ti % 2]
            src = xp[r0:r0 + rows, :] if r0 < Tn else xs[0:16, :]
            k.dma(x[:rows, :], src)
            k.dma(cs[:rows, :], ropecs[r0:r0 + rows, :])
            rmsnorm_T(x, rows, gbc, xn, pT, xnT, ss, junk)
            for c0 in range(0, EVC, 512):
                w = min(512, EVC - c0)
                ps = pA[ci % 2]
                for kk in range(8):
                    k.pe.matmul(out=ps[:rows, :w], lhsT=xnT[:, kk, :rows], rhs=W0[:, kk, c0:c0 + w], start=(kk == 0), stop=(kk == 7))
                if ci % 2:
                    k.act.copy(out=pj[:rows, c0:c0 + w], in_=ps[:rows, :w])
                else:
                    k.dve.tensor_copy(out=pj[:rows, c0:c0 + w], in_=ps[:rows, :w])
                ci += 1
            k.dma(RW[r0:r0 + rows, :], pj[:rows, 0:RWC])
            k.dve.tensor_tensor(out=gt[:rows, :], in0=pj[:rows, 3072:3096], in1=gb[:rows, :], op=ALU.add)
            k.act.activation(out=gt[:rows, :], in_=gt[:rows, :], func=AF.Sigmoid)
            k.dma(GT[r0:r0 + rows, :], gt[:rows, :])
            cosb = lambda n: cs[:rows, 0:32].unsq(1).bc([rows, n, 32])
            sinb = lambda n: cs[:rows, 32:64].unsq(1).bc([rows, n, 32])
            qv = pj[:rows, 1792:2304].rearrange("p (h d) -> p h d", h=8)
            views = [(qv, 8, qb[:rows, :, :])]
            for c in range(3):
                kv_ = pj[:rows, 2304 + c * 256:2304 + c * 256 + 128].rearrange("p (g d) -> p g d", g=2)
                views.append((kv_, 2, None))
            for vi, (xv, n, ob) in enumerate(views):
                E1 = k.dve if vi % 2 == 0 else k.pool
                x1 = xv[:, :, 0:32]; x2 = xv[:, :, 32:64]
                t = [tt[:rows, 0:n, :] for tt in tmp]
                E1.tensor_tensor(out=t[0], in0=x1, in1=cosb(n), op=ALU.mult)
                E1.tensor_tensor(out=t[1], in0=x2, in1=sinb(n), op=ALU.mult)
                E1.tensor_tensor(out=t[2], in0=x2, in1=cosb(n), op=ALU.mult)
                E1.tensor_tensor(out=t[3], in0=x1, in1=sinb(n), op=ALU.mult)
                if ob is not None:
                    E1.tensor_tensor(out=ob[:, :, 0:32], in0=t[0], in1=t[1], op=ALU.subtract)
                    E1.tensor_tensor(out=ob[:, :, 32:64], in0=t[2], in1=t[3], op=ALU.add)
                else:
                    E1.tensor_tensor(out=x1, in0=t[0], in1=t[1], op=ALU.subtract)
                    E1.tensor_tensor(out=x2, in0=t[2], in1=t[3], op=ALU.add)
            kvv = pj[:rows, 2304:3072].rearrange("p (c j g d) -> p c j g d", c=3, j=2, g=2)
            for c in range(3):
                k.dve.tensor_copy(out=kb[:rows, 2 * c:2 * c + 2, :], in_=kvv[:, c, 0, :, :])
                k.pool.tensor_copy(out=vb[:rows, c, :, 0:64], in_=kvv[:, c, 1, :, :])
            if r0 < Tn:
                k.dma(o_nsa_p[r0:r0 + rows, :], pj[:rows, 2304:2816])
                if r0 >= Tn - WN:
                    k.dma(o_win_p[r0 - (Tn - WN):r0 - (Tn - WN) + rows, :], pj[:rows, 2816:3072])
                if ti == NT - 1:
                    k.dma(o_sh_p[0:1, :], pj[127:128, 0:RWC])
            else:
                k.dma(o_nsa_s[0:16, :], pj[:16, 2304:2816])
                for bl in range(4):
                    k.dma(o_win_s[bl, 508:512, :], pj[bl * 4:bl * 4 + 4, 2816:3072])
                    k.dma(o_win_s[bl, 0:508, :], st_win[bl, 4:512, :])
                    k.dma(o_sh_s[bl:bl + 1, :], pj[bl * 4 + 3:bl * 4 + 4, 0:RWC])
            for h in range(8):
                k.pe.transpose(out=pQ[:, h, :rows], in_=qb[:rows, h, :], identity=identb[:rows, :rows])
            k.act.copy(out=qT[:, :, :rows], in_=pQ[:, :, :rows])
            for h in range(6):
                k.pe.transpose(out=pK[:, h, :rows], in_=kb[:rows, h, :], identity=identb[:rows, :rows])
            for g in range(2):
                k.pe.transpose(out=pK[:, 6 + g, :rows], in_=vb[:rows, 0, g, 0:64], identity=identb[:rows, :rows])
            k.dve.tensor_copy(out=kT[:, :, :rows], in_=pK[:, 0:6, :rows])
            k.dve.tensor_copy(out=vcT[:, :, :rows], in_=pK[:, 6:8, :rows])
            k.dma(QT[:, :, r0:r0 + rows], qT[:, :, :rows])
            k.dma(KT[:, :, r0:r0 + rows], kT[:, :, :rows])
            k.dma(VcT[:, :, r0:r0 + rows], vcT[:, :, :rows])
            k.dma(VT[r0:r0 + rows, :, :, :], vb[:rows, 1:3, :, :])
    if 'B' in stages:
        phase_rwkv(k, locals())
    if 'C' in stages:
        phase_nsa_prompt(k, locals())
    LL = locals()
    if 'S' in stages:
        phase_nsa_sample(k, LL)
    if 'D' in stages:
        layer_tail(k, LL, 0, AO, w_out0, None, H1, None)
    if 'E' in stages:
        phase_proj1(k, LL)
    if 'F' in stages:
        phase_moba_prompt(k, LL)
    if 'M' in stages:
        phase_moba_sample(k, LL)
    if 'G' in stages:
        layer_tail(k, LL, 1, MO, w_out1, H1, None, (y_p, y_s))
    nc2 = k.finish()
    return nc2, k


def phase_rwkv(k, L):
    Tn = L['Tn']; RW = L['RW']; AO = L['AO']; identf = L['identf']; identb = L['identb']
    rw_mu = L['rw_mu']; rw_vec = L['rw_vec']; st_wkv = L['st_wkv']; st_shift = L['st_shift']
    with k.scope():
        mu = k.sb('mu', [64, RWC]); k.dma(mu[:, :], rw_mu[0:1, :].pbc(64))
        vec = k.sb('vec', [64, 7, 512])
        for i in range(7):
            k.dma(vec[:, i, :], rw_vec[i:i + 1, :].pbc(64))
        w0b, a0b, kkb, kab, rkb, lgb, lbb = [vec[:, i, :] for i in range(7)]
        wup = k.sb('wup', [64, 512]); k.dma(wup[:, :], L['rw_wup'][:, :])
        aup = k.sb('aup', [64, 512]); k.dma(aup[:, :], L['rw_aup'][:, :])
        gup = k.sb('gup', [128, 512]); k.dma(gup[:, :], L['rw_gup'][:, :])
        mk = k.sb('mk', [64, 3, 64]); k.dma(mk[:, :, :], L['masks64'][:, :, :])
        ones = k.sb('ones', [64, 1]); k.op('dve', lambda e: e.memset(ones[:, :].ap, 1.0), (), (ones,))
        banks = [k.ps('bank', [128, 512]) for _ in range(6)]
        pfbs = [k.ps('pfb', [64, 8, 64], BF16) for _ in range(2)]
        bi = [0]

        def bank():
            b = banks[bi[0] % 6]
            bi[0] += 1
            return b

        ST = k.sb('ST', [64, 8, 64]); STb = k.sb('STb', [64, 8, 64], BF16)
        vbs = [k.sb('vb16', [64, 512], BF16) for _ in range(2)]
        cur = [k.sb('cur', [64, RWC]) for _ in range(2)]
        prv = [k.sb('prv', [64, RWC]) for _ in range(2)]
        Lt = k.sb('Lt', [64, 256]); LT = k.sb('LT', [128, 3, 64])
        lw = k.sb('lw', [64, 512]); av = k.sb('av', [64, 512]); gvs = [k.sb('gv', [64, 512]) for _ in range(2)]
        kk = k.sb('kk', [64, 512]); sq = k.sb('sq', [64, 512]); k2s = [k.sb('k2', [64, 512]) for _ in range(2)]; bv = k.sb('bv', [64, 512])
        sm = k.sb('sm', [64, 8, 4]); sm2 = k.sb('sm2', [64, 8, 4])
        eP = k.sb('eP', [64, 512]); eN = k.sb('eN', [64, 512]); ePm = k.sb('ePm', [64, 512])
        Fs = [k.sb('F', [64, 4, 512], BF16) for _ in range(2)]
        FTs = [[k.sb('FT%d' % i, [64, 8, 64], BF16) for i in range(4)] for _ in range(2)]
        GCs = [k.sb('GC', [64, 8]) for _ in range(2)]
        Mbs = [[k.sb('Mb%d' % i, [64, 8, 64], BF16) for i in range(5)] for _ in range(2)]
        Bt = [k.sb('Bt%d' % i, [64, 8, 64], BF16) for i in range(2)]
        BTt = [k.sb('BTt%d' % i, [64, 8, 64], BF16) for i in range(2)]
        Nts = [k.sb('Nt', [64, 8, 64], BF16) for _ in range(2)]
        Zn = k.sb('Zn', [64, 8, 64], BF16); UT = k.sb('UT', [64, 8, 64], BF16)
        yv = k.sb('yv', [64, 512]); yc = k.sb('yc', [64, 512]); t1 = k.sb('t1', [64, 512]); ob = k.sb('ob', [64, 512], BF16)
        Stmp = k.sb('Stmp', [64, 8, 64])
        ci = [0]

        def hv(t, C):
            return t[:C, :].rearrange("p (h d) -> p h d", h=8)

        def chunk(r0, C, first_prev):
            sb_ = ci[0] % 2
            c = cur[sb_]; p = prv[sb_]; ci[0] += 1
            gv = gvs[sb_]; k2 = k2s[sb_]; vb16 = vbs[sb_]; F = Fs[sb_]; FT = FTs[sb_]; GC = GCs[sb_]; Mb = Mbs[sb_]; Nt = Nts[sb_]
            k.dma(c[:C, :], RW[r0:r0 + C, :])
            if first_prev is None:
                k.dma(p[:C, :], RW[r0 - 1:r0 - 1 + C, :])
            else:
                if first_prev == 'zero':
                    k.op('dve', lambda e: e.memset(p[0:1, :].ap, 0.0), (), (p,))
                else:
                    k.dma(p[0:1, :], first_prev)
                k.dma(p[1:C, :], RW[r0:r0 + C - 1, :])
            k.dve.tensor_tensor(out=p[:C, :], in0=p[:C, :], in1=c[:C, :], op=ALU.subtract)
            k.pool.tensor_tensor(out=p[:C, :], in0=p[:C, :], in1=mu[:C, :], op=ALU.mult)
            k.dve.tensor_tensor(out=c[:C, :], in0=c[:C, :], in1=p[:C, :], op=ALU.add)
            xm = c
            k.pool.tensor_copy(out=vb16[:C, :], in_=c[:C, 1024:1536])
            r_ = xm[:C, 0:512]; k_ = xm[:C, 512:1024]; v_ = xm[:C, 1024:1536]
            k.act.activation(out=Lt[:C, 0:64], in_=xm[:C, 1536:1600], func=AF.Tanh)
            k.act.activation(out=Lt[:C, 128:256], in_=xm[:C, 1664:1792], func=AF.Sigmoid)
            k.dve.tensor_copy(out=Lt[:C, 64:128], in_=xm[:C, 1600:1664])
            pb = bank()
            pl = pb[:, 0:192].rearrange("p (a t) -> p a t", a=3)
            k.pe.transpose(out=pl[0:64, 0, :C], in_=Lt[:C, 0:64], identity=identf[:C, :C])
            k.pe.transpose(out=pl[0:64, 1, :C], in_=Lt[:C, 64:128], identity=identf[:C, :C])
            k.pe.transpose(out=pl[:, 2, :C], in_=Lt[:C, 128:256], identity=identf[:C, :C])
            k.dve.tensor_copy(out=LT[0:64, 0:2, :C], in_=pl[0:64, 0:2, :C])
            k.dve.tensor_copy(out=LT[:, 2, :C], in_=pl[:, 2, :C])
            pW = bank(); pA = bank(); pG = bank()
            k.pe.matmul(out=pW[:C, :], lhsT=LT[0:64, 0, :C], rhs=wup[:, :], start=True, stop=True)
            k.pe.matmul(out=pA[:C, :], lhsT=LT[0:64, 1, :C], rhs=aup[:, :], start=True, stop=True)
            k.pe.matmul(out=pG[:C, :], lhsT=LT[:, 2, :C], rhs=gup[:, :], start=True, stop=True)
            k.dve.tensor_tensor(out=lw[:C, :], in0=pW[:C, :], in1=w0b[:C, :], op=ALU.add)
            k.act.activation(out=lw[:C, :], in_=lw[:C, :], func=AF.Sigmoid)
            k.dve.tensor_scalar(out=lw[:C, :], in0=lw[:C, :], scalar1=-0.6065306597126334, scalar2=None, op0=ALU.mult)
            k.dve.tensor_tensor(out=av[:C, :], in0=pA[:C, :], in1=a0b[:C, :], op=ALU.add)
            k.act.activation(out=av[:C, :], in_=av[:C, :], func=AF.Sigmoid)
            k.act.copy(out=gv[:C, :], in_=pG[:C, :])
            k.pool.tensor_tensor(out=kk[:C, :], in0=k_, in1=kkb[:C, :], op=ALU.mult)
            k.pool.tensor_tensor(out=sq[:C, :], in0=kk[:C, :], in1=kk[:C, :], op=ALU.mult)
            k.dve.reduce_sum(out=sm[:C, :, 0], in_=hv(sq, C), axis=AX.X)
            k.act.activation(out=sm[:C, :, 1], in_=sm[:C, :, 0], func=AF.Sqrt)
            k.dve.tensor_scalar(out=sm[:C, :, 1], in0=sm[:C, :, 1], scalar1=1e-12, scalar2=None, op0=ALU.max)
            k.dve.reciprocal(out=sm[:C, :, 2], in_=sm[:C, :, 1])
            if r0 == 0:
                k.dbg('kk0', kk[:C, :], [64, 512]); k.dbg('sq', sq[:C, :], [64, 512]); k.dbg('sm', sm[:C, :, :], [64, 8, 4])
            k.dve.tensor_tensor(out=hv(kk, C), in0=hv(kk, C), in1=sm[:C, :, 2:3].bc([C, 8, 64]), op=ALU.mult)
            k.dve.scalar_tensor_tensor(out=k2[:C, :], in0=av[:C, :], scalar=-1.0, in1=kab[:C, :], op0=ALU.add, op1=ALU.mult)
            k.dve.scalar_tensor_tensor(out=k2[:C, :], in0=k2[:C, :], scalar=1.0, in1=k_, op0=ALU.add, op1=ALU.mult)
            k.pool.tensor_tensor(out=bv[:C, :], in0=kk[:C, :], in1=av[:C, :], op=ALU.mult)
            pC = bank()
            k.pe.matmul(out=pC[:C, :], lhsT=mk[:C, 0, :C], rhs=lw[:C, :], start=True, stop=True)
            k.act.activation(out=eP[:C, :], in_=pC[:C, :], func=AF.Exp)
            k.act.activation(out=eN[:C, :], in_=pC[:C, :], func=AF.Exp, scale=-1.0)
            k.dve.tensor_tensor(out=ePm[:C, :], in0=pC[:C, :], in1=lw[:C, :], op=ALU.subtract)
            k.act.activation(out=ePm[:C, :], in_=ePm[:C, :], func=AF.Exp)
            k.dve.tensor_tensor(out=F[:C, 0, :], in0=kk[:C, :], in1=ePm[:C, :], op=ALU.mult)
            k.pool.tensor_tensor(out=F[:C, 1, :], in0=bv[:C, :], in1=eN[:C, :], op=ALU.mult)
            k.dve.tensor_tensor(out=F[:C, 2, :], in0=k2[:C, :], in1=eN[:C, :], op=ALU.mult)
            k.pool.tensor_tensor(out=F[:C, 3, :], in0=r_, in1=eP[:C, :], op=ALU.mult)
            pg = bank()
            for h in range(8):
                k.pe.matmul(out=pg[0:64, h:h + 1], lhsT=lw[:C, h * 64:(h + 1) * 64], rhs=ones[:C, 0:1], start=True, stop=True)
            k.act.activation(out=GC[:, :], in_=pg[0:64, 0:8], func=AF.Exp)
            for kind in range(4):
                pfv = pfbs[kind % 2]
                for h in range(8):
                    k.pe.transpose(out=pfv[:, h, :C], in_=F[:C, kind, h * 64:(h + 1) * 64], identity=identb[:C, :C])
                k.dve.tensor_copy(out=FT[kind][:, :, :C], in_=pfv[:, :, :C])
            aT, bT, khT, rT = FT
            combos = [(bT, aT, 1), (aT, bT, 2), (khT, aT, 1), (bT, rT, 0), (khT, rT, 0)]
            for i, (lt, rt, mi) in enumerate(combos):
                pm = bank()
                pmv = pm[0:64, :].rearrange("p (h t) -> p h t", h=8)
                for h in range(8):
                    k.pe.matmul(out=pmv[:C, h, :C], lhsT=lt[:, h, :C], rhs=rt[:, h, :C], start=True, stop=True)
                (k.dve if i % 2 == 0 else k.pool).tensor_tensor(out=Mb[i][:C, :, :C], in0=pmv[:C, :, :C], in1=mk[:C, mi:mi + 1, :C].bc([C, 8, C]), op=ALU.mult) if i % 2 == 0 else k.dve.tensor_tensor(out=Mb[i][:C, :, :C], in0=pmv[:C, :, :C], in1=mk[:C, mi:mi + 1, :C].bc([C, 8, C]), op=ALU.mult)
            A, AT, Mka, Mbr, Mkr = Mb
            k.dve.tensor_tensor(out=Nt[:C, :, :C], in0=identf[:C, 0:C].unsq(1).bc([C, 8, C]), in1=A[:C, :, :C], op=ALU.subtract)
            Bc, BTc = A, AT
            nlev = {64: 5, 4: 1}[C]
            for lev in range(nlev):
                Bn = Bt[lev % 2]; BTn = BTt[lev % 2]
                p1 = bank(); p2 = bank()
                p1v = p1[0:64, :].rearrange("p (h t) -> p h t", h=8); p2v = p2[0:64, :].rearrange("p (h t) -> p h t", h=8)
                if lev < nlev - 1:
                    for h in range(8):
                        k.pe.matmul(out=p1v[:C, h, :C], lhsT=BTc[:C, h, :C], rhs=Bc[:C, h, :C], start=True, stop=True)
                for h in range(8):
                    k.pe.matmul(out=p2v[:C, h, :C], lhsT=Bc[:C, h, :C], rhs=BTc[:C, h, :C], start=True, stop=True)
                if lev < nlev - 1:
                    k.dve.tensor_copy(out=Bn[:C, :, :C], in_=p1v[:C, :, :C])
                k.dve.tensor_copy(out=BTn[:C, :, :C], in_=p2v[:C, :, :C])
                p3 = bank(); p3v = p3[0:64, :].rearrange("p (h t) -> p h t", h=8)
                for h in range(8):
                    k.pe.matmul(out=p3v[:C, h, :C], lhsT=BTn[:C, h, :C], rhs=Nt[:C, h, :C], start=True, stop=True)
                k.dve.tensor_tensor(out=Nt[:C, :, :C], in0=Nt[:C, :, :C], in1=p3v[:C, :, :C], op=ALU.add)
                Bc, BTc = Bn, BTn
            def s2():
                vh = lambda h: vb16[:C, h * 64:(h + 1) * 64]
                pz = bank(); pzv = pz[0:64, :].rearrange("p (h t) -> p h t", h=8)
                for h in range(8):
                    k.pe.matmul(out=pzv[:C, h, :], lhsT=aT[:, h, :C], rhs=STb[:, h, :], start=True, stop=False)
                    k.pe.matmul(out=pzv[:C, h, :], lhsT=Mka[:C, h, :C], rhs=vh(h), start=False, stop=True)
                k.dve.tensor_scalar(out=Zn[:C, :, :], in0=pzv[:C, :, :], scalar1=-1.0, scalar2=None, op0=ALU.mult)
                pu = bank(); puv = pu[0:64, :].rearrange("p (h t) -> p h t", h=8)
                for h in range(8):
                    k.pe.matmul(out=puv[:C, h, :], lhsT=Nt[:C, h, :C], rhs=Zn[:C, h, :], start=True, stop=True)
                k.dve.tensor_copy(out=UT[:C, :, :], in_=puv[:C, :, :])
                py = bank(); pyv = py[0:64, :].rearrange("p (h t) -> p h t", h=8)
                for h in range(8):
                    k.pe.matmul(out=pyv[:C, h, :], lhsT=rT[:, h, :C], rhs=STb[:, h, :], start=True, stop=False)
                    k.pe.matmul(out=pyv[:C, h, :], lhsT=Mbr[:C, h, :C], rhs=UT[:C, h, :], start=False, stop=False)
                    k.pe.matmul(out=pyv[:C, h, :], lhsT=Mkr[:C, h, :C], rhs=vh(h), start=False, stop=True)
                k.act.copy(out=yv[:C, :], in_=py[:C, :])
                pS = bank(); pSv = pS[0:64, :].rearrange("p (h t) -> p h t", h=8)
                for h in range(8):
                    k.pe.matmul(out=pSv[:, h, :], lhsT=F[:C, 1, h * 64:(h + 1) * 64], rhs=UT[:C, h, :], start=True, stop=False)
                    k.pe.matmul(out=pSv[:, h, :], lhsT=F[:C, 2, h * 64:(h + 1) * 64], rhs=vh(h), start=False, stop=True)
                k.dve.tensor_tensor(out=ST[:, :, :], in0=ST[:, :, :], in1=pSv[:, :, :], op=ALU.add)
                k.dve.tensor_tensor(out=ST[:, :, :], in0=ST[:, :, :], in1=GC[:, :].unsq(2).bc([64, 8, 64]), op=ALU.mult)
                k.pool.tensor_copy(out=STb[:, :, :], in_=ST[:, :, :])
                k.dve.reduce_sum(out=sm2[:C, :, 0], in_=hv(yv, C), axis=AX.X)
                k.dve.tensor_scalar(out=sm2[:C, :, 0], in0=sm2[:C, :, 0], scalar1=1.0 / 64, scalar2=None, op0=ALU.mult)
                k.dve.tensor_tensor(out=hv(yc, C), in0=hv(yv, C), in1=sm2[:C, :, 0:1].bc([C, 8, 64]), op=ALU.subtract)
                k.pool.tensor_tensor(out=t1[:C, :], in0=yc[:C, :], in1=yc[:C, :], op=ALU.mult)
                k.dve.reduce_sum(out=sm2[:C, :, 1], in_=hv(t1, C), axis=AX.X)
                k.dve.tensor_scalar(out=sm2[:C, :, 1], in0=sm2[:C, :, 1], scalar1=1.0 / 64, scalar2=64e-5, op0=ALU.mult, op1=ALU.add)
                k.act.activation(out=sm2[:C, :, 1], in_=sm2[:C, :, 1], func=AF.Sqrt)
                k.dve.reciprocal(out=sm2[:C, :, 2], in_=sm2[:C, :, 1])
                k.dve.tensor_tensor(out=hv(yc, C), in0=hv(yc, C), in1=sm2[:C, :, 2:3].bc([C, 8, 64]), op=ALU.mult)
                k.dve.tensor_tensor(out=yc[:C, :], in0=yc[:C, :], in1=lgb[:C, :], op=ALU.mult)
                k.dve.tensor_tensor(out=yc[:C, :], in0=yc[:C, :], in1=lbb[:C, :], op=ALU.add)
                k.pool.tensor_tensor(out=t1[:C, :], in0=r_, in1=k2[:C, :], op=ALU.mult)
                k.pool.tensor_tensor(out=t1[:C, :], in0=t1[:C, :], in1=rkb[:C, :], op=ALU.mult)
                k.dve.reduce_sum(out=sm2[:C, :, 3], in_=hv(t1, C), axis=AX.X)
                k.dve.tensor_tensor(out=hv(t1, C), in0=xm[:C, 1024:1536].rearrange("p (h d) -> p h d", h=8), in1=sm2[:C, :, 3:4].bc([C, 8, 64]), op=ALU.mult)
                k.dve.tensor_tensor(out=yc[:C, :], in0=yc[:C, :], in1=t1[:C, :], op=ALU.add)
                k.dve.tensor_tensor(out=ob[:C, :], in0=yc[:C, :], in1=gv[:C, :], op=ALU.mult)
                k.dma(AO[r0:r0 + C, 0:512], ob[:C, :])
                if r0 == 0:
                    k.dbg('xm', xm[:C, :], [64, RWC]); k.dbg('lw', lw[:C, :], [64, 512]); k.dbg('av', av[:C, :], [64, 512])
                    k.dbg('kk', kk[:C, :], [64, 512]); k.dbg('k2', k2[:C, :], [64, 512]); k.dbg('F', F[:C, :, :], [64, 4, 512])
                    k.dbg('aT', FT[0][:, :, :], [64, 8, 64]); k.dbg('A', Mb[0][:, :, :], [64, 8, 64]); k.dbg('AT', Mb[1][:, :, :], [64, 8, 64])
                    k.dbg('N', Nt[:, :, :], [64, 8, 64]); k.dbg('UT', UT[:, :, :], [64, 8, 64]); k.dbg('yv', yv[:C, :], [64, 512])
                    k.dbg('ST', ST[:, :, :], [64, 8, 64]); k.dbg('GC', GC[:, :], [64, 8]); k.dbg('gv', gv[:C, :], [64, 512])
                    k.dbg('ob', ob[:C, :], [64, 512], BF16)
            return s2

        def store_state(dst):
            pt = bank(); ptv = pt[0:64, :].rearrange("p (h t) -> p h t", h=8)
            for h in range(8):
                k.pe.transpose(out=ptv[:, h, :], in_=ST[:, h, :], identity=identf[0:64, 0:64])
            k.dve.tensor_copy(out=Stmp[:, :, :], in_=ptv[:, :, :])
            k.dma(dst.rearrange("h i j -> i h j"), Stmp[:, :, :])

        k.op('dve', lambda e: e.memset(ST[:, :, :].ap, 0.0), (), (ST,))
        k.op('dve', lambda e: e.memset(STb[:, :, :].ap, 0.0), (), (STb,))
        pend = None
        for ch in range(Tn // 64):
            nxt = chunk(ch * 64, 64, 'zero' if ch == 0 else None)
            if pend is not None:
                pend()
            pend = nxt
        pend()
        store_state(L['o_wkv_p'][:, :, :])
        for bl in range(4):
            k.dma(Stmp[:, :, :], st_wkv[bl].rearrange("h i j -> i h j"))
            pt = bank(); ptv = pt[0:64, :].rearrange("p (h t) -> p h t", h=8)
            for h in range(8):
                k.pe.transpose(out=ptv[:, h, :], in_=Stmp[:, h, :], identity=identf[0:64, 0:64])
            k.dve.tensor_copy(out=ST[:, :, :], in_=ptv[:, :, :])
            k.dve.tensor_copy(out=STb[:, :, :], in_=ptv[:, :, :])
            chunk(Tn + bl * 4, 4, st_shift[bl:bl + 1, :])()
            store_state(L['o_wkv_s'][bl])


def pipe(items, sc, pv):
    items = list(items)
    if not items:
        return
    nxt = sc(items[0])
    for i, it in enumerate(items):
        cur_e = nxt
        if i + 1 < len(items):
            nxt = sc(items[i + 1])
        pv(it, cur_e)


def gelu_tanh(k, out_bf, x, tmpa, tmpb, shape_idx):
    k.dve.tensor_tensor(out=tmpa, in0=x, in1=x, op=ALU.mult)
    k.dve.tensor_scalar(out=tmpa, in0=tmpa, scalar1=0.044715, scalar2=1.0, op0=ALU.mult, op1=ALU.add)
    k.dve.tensor_tensor(out=tmpa, in0=tmpa, in1=x, op=ALU.mult)
    k.act.activation(out=tmpb, in_=tmpa, func=AF.Tanh, scale=0.7978845608028654)
    k.dve.tensor_scalar(out=tmpb, in0=tmpb, scalar1=1.0, scalar2=0.5, op0=ALU.add, op1=ALU.mult)
    k.dve.tensor_tensor(out=out_bf, in0=tmpb, in1=x, op=ALU.mult)


def load_cmp_weights(k, L, pbias):
    cmp_w1 = L['cmp_w1']; cmp_pe = L['cmp_pe']; cmp_w2 = L['cmp_w2']
    W = {}
    tiles_ = [(k.sb('cw1_%d' % kind, [64, 32, 64], BF16), k.sb('cw2_%d' % kind, [64, 64], BF16),
               k.sb('peT%d' % kind, [64, 32], BF16), k.sb('cbias%d' % kind, [64, 1])) for kind in range(2)]
    sc_ = k.scope(); sc_.__enter__()
    stg = k.sb('cw_stg', [64, 32, 64]); stg2 = k.sb('cw_stg2', [64, 64]); pes = k.sb('pe_stg', [64, 32])
    for kind in range(2):
        w1, w2, peT, bias = tiles_[kind]
        k.dma(stg[:, :, :], cmp_w1[kind].rearrange("c d e -> d c e"))
        k.dve.tensor_copy(out=w1[:, :, :], in_=stg[:, :, :])
        k.dma(stg2[:, :], cmp_w2[kind])
        k.dve.tensor_copy(out=w2[:, :], in_=stg2[:, :])
        k.dma(pes[:, :], cmp_pe[kind].rearrange("c d -> d c"), allow_slow_non_contiguous=True)
        k.dve.tensor_copy(out=peT[:, :], in_=pes[:, :])
        for c in range(32):
            k.pe.matmul(out=pbias[0:64, kind:kind + 1], lhsT=w1[:, c, :], rhs=peT[:, c:c + 1], start=(c == 0), stop=(c == 31))
        k.dve.tensor_copy(out=bias[:, :], in_=pbias[0:64, kind:kind + 1])
        W[kind] = (w1, w2, bias)
    sc_.__exit__(None, None, None)
    return W


def compress(k, W, kind, srcT, NBn, ps_bank, hx, ta, tb, hbf):
    w1, w2, bias = W[kind]
    for c in range(32):
        k.pe.matmul(out=ps_bank[0:64, 0:NBn], lhsT=w1[:, c, :], rhs=srcT[:, c:c + 16 * (NBn - 1) + 1:16], start=(c == 0), stop=(c == 31))
    k.act.activation(out=hx[:, 0:NBn], in_=ps_bank[0:64, 0:NBn], func=AF.Identity, bias=bias[:, 0:1])
    gelu_tanh(k, hbf[:, 0:NBn], hx[:, 0:NBn], ta[:, 0:NBn], tb[:, 0:NBn], None)


def phase_nsa_prompt(k, L):
    Tn = L['Tn']; NT = L['NT']; NSEL = L['NSEL']; NB = L['NB']; NBT = L['NBT']
    QT = L['QT']; KT = L['KT']; VcT = L['VcT']; VT = L['VT']; GT = L['GT']; AO = L['AO']; identb = L['identb']
    with k.scope():
        ps_s = [k.ps('ps_s', [128, 512]) for _ in range(3)]
        W = load_cmp_weights(k, L, ps_s[0])
        cm = k.sb('cm', [128, 17, 128], BF16); k.dma(cm[:, :, :], L['CMt'][:, :, :])
        tri = k.sb('tri', [128, 2, 128], BF16); k.dma(tri[:, :, :], L['TRIt'][:, :, :])
        po_sel = [k.ps('po_sel', [128, 4, 65]) for _ in range(1)]
        po_win = [k.ps('po_win', [128, 4, 65]) for _ in range(1)]
        po_c = [k.ps('po_c', [128, 65 + NSEL]) for _ in range(2)]
        ps_t = k.ps('ps_t', [128, 128], BF16)
        si = [0]

        def sbank():
            b = ps_s[si[0] % 3]; si[0] += 1
            return b

        KcT = k.sb('KcT', [64, Tn], BF16); VcTs = k.sb('VcTs', [64, Tn], BF16)
        KA = NSEL + 64
        KsT = k.sb('KsT', [KA, Tn], BF16); KwT = k.sb('KwT', [64, Tn], BF16)
        k.dma(KsT[0:NSEL, :], L['EEXP'][0:NSEL, 0:Tn])
        qsel = [k.sb('qsel', [KA, 4, 128], BF16) for _ in range(2)]
        Vs = k.sb('Vs', [128, NT, 65], BF16); Vw = k.sb('Vw', [128, NT, 65], BF16)
        KcC = k.sb('KcC', [64, NBT * 128], BF16)
        VcC = k.sb('VcC', [128, NBT, 65 + NSEL], BF16)
        hx = k.sb('hx', [64, 512]); ta = k.sb('ta', [64, 512]); tb = k.sb('tb', [64, 512])
        hk = k.sb('hk', [64, 512], BF16); hv_ = k.sb('hv_', [64, 512], BF16)
        q4 = [k.sb('q4', [64, 4, 128], BF16) for _ in range(2)]
        gtile = [k.sb('gtile', [128, 24]) for _ in range(2)]
        fbt = [k.sb('fbt', [128, NSEL]) for _ in range(2)]
        ebuf = [k.sb('ebuf', [128, 512], BF16) for _ in range(3)]
        ei = [0]

        def enext():
            b = ebuf[ei[0] % 3]; ei[0] += 1
            return b

        acc = k.sb('acc', [128, 4, 64]); accb = k.sb('accb', [128, 4, 64], BF16); tmp4 = k.sb('tmp4', [128, 4, 64])
        rr = k.sb('rr', [128, 16]); imp = k.sb('imp', [128, NSEL]); score = k.sb('score', [128, NSEL]); sc2 = k.sb('sc2', [128, NSEL])
        m8 = k.sb('m8', [128, 16]); negs = k.sb('negs', [128, NSEL], BF16)
        negT4 = k.sb('negT4', [NSEL, 4, 128], BF16)
        for g in range(2):
            k.dma(KcT[:, :], KT[:, 0 + g, 0:Tn]); k.dma(VcTs[:, :], VcT[:, g, 0:Tn])
            k.dma(KsT[NSEL:KA, :], KT[:, 2 + g, 0:Tn]); k.dma(KwT[:, :], KT[:, 4 + g, 0:Tn])
            k.dma(Vs[:, :, :], VT[0:Tn, 0, g, :].rearrange("(kt p) c -> p kt c", p=128))
            k.dma(Vw[:, :, :], VT[0:Tn, 1, g, :].rearrange("(kt p) c -> p kt c", p=128))
            k.dma(VcC[:, :, 65:65 + NSEL], L['OVt'][:, :, :])
            k.op('dve', lambda e: e.memset(VcC[:, :, 64:65].ap, 1.0), (), (VcC,))
            pb = sbank()
            compress(k, W, 0, KcT, NB, pb, hx, ta, tb, hk)
            pb2 = sbank()
            k.pe.matmul(out=pb2[0:64, 0:NB], lhsT=W[0][1][:, :], rhs=hk[:, 0:NB], start=True, stop=True)
            k.dve.tensor_copy(out=KcC[:, 0:NB], in_=pb2[0:64, 0:NB])
            pb = sbank()
            compress(k, W, 1, VcTs, NB, pb, hx, ta, tb, hv_)
            for ni in range(NBT):
                nn = min(128, NB - ni * 128)
                pb3 = sbank()
                k.pe.matmul(out=pb3[:nn, 0:64], lhsT=hv_[:, ni * 128:ni * 128 + nn], rhs=W[1][1][:, :], start=True, stop=True)
                k.dve.tensor_copy(out=VcC[:nn, ni, 0:64], in_=pb3[:nn, 0:64])
            def loads(tt):
                k.dma(q4[tt % 2][:, :, :], QT[:, g * 4:(g + 1) * 4, tt * 128:tt * 128 + 128])
                k.dma(qsel[tt % 2][NSEL:KA, :, :], QT[:, g * 4:(g + 1) * 4, tt * 128:tt * 128 + 128])
                k.dma(gtile[tt % 2][:, :], GT[tt * 128:tt * 128 + 128, :])
                k.dma(fbt[tt % 2][:, :], L['FBt'][tt * 128:tt * 128 + 128, :])

            loads(0)
            for tt in range(NT):
                t0 = tt * 128
                q = q4[tt % 2]; gt_ = gtile[tt % 2]; fb = fbt[tt % 2]
                if tt + 1 < NT:
                    loads(tt + 1)
                gv3 = gt_[:, :].rearrange("p (h b) -> p h b", b=3)
                nvalid = min(NB, (t0 + 96) // 16 + 1)
                nnt = (nvalid + 127) // 128
                qf = q[:, :, :].rearrange("p r t -> p (r t)")
                es_ = []
                for ni in range(nnt):
                    nn = min(128, NB - ni * 128)
                    pss = sbank()
                    k.pe.matmul(out=pss[:nn, :], lhsT=KcC[:, ni * 128:ni * 128 + nn], rhs=qf, start=True, stop=True)
                    e = enext()
                    k.act.activation(out=e[:nn, :], in_=pss[:nn, :], func=AF.Exp, scale=0.125)
                    delta = t0 - 2048 * ni
                    if delta < 17 * 128:
                        assert delta >= 0
                        ev = e[:nn, :].rearrange("p (r t) -> p r t", r=4)
                        k.pool.tensor_tensor(out=ev, in0=ev, in1=cm[:nn, delta // 128:delta // 128 + 1, :].bc([nn, 4, 128]), op=ALU.mult)
                    es_.append((e, nn))
                for r in range(4):
                    po = po_c[r % 2]
                    for ni, (e, nn) in enumerate(es_):
                        k.pe.matmul(out=po[:, :], lhsT=e[:nn, r * 128:(r + 1) * 128], rhs=VcC[:nn, ni, :], start=(ni == 0), stop=(ni == nnt - 1))
                    k.dve.tensor_scalar(out=rr[:, r:r + 1], in0=po[:, 64:65], scalar1=1e-30, scalar2=None, op0=ALU.max)
                    k.dve.reciprocal(out=rr[:, 4 + r:5 + r], in_=rr[:, r:r + 1])
                    k.dve.tensor_tensor(out=rr[:, 8 + r:9 + r], in0=rr[:, 4 + r:5 + r], in1=gv3[:, g * 4 + r, 0:1], op=ALU.mult)
                    k.dve.tensor_scalar(out=acc[:, r, :], in0=po[:, 0:64], scalar1=rr[:, 8 + r:9 + r], scalar2=None, op0=ALU.mult)
                    if r == 0:
                        k.dve.tensor_scalar(out=imp[:, :], in0=po[:, 65:65 + NSEL], scalar1=rr[:, 4 + r:5 + r], scalar2=None, op0=ALU.mult)
                    else:
                        k.dve.scalar_tensor_tensor(out=imp[:, :], in0=po[:, 65:65 + NSEL], scalar=rr[:, 4 + r:5 + r], in1=imp[:, :], op0=ALU.mult, op1=ALU.add)
                k.dve.tensor_tensor(out=score[:, :], in0=imp[:, :], in1=fb[:, :], op=ALU.add)
                k.dve.max(out=m8[:, 0:8], in_=score[:, :])
                k.dve.match_replace(out=sc2[:, :], in_to_replace=m8[:, 0:8], in_values=score[:, :], imm_value=-3.0e38)
                k.dve.max(out=m8[:, 8:16], in_=sc2[:, :])
                k.dve.tensor_scalar(out=rr[:, 12:13], in0=m8[:, 15:16], scalar1=-1.0e8, scalar2=None, op0=ALU.max)
                k.dve.tensor_scalar(out=negs[:, :], in0=score[:, :], scalar1=rr[:, 12:13], scalar2=NEG, op0=ALU.is_lt, op1=ALU.mult)
                k.pe.transpose(out=ps_t[0:NSEL, :], in_=negs[:, :], identity=identb[:, :])
                qs_ = qsel[tt % 2]
                k.dve.tensor_copy(out=qs_[0:NSEL, :, :], in_=ps_t[0:NSEL, :].unsq(1).bc([NSEL, 4, 128]))
                qsf = qs_[:, :, :].rearrange("p r t -> p (r t)")
                qf = q[:, :, :].rearrange("p r t -> p (r t)")
                po = po_sel[0]

                def sc_sel(kt):
                    pss = sbank()
                    k.pe.matmul(out=pss[:, :], lhsT=KsT[:, kt * 128:(kt + 1) * 128], rhs=qsf, start=True, stop=True)
                    e = enext()
                    k.act.activation(out=e[:, :], in_=pss[:, :], func=AF.Exp, scale=0.125)
                    if kt == tt:
                        ev = e[:, :].rearrange("p (r t) -> p r t", r=4)
                        k.pool.tensor_tensor(out=ev, in0=ev, in1=tri[:, 0:1, :].bc([128, 4, 128]), op=ALU.mult)
                    return e

                def pv_sel(kt, e):
                    for r in range(4):
                        k.pe.matmul(out=po[:, r, :], lhsT=e[:, r * 128:(r + 1) * 128], rhs=Vs[:, kt, :], start=(kt == 0 and r == 0), stop=(kt == tt), skip_group_check=True)

                pipe(range(tt + 1), sc_sel, pv_sel)
                k.dve.tensor_scalar(out=rr[:, 0:4], in0=po[:, :, 64], scalar1=1e-30, scalar2=None, op0=ALU.max)
                k.dve.reciprocal(out=rr[:, 4:8], in_=rr[:, 0:4])
                k.dve.tensor_tensor(out=rr[:, 8:12], in0=rr[:, 4:8], in1=gv3[:, g * 4:(g + 1) * 4, 1], op=ALU.mult)
                k.dve.tensor_tensor(out=tmp4[:, :, :], in0=po[:, :, 0:64], in1=rr[:, 8:12].unsq(2).bc([128, 4, 64]), op=ALU.mult)
                k.dve.tensor_tensor(out=acc[:, :, :], in0=acc[:, :, :], in1=tmp4[:, :, :], op=ALU.add)
                po = po_win[0]
                k0 = max(0, tt - 4)
                pow_ = po

                def sc_win(kt):
                    pss = sbank()
                    k.pe.matmul(out=pss[:, :], lhsT=KwT[:, kt * 128:(kt + 1) * 128], rhs=qf, start=True, stop=True)
                    e = enext()
                    k.act.activation(out=e[:, :], in_=pss[:, :], func=AF.Exp, scale=0.125)
                    ev = e[:, :].rearrange("p (r t) -> p r t", r=4)
                    if kt == tt:
                        k.pool.tensor_tensor(out=ev, in0=ev, in1=tri[:, 0:1, :].bc([128, 4, 128]), op=ALU.mult)
                    elif kt == tt - 4:
                        k.pool.tensor_tensor(out=ev, in0=ev, in1=tri[:, 1:2, :].bc([128, 4, 128]), op=ALU.mult)
                    return e

                def pv_win(kt, e):
                    for r in range(4):
                        k.pe.matmul(out=pow_[:, r, :], lhsT=e[:, r * 128:(r + 1) * 128], rhs=Vw[:, kt, :], start=(kt == k0 and r == 0), stop=(kt == tt), skip_group_check=True)

                pipe(range(k0, tt + 1), sc_win, pv_win)
                k.dve.tensor_scalar(out=rr[:, 0:4], in0=po[:, :, 64], scalar1=1e-30, scalar2=None, op0=ALU.max)
                k.dve.reciprocal(out=rr[:, 4:8], in_=rr[:, 0:4])
                k.dve.tensor_tensor(out=rr[:, 8:12], in0=rr[:, 4:8], in1=gv3[:, g * 4:(g + 1) * 4, 2], op=ALU.mult)
                k.dve.tensor_tensor(out=tmp4[:, :, :], in0=po[:, :, 0:64], in1=rr[:, 8:12].unsq(2).bc([128, 4, 64]), op=ALU.mult)
                k.dve.tensor_tensor(out=accb[:, :, :], in0=acc[:, :, :], in1=tmp4[:, :, :], op=ALU.add)
                k.dma(AO[t0:t0 + 128, 512 + g * 256:512 + (g + 1) * 256], accb[:, :, :].rearrange("p r d -> p (r d)"))


def layer_tail(k, L, layer, mix, w_out, h_in, h_out, y_out):
    Tn = L['Tn']; NT = L['NT']; identb = L['identb']; H1 = L['H1']; ACTT = L['ACTT']
    rmsnorm_T = L['rmsnorm_T']; load_w_bf16 = L['load_w_bf16']
    xp = L['xp']; xs = L['xs']
    groups = [(i * 512, min(512, Tn - i * 512)) for i in range((Tn + 511) // 512)] + [(Tn, 16)]
    with k.scope():
        Wo = k.sb('Wo', [128, 8, 1024], BF16); load_w_bf16(Wo, w_out, 1024, 1024)
        Wg = k.sb('Wg', [128, 8, DFF], BF16); load_w_bf16(Wg, L['ffn_g'][layer], 1024, DFF)
        Wu = k.sb('Wu', [128, 8, DFF], BF16); load_w_bf16(Wu, L['ffn_u'][layer], 1024, DFF)
        gbc = k.sb('gbc', [128, 1024]); k.dma(gbc[:, :], L['norm_ffn'][layer:layer + 1, :].pbc(128))
        mt = k.sb('mt', [128, 1024], BF16); mT = k.sb('mT', [128, 8, 128], BF16)
        ht = [k.sb('ht', [128, 1024]) for _ in range(2)]
        junk = k.sb('junk', [128, 1024], BF16); xn = k.sb('xn', [128, 1024], BF16); ss = k.sb('ss', [128, 4])
        xnT = k.sb('xnT', [128, 8, 512], BF16); xnT1 = k.sb('xnT1', [128, 8, 128], BF16)
        sg = k.sb('sg', [128, 512], BF16); at = [k.sb('at', [128, 512], BF16) for _ in range(2)]
        pT = k.ps('pT', [128, 8, 128], BF16)
        pA = [k.ps('pA', [128, 512]) for _ in range(2)]
        pG = [k.ps('pG', [128, 512]) for _ in range(2)]; pU = [k.ps('pU', [128, 512]) for _ in range(2)]
        ci = 0
        for (g0, gn) in groups:
            ntile = (gn + 127) // 128
            for j in range(ntile):
                r0 = g0 + j * 128; rows = min(128, gn - j * 128)
                h = ht[ci % 2]; ci += 1
                k.dma(mt[:rows, :], mix[r0:r0 + rows, :])
                if h_in is None:
                    k.dma(h[:rows, :], xp[r0:r0 + rows, :] if r0 < Tn else xs[0:16, :])
                else:
                    k.dma(h[:rows, :], h_in[r0:r0 + rows, :])
                for kk in range(8):
                    k.pe.transpose(out=pT[:, kk, :rows], in_=mt[:rows, kk * 128:(kk + 1) * 128], identity=identb[:rows, :rows])
                k.act.copy(out=mT[:, :, :rows], in_=pT[:, :, :rows])
                for c in range(2):
                    ps = pA[c]
                    for kk in range(8):
                        k.pe.matmul(out=ps[:rows, :], lhsT=mT[:, kk, :rows], rhs=Wo[:, kk, c * 512:(c + 1) * 512], start=(kk == 0), stop=(kk == 7))
                    k.dve.tensor_tensor(out=h[:rows, c * 512:(c + 1) * 512], in0=h[:rows, c * 512:(c + 1) * 512], in1=ps[:rows, :], op=ALU.add)
                k.dma(H1[r0:r0 + rows, :], h[:rows, :])
                rmsnorm_T(h, rows, gbc, xn, pT, xnT1, ss, junk)
                k.dve.tensor_copy(out=xnT[:, :, j * 128:j * 128 + rows], in_=xnT1[:, :, :rows])
            for hc in range(22):
                pg = pG[hc % 2]; pu = pU[hc % 2]
                for kk in range(8):
                    k.pe.matmul(out=pg[:, :gn], lhsT=Wg[:, kk, hc * 128:(hc + 1) * 128], rhs=xnT[:, kk, :gn], start=(kk == 0), stop=(kk == 7))
                for kk in range(8):
                    k.pe.matmul(out=pu[:, :gn], lhsT=Wu[:, kk, hc * 128:(hc + 1) * 128], rhs=xnT[:, kk, :gn], start=(kk == 0), stop=(kk == 7))
                a = at[hc % 2]
                k.act.activation(out=sg[:, :gn], in_=pg[:, :gn], func=AF.Silu)
                k.dve.tensor_tensor(out=a[:, :gn], in0=sg[:, :gn], in1=pu[:, :gn], op=ALU.mult)
                k.dma(ACTT[hc, :, g0:g0 + gn], a[:, :gn])
    with k.scope():
        Wd = k.sb('Wd', [128, 22, 1024], BF16); load_w_bf16(Wd, L['ffn_d'][layer], DFF, 1024)
        gbc = k.sb('gbc', [128, 1024])
        if y_out is not None:
            k.dma(gbc[:, :], L['norm_final'][0:1, :].pbc(128))
        aT = [k.sb('aT', [128, 22, 128], BF16) for _ in range(2)]
        ht = [k.sb('ht', [128, 1024]) for _ in range(2)]
        junk = k.sb('junk', [128, 1024]); ss = k.sb('ss', [128, 4]); yt = k.sb('yt', [128, 1024])
        pA = [k.ps('pA', [128, 512]) for _ in range(4)]
        ci = 0
        for ti, (r0, rows) in enumerate(L['tiles']):
            a = aT[ti % 2]; h = ht[ti % 2]
            k.dma(a[:, :, :rows], ACTT[:, :, r0:r0 + rows].rearrange("c p t -> p c t"))
            k.dma(h[:rows, :], H1[r0:r0 + rows, :])
            for c in range(2):
                ps = pA[ci % 4]; ci += 1
                for hc in range(22):
                    k.pe.matmul(out=ps[:rows, :], lhsT=a[:, hc, :rows], rhs=Wd[:, hc, c * 512:(c + 1) * 512], start=(hc == 0), stop=(hc == 21))
                k.dve.tensor_tensor(out=h[:rows, c * 512:(c + 1) * 512], in0=h[:rows, c * 512:(c + 1) * 512], in1=ps[:rows, :], op=ALU.add)
            if y_out is None:
                k.dma(H1[r0:r0 + rows, :], h[:rows, :])
            else:
                k.act.activation(out=junk[:rows, :], in_=h[:rows, :], func=AF.Square, accum_out=ss[:rows, 0:1])
                k.dve.tensor_scalar(out=ss[:rows, 1:2], in0=ss[:rows, 0:1], scalar1=1.0 / 1024, scalar2=1e-6, op0=ALU.mult, op1=ALU.add)
                k.act.activation(out=ss[:rows, 3:4], in_=ss[:rows, 1:2], func=AF.Sqrt)
                k.dve.reciprocal(out=ss[:rows, 2:3], in_=ss[:rows, 3:4])
                k.dve.scalar_tensor_tensor(out=yt[:rows, :], in0=h[:rows, :], scalar=ss[:rows, 2:3], in1=gbc[:rows, :], op0=ALU.mult, op1=ALU.mult)
                if r0 < Tn:
                    k.dma(y_out[0][r0:r0 + rows, :], yt[:rows, :])
                else:
                    k.dma(y_out[1][0:16, :], yt[:16, :])


def phase_proj1(k, L):
    Tn = L['Tn']; NT = L['NT']; identb = L['identb']; identf = L['identf']; H1 = L['H1']
    NBLK = L['NBLK']; NBLKP = L['NBLKP']
    rmsnorm_T = L['rmsnorm_T']; load_w_bf16 = L['load_w_bf16']
    QT2 = L['QT2']; KT2 = L['KT2']; VT2 = L['VT2']; NS2 = L['NS2']
    with k.scope():
        W1 = k.sb('W1', [128, 8, ODC], BF16); load_w_bf16(W1, L['w_in1'], 1024, ODC)
        gbc = k.sb('gbc', [128, 1024]); k.dma(gbc[:, :], L['norm_mix'][1:2, :].pbc(128))
        xt = [k.sb('xt', [128, 1024]) for _ in range(2)]
        junk = k.sb('junk', [128, 1024], BF16); xn = k.sb('xn', [128, 1024], BF16); ss = k.sb('ss', [128, 4])
        xnT = k.sb('xnT', [128, 8, 128], BF16)
        proj = [k.sb('proj', [128, ODC]) for _ in range(2)]
        cs = k.sb('cs', [128, 64])
        tmp = [k.sb('tmp%d' % i, [128, 16, 32]) for i in range(4)]
        qb = k.sb('qb', [128, 16, 64], BF16); kb = k.sb('kb', [128, 4, 64], BF16)
        vb = k.sb('vb', [128, 4, 65], BF16)
        k.op('dve', lambda e: e.memset(vb[:, :, :].ap, 1.0), (), (vb,))
        ones = k.sb('ones', [128, 1]); k.op('dve', lambda e: e.memset(ones[:, :].ap, 1.0), (), (ones,))
        qT = k.sb('qT', [64, 16, 128], BF16); kT = k.sb('kT', [64, 4, 128], BF16)
        qTf = k.sb('qTf', [64, 16, 128])
        meansT = k.sb('meansT', [64, 4, NBLKP])
        k.op('dve', lambda e: e.memset(meansT[:, :, :].ap, 0.0), (), (meansT,))
        gbm = k.sb('gbm', [128, 2, NBLKP])
        score = k.sb('score', [128, 16, NBLKP]); m8 = k.sb('m8', [128, 16, 8]); thr = k.sb('thr', [128, 16])
        negs = k.sb('negs', [128, 16, NBLKP]); negb = k.sb('negb', [128, 16, NBLKP], BF16)
        pT = k.ps('pT', [128, 8, 128], BF16)
        pA = [k.ps('pA', [128, 512]) for _ in range(2)]
        pQ = k.ps('pQ', [64, 8, 128], BF16)
        pK = k.ps('pK', [64, 4, 128], BF16)
        pQf = [k.ps('pQf', [64, 4, 128]) for _ in range(1)]
        pM = k.ps('pM', [64, 4, NBLKP])
        pGt = k.ps('pGt', [128, 16, NBLKP])
        ci = 0
        for ti, (r0, rows) in enumerate(L['tiles']):
            x = xt[ti % 2]; pj = proj[ti % 2]
            k.dma(x[:rows, :], H1[r0:r0 + rows, :])
            k.dma(cs[:rows, :], L['ropecs'][r0:r0 + rows, :])
            rmsnorm_T(x, rows, gbc, xn, pT, xnT, ss, junk)
            for c0 in range(0, ODC, 512):
                ps = pA[ci % 2]; ci += 1
                for kk in range(8):
                    k.pe.matmul(out=ps[:rows, :], lhsT=xnT[:, kk, :rows], rhs=W1[:, kk, c0:c0 + 512], start=(kk == 0), stop=(kk == 7))
                (k.act.copy if ci % 2 else k.dve.tensor_copy)(out=pj[:rows, c0:c0 + 512], in_=ps[:rows, :])
            cosb = lambda n: cs[:rows, 0:32].unsq(1).bc([rows, n, 32])
            sinb = lambda n: cs[:rows, 32:64].unsq(1).bc([rows, n, 32])
            for vi, (c0, n) in enumerate([(0, 16), (1024, 4)]):
                xv = pj[:rows, c0:c0 + n * 64].rearrange("p (h d) -> p h d", h=n)
                E1 = k.dve if vi == 0 else k.pool
                x1 = xv[:, :, 0:32]; x2 = xv[:, :, 32:64]
                t = [tt_[:rows, 0:n, :] for tt_ in tmp]
                E1.tensor_tensor(out=t[0], in0=x1, in1=cosb(n), op=ALU.mult)
                E1.tensor_tensor(out=t[1], in0=x2, in1=sinb(n), op=ALU.mult)
                E1.tensor_tensor(out=t[2], in0=x2, in1=cosb(n), op=ALU.mult)
                E1.tensor_tensor(out=t[3], in0=x1, in1=sinb(n), op=ALU.mult)
                E1.tensor_tensor(out=x1, in0=t[0], in1=t[1], op=ALU.subtract)
                E1.tensor_tensor(out=x2, in0=t[2], in1=t[3], op=ALU.add)
            qv = pj[:rows, 0:1024].rearrange("p (h d) -> p h d", h=16)
            kv_ = pj[:rows, 1024:1280].rearrange("p (h d) -> p h d", h=4)
            vv = pj[:rows, 1280:1536].rearrange("p (h d) -> p h d", h=4)
            k.dve.tensor_copy(out=qb[:rows, :, :], in_=qv)
            k.pool.tensor_copy(out=kb[:rows, :, :], in_=kv_)
            k.pool.tensor_copy(out=vb[:rows, :, 0:64], in_=vv)
            if r0 < Tn:
                k.dma(L['o_moba_p'][r0:r0 + rows, :], pj[:rows, 1024:1536])
            else:
                k.dma(L['o_moba_s'][0:16, :], pj[:16, 1024:1536])
            for hh in range(2):
                for h in range(8):
                    k.pe.transpose(out=pQ[:, h, :rows], in_=qb[:rows, hh * 8 + h, :], identity=identb[:rows, :rows])
                k.act.copy(out=qT[:, hh * 8:(hh + 1) * 8, :rows], in_=pQ[:, :, :rows])
            for h in range(4):
                k.pe.transpose(out=pK[:, h, :rows], in_=kb[:rows, h, :], identity=identb[:rows, :rows])
            k.dve.tensor_copy(out=kT[:, :, :rows], in_=pK[:, :, :rows])
            k.dma(QT2[:, :, r0:r0 + rows], qT[:, :, :rows])
            k.dma(KT2[:, :, r0:r0 + rows], kT[:, :, :rows])
            k.dma(VT2[r0:r0 + rows, :, :], vb[:rows, :, :])
            for hq in range(4):
                pq = pQf[0]
                for h in range(4):
                    k.pe.transpose(out=pq[:, h, :rows], in_=pj[:rows, (hq * 4 + h) * 64:(hq * 4 + h + 1) * 64], identity=identf[:rows, :rows])
                k.act.copy(out=qTf[:, hq * 4:(hq + 1) * 4, :rows], in_=pq[:, :, :rows])
            if r0 >= Tn:
                k.dma(L['QF2'][:, :, :], qTf[:, :, 0:16])
                continue
            blk = r0 // 256
            for h in range(16):
                k.pe.matmul(out=pGt[:rows, h, :], lhsT=qTf[:, h, :rows], rhs=meansT[:, h // 4, :], start=(h == 0), stop=(h == 15), skip_group_check=True)
            k.dma(gbm[:rows, :, :], L['GBM'][r0:r0 + rows, :, :])
            k.dve.tensor_tensor(out=score[:rows, :, :], in0=pGt[:rows, :, :], in1=gbm[:rows, 0:1, :].bc([rows, 16, NBLKP]), op=ALU.add)
            for h in range(16):
                k.dve.max(out=m8[:rows, h, :], in_=score[:rows, h, :])
            k.dve.tensor_scalar(out=thr[:rows, :], in0=m8[:rows, :, 2], scalar1=-1.0e8, scalar2=None, op0=ALU.max)
            k.dve.tensor_tensor(out=negs[:rows, :, :], in0=score[:rows, :, :], in1=thr[:rows, :].unsq(2).bc([rows, 16, NBLKP]), op=ALU.is_lt)
            k.dve.tensor_tensor(out=negs[:rows, :, :], in0=negs[:rows, :, :], in1=gbm[:rows, 1:2, :].bc([rows, 16, NBLKP]), op=ALU.mult)
            k.dve.tensor_scalar(out=negb[:rows, :, :], in0=negs[:rows, :, :], scalar1=NEG, scalar2=None, op0=ALU.mult)
            k.dma(NS2[r0:r0 + rows, :, :], negb[:rows, :, :])
            for g in range(4):
                first = (r0 % 256 == 0)
                k.pe.matmul(out=pM[:, g, blk:blk + 1], lhsT=pj[:rows, 1024 + g * 64:1024 + (g + 1) * 64], rhs=ones[:rows, 0:1],
                            start=(ti == 0 and g == 0), stop=not first, skip_group_check=True)
            if r0 % 256 == 128:
                k.dve.tensor_scalar(out=meansT[:, :, blk:blk + 1], in0=pM[:, :, blk:blk + 1], scalar1=1.0 / 2048, scalar2=None, op0=ALU.mult)


def phase_moba_prompt(k, L):
    Tn = L['Tn']; NT = L['NT']; NBLK = L['NBLK']; NBLKP = L['NBLKP']; identb = L['identb']
    QT2 = L['QT2']; KT2 = L['KT2']; VT2 = L['VT2']; NS2 = L['NS2']; MO = L['MO']
    with k.scope():
        tri = k.sb('tri', [128, 2, 128], BF16); k.dma(tri[:, :, :], L['TRIt'][:, :, :])
        ps_s = [k.ps('ps_s', [128, 512]) for _ in range(3)]
        po_ = [k.ps('po', [128, 4, 65]) for _ in range(2)]
        ps_t = k.ps('ps_t', [NBLKP, 4, 128], BF16)
        KA = NBLKP + 64
        K2 = k.sb('K2', [KA, Tn], BF16); V2 = k.sb('V2', [128, NT, 65], BF16)
        k.dma(K2[0:NBLKP, :], L['EEXP2'][0:NBLKP, 0:Tn])
        q4 = [k.sb('q4', [KA, 4, 128], BF16) for _ in range(2)]
        ns = [k.sb('ns', [128, 4, NBLKP], BF16) for _ in range(2)]
        negT4 = k.sb('negT4', [NBLKP, 4, 128], BF16)
        ebuf = [k.sb('ebuf', [128, 512], BF16) for _ in range(3)]
        rr = k.sb('rr', [128, 8]); accb = k.sb('accb', [128, 4, 64], BF16)
        cnt = [0]
        for g in range(4):
            k.dma(K2[NBLKP:KA, :], KT2[:, g, 0:Tn])
            k.dma(V2[:, :, :], VT2[0:Tn, g, :].rearrange("(kt p) c -> p kt c", p=128))
            def loads(tt):
                k.dma(q4[tt % 2][NBLKP:KA, :, :], QT2[:, g * 4:(g + 1) * 4, tt * 128:tt * 128 + 128])
                k.dma(ns[tt % 2][:, :, :], NS2[tt * 128:tt * 128 + 128, g * 4:(g + 1) * 4, :])

            loads(0)
            for tt in range(NT):
                t0 = tt * 128
                q = q4[tt % 2]; n_ = ns[tt % 2]
                if tt + 1 < NT:
                    loads(tt + 1)
                for r in range(4):
                    k.pe.transpose(out=ps_t[:, r, :], in_=n_[:, r, :], identity=identb[:, :])
                k.dve.tensor_copy(out=q[0:NBLKP, :, :], in_=ps_t[:, :, :])
                qf = q[:, :, :].rearrange("p r t -> p (r t)")
                po = po_[tt % 2]
                def sc_m(kt):
                    pss = ps_s[cnt[0] % 3]
                    k.pe.matmul(out=pss[:, :], lhsT=K2[:, kt * 128:(kt + 1) * 128], rhs=qf, start=True, stop=True)
                    e = ebuf[cnt[0] % 3]; cnt[0] += 1
                    k.act.activation(out=e[:, :], in_=pss[:, :], func=AF.Exp, scale=0.125)
                    if kt == tt:
                        ev = e[:, :].rearrange("p (r t) -> p r t", r=4)
                        k.pool.tensor_tensor(out=ev, in0=ev, in1=tri[:, 0:1, :].bc([128, 4, 128]), op=ALU.mult)
                    return e

                def pv_m(kt, e):
                    for r in range(4):
                        k.pe.matmul(out=po[:, r, :], lhsT=e[:, r * 128:(r + 1) * 128], rhs=V2[:, kt, :], start=(kt == 0 and r == 0), stop=(kt == tt), skip_group_check=True)

                pipe(range(tt + 1), sc_m, pv_m)
                k.dve.tensor_scalar(out=rr[:, 0:4], in0=po[:, :, 64], scalar1=1e-30, scalar2=None, op0=ALU.max)
                k.dve.reciprocal(out=rr[:, 4:8], in_=rr[:, 0:4])
                k.dve.tensor_tensor(out=accb[:, :, :], in0=po[:, :, 0:64], in1=rr[:, 4:8].unsq(2).bc([128, 4, 64]), op=ALU.mult)
                k.dma(MO[t0:t0 + 128, g * 256:(g + 1) * 256], accb[:, :, :].rearrange("p r d -> p (r d)"))


def page_indices(k, L):
    P = L['P']
    pti = k.sb('pti', [128, 4 * P], I32); ptf = k.sb('ptf', [128, 4 * P]); io = k.sb('io', [128, 1])
    idx = k.sb('idx', [128, 4 * P], I32)
    k.dma(pti[:, :], L['ptab'][:, :].rearrange("b p -> (b p)").unsq(0).pbc(128) if False else L['ptab'][:, :].rearrange("(o b) p -> o (b p)", o=1).pbc(128))
    k.dma(io[:, :], L['IOTA'][:, :])
    k.dve.tensor_copy(out=ptf[:, :], in_=pti[:, :])
    k.dve.tensor_scalar(out=ptf[:, :], in0=ptf[:, :], scalar1=128.0, scalar2=None, op0=ALU.mult)
    k.dve.tensor_tensor(out=ptf[:, :], in0=ptf[:, :], in1=io[:, 0:1].bc([128, 4 * P]), op=ALU.add)
    k.dve.tensor_copy(out=idx[:, :], in_=ptf[:, :])
    return idx


def gather_page(k, pg, cache, idx, col):
    k.raw16('pool', lambda e: e.indirect_dma_start(out=pg[:, :].ap, out_offset=None, in_=cache[:, :].ap,
                                                   in_offset=bass.IndirectOffsetOnAxis(ap=idx[:, col:col + 1].ap, axis=0)),
            reads=[cache.t if isinstance(cache, V) else cache, idx], writes=[pg])


def phase_nsa_sample(k, L):
    Tn = L['Tn']; P = L['P']; LP = L['LP']; NSELS = L['NSELS']; NBS = L['NBS']; NBTS = L['NBTS']
    QT = L['QT']; KT = L['KT']; VT = L['VT']; GT = L['GT']; AO = L['AO']; identb = L['identb']; identf = L['identf']
    cache = L['cache_nsa']; st_win = L['st_win']
    NJ = LP // 64
    with k.scope():
        ps_s = [k.ps('ps_s', [128, 512]) for _ in range(2)]
        W = load_cmp_weights(k, L, ps_s[0])
        idx = page_indices(k, L)
        eexp = k.sb('eexp', [NJ, LP], BF16); k.dma(eexp[:, :], L['EEXP'][0:NJ, 0:LP])
        smk = k.sb('smk', [128, 2, 16], BF16); k.dma(smk[:, :, :], L['SMK'][:, :, :])
        fbs = k.sb('fbs', [4, NSELS]); k.dma(fbs[:, :], L['FBs'][:, :])
        pX = [k.ps('pX', [64, 4, 128]) for _ in range(1)]
        pXb = [k.ps('pXb', [64, 8, 128], BF16) for _ in range(1)]
        stg = [k.sb('stg', [128, 384], BF16) for _ in range(2)]
        po_a = [k.ps('po_a', [4, 2, 65 + NSELS]) for _ in range(2)]
        po_b = k.ps('po_b', [4, 4, 65])
        ps_t = k.ps('ps_t', [128, 16], BF16)
        si = [0]

        def sbank():
            b = ps_s[si[0] % 2]; si[0] += 1
            return b

        pg = [k.sb('pg', [128, 512]) for _ in range(3)]
        KcTs = k.sb('KcTs', [64, 2, LP], BF16); VcTs = k.sb('VcTs', [64, 2, LP], BF16); KsTs = k.sb('KsTs', [64, 2, LP], BF16)
        Vss = k.sb('Vss', [128, P, 2, 65], BF16)
        k.op('dve', lambda e: e.memset(Vss[:, :, :, 64:65].ap, 1.0), (), (Vss,))
        KwTs = k.sb('KwTs', [64, 2, 512], BF16); Vws = k.sb('Vws', [128, 4, 2, 65], BF16)
        k.op('dve', lambda e: e.memset(Vws[:, :, :, 64:65].ap, 1.0), (), (Vws,))
        wt = [k.sb('wt', [128, 256]) for _ in range(2)]
        KcCs = k.sb('KcCs', [64, NBTS * 128], BF16)
        VcCs = k.sb('VcCs', [128, NBTS, 65 + NSELS], BF16)
        k.dma(VcCs[:, :, 65:65 + NSELS], L['OVs'][:, :, :])
        k.op('dve', lambda e: e.memset(VcCs[:, :, 64:65].ap, 1.0), (), (VcCs,))
        hx = k.sb('hx', [64, 512]); ta = k.sb('ta', [64, 512]); tb = k.sb('tb', [64, 512])
        hk = k.sb('hk', [64, 512], BF16); hv_ = k.sb('hv_', [64, 512], BF16)
        qs_all = k.sb('qs_all', [64, 8, 16], BF16); k.dma(qs_all[:, :, :], QT[:, :, Tn:Tn + 16])
        knew = k.sb('knew', [64, 6, 16], BF16); k.dma(knew[:, :, :], KT[:, :, Tn:Tn + 16])
        vnew = [k.sb('vnew', [4, 2, 2, 65], BF16) for _ in range(2)]
        gts = [k.sb('gts', [4, 24]) for _ in range(2)]
        q16 = k.sb('q16', [64, 4, 4], BF16)
        ebuf = [k.sb('ebuf', [128, 16], BF16) for _ in range(4)]
        ei = [0]

        def enext():
            b = ebuf[ei[0] % 4]; ei[0] += 1
            return b

        acc = k.sb('acc', [4, 8, 64]); accb = k.sb('accb', [4, 8, 64], BF16); tmp4 = k.sb('tmp4', [4, 4, 64])
        rr = k.sb('rr', [4, 16]); imp = k.sb('imp', [4, NSELS]); score = k.sb('score', [4, NSELS]); sc2 = k.sb('sc2', [4, NSELS])
        m8 = k.sb('m8', [4, 16]); negs = k.sb('negs', [4, NSELS], BF16)
        negT16 = k.sb('negT16', [NJ, 4, 4], BF16)
        SD = L['cfg'].get('sdbg', 99)
        for bl in range(4 if SD >= 99 else 1):
            vn = vnew[bl % 2]; gt_ = gts[bl % 2]
            if SD < 1:
                break
            import os
            if os.environ.get('SKIPVN') != '1':
                k.dma(vn[:, :, :, :], VT[Tn + bl * 4:Tn + bl * 4 + 4, :, :, :])
                k.dma(gt_[:, :], GT[Tn + bl * 4:Tn + bl * 4 + 4, :])
            gv3 = gt_[:, :].rearrange("p (h b) -> p h b", b=3)
            for lp in range(P if os.environ.get('SKIPG') != '1' else 0):
                pgt = pg[lp % 3]
                gather_page(k, pgt, cache, idx, bl * P + lp)
                sg_ = stg[lp % 2]
                k.dve.tensor_copy(out=sg_[:, :], in_=pgt[:, 0:384])
                k.pool.tensor_copy(out=Vss[:, lp, :, 0:64], in_=pgt[:, 384:512].rearrange("p (g d) -> p g d", g=2))
                px0 = pXb[0]
                for i in range(6):
                    k.pe.transpose(out=px0[:, i, :], in_=sg_[:, i * 64:(i + 1) * 64], identity=identb[:, :])
                k.dve.tensor_copy(out=KcTs[:, :, lp * 128:(lp + 1) * 128], in_=px0[:, 0:2, :])
                k.dve.tensor_copy(out=VcTs[:, :, lp * 128:(lp + 1) * 128], in_=px0[:, 2:4, :])
                k.dve.tensor_copy(out=KsTs[:, :, lp * 128:(lp + 1) * 128], in_=px0[:, 4:6, :])
            if SD < 2:
                break
            for wi in range(4):
                w_ = wt[wi % 2]
                k.dma(w_[:, :], st_win[bl, wi * 128:(wi + 1) * 128, :])
                px1 = pX[0]
                for g in range(2):
                    k.pe.transpose(out=px1[:, g, :], in_=w_[:, g * 64:(g + 1) * 64], identity=identf[:, :])
                k.dve.tensor_copy(out=KwTs[:, :, wi * 128:(wi + 1) * 128], in_=px1[:, 0:2, :])
                k.pool.tensor_copy(out=Vws[:, wi, :, 0:64], in_=w_[:, 128:256].rearrange("p (g d) -> p g d", g=2))
            if SD < 3:
                break
            for g in range(2):
                pb = sbank()
                compress(k, W, 0, KcTs[:, g, :], NBS, pb, hx, ta, tb, hk)
                pb2 = sbank()
                k.pe.matmul(out=pb2[0:64, 0:NBS], lhsT=W[0][1][:, :], rhs=hk[:, 0:NBS], start=True, stop=True)
                k.dve.tensor_copy(out=KcCs[:, 0:NBS], in_=pb2[0:64, 0:NBS])
                pb = sbank()
                compress(k, W, 1, VcTs[:, g, :], NBS, pb, hx, ta, tb, hv_)
                for ni in range(NBTS):
                    nn = min(128, NBS - ni * 128)
                    pb3 = sbank()
                    k.pe.matmul(out=pb3[:nn, 0:64], lhsT=hv_[:, ni * 128:ni * 128 + nn], rhs=W[1][1][:, :], start=True, stop=True)
                    k.dve.tensor_copy(out=VcCs[:nn, ni, 0:64], in_=pb3[:nn, 0:64])
                if SD < 4:
                    continue
                k.dve.tensor_copy(out=q16[:, :, :], in_=qs_all[:, g * 4:(g + 1) * 4, bl * 4:(bl + 1) * 4])
                qf = q16[:, :, :].rearrange("p r t -> p (r t)")
                for ni in range(NBTS):
                    nn = min(128, NBS - ni * 128)
                    pss = sbank()
                    k.pe.matmul(out=pss[:nn, 0:16], lhsT=KcCs[:, ni * 128:ni * 128 + nn], rhs=qf, start=True, stop=True)
                    e = enext()
                    k.act.activation(out=e[:nn, :], in_=pss[:nn, 0:16], func=AF.Exp, scale=0.125)
                    for r in range(4):
                        k.pe.matmul(out=po_a[r // 2][:, r % 2, :], lhsT=e[:nn, r * 4:(r + 1) * 4], rhs=VcCs[:nn, ni, :],
                                    start=(ni == 0 and r % 2 == 0), stop=(ni == NBTS - 1), skip_group_check=True)
                for r in range(4):
                    po = po_a[r // 2][:, r % 2, :]
                    k.dve.tensor_scalar(out=rr[:, r:r + 1], in0=po[:, 64:65], scalar1=1e-30, scalar2=None, op0=ALU.max)
                    k.dve.reciprocal(out=rr[:, 4 + r:5 + r], in_=rr[:, r:r + 1])
                    k.dve.tensor_tensor(out=rr[:, 8 + r:9 + r], in0=rr[:, 4 + r:5 + r], in1=gv3[:, g * 4 + r, 0:1], op=ALU.mult)
                    k.dve.tensor_scalar(out=acc[:, g * 4 + r, :], in0=po[:, 0:64], scalar1=rr[:, 8 + r:9 + r], scalar2=None, op0=ALU.mult)
                    if r == 0:
                        k.dve.tensor_scalar(out=imp[:, :], in0=po[:, 65:65 + NSELS], scalar1=rr[:, 4 + r:5 + r], scalar2=None, op0=ALU.mult)
                    else:
                        k.dve.scalar_tensor_tensor(out=imp[:, :], in0=po[:, 65:65 + NSELS], scalar=rr[:, 4 + r:5 + r], in1=imp[:, :], op0=ALU.mult, op1=ALU.add)
                if SD < 5:
                    continue
                k.dve.tensor_tensor(out=score[:, :], in0=imp[:, :], in1=fbs[:, :], op=ALU.add)
                k.dve.max(out=m8[:, 0:8], in_=score[:, :])
                k.dve.match_replace(out=sc2[:, :], in_to_replace=m8[:, 0:8], in_values=score[:, :], imm_value=-3.0e38)
                k.dve.max(out=m8[:, 8:16], in_=sc2[:, :])
                k.dve.tensor_scalar(out=rr[:, 12:13], in0=m8[:, 15:16], scalar1=-1.0e8, scalar2=None, op0=ALU.max)
                k.dve.tensor_scalar(out=negs[:, :], in0=score[:, :], scalar1=rr[:, 12:13], scalar2=NEG, op0=ALU.is_lt, op1=ALU.mult)
                k.pe.transpose(out=ps_t[0:NJ, 0:4], in_=negs[:, 0:NJ], identity=identb[0:4, 0:4])
                k.dve.tensor_copy(out=negT16[:, :, :], in_=ps_t[0:NJ, 0:4].unsq(1).bc([NJ, 4, 4]))
                nf = negT16[:, :, :].rearrange("p r t -> p (r t)")
                if SD < 6:
                    continue
                def sc_s(kt):
                    pss = sbank()
                    e = enext()
                    if kt < P:
                        k.pe.matmul(out=pss[:, 0:16], lhsT=KsTs[:, g, kt * 128:(kt + 1) * 128], rhs=qf, start=True, stop=False)
                        k.pe.matmul(out=pss[:, 0:16], lhsT=eexp[:, kt * 128:(kt + 1) * 128], rhs=nf, start=False, stop=True)
                        k.act.activation(out=e[:, :], in_=pss[:, 0:16], func=AF.Exp, scale=0.125)
                        return (e, 128, Vss[:, kt, g, :])
                    k.pe.matmul(out=pss[0:4, 0:16], lhsT=knew[:, 2 + g, bl * 4:(bl + 1) * 4], rhs=qf, start=True, stop=True)
                    k.act.activation(out=e[0:4, :], in_=pss[0:4, 0:16], func=AF.Exp, scale=0.125)
                    k.pool.tensor_tensor(out=e[0:4, :], in0=e[0:4, :], in1=smk[0:4, 1, :], op=ALU.mult)
                    return (e, 4, vn[:, 0, g, :])

                def pv_s(kt, tup):
                    e, nk, vv = tup
                    for r in range(4):
                        k.pe.matmul(out=po_b[:, r, :], lhsT=e[:nk, r * 4:(r + 1) * 4], rhs=vv, start=(kt == 0 and r == 0), stop=(kt == P), skip_group_check=True)

                pipe(range(P + 1), sc_s, pv_s)
                k.dve.tensor_scalar(out=rr[:, 0:4], in0=po_b[:, :, 64], scalar1=1e-30, scalar2=None, op0=ALU.max)
                k.dve.reciprocal(out=rr[:, 4:8], in_=rr[:, 0:4])
                k.dve.tensor_tensor(out=rr[:, 8:12], in0=rr[:, 4:8], in1=gv3[:, g * 4:(g + 1) * 4, 1], op=ALU.mult)
                k.dve.tensor_tensor(out=tmp4[:, :, :], in0=po_b[:, :, 0:64], in1=rr[:, 8:12].unsq(2).bc([4, 4, 64]), op=ALU.mult)
                k.dve.tensor_tensor(out=acc[:, g * 4:(g + 1) * 4, :], in0=acc[:, g * 4:(g + 1) * 4, :], in1=tmp4[:, :, :], op=ALU.add)
                if SD < 7:
                    continue
                def sc_w(kt):
                    pss = sbank()
                    e = enext()
                    if kt < 4:
                        k.pe.matmul(out=pss[:, 0:16], lhsT=KwTs[:, g, kt * 128:(kt + 1) * 128], rhs=qf, start=True, stop=True)
                        k.act.activation(out=e[:, :], in_=pss[:, 0:16], func=AF.Exp, scale=0.125)
                        if kt == 0:
                            k.pool.tensor_tensor(out=e[:, :], in0=e[:, :], in1=smk[:, 0, :], op=ALU.mult)
                        return (e, 128, Vws[:, kt, g, :])
                    k.pe.matmul(out=pss[0:4, 0:16], lhsT=knew[:, 4 + g, bl * 4:(bl + 1) * 4], rhs=qf, start=True, stop=True)
                    k.act.activation(out=e[0:4, :], in_=pss[0:4, 0:16], func=AF.Exp, scale=0.125)
                    k.pool.tensor_tensor(out=e[0:4, :], in0=e[0:4, :], in1=smk[0:4, 1, :], op=ALU.mult)
                    return (e, 4, vn[:, 1, g, :])

                def pv_w(kt, tup):
                    e, nk, vv = tup
                    for r in range(4):
                        k.pe.matmul(out=po_b[:, r, :], lhsT=e[:nk, r * 4:(r + 1) * 4], rhs=vv, start=(kt == 0 and r == 0), stop=(kt == 4), skip_group_check=True)

                pipe(range(5), sc_w, pv_w)
                k.dve.tensor_scalar(out=rr[:, 0:4], in0=po_b[:, :, 64], scalar1=1e-30, scalar2=None, op0=ALU.max)
                k.dve.reciprocal(out=rr[:, 4:8], in_=rr[:, 0:4])
                k.dve.tensor_tensor(out=rr[:, 8:12], in0=rr[:, 4:8], in1=gv3[:, g * 4:(g + 1) * 4, 2], op=ALU.mult)
                k.dve.tensor_tensor(out=tmp4[:, :, :], in0=po_b[:, :, 0:64], in1=rr[:, 8:12].unsq(2).bc([4, 4, 64]), op=ALU.mult)
                k.dve.tensor_tensor(out=acc[:, g * 4:(g + 1) * 4, :], in0=acc[:, g * 4:(g + 1) * 4, :], in1=tmp4[:, :, :], op=ALU.add)
            k.dve.tensor_copy(out=accb[:, :, :], in_=acc[:, :, :])
            k.dma(AO[Tn + bl * 4:Tn + bl * 4 + 4, 512:1024], accb[:, :, :].rearrange("p h d -> p (h d)"))


def phase_moba_sample(k, L):
    Tn = L['Tn']; P = L['P']; LP = L['LP']; NBLKS = L['NBLKS']; NBLKSP = L['NBLKSP']
    QT2 = L['QT2']; KT2 = L['KT2']; VT2 = L['VT2']; MO = L['MO']; identb = L['identb']; identf = L['identf']
    cache = L['cache_moba']
    with k.scope():
        idx = page_indices(k, L)
        eexp = k.sb('eexp', [NBLKSP, LP], BF16); k.dma(eexp[:, :], L['EEXP2'][0:NBLKSP, 0:LP])
        smk = k.sb('smk', [128, 2, 16], BF16); k.dma(smk[:, :, :], L['SMK'][:, :, :])
        ones = k.sb('ones', [128, 1]); k.op('dve', lambda e: e.memset(ones[:, :].ap, 1.0), (), (ones,))
        ps_s = [k.ps('ps_s', [128, 512]) for _ in range(2)]
        pXb = [k.ps('pXb', [64, 4, 128], BF16) for _ in range(2)]
        stg = [k.sb('stg', [128, 256], BF16) for _ in range(2)]; stf = [k.sb('stf', [128, 256]) for _ in range(2)]
        pM = k.ps('pM', [64, 4, NBLKSP])
        pGt = k.ps('pGt', [4, 16, NBLKSP])
        po_b = k.ps('po_b', [4, 4, 65])
        ps_t = k.ps('ps_t', [NBLKSP, 16, 4], BF16)
        pg = [k.sb('pg', [128, 512]) for _ in range(3)]
        K2s = k.sb('K2s', [64, 4, LP], BF16); V2s = k.sb('V2s', [128, P, 4, 65], BF16)
        k.op('dve', lambda e: e.memset(V2s[:, :, :, 64:65].ap, 1.0), (), (V2s,))
        meansT = k.sb('meansT', [64, 4, NBLKSP])
        qf32 = k.sb('qf32', [64, 16, 16]); k.dma(qf32[:, :, :], L['QF2'][:, :, :])
        qs_all = k.sb('qs_all', [64, 16, 16], BF16); k.dma(qs_all[:, :, :], QT2[:, :, Tn:Tn + 16])
        knew = k.sb('knew', [64, 4, 16], BF16); k.dma(knew[:, :, :], KT2[:, :, Tn:Tn + 16])
        vnew = [k.sb('vnew', [4, 4, 65], BF16) for _ in range(2)]
        q16 = k.sb('q16', [64, 4, 4], BF16)
        score = k.sb('score', [4, 16, NBLKSP]); m8 = k.sb('m8', [4, 16, 8]); thr = k.sb('thr', [4, 16])
        negs = k.sb('negs', [4, 16, NBLKSP]); negb = k.sb('negb', [4, 16, NBLKSP], BF16)
        negT = k.sb('negT', [NBLKSP, 16, 4], BF16)
        ebuf = [k.sb('ebuf', [128, 16], BF16) for _ in range(4)]
        rr = k.sb('rr', [4, 8]); accb = k.sb('accb', [4, 16, 64], BF16)
        k.op('dve', lambda e: e.memset(score[:, :, :].ap, -2.0e9), (), (score,))
        cnt = [0]
        for bl in range(4):
            vn = vnew[bl % 2]
            k.dma(vn[:, :, :], VT2[Tn + bl * 4:Tn + bl * 4 + 4, :, :])
            for lp in range(P):
                pgt = pg[lp % 3]
                gather_page(k, pgt, cache, idx, bl * P + lp)
                sg_ = stg[lp % 2]; sf_ = stf[lp % 2]
                k.dve.tensor_copy(out=sg_[:, :], in_=pgt[:, 0:256])
                k.dve.tensor_copy(out=sf_[:, :], in_=pgt[:, 0:256])
                k.pool.tensor_copy(out=V2s[:, lp, :, 0:64], in_=pgt[:, 256:512].rearrange("p (g d) -> p g d", g=4))
                px = pXb[lp % 2]
                for g in range(4):
                    k.pe.transpose(out=px[:, g, :], in_=sg_[:, g * 64:(g + 1) * 64], identity=identb[:, :])
                k.dve.tensor_copy(out=K2s[:, :, lp * 128:(lp + 1) * 128], in_=px[:, :, :])
                blk = lp // 2
                for g in range(4):
                    k.pe.matmul(out=pM[:, g, blk:blk + 1], lhsT=sf_[:, g * 64:(g + 1) * 64], rhs=ones[:, 0:1],
                                start=(lp == 0 and g == 0), stop=(lp % 2 == 1), skip_group_check=True)
            k.dve.tensor_scalar(out=meansT[:, :, 0:NBLKS], in0=pM[:, :, 0:NBLKS], scalar1=1.0 / 2048, scalar2=None, op0=ALU.mult)
            for h in range(16):
                k.pe.matmul(out=pGt[:, h, 0:NBLKS], lhsT=qf32[:, h, bl * 4:(bl + 1) * 4], rhs=meansT[:, h // 4, 0:NBLKS], start=(h == 0), stop=(h == 15), skip_group_check=True)
            k.dve.tensor_copy(out=score[:, :, 0:NBLKS], in_=pGt[:, :, 0:NBLKS])
            for h in range(16):
                k.dve.max(out=m8[:, h, :], in_=score[:, h, :])
            k.dve.tensor_scalar(out=thr[:, :], in0=m8[:, :, 2], scalar1=-1.0e8, scalar2=None, op0=ALU.max)
            k.dve.tensor_tensor(out=negs[:, :, :], in0=score[:, :, :], in1=thr[:, :].unsq(2).bc([4, 16, NBLKSP]), op=ALU.is_lt)
            k.dve.tensor_scalar(out=negb[:, :, :], in0=negs[:, :, :], scalar1=NEG, scalar2=None, op0=ALU.mult)
            for h in range(16):
                k.pe.transpose(out=ps_t[:, h, :], in_=negb[:, h, :], identity=identb[0:4, 0:4])
            k.dve.tensor_copy(out=negT[:, :, :], in_=ps_t[:, :, :])
            for g in range(4):
                k.dve.tensor_copy(out=q16[:, :, :], in_=qs_all[:, g * 4:(g + 1) * 4, bl * 4:(bl + 1) * 4])
                qf = q16[:, :, :].rearrange("p r t -> p (r t)")
                nf = negT[:, g * 4:(g + 1) * 4, :].rearrange("p r t -> p (r t)")
                def sc_ms(kt):
                    pss = ps_s[cnt[0] % 2]
                    e = ebuf[cnt[0] % 4]; cnt[0] += 1
                    if kt < P:
                        k.pe.matmul(out=pss[:, 0:16], lhsT=K2s[:, g, kt * 128:(kt + 1) * 128], rhs=qf, start=True, stop=False)
                        k.pe.matmul(out=pss[:, 0:16], lhsT=eexp[:, kt * 128:(kt + 1) * 128], rhs=nf, start=False, stop=True)
                        k.act.activation(out=e[:, :], in_=pss[:, 0:16], func=AF.Exp, scale=0.125)
                        return (e, 128, V2s[:, kt, g, :])
                    k.pe.matmul(out=pss[0:4, 0:16], lhsT=knew[:, g, bl * 4:(bl + 1) * 4], rhs=qf, start=True, stop=True)
                    k.act.activation(out=e[0:4, :], in_=pss[0:4, 0:16], func=AF.Exp, scale=0.125)
                    k.pool.tensor_tensor(out=e[0:4, :], in0=e[0:4, :], in1=smk[0:4, 1, :], op=ALU.mult)
                    return (e, 4, vn[:, g, :])

                def pv_ms(kt, tup):
                    e, nk, vv = tup
                    for r in range(4):
                        k.pe.matmul(out=po_b[:, r, :], lhsT=e[:nk, r * 4:(r + 1) * 4], rhs=vv, start=(kt == 0 and r == 0), stop=(kt == P), skip_group_check=True)

                pipe(range(P + 1), sc_ms, pv_ms)
                k.dve.tensor_scalar(out=rr[:, 0:4], in0=po_b[:, :, 64], scalar1=1e-30, scalar2=None, op0=ALU.max)
                k.dve.reciprocal(out=rr[:, 4:8], in_=rr[:, 0:4])
                k.dve.tensor_tensor(out=accb[:, g * 4:(g + 1) * 4, :], in0=po_b[:, :, 0:64], in1=rr[:, 4:8].unsq(2).bc([4, 4, 64]), op=ALU.mult)
            k.dma(MO[Tn + bl * 4:Tn + bl * 4 + 4, :], accb[:, :, :].rearrange("p h d -> p (h d)"))


def host_consts(cfg):
    Tn, P = cfg['T'], cfg['P']
    TR = Tn + 128
    LP = P * 128
    pos = np.concatenate([np.arange(Tn), np.tile(LP + np.arange(4), 4), np.zeros(112)]).astype(np.float32)
    half = 32
    inv = (10000.0 ** (-np.arange(half, dtype=np.float32) / half)).astype(np.float32)
    ang = pos[:, None] * inv[None, :]
    ropecs = np.concatenate([np.cos(ang), np.sin(ang)], axis=1).astype(np.float32)
    return {
        'ropecs': ropecs,
        'identb': np.eye(128, dtype=np.float32).astype(ml_dtypes.bfloat16),
        'identf': np.eye(128, dtype=np.float32),
        **nsa_consts(Tn, LP),
        'masks64': np.stack([np.triu(np.ones((64, 64), np.float32)), np.triu(np.ones((64, 64), np.float32), 1), np.tril(np.ones((64, 64), np.float32), -1)], axis=1),
    }


def nsa_consts(Tn, LP):
    bf = ml_dtypes.bfloat16
    NBLK = Tn // 256; NBLKP = max(8, NBLK)
    tq = np.arange(Tn)[:, None]; nb = np.arange(NBLKP)[None, :]
    gb0 = np.where(nb < tq // 256, 0.0, -1.0e9).astype(np.float32)
    gb1 = (nb != tq // 256).astype(np.float32)
    GBM = np.stack([gb0, gb1], axis=1)
    Wd_ = max(Tn, LP)
    EE2 = (np.arange(Wd_)[None, :] // 256 == np.arange(64)[:, None]).astype(np.float32).astype(bf)
    NSEL = Tn // 64; NB = Tn // 16 - 1; NBT = (NB + 127) // 128
    t = np.arange(Tn)[:, None]; j = np.arange(NSEL)[None, :]
    cur = t // 64
    forced = ((j == cur) | (j == cur - 1) | (j == 0)).astype(np.float32)
    FB = np.where(j <= cur, 100.0 * forced, -1.0e9).astype(np.float32)
    n = np.arange(NBT * 128)[:, None]
    lo = np.maximum(n * 16, j * 64); hi = np.minimum(n * 16 + 32, (j + 1) * 64)
    ov = (np.clip(hi - lo, 0, None).astype(np.float32) / 32.0)
    ov[NB:] = 0
    OV = ov.reshape(NBT, 128, NSEL).transpose(1, 0, 2).astype(bf)
    nl = np.arange(128)[:, None, None]; idx = np.arange(17)[None, :, None]; tl = np.arange(128)[None, None, :]
    CM = (16 * nl + 31 - 128 * idx <= tl).astype(np.float32).astype(bf)
    s_ = np.arange(128)[:, None]; t_ = np.arange(128)[None, :]
    TRI = np.stack([(s_ <= t_), (s_ > t_)], axis=1).astype(np.float32).astype(bf)
    W = max(Tn, LP)
    EE = (np.arange(W)[None, :] // 64 == np.arange(128)[:, None]).astype(np.float32).astype(bf)
    NSELS = LP // 64 + 1; NBS = LP // 16 - 1; NBTS = (NBS + 127) // 128
    js = np.arange(NSELS)[None, :]
    FBs = np.tile((100.0 * ((js == 0) | (js == NSELS - 2) | (js == NSELS - 1))).astype(np.float32), (4, 1))
    ns_ = np.arange(NBTS * 128)[:, None]
    lo = np.maximum(ns_ * 16, js * 64); hi = np.minimum(ns_ * 16 + 32, (js + 1) * 64)
    ovs = (np.clip(hi - lo, 0, None).astype(np.float32) / 32.0); ovs[NBS:] = 0
    OVs = ovs.reshape(NBTS, 128, NSELS).transpose(1, 0, 2).astype(bf)
    i_ = np.arange(128)[:, None]; tq_ = np.tile(np.arange(4), 4)[None, :]
    SMK = np.stack([(i_ > tq_), (i_ <= tq_)], axis=1).astype(np.float32).astype(bf)
    return {'FBt': FB, 'OVt': OV, 'CMt': CM, 'TRIt': TRI, 'EEXP': EE, 'GBM': GBM, 'EEXP2': EE2,
            'FBs': FBs, 'OVs': OVs, 'SMK': SMK, 'IOTA': np.arange(128, dtype=np.float32).reshape(128, 1)}


def make_in_maps(cfg, inputs, ncores):
    Tn, P = cfg['T'], cfg['P']
    hc = host_consts(cfg)
    B = inputs['x_prompt'].shape[0]
    maps = []
    for c in range(ncores):
        b = c % B
        sl = slice(4 * c, 4 * c + 4)
        m = {
            'xp': np.ascontiguousarray(inputs['x_prompt'][b]),
            'xs': np.ascontiguousarray(inputs['x_sample'][sl]).reshape(16, 1024),
            'st_win': np.ascontiguousarray(inputs['state_win_kv'][0, sl]).reshape(4, 512, 256),
            'st_wkv': np.ascontiguousarray(inputs['state_wkv'][0, sl]),
            'st_shift': np.ascontiguousarray(inputs['state_shift'][0, sl]),
            'ptab': np.ascontiguousarray(inputs['page_table'][sl]).astype(np.int32),
            'norm_mix': inputs['norm_mix'], 'norm_ffn': inputs['norm_ffn'], 'norm_final': inputs['norm_final'].reshape(1, 1024),
            'w_in0': inputs['even_w_in'][0], 'w_out0': inputs['even_w_out'][0],
            'gate_b': inputs['nsa_gate_b'][0].reshape(1, 24),
            'cache_nsa': inputs['cache_nsa_kv'][0].reshape(-1, 512), 'cache_moba': inputs['cache_moba_kv'][0].reshape(-1, 512),
            'w_in1': inputs['odd_w_in'][0], 'w_out1': inputs['odd_w_out'][0],
            'ffn_g0': inputs['ffn_w_gate'][0], 'ffn_g1': inputs['ffn_w_gate'][1], 'ffn_u0': inputs['ffn_w_up'][0], 'ffn_u1': inputs['ffn_w_up'][1],
            'ffn_d0': inputs['ffn_w_down'][0], 'ffn_d1': inputs['ffn_w_down'][1],
            'cmp_w1': inputs['nsa_cmp_w1'][0], 'cmp_pe': inputs['nsa_cmp_pe'][0], 'cmp_w2': inputs['nsa_cmp_w2'][0],
            'rw_mu': inputs['rwkv_mu'][0].reshape(1, RWC),
            'rw_vec': np.stack([inputs['rwkv_w0'][0], inputs['rwkv_a0'][0], inputs['rwkv_k_k'][0], inputs['rwkv_k_a'][0],
                                inputs['rwkv_r_k'][0].reshape(512), inputs['rwkv_ln_g'][0], inputs['rwkv_ln_b'][0]]).astype(np.float32),
            'rw_wup': inputs['rwkv_w_up'][0], 'rw_aup': inputs['rwkv_a_up'][0], 'rw_gup': inputs['rwkv_g_up'][0],
        }
        m.update(hc)
        maps.append(m)
    return maps


CFG_FULL = {'T': 4096, 'P': 64, 'NPHYS': 2560, 'stages': 'ABCSDEFMG'}


def kernel(**inputs):
    cfg = dict(CFG_FULL)
    inputs = {k_: np.asarray(v) for k_, v in inputs.items()}
    nc, kb = build(cfg)
    maps = make_in_maps(cfg, inputs, 8)
    res = run_bass_kernel_spmd(nc, maps, core_ids=list(range(8)))
    R = res.results
    f32 = np.float32
    y_p = np.stack([R[c]['y_p'] for c in range(4)]).astype(f32)
    y_s = np.concatenate([R[c]['y_s'].reshape(4, 4, 1024) for c in range(8)]).astype(f32)
    nsa_p = np.stack([R[c]['o_nsa_p'].reshape(4096, 4, 2, 64) for c in range(4)])[None].astype(f32)
    nsa_s = np.concatenate([R[c]['o_nsa_s'].reshape(4, 4, 4, 2, 64) for c in range(8)])[None].astype(f32)
    moba_p = np.stack([R[c]['o_moba_p'].reshape(4096, 2, 4, 64) for c in range(4)])[None].astype(f32)
    moba_s = np.concatenate([R[c]['o_moba_s'].reshape(4, 4, 2, 4, 64) for c in range(8)])[None].astype(f32)
    win_p = np.stack([R[c]['o_win_p'].reshape(512, 2, 2, 64) for c in range(4)])[None].astype(f32)
    win_s = np.concatenate([R[c]['o_win_s'].reshape(4, 512, 2, 2, 64) for c in range(8)])[None].astype(f32)
    wkv_p = np.stack([R[c]['o_wkv_p'] for c in range(4)])[None].astype(f32)
    wkv_s = np.concatenate([R[c]['o_wkv_s'] for c in range(8)])[None].astype(f32)
    sh_p = np.concatenate([R[c]['o_sh_p'] for c in range(4)])[None].astype(f32)
    sh_s = np.concatenate([R[c]['o_sh_s'] for c in range(8)])[None].astype(f32)
    return (y_p, y_s, nsa_p, nsa_s, moba_p, moba_s, win_p, win_s, wkv_p, wkv_s, sh_p, sh_s)
```

```python
import numpy as np
import ml_dtypes
from contextlib import ExitStack, contextmanager
import concourse.bass as bass
import concourse.mybir as mybir
from concourse.bass_utils import run_bass_kernel_spmd

F32 = mybir.dt.float32
BF16 = mybir.dt.bfloat16
I32 = mybir.dt.int32
AF = mybir.ActivationFunctionType
ALU = mybir.AluOpType
AX = mybir.AxisListType

ENGS = ['pe', 'act', 'dve', 'pool', 'sp']
NDMA = 12
SAME_SYNC = {'pe': False, 'act': True, 'dve': True, 'pool': True, 'sp': False}
WRITE_KEYS = ('out', 'accum_out', 'out_max', 'out_indices', 'out_ap')


class T:
    def __init__(self, h, name):
        self.h = h
        self.name = name
        self.w = {}
        self.r = {}

    def __getitem__(self, idx):
        return V(self.h[idx], self)


class V:
    def __init__(self, ap, t):
        self.ap = ap
        self.t = t

    def __getitem__(self, idx):
        return V(self.ap[idx], self.t)

    def rearrange(self, s, **kw):
        return V(self.ap.rearrange(s, **kw), self.t)

    def bc(self, shape):
        return V(self.ap.to_broadcast(list(shape)), self.t)

    def pbc(self, n):
        return V(self.ap.partition_broadcast(n), self.t)

    def bitcast(self, dt):
        return V(self.ap.bitcast(dt), self.t)

    def unsq(self, ax):
        return V(self.ap.unsqueeze(ax), self.t)

    @property
    def shape(self):
        return self.ap.shape


def _merge(d, s):
    for k, v in s.items():
        if d.get(k, 0) < v:
            d[k] = v


class EngProxy:
    def __init__(self, kb, name):
        self.kb = kb
        self.name = name

    def __getattr__(self, opname):
        kb = self.kb
        name = self.name

        def call(**kw):
            reads, writes = [], []
            kw2 = {}
            for key, v in kw.items():
                if isinstance(v, V):
                    (writes if key in WRITE_KEYS else reads).append(v.t)
                    kw2[key] = v.ap
                else:
                    kw2[key] = v
            return kb.op(name, lambda e: getattr(e, opname)(**kw2), reads, writes)

        return call


class KB:
    def __init__(self):
        self.nc = bass.Bass("TRN2", target_bir_lowering=False)
        self.es = ExitStack()
        self.cnt = {}
        self.known = {e: {} for e in ENGS}
        self.sem = {}
        nc = self.nc
        self.eng = {'pe': nc.tensor, 'act': nc.scalar, 'dve': nc.vector, 'pool': nc.gpsimd, 'sp': nc.sync}
        for e in ENGS:
            self._mksem('c_' + e)
        for i in range(NDMA):
            self._mksem('d%d' % i)
        for i in range(4):
            self._mksem('g%d' % i)
        self.dma_rr = 0
        self.g_rr = 0
        self.pe = EngProxy(self, 'pe')
        self.act = EngProxy(self, 'act')
        self.dve = EngProxy(self, 'dve')
        self.pool = EngProxy(self, 'pool')
        self.stack = [self.es]
        self.ninst = 0
        self.uid = 0

    def _mksem(self, name):
        self.sem[name] = self.es.enter_context(self.nc.semaphore(name))
        self.cnt[name] = 0

    def dram(self, name, shape, dt, kind="Internal"):
        h = self.nc.dram_tensor(name, list(shape), dt, kind=kind)
        return T(h.ap(), name)

    def sb(self, name, shape, dt=F32):
        self.uid += 1
        h = self.stack[-1].enter_context(self.nc.sbuf_tensor("%s_%d" % (name, self.uid), list(shape), dt))
        return T(h, name)

    def ps(self, name, shape, dt=F32):
        self.uid += 1
        h = self.stack[-1].enter_context(self.nc.psum_tensor("%s_%d" % (name, self.uid), list(shape), dt))
        return T(h, name)

    @contextmanager
    def scope(self):
        es = ExitStack()
        self.stack.append(es)
        try:
            yield
        finally:
            self.barrier()
            self.stack.pop()
            es.close()

    def _deps(self, eng, reads, writes, own):
        deps = {}
        for t in reads:
            _merge(deps, t.w)
        for t in writes:
            _merge(deps, t.w)
            _merge(deps, t.r)
        kn = self.known[eng]
        e = self.eng[eng]
        for s, v in deps.items():
            if s == own and not SAME_SYNC[eng]:
                continue
            if kn.get(s, 0) >= v:
                continue
            e.wait_ge(self.sem[s], v)
            self.ninst += 1
            kn[s] = v

    def _mark(self, s, val, reads, writes):
        for t in reads:
            if t.r.get(s, 0) < val:
                t.r[s] = val
        for t in writes:
            if t.w.get(s, 0) < val:
                t.w[s] = val

    def op(self, eng, fn, reads=(), writes=()):
        own = 'c_' + eng
        self._deps(eng, reads, writes, own)
        self.cnt[own] += 1
        val = self.cnt[own]
        fn(self.eng[eng]).then_inc(self.sem[own], 1)
        self.ninst += 1
        self._mark(own, val, reads, writes)

    def dma(self, out, in_, q='sp', **kw):
        reads, writes = [in_.t], [out.t]
        s = 'd%d' % self.dma_rr
        self.dma_rr = (self.dma_rr + 1) % NDMA
        self._deps(q, reads, writes, None)
        self.cnt[s] += 16
        val = self.cnt[s]
        self.eng[q].dma_start(out=out.ap, in_=in_.ap, **kw).then_inc(self.sem[s], 16)
        self.ninst += 1
        self._mark(s, val, reads, writes)

    def raw16(self, q, fn, reads=(), writes=()):
        s = 'g%d' % self.g_rr
        self.g_rr = (self.g_rr + 1) % 4
        self._deps(q, reads, writes, None)
        self.cnt[s] += 16
        val = self.cnt[s]
        fn(self.eng[q]).then_inc(self.sem[s], 16)
        self.ninst += 1
        self._mark(s, val, reads, writes)

    def barrier(self, engs=ENGS):
        for e in engs:
            kn = self.known[e]
            for s, v in self.cnt.items():
                if v > 0 and kn.get(s, 0) < v and s != 'c_' + e:
                    self.eng[e].wait_ge(self.sem[s], v)
                    kn[s] = v

    def dbg(self, name, v, shape, dt=F32):
        if not getattr(self, 'dbg_on', False):
            return
        o = self.dram('dbg_' + name, shape, dt, "ExternalOutput")
        idx = tuple(slice(0, n) for n in shape)
        self.dma(o[idx], v)

    def finish(self):
        self.barrier(['sp'])
        self.es.close()
        return self.nc


RWC = 1792
EVC = 3096
ODC = 1536
DFF = 2816
NEG = -240000.0


def build(cfg):
    Tn, P, NPH = cfg['T'], cfg['P'], cfg['NPHYS']
    stages = cfg.get('stages', 'A')
    NT = Tn // 128
    LP = P * 128
    TR = Tn + 128
    k = KB()
    k.dbg_on = cfg.get('dbg', False)
    nc = k.nc
    IN = lambda n, s, d=F32: k.dram(n, s, d, "ExternalInput")
    OUT = lambda n, s, d=F32: k.dram(n, s, d, "ExternalOutput")
    xp = IN('xp', [Tn, 1024]); xs = IN('xs', [16, 1024])
    st_win = IN('st_win', [4, 512, 256]); st_wkv = IN('st_wkv', [4, 8, 64, 64]); st_shift = IN('st_shift', [4, RWC])
    ptab = IN('ptab', [4, P], I32)
    norm_mix = IN('norm_mix', [2, 1024]); norm_ffn = IN('norm_ffn', [2, 1024]); norm_final = IN('norm_final', [1, 1024])
    w_in0 = IN('w_in0', [1024, EVC]); w_out0 = IN('w_out0', [1024, 1024])
    gate_b = IN('gate_b', [1, 24])
    w_in1 = IN('w_in1', [1024, ODC]); w_out1 = IN('w_out1', [1024, 1024])
    ffn_g = [IN('ffn_g%d' % i, [1024, DFF]) for i in range(2)]; ffn_u = [IN('ffn_u%d' % i, [1024, DFF]) for i in range(2)]
    ffn_d = [IN('ffn_d%d' % i, [DFF, 1024]) for i in range(2)]
    NBLK = Tn // 256; NBLKP = max(8, NBLK)
    GBM = IN('GBM', [Tn, 2, NBLKP]); EEXP2 = IN('EEXP2', [64, max(Tn, LP)], BF16)
    cache_nsa = IN('cache_nsa', [NPH * 128, 512]); cache_moba = IN('cache_moba', [NPH * 128, 512])
    NSELS = LP // 64 + 1; NBS = LP // 16 - 1; NBTS = (NBS + 127) // 128
    NBLKS = LP // 256; NBLKSP = max(8, NBLKS)
    FBs = IN('FBs', [4, NSELS]); OVs = IN('OVs', [128, NBTS, NSELS], BF16)
    SMK = IN('SMK', [128, 2, 16], BF16)
    IOTA = IN('IOTA', [128, 1])
    rw_mu = IN('rw_mu', [1, RWC]); rw_vec = IN('rw_vec', [7, 512])
    rw_wup = IN('rw_wup', [64, 512]); rw_aup = IN('rw_aup', [64, 512]); rw_gup = IN('rw_gup', [128, 512])
    masks64 = IN('masks64', [64, 3, 64])
    NSEL = Tn // 64; NB = Tn // 16 - 1; NBT = (NB + 127) // 128
    cmp_w1 = IN('cmp_w1', [2, 32, 64, 64]); cmp_pe = IN('cmp_pe', [2, 32, 64]); cmp_w2 = IN('cmp_w2', [2, 64, 64])
    FBt = IN('FBt', [Tn, NSEL]); OVt = IN('OVt', [128, NBT, NSEL], BF16); CMt = IN('CMt', [128, 17, 128], BF16)
    TRIt = IN('TRIt', [128, 2, 128], BF16); EEXP = IN('EEXP', [128, max(Tn, LP)], BF16)
    ropecs = IN('ropecs', [TR, 64])
    identb_d = IN('identb', [128, 128], BF16); identf_d = IN('identf', [128, 128])
    y_p = OUT('y_p', [Tn, 1024]); y_s = OUT('y_s', [16, 1024])
    o_nsa_p = OUT('o_nsa_p', [Tn, 512]); o_nsa_s = OUT('o_nsa_s', [16, 512])
    o_moba_p = OUT('o_moba_p', [Tn, 512]); o_moba_s = OUT('o_moba_s', [16, 512])
    WN = min(512, Tn)
    o_win_p = OUT('o_win_p', [WN, 256]); o_win_s = OUT('o_win_s', [4, 512, 256])
    o_wkv_p = OUT('o_wkv_p', [8, 64, 64]); o_wkv_s = OUT('o_wkv_s', [4, 8, 64, 64])
    o_sh_p = OUT('o_sh_p', [1, RWC]); o_sh_s = OUT('o_sh_s', [4, RWC])
    RW = k.dram('RW', [TR, RWC], F32)
    QT = k.dram('QT', [64, 8, TR], BF16)
    KT = k.dram('KT', [64, 6, TR], BF16)
    VcT = k.dram('VcT', [64, 2, TR], BF16)
    VT = k.dram('VT', [TR, 2, 2, 65], BF16)
    GT = k.dram('GT', [TR, 24], F32)
    AO = k.dram('AO', [TR, 1024], BF16, "ExternalOutput" if cfg.get('dbg_ao') else "Internal")
    H1 = k.dram('H1', [TR, 1024], F32, "ExternalOutput" if cfg.get('dbg_ao') else "Internal")
    ACTT = k.dram('ACTT', [22, 128, TR], BF16)
    QT2 = k.dram('QT2', [64, 16, TR], BF16); KT2 = k.dram('KT2', [64, 4, TR], BF16); VT2 = k.dram('VT2', [TR, 4, 65], BF16)
    NS2 = k.dram('NS2', [Tn, 16, NBLKP], BF16)
    QF2 = k.dram('QF2', [64, 16, 16], F32)
    MO = k.dram('MO', [TR, 1024], BF16, "ExternalOutput" if cfg.get('dbg_ao') else "Internal")

    identb = k.sb('identb', [128, 128], BF16); identf = k.sb('identf', [128, 128], F32)
    k.dma(identb[:, :], identb_d[:, :]); k.dma(identf[:, :], identf_d[:, :])

    tiles = [(i * 128, 128) for i in range(NT)] + [(Tn, 16)]

    def rmsnorm_T(xt, rows, gbc, xn, pT, xnT, ss, junk):
        k.act.activation(out=junk[:rows, :], in_=xt[:rows, :], func=AF.Square, accum_out=ss[:rows, 0:1])
        k.dve.tensor_scalar(out=ss[:rows, 1:2], in0=ss[:rows, 0:1], scalar1=1.0 / 1024, scalar2=1e-6, op0=ALU.mult, op1=ALU.add)
        k.act.activation(out=ss[:rows, 3:4], in_=ss[:rows, 1:2], func=AF.Sqrt)
        k.dve.reciprocal(out=ss[:rows, 2:3], in_=ss[:rows, 3:4])
        k.dve.scalar_tensor_tensor(out=xn[:rows, :], in0=xt[:rows, :], scalar=ss[:rows, 2:3], in1=gbc[:rows, :], op0=ALU.mult, op1=ALU.mult)
        for kk in range(8):
            k.pe.transpose(out=pT[:, kk, :rows], in_=xn[:rows, kk * 128:(kk + 1) * 128], identity=identb[:rows, :rows])
        k.act.copy(out=xnT[:, :, :rows], in_=pT[:, :, :rows])

    def load_w_bf16(Wsb, wd, K, N):
        with k.scope():
            stg = [k.sb('wstg', [128, N], F32) for _ in range(2)]
            for kk in range(K // 128):
                s = stg[kk % 2]
                k.dma(s[:, :], wd[kk * 128:(kk + 1) * 128, :])
                (k.pool if kk % 2 else k.dve).tensor_copy(out=Wsb[:, kk, :], in_=s[:, :])

    with k.scope():
        W0 = k.sb('W0', [128, 8, EVC], BF16)
        load_w_bf16(W0, w_in0, 1024, EVC)
        gbc = k.sb('gbc', [128, 1024]); k.dma(gbc[:, :], norm_mix[0:1, :].pbc(128))
        gb = k.sb('gb', [128, 24]); k.dma(gb[:, :], gate_b[0:1, :].pbc(128))
        xt = [k.sb('xt', [128, 1024]) for _ in range(2)]
        junk = k.sb('junk', [128, 1024], BF16)
        xn = k.sb('xn', [128, 1024], BF16)
        ss = k.sb('ss', [128, 4])
        xnT = k.sb('xnT', [128, 8, 128], BF16)
        proj = [k.sb('proj', [128, EVC]) for _ in range(2)]
        cs = k.sb('cs', [128, 64])
        tmp = [k.sb('tmp%d' % i, [128, 8, 32]) for i in range(4)]
        qb = k.sb('qb', [128, 8, 64], BF16)
        kb = k.sb('kb', [128, 6, 64], BF16)
        vb = k.sb('vb', [128, 3, 2, 65], BF16)
        k.dve.memset(ap=vb[:, :, :, :], constant=1.0) if False else k.op('dve', lambda e: e.memset(vb[:, :, :, :].ap, 1.0), (), (vb,))
        gt = k.sb('gt', [128, 24])
        qT = k.sb('qT', [64, 8, 128], BF16); kT = k.sb('kT', [64, 6, 128], BF16); vcT = k.sb('vcT', [64, 2, 128], BF16)
        pT = k.ps('pT', [128, 8, 128], BF16)
        pA = [k.ps('pA', [128, 512]) for _ in range(2)]
        pQ = k.ps('pQ', [64, 8, 128], BF16)
        pK = k.ps('pK', [64, 8, 128], BF16)
        ci = 0
        for ti, (r0, rows) in enumerate(tiles):
            x = xt[ti % 2]
            pj = proj[ti % 2]
            src = xp[r0:r0 + rows, :] if r0 < Tn else xs[0:16, :]
            k.dma(x[:rows, :], src)
            k.dma(cs[:rows, :], ropecs[r0:r0 + rows, :])
            rmsnorm_T(x, rows, gbc, xn, pT, xnT, ss, junk)
            for c0 in range(0, EVC, 512):
                w = min(512, EVC - c0)
                ps = pA[ci % 2]
                for kk in range(8):
                    k.pe.matmul(out=ps[:rows, :w], lhsT=xnT[:, kk, :rows], rhs=W0[:, kk, c0:c0 + w], start=(kk == 0), stop=(kk == 7))
                if ci % 2:
                    k.act.copy(out=pj[:rows, c0:c0 + w], in_=ps[:rows, :w])
                else:
                    k.dve.tensor_copy(out=pj[:rows, c0:c0 + w], in_=ps[:rows, :w])
                ci += 1
            k.dma(RW[r0:r0 + rows, :], pj[:rows, 0:RWC])
            k.dve.tensor_tensor(out=gt[:rows, :], in0=pj[:rows, 3072:3096], in1=gb[:rows, :], op=ALU.add)
            k.act.activation(out=gt[:rows, :], in_=gt[:rows, :], func=AF.Sigmoid)
            k.dma(GT[r0:r0 + rows, :], gt[:rows, :])
            cosb = lambda n: cs[:rows, 0:32].unsq(1).bc([rows, n, 32])
            sinb = lambda n: cs[:rows, 32:64].unsq(1).bc([rows, n, 32])
            qv = pj[:rows, 1792:2304].rearrange("p (h d) -> p h d", h=8)
            views = [(qv, 8, qb[:rows, :, :])]
            for c in range(3):
                kv_ = pj[:rows, 2304 + c * 256:2304 + c * 256 + 128].rearrange("p (g d) -> p g d", g=2)
                views.append((kv_, 2, None))
            for vi, (xv, n, ob) in enumerate(views):
                E1 = k.dve if vi % 2 == 0 else k.pool
                x1 = xv[:, :, 0:32]; x2 = xv[:, :, 32:64]
                t = [tt[:rows, 0:n, :] for tt in tmp]
                E1.tensor_tensor(out=t[0], in0=x1, in1=cosb(n), op=ALU.mult)
                E1.tensor_tensor(out=t[1], in0=x2, in1=sinb(n), op=ALU.mult)
                E1.tensor_tensor(out=t[2], in0=x2, in1=cosb(n), op=ALU.mult)
                E1.tensor_tensor(out=t[3], in0=x1, in1=sinb(n), op=ALU.mult)
                if ob is not None:
                    E1.tensor_tensor(out=ob[:, :, 0:32], in0=t[0], in1=t[1], op=ALU.subtract)
                    E1.tensor_tensor(out=ob[:, :, 32:64], in0=t[2], in1=t[3], op=ALU.add)
                else:
                    E1.tensor_tensor(out=x1, in0=t[0], in1=t[1], op=ALU.subtract)
                    E1.tensor_tensor(out=x2, in0=t[2], in1=t[3], op=ALU.add)
            kvv = pj[:rows, 2304:3072].rearrange("p (c j g d) -> p c j g d", c=3, j=2, g=2)
            for c in range(3):
                k.dve.tensor_copy(out=kb[:rows, 2 * c:2 * c + 2, :], in_=kvv[:, c, 0, :, :])
                k.pool.tensor_copy(out=vb[:rows, c, :, 0:64], in_=kvv[:, c, 1, :, :])
            if r0 < Tn:
                k.dma(o_nsa_p[r0:r0 + rows, :], pj[:rows, 2304:2816])
                if r0 >= Tn - WN:
                    k.dma(o_win_p[r0 - (Tn - WN):r0 - (Tn - WN) + rows, :], pj[:rows, 2816:3072])
                if ti == NT - 1:
                    k.dma(o_sh_p[0:1, :], pj[127:128, 0:RWC])
            else:
                k.dma(o_nsa_s[0:16, :], pj[:16, 2304:2816])
                for bl in range(4):
                    k.dma(o_win_s[bl, 508:512, :], pj[bl * 4:bl * 4 + 4, 2816:3072])
                    k.dma(o_win_s[bl, 0:508, :], st_win[bl, 4:512, :])
                    k.dma(o_sh_s[bl:bl + 1, :], pj[bl * 4 + 3:bl * 4 + 4, 0:RWC])
            for h in range(8):
                k.pe.transpose(out=pQ[:, h, :rows], in_=qb[:rows, h, :], identity=identb[:rows, :rows])
            k.act.copy(out=qT[:, :, :rows], in_=pQ[:, :, :rows])
            for h in range(6):
                k.pe.transpose(out=pK[:, h, :rows], in_=kb[:rows, h, :], identity=identb[:rows, :rows])
            for g in range(2):
                k.pe.transpose(out=pK[:, 6 + g, :rows], in_=vb[:rows, 0, g, 0:64], identity=identb[:rows, :rows])
            k.dve.tensor_copy(out=kT[:, :, :rows], in_=pK[:, 0:6, :rows])
            k.dve.tensor_copy(out=vcT[:, :, :rows], in_=pK[:, 6:8, :rows])
            k.dma(QT[:, :, r0:r0 + rows], qT[:, :, :rows])
            k.dma(KT[:, :, r0:r0 + rows], kT[:, :, :rows])
            k.dma(VcT[:, :, r0:r0 + rows], vcT[:, :, :rows])
            k.dma(VT[r0:r0 + rows, :, :, :], vb[:rows, 1:3, :, :])
    if 'B' in stages:
        phase_rwkv(k, locals())
    if 'C' in stages:
        phase_nsa_prompt(k, locals())
    LL = locals()
    if 'S' in stages:
        phase_nsa_sample(k, LL)
    if 'D' in stages:
        layer_tail(k, LL, 0, AO, w_out0, None, H1, None)
    if 'E' in stages:
        phase_proj1(k, LL)
    if 'F' in stages:
        phase_moba_prompt(k, LL)
    if 'M' in stages:
        phase_moba_sample(k, LL)
    if 'G' in stages:
        layer_tail(k, LL, 1, MO, w_out1, H1, None, (y_p, y_s))
    nc2 = k.finish()
    return nc2, k


def phase_rwkv(k, L):
    Tn = L['Tn']; RW = L['RW']; AO = L['AO']; identf = L['identf']; identb = L['identb']
    rw_mu = L['rw_mu']; rw_vec = L['rw_vec']; st_wkv = L['st_wkv']; st_shift = L['st_shift']
    with k.scope():
        mu = k.sb('mu', [64, RWC]); k.dma(mu[:, :], rw_mu[0:1, :].pbc(64))
        vec = k.sb('vec', [64, 7, 512])
        for i in range(7):
            k.dma(vec[:, i, :], rw_vec[i:i + 1, :].pbc(64))
        w0b, a0b, kkb, kab, rkb, lgb, lbb = [vec[:, i, :] for i in range(7)]
        wup = k.sb('wup', [64, 512]); k.dma(wup[:, :], L['rw_wup'][:, :])
        aup = k.sb('aup', [64, 512]); k.dma(aup[:, :], L['rw_aup'][:, :])
        gup = k.sb('gup', [128, 512]); k.dma(gup[:, :], L['rw_gup'][:, :])
        mk = k.sb('mk', [64, 3, 64]); k.dma(mk[:, :, :], L['masks64'][:, :, :])
        ones = k.sb('ones', [64, 1]); k.op('dve', lambda e: e.memset(ones[:, :].ap, 1.0), (), (ones,))
        banks = [k.ps('bank', [128, 512]) for _ in range(6)]
        pfbs = [k.ps('pfb', [64, 8, 64], BF16) for _ in range(2)]
        bi = [0]

        def bank():
            b = banks[bi[0] % 6]
            bi[0] += 1
            return b

        ST = k.sb('ST', [64, 8, 64]); STb = k.sb('STb', [64, 8, 64], BF16)
        vbs = [k.sb('vb16', [64, 512], BF16) for _ in range(2)]
        cur = [k.sb('cur', [64, RWC]) for _ in range(2)]
        prv = [k.sb('prv', [64, RWC]) for _ in range(2)]
        Lt = k.sb('Lt', [64, 256]); LT = k.sb('LT', [128, 3, 64])
        lw = k.sb('lw', [64, 512]); av = k.sb('av', [64, 512]); gvs = [k.sb('gv', [64, 512]) for _ in range(2)]
        kk = k.sb('kk', [64, 512]); sq = k.sb('sq', [64, 512]); k2s = [k.sb('k2', [64, 512]) for _ in range(2)]; bv = k.sb('bv', [64, 512])
        sm = k.sb('sm', [64, 8, 4]); sm2 = k.sb('sm2', [64, 8, 4])
        eP = k.sb('eP', [64, 512]); eN = k.sb('eN', [64, 512]); ePm = k.sb('ePm', [64, 512])
        Fs = [k.sb('F', [64, 4, 512], BF16) for _ in range(2)]
        FTs = [[k.sb('FT%d' % i, [64, 8, 64], BF16) for i in range(4)] for _ in range(2)]
        GCs = [k.sb('GC', [64, 8]) for _ in range(2)]
        Mbs = [[k.sb('Mb%d' % i, [64, 8, 64], BF16) for i in range(5)] for _ in range(2)]
        Bt = [k.sb('Bt%d' % i, [64, 8, 64], BF16) for i in range(2)]
        BTt = [k.sb('BTt%d' % i, [64, 8, 64], BF16) for i in range(2)]
        Nts = [k.sb('Nt', [64, 8, 64], BF16) for _ in range(2)]
        Zn = k.sb('Zn', [64, 8, 64], BF16); UT = k.sb('UT', [64, 8, 64], BF16)
        yv = k.sb('yv', [64, 512]); yc = k.sb('yc', [64, 512]); t1 = k.sb('t1', [64, 512]); ob = k.sb('ob', [64, 512], BF16)
        Stmp = k.sb('Stmp', [64, 8, 64])
        ci = [0]

        def hv(t, C):
            return t[:C, :].rearrange("p (h d) -> p h d", h=8)

        def chunk(r0, C, first_prev):
            sb_ = ci[0] % 2
            c = cur[sb_]; p = prv[sb_]; ci[0] += 1
            gv = gvs[sb_]; k2 = k2s[sb_]; vb16 = vbs[sb_]; F = Fs[sb_]; FT = FTs[sb_]; GC = GCs[sb_]; Mb = Mbs[sb_]; Nt = Nts[sb_]
            k.dma(c[:C, :], RW[r0:r0 + C, :])
            if first_prev is None:
                k.dma(p[:C, :], RW[r0 - 1:r0 - 1 + C, :])
            else:
                if first_prev == 'zero':
                    k.op('dve', lambda e: e.memset(p[0:1, :].ap, 0.0), (), (p,))
                else:
                    k.dma(p[0:1, :], first_prev)
                k.dma(p[1:C, :], RW[r0:r0 + C - 1, :])
            k.dve.tensor_tensor(out=p[:C, :], in0=p[:C, :], in1=c[:C, :], op=ALU.subtract)
            k.pool.tensor_tensor(out=p[:C, :], in0=p[:C, :], in1=mu[:C, :], op=ALU.mult)
            k.dve.tensor_tensor(out=c[:C, :], in0=c[:C, :], in1=p[:C, :], op=ALU.add)
            xm = c
            k.pool.tensor_copy(out=vb16[:C, :], in_=c[:C, 1024:1536])
            r_ = xm[:C, 0:512]; k_ = xm[:C, 512:1024]; v_ = xm[:C, 1024:1536]
            k.act.activation(out=Lt[:C, 0:64], in_=xm[:C, 1536:1600], func=AF.Tanh)
            k.act.activation(out=Lt[:C, 128:256], in_=xm[:C, 1664:1792], func=AF.Sigmoid)
            k.dve.tensor_copy(out=Lt[:C, 64:128], in_=xm[:C, 1600:1664])
            pb = bank()
            pl = pb[:, 0:192].rearrange("p (a t) -> p a t", a=3)
            k.pe.transpose(out=pl[0:64, 0, :C], in_=Lt[:C, 0:64], identity=identf[:C, :C])
            k.pe.transpose(out=pl[0:64, 1, :C], in_=Lt[:C, 64:128], identity=identf[:C, :C])
            k.pe.transpose(out=pl[:, 2, :C], in_=Lt[:C, 128:256], identity=identf[:C, :C])
            k.dve.tensor_copy(out=LT[0:64, 0:2, :C], in_=pl[0:64, 0:2, :C])
            k.dve.tensor_copy(out=LT[:, 2, :C], in_=pl[:, 2, :C])
            pW = bank(); pA = bank(); pG = bank()
            k.pe.matmul(out=pW[:C, :], lhsT=LT[0:64, 0, :C], rhs=wup[:, :], start=True, stop=True)
            k.pe.matmul(out=pA[:C, :], lhsT=LT[0:64, 1, :C], rhs=aup[:, :], start=True, stop=True)
            k.pe.matmul(out=pG[:C, :], lhsT=LT[:, 2, :C], rhs=gup[:, :], start=True, stop=True)
            k.dve.tensor_tensor(out=lw[:C, :], in0=pW[:C, :], in1=w0b[:C, :], op=ALU.add)
            k.act.activation(out=lw[:C, :], in_=lw[:C, :], func=AF.Sigmoid)
            k.dve.tensor_scalar(out=lw[:C, :], in0=lw[:C, :], scalar1=-0.6065306597126334, scalar2=None, op0=ALU.mult)
            k.dve.tensor_tensor(out=av[:C, :], in0=pA[:C, :], in1=a0b[:C, :], op=ALU.add)
            k.act.activation(out=av[:C, :], in_=av[:C, :], func=AF.Sigmoid)
            k.act.copy(out=gv[:C, :], in_=pG[:C, :])
            k.pool.tensor_tensor(out=kk[:C, :], in0=k_, in1=kkb[:C, :], op=ALU.mult)
            k.pool.tensor_tensor(out=sq[:C, :], in0=kk[:C, :], in1=kk[:C, :], op=ALU.mult)
            k.dve.reduce_sum(out=sm[:C, :, 0], in_=hv(sq, C), axis=AX.X)
            k.act.activation(out=sm[:C, :, 1], in_=sm[:C, :, 0], func=AF.Sqrt)
            k.dve.tensor_scalar(out=sm[:C, :, 1], in0=sm[:C, :, 1], scalar1=1e-12, scalar2=None, op0=ALU.max)
            k.dve.reciprocal(out=sm[:C, :, 2], in_=sm[:C, :, 1])
            if r0 == 0:
                k.dbg('kk0', kk[:C, :], [64, 512]); k.dbg('sq', sq[:C, :], [64, 512]); k.dbg('sm', sm[:C, :, :], [64, 8, 4])
            k.dve.tensor_tensor(out=hv(kk, C), in0=hv(kk, C), in1=sm[:C, :, 2:3].bc([C, 8, 64]), op=ALU.mult)
            k.dve.scalar_tensor_tensor(out=k2[:C, :], in0=av[:C, :], scalar=-1.0, in1=kab[:C, :], op0=ALU.add, op1=ALU.mult)
            k.dve.scalar_tensor_tensor(out=k2[:C, :], in0=k2[:C, :], scalar=1.0, in1=k_, op0=ALU.add, op1=ALU.mult)
            k.pool.tensor_tensor(out=bv[:C, :], in0=kk[:C, :], in1=av[:C, :], op=ALU.mult)
            pC = bank()
            k.pe.matmul(out=pC[:C, :], lhsT=mk[:C, 0, :C], rhs=lw[:C, :], start=True, stop=True)
            k.act.activation(out=eP[:C, :], in_=pC[:C, :], func=AF.Exp)
            k.act.activation(out=eN[:C, :], in_=pC[:C, :], func=AF.Exp, scale=-1.0)
            k.dve.tensor_tensor(out=ePm[:C, :], in0=pC[:C, :], in1=lw[:C, :], op=ALU.subtract)
            k.act.activation(out=ePm[:C, :], in_=ePm[:C, :], func=AF.Exp)
            k.dve.tensor_tensor(out=F[:C, 0, :], in0=kk[:C, :], in1=ePm[:C, :], op=ALU.mult)
            k.pool.tensor_tensor(out=F[:C, 1, :], in0=bv[:C, :], in1=eN[:C, :], op=ALU.mult)
            k.dve.tensor_tensor(out=F[:C, 2, :], in0=k2[:C, :], in1=eN[:C, :], op=ALU.mult)
            k.pool.tensor_tensor(out=F[:C, 3, :], in0=r_, in1=eP[:C, :], op=ALU.mult)
            pg = bank()
            for h in range(8):
                k.pe.matmul(out=pg[0:64, h:h + 1], lhsT=lw[:C, h * 64:(h + 1) * 64], rhs=ones[:C, 0:1], start=True, stop=True)
            k.act.activation(out=GC[:, :], in_=pg[0:64, 0:8], func=AF.Exp)
            for kind in range(4):
                pfv = pfbs[kind % 2]
                for h in range(8):
                    k.pe.transpose(out=pfv[:, h, :C], in_=F[:C, kind, h * 64:(h + 1) * 64], identity=identb[:C, :C])
                k.dve.tensor_copy(out=FT[kind][:, :, :C], in_=pfv[:, :, :C])
            aT, bT, khT, rT = FT
            combos = [(bT, aT, 1), (aT, bT, 2), (khT, aT, 1), (bT, rT, 0), (khT, rT, 0)]
            for i, (lt, rt, mi) in enumerate(combos):
                pm = bank()
                pmv = pm[0:64, :].rearrange("p (h t) -> p h t", h=8)
                for h in range(8):
                    k.pe.matmul(out=pmv[:C, h, :C], lhsT=lt[:, h, :C], rhs=rt[:, h, :C], start=True, stop=True)
                (k.dve if i % 2 == 0 else k.pool).tensor_tensor(out=Mb[i][:C, :, :C], in0=pmv[:C, :, :C], in1=mk[:C, mi:mi + 1, :C].bc([C, 8, C]), op=ALU.mult) if i % 2 == 0 else k.dve.tensor_tensor(out=Mb[i][:C, :, :C], in0=pmv[:C, :, :C], in1=mk[:C, mi:mi + 1, :C].bc([C, 8, C]), op=ALU.mult)
            A, AT, Mka, Mbr, Mkr = Mb
            k.dve.tensor_tensor(out=Nt[:C, :, :C], in0=identf[:C, 0:C].unsq(1).bc([C, 8, C]), in1=A[:C, :, :C], op=ALU.subtract)
            Bc, BTc = A, AT
            nlev = {64: 5, 4: 1}[C]
            for lev in range(nlev):
                Bn = Bt[lev % 2]; BTn = BTt[lev % 2]
                p1 = bank(); p2 = bank()
                p1v = p1[0:64, :].rearrange("p (h t) -> p h t", h=8); p2v = p2[0:64, :].rearrange("p (h t) -> p h t", h=8)
                if lev < nlev - 1:
                    for h in range(8):
                        k.pe.matmul(out=p1v[:C, h, :C], lhsT=BTc[:C, h, :C], rhs=Bc[:C, h, :C], start=True, stop=True)
                for h in range(8):
                    k.pe.matmul(out=p2v[:C, h, :C], lhsT=Bc[:C, h, :C], rhs=BTc[:C, h, :C], start=True, stop=True)
                if lev < nlev - 1:
                    k.dve.tensor_copy(out=Bn[:C, :, :C], in_=p1v[:C, :, :C])
                k.dve.tensor_copy(out=BTn[:C, :, :C], in_=p2v[:C, :, :C])
                p3 = bank(); p3v = p3[0:64, :].rearrange("p (h t) -> p h t", h=8)
                for h in range(8):
                    k.pe.matmul(out=p3v[:C, h, :C], lhsT=BTn[:C, h, :C], rhs=Nt[:C, h, :C], start=True, stop=True)
                k.dve.tensor_tensor(out=Nt[:C, :, :C], in0=Nt[:C, :, :C], in1=p3v[:C, :, :C], op=ALU.add)
                Bc, BTc = Bn, BTn
            def s2():
                vh = lambda h: vb16[:C, h * 64:(h + 1) * 64]
                pz = bank(); pzv = pz[0:64, :].rearrange("p (h t) -> p h t", h=8)
                for h in range(8):
                    k.pe.matmul(out=pzv[:C, h, :], lhsT=aT[:, h, :C], rhs=STb[:, h, :], start=True, stop=False)
                    k.pe.matmul(out=pzv[:C, h, :], lhsT=Mka[:C, h, :C], rhs=vh(h), start=False, stop=True)
                k.dve.tensor_scalar(out=Zn[:C, :, :], in0=pzv[:C, :, :], scalar1=-1.0, scalar2=None, op0=ALU.mult)
                pu = bank(); puv = pu[0:64, :].rearrange("p (h t) -> p h t", h=8)
                for h in range(8):
                    k.pe.matmul(out=puv[:C, h, :], lhsT=Nt[:C, h, :C], rhs=Zn[:C, h, :], start=True, stop=True)
                k.dve.tensor_copy(out=UT[:C, :, :], in_=puv[:C, :, :])
                py = bank(); pyv = py[0:64, :].rearrange("p (h t) -> p h t", h=8)
                for h in range(8):
                    k.pe.matmul(out=pyv[:C, h, :], lhsT=rT[:, h, :C], rhs=STb[:, h, :], start=True, stop=False)
                    k.pe.matmul(out=pyv[:C, h, :], lhsT=Mbr[:C, h, :C], rhs=UT[:C, h, :], start=False, stop=False)
                    k.pe.matmul(out=pyv[:C, h, :], lhsT=Mkr[:C, h, :C], rhs=vh(h), start=False, stop=True)
                k.act.copy(out=yv[:C, :], in_=py[:C, :])
                pS = bank(); pSv = pS[0:64, :].rearrange("p (h t) -> p h t", h=8)
                for h in range(8):
                    k.pe.matmul(out=pSv[:, h, :], lhsT=F[:C, 1, h * 64:(h + 1) * 64], rhs=UT[:C, h, :], start=True, stop=False)
                    k.pe.matmul(out=pSv[:, h, :], lhsT=F[:C, 2, h * 64:(h + 1) * 64], rhs=vh(h), start=False, stop=True)
                k.dve.tensor_tensor(out=ST[:, :, :], in0=ST[:, :, :], in1=pSv[:, :, :], op=ALU.add)
                k.dve.tensor_tensor(out=ST[:, :, :], in0=ST[:, :, :], in1=GC[:, :].unsq(2).bc([64, 8, 64]), op=ALU.mult)
                k.pool.tensor_copy(out=STb[:, :, :], in_=ST[:, :, :])
                k.dve.reduce_sum(out=sm2[:C, :, 0], in_=hv(yv, C), axis=AX.X)
                k.dve.tensor_scalar(out=sm2[:C, :, 0], in0=sm2[:C, :, 0], scalar1=1.0 / 64, scalar2=None, op0=ALU.mult)
                k.dve.tensor_tensor(out=hv(yc, C), in0=hv(yv, C), in1=sm2[:C, :, 0:1].bc([C, 8, 64]), op=ALU.subtract)
                k.pool.tensor_tensor(out=t1[:C, :], in0=yc[:C, :], in1=yc[:C, :], op=ALU.mult)
                k.dve.reduce_sum(out=sm2[:C, :, 1], in_=hv(t1, C), axis=AX.X)
                k.dve.tensor_scalar(out=sm2[:C, :, 1], in0=sm2[:C, :, 1], scalar1=1.0 / 64, scalar2=64e-5, op0=ALU.mult, op1=ALU.add)
                k.act.activation(out=sm2[:C, :, 1], in_=sm2[:C, :, 1], func=AF.Sqrt)
                k.dve.reciprocal(out=sm2[:C, :, 2], in_=sm2[:C, :, 1])
                k.dve.tensor_tensor(out=hv(yc, C), in0=hv(yc, C), in1=sm2[:C, :, 2:3].bc([C, 8, 64]), op=ALU.mult)
                k.dve.tensor_tensor(out=yc[:C, :], in0=yc[:C, :], in1=lgb[:C, :], op=ALU.mult)
                k.dve.tensor_tensor(out=yc[:C, :], in0=yc[:C, :], in1=lbb[:C, :], op=ALU.add)
                k.pool.tensor_tensor(out=t1[:C, :], in0=r_, in1=k2[:C, :], op=ALU.mult)
                k.pool.tensor_tensor(out=t1[:C, :], in0=t1[:C, :], in1=rkb[:C, :], op=ALU.mult)
                k.dve.reduce_sum(out=sm2[:C, :, 3], in_=hv(t1, C), axis=AX.X)
                k.dve.tensor_tensor(out=hv(t1, C), in0=xm[:C, 1024:1536].rearrange("p (h d) -> p h d", h=8), in1=sm2[:C, :, 3:4].bc([C, 8, 64]), op=ALU.mult)
                k.dve.tensor_tensor(out=yc[:C, :], in0=yc[:C, :], in1=t1[:C, :], op=ALU.add)
                k.dve.tensor_tensor(out=ob[:C, :], in0=yc[:C, :], in1=gv[:C, :], op=ALU.mult)
                k.dma(AO[r0:r0 + C, 0:512], ob[:C, :])
                if r0 == 0:
                    k.dbg('xm', xm[:C, :], [64, RWC]); k.dbg('lw', lw[:C, :], [64, 512]); k.dbg('av', av[:C, :], [64, 512])
                    k.dbg('kk', kk[:C, :], [64, 512]); k.dbg('k2', k2[:C, :], [64, 512]); k.dbg('F', F[:C, :, :], [64, 4, 512])
                    k.dbg('aT', FT[0][:, :, :], [64, 8, 64]); k.dbg('A', Mb[0][:, :, :], [64, 8, 64]); k.dbg('AT', Mb[1][:, :, :], [64, 8, 64])
                    k.dbg('N', Nt[:, :, :], [64, 8, 64]); k.dbg('UT', UT[:, :, :], [64, 8, 64]); k.dbg('yv', yv[:C, :], [64, 512])
                    k.dbg('ST', ST[:, :, :], [64, 8, 64]); k.dbg('GC', GC[:, :], [64, 8]); k.dbg('gv', gv[:C, :], [64, 512])
                    k.dbg('ob', ob[:C, :], [64, 512], BF16)
            return s2

        def store_state(dst):
            pt = bank(); ptv = pt[0:64, :].rearrange("p (h t) -> p h t", h=8)
            for h in range(8):
                k.pe.transpose(out=ptv[:, h, :], in_=ST[:, h, :], identity=identf[0:64, 0:64])
            k.dve.tensor_copy(out=Stmp[:, :, :], in_=ptv[:, :, :])
            k.dma(dst.rearrange("h i j -> i h j"), Stmp[:, :, :])

        k.op('dve', lambda e: e.memset(ST[:, :, :].ap, 0.0), (), (ST,))
        k.op('dve', lambda e: e.memset(STb[:, :, :].ap, 0.0), (), (STb,))
        pend = None
        for ch in range(Tn // 64):
            nxt = chunk(ch * 64, 64, 'zero' if ch == 0 else None)
            if pend is not None:
                pend()
            pend = nxt
        pend()
        store_state(L['o_wkv_p'][:, :, :])
        for bl in range(4):
            k.dma(Stmp[:, :, :], st_wkv[bl].rearrange("h i j -> i h j"))
            pt = bank(); ptv = pt[0:64, :].rearrange("p (h t) -> p h t", h=8)
            for h in range(8):
                k.pe.transpose(out=ptv[:, h, :], in_=Stmp[:, h, :], identity=identf[0:64, 0:64])
            k.dve.tensor_copy(out=ST[:, :, :], in_=ptv[:, :, :])
            k.dve.tensor_copy(out=STb[:, :, :], in_=ptv[:, :, :])
            chunk(Tn + bl * 4, 4, st_shift[bl:bl + 1, :])()
            store_state(L['o_wkv_s'][bl])


def pipe(items, sc, pv):
    items = list(items)
    if not items:
        return
    nxt = sc(items[0])
    for i, it in enumerate(items):
        cur_e = nxt
        if i + 1 < len(items):
            nxt = sc(items[i + 1])
        pv(it, cur_e)


def gelu_tanh(k, out_bf, x, tmpa, tmpb, shape_idx):
    k.dve.tensor_tensor(out=tmpa, in0=x, in1=x, op=ALU.mult)
    k.dve.tensor_scalar(out=tmpa, in0=tmpa, scalar1=0.044715, scalar2=1.0, op0=ALU.mult, op1=ALU.add)
    k.dve.tensor_tensor(out=tmpa, in0=tmpa, in1=x, op=ALU.mult)
    k.act.activation(out=tmpb, in_=tmpa, func=AF.Tanh, scale=0.7978845608028654)
    k.dve.tensor_scalar(out=tmpb, in0=tmpb, scalar1=1.0, scalar2=0.5, op0=ALU.add, op1=ALU.mult)
    k.dve.tensor_tensor(out=out_bf, in0=tmpb, in1=x, op=ALU.mult)


def load_cmp_weights(k, L, pbias):
    cmp_w1 = L['cmp_w1']; cmp_pe = L['cmp_pe']; cmp_w2 = L['cmp_w2']
    W = {}
    tiles_ = [(k.sb('cw1_%d' % kind, [64, 32, 64], BF16), k.sb('cw2_%d' % kind, [64, 64], BF16),
               k.sb('peT%d' % kind, [64, 32], BF16), k.sb('cbias%d' % kind, [64, 1])) for kind in range(2)]
    sc_ = k.scope(); sc_.__enter__()
    stg = k.sb('cw_stg', [64, 32, 64]); stg2 = k.sb('cw_stg2', [64, 64]); pes = k.sb('pe_stg', [64, 32])
    for kind in range(2):
        w1, w2, peT, bias = tiles_[kind]
        k.dma(stg[:, :, :], cmp_w1[kind].rearrange("c d e -> d c e"))
        k.dve.tensor_copy(out=w1[:, :, :], in_=stg[:, :, :])
        k.dma(stg2[:, :], cmp_w2[kind])
        k.dve.tensor_copy(out=w2[:, :], in_=stg2[:, :])
        k.dma(pes[:, :], cmp_pe[kind].rearrange("c d -> d c"), allow_slow_non_contiguous=True)
        k.dve.tensor_copy(out=peT[:, :], in_=pes[:, :])
        for c in range(32):
            k.pe.matmul(out=pbias[0:64, kind:kind + 1], lhsT=w1[:, c, :], rhs=peT[:, c:c + 1], start=(c == 0), stop=(c == 31))
        k.dve.tensor_copy(out=bias[:, :], in_=pbias[0:64, kind:kind + 1])
        W[kind] = (w1, w2, bias)
    sc_.__exit__(None, None, None)
    return W


def compress(k, W, kind, srcT, NBn, ps_bank, hx, ta, tb, hbf):
    w1, w2, bias = W[kind]
    for c in range(32):
        k.pe.matmul(out=ps_bank[0:64, 0:NBn], lhsT=w1[:, c, :], rhs=srcT[:, c:c + 16 * (NBn - 1) + 1:16], start=(c == 0), stop=(c == 31))
    k.act.activation(out=hx[:, 0:NBn], in_=ps_bank[0:64, 0:NBn], func=AF.Identity, bias=bias[:, 0:1])
    gelu_tanh(k, hbf[:, 0:NBn], hx[:, 0:NBn], ta[:, 0:NBn], tb[:, 0:NBn], None)


def phase_nsa_prompt(k, L):
    Tn = L['Tn']; NT = L['NT']; NSEL = L['NSEL']; NB = L['NB']; NBT = L['NBT']
    QT = L['QT']; KT = L['KT']; VcT = L['VcT']; VT = L['VT']; GT = L['GT']; AO = L['AO']; identb = L['identb']
    with k.scope():
        ps_s = [k.ps('ps_s', [128, 512]) for _ in range(3)]
        W = load_cmp_weights(k, L, ps_s[0])
        cm = k.sb('cm', [128, 17, 128], BF16); k.dma(cm[:, :, :], L['CMt'][:, :, :])
        tri = k.sb('tri', [128, 2, 128], BF16); k.dma(tri[:, :, :], L['TRIt'][:, :, :])
        po_sel = [k.ps('po_sel', [128, 4, 65]) for _ in range(1)]
        po_win = [k.ps('po_win', [128, 4, 65]) for _ in range(1)]
        po_c = [k.ps('po_c', [128, 65 + NSEL]) for _ in range(2)]
        ps_t = k.ps('ps_t', [128, 128], BF16)
        si = [0]

        def sbank():
            b = ps_s[si[0] % 3]; si[0] += 1
            return b

        KcT = k.sb('KcT', [64, Tn], BF16); VcTs = k.sb('VcTs', [64, Tn], BF16)
        KA = NSEL + 64
        KsT = k.sb('KsT', [KA, Tn], BF16); KwT = k.sb('KwT', [64, Tn], BF16)
        k.dma(KsT[0:NSEL, :], L['EEXP'][0:NSEL, 0:Tn])
        qsel = [k.sb('qsel', [KA, 4, 128], BF16) for _ in range(2)]
        Vs = k.sb('Vs', [128, NT, 65], BF16); Vw = k.sb('Vw', [128, NT, 65], BF16)
        KcC = k.sb('KcC', [64, NBT * 128], BF16)
        VcC = k.sb('VcC', [128, NBT, 65 + NSEL], BF16)
        hx = k.sb('hx', [64, 512]); ta = k.sb('ta', [64, 512]); tb = k.sb('tb', [64, 512])
        hk = k.sb('hk', [64, 512], BF16); hv_ = k.sb('hv_', [64, 512], BF16)
        q4 = [k.sb('q4', [64, 4, 128], BF16) for _ in range(2)]
        gtile = [k.sb('gtile', [128, 24]) for _ in range(2)]
        fbt = [k.sb('fbt', [128, NSEL]) for _ in range(2)]
        ebuf = [k.sb('ebuf', [128, 512], BF16) for _ in range(3)]
        ei = [0]

        def enext():
            b = ebuf[ei[0] % 3]; ei[0] += 1
            return b

        acc = k.sb('acc', [128, 4, 64]); accb = k.sb('accb', [128, 4, 64], BF16); tmp4 = k.sb('tmp4', [128, 4, 64])
        rr = k.sb('rr', [128, 16]); imp = k.sb('imp', [128, NSEL]); score = k.sb('score', [128, NSEL]); sc2 = k.sb('sc2', [128, NSEL])
        m8 = k.sb('m8', [128, 16]); negs = k.sb('negs', [128, NSEL], BF16)
        negT4 = k.sb('negT4', [NSEL, 4, 128], BF16)
        for g in range(2):
            k.dma(KcT[:, :], KT[:, 0 + g, 0:Tn]); k.dma(VcTs[:, :], VcT[:, g, 0:Tn])
            k.dma(KsT[NSEL:KA, :], KT[:, 2 + g, 0:Tn]); k.dma(KwT[:, :], KT[:, 4 + g, 0:Tn])
            k.dma(Vs[:, :, :], VT[0:Tn, 0, g, :].rearrange("(kt p) c -> p kt c", p=128))
            k.dma(Vw[:, :, :], VT[0:Tn, 1, g, :].rearrange("(kt p) c -> p kt c", p=128))
            k.dma(VcC[:, :, 65:65 + NSEL], L['OVt'][:, :, :])
            k.op('dve', lambda e: e.memset(VcC[:, :, 64:65].ap, 1.0), (), (VcC,))
            pb = sbank()
            compress(k, W, 0, KcT, NB, pb, hx, ta, tb, hk)
            pb2 = sbank()
            k.pe.matmul(out=pb2[0:64, 0:NB], lhsT=W[0][1][:, :], rhs=hk[:, 0:NB], start=True, stop=True)
            k.dve.tensor_copy(out=KcC[:, 0:NB], in_=pb2[0:64, 0:NB])
            pb = sbank()
            compress(k, W, 1, VcTs, NB, pb, hx, ta, tb, hv_)
            for ni in range(NBT):
                nn = min(128, NB - ni * 128)
                pb3 = sbank()
                k.pe.matmul(out=pb3[:nn, 0:64], lhsT=hv_[:, ni * 128:ni * 128 + nn], rhs=W[1][1][:, :], start=True, stop=True)
                k.dve.tensor_copy(out=VcC[:nn, ni, 0:64], in_=pb3[:nn, 0:64])
            def loads(tt):
                k.dma(q4[tt % 2][:, :, :], QT[:, g * 4:(g + 1) * 4, tt * 128:tt * 128 + 128])
                k.dma(qsel[tt % 2][NSEL:KA, :, :], QT[:, g * 4:(g + 1) * 4, tt * 128:tt * 128 + 128])
                k.dma(gtile[tt % 2][:, :], GT[tt * 128:tt * 128 + 128, :])
                k.dma(fbt[tt % 2][:, :], L['FBt'][tt * 128:tt * 128 + 128, :])

            loads(0)
            for tt in range(NT):
                t0 = tt * 128
                q = q4[tt % 2]; gt_ = gtile[tt % 2]; fb = fbt[tt % 2]
                if tt + 1 < NT:
                    loads(tt + 1)
                gv3 = gt_[:, :].rearrange("p (h b) -> p h b", b=3)
                nvalid = min(NB, (t0 + 96) // 16 + 1)
                nnt = (nvalid + 127) // 128
                qf = q[:, :, :].rearrange("p r t -> p (r t)")
                es_ = []
                for ni in range(nnt):
                    nn = min(128, NB - ni * 128)
                    pss = sbank()
                    k.pe.matmul(out=pss[:nn, :], lhsT=KcC[:, ni * 128:ni * 128 + nn], rhs=qf, start=True, stop=True)
                    e = enext()
                    k.act.activation(out=e[:nn, :], in_=pss[:nn, :], func=AF.Exp, scale=0.125)
                    delta = t0 - 2048 * ni
                    if delta < 17 * 128:
                        assert delta >= 0
                        ev = e[:nn, :].rearrange("p (r t) -> p r t", r=4)
                        k.pool.tensor_tensor(out=ev, in0=ev, in1=cm[:nn, delta // 128:delta // 128 + 1, :].bc([nn, 4, 128]), op=ALU.mult)
                    es_.append((e, nn))
                for r in range(4):
                    po = po_c[r % 2]
                    for ni, (e, nn) in enumerate(es_):
                        k.pe.matmul(out=po[:, :], lhsT=e[:nn, r * 128:(r + 1) * 128], rhs=VcC[:nn, ni, :], start=(ni == 0), stop=(ni == nnt - 1))
                    k.dve.tensor_scalar(out=rr[:, r:r + 1], in0=po[:, 64:65], scalar1=1e-30, scalar2=None, op0=ALU.max)
                    k.dve.reciprocal(out=rr[:, 4 + r:5 + r], in_=rr[:, r:r + 1])
                    k.dve.tensor_tensor(out=rr[:, 8 + r:9 + r], in0=rr[:, 4 + r:5 + r], in1=gv3[:, g * 4 + r, 0:1], op=ALU.mult)
                    k.dve.tensor_scalar(out=acc[:, r, :], in0=po[:, 0:64], scalar1=rr[:, 8 + r:9 + r], scalar2=None, op0=ALU.mult)
                    if r == 0:
                        k.dve.tensor_scalar(out=imp[:, :], in0=po[:, 65:65 + NSEL], scalar1=rr[:, 4 + r:5 + r], scalar2=None, op0=ALU.mult)
                    else:
                        k.dve.scalar_tensor_tensor(out=imp[:, :], in0=po[:, 65:65 + NSEL], scalar=rr[:, 4 + r:5 + r], in1=imp[:, :], op0=ALU.mult, op1=ALU.add)
                k.dve.tensor_tensor(out=score[:, :], in0=imp[:, :], in1=fb[:, :], op=ALU.add)
                k.dve.max(out=m8[:, 0:8], in_=score[:, :])
                k.dve.match_replace(out=sc2[:, :], in_to_replace=m8[:, 0:8], in_values=score[:, :], imm_value=-3.0e38)
                k.dve.max(out=m8[:, 8:16], in_=sc2[:, :])
                k.dve.tensor_scalar(out=rr[:, 12:13], in0=m8[:, 15:16], scalar1=-1.0e8, scalar2=None, op0=ALU.max)
                k.dve.tensor_scalar(out=negs[:, :], in0=score[:, :], scalar1=rr[:, 12:13], scalar2=NEG, op0=ALU.is_lt, op1=ALU.mult)
                k.pe.transpose(out=ps_t[0:NSEL, :], in_=negs[:, :], identity=identb[:, :])
                qs_ = qsel[tt % 2]
                k.dve.tensor_copy(out=qs_[0:NSEL, :, :], in_=ps_t[0:NSEL, :].unsq(1).bc([NSEL, 4, 128]))
                qsf = qs_[:, :, :].rearrange("p r t -> p (r t)")
                qf = q[:, :, :].rearrange("p r t -> p (r t)")
                po = po_sel[0]

                def sc_sel(kt):
                    pss = sbank()
                    k.pe.matmul(out=pss[:, :], lhsT=KsT[:, kt * 128:(kt + 1) * 128], rhs=qsf, start=True, stop=True)
                    e = enext()
                    k.act.activation(out=e[:, :], in_=pss[:, :], func=AF.Exp, scale=0.125)
                    if kt == tt:
                        ev = e[:, :].rearrange("p (r t) -> p r t", r=4)
                        k.pool.tensor_tensor(out=ev, in0=ev, in1=tri[:, 0:1, :].bc([128, 4, 128]), op=ALU.mult)
                    return e

                def pv_sel(kt, e):
                    for r in range(4):
                        k.pe.matmul(out=po[:, r, :], lhsT=e[:, r * 128:(r + 1) * 128], rhs=Vs[:, kt, :], start=(kt == 0 and r == 0), stop=(kt == tt), skip_group_check=True)

                pipe(range(tt + 1), sc_sel, pv_sel)
                k.dve.tensor_scalar(out=rr[:, 0:4], in0=po[:, :, 64], scalar1=1e-30, scalar2=None, op0=ALU.max)
                k.dve.reciprocal(out=rr[:, 4:8], in_=rr[:, 0:4])
                k.dve.tensor_tensor(out=rr[:, 8:12], in0=rr[:, 4:8], in1=gv3[:, g * 4:(g + 1) * 4, 1], op=ALU.mult)
                k.dve.tensor_tensor(out=tmp4[:, :, :], in0=po[:, :, 0:64], in1=rr[:, 8:12].unsq(2).bc([128, 4, 64]), op=ALU.mult)
                k.dve.tensor_tensor(out=acc[:, :, :], in0=acc[:, :, :], in1=tmp4[:, :, :], op=ALU.add)
                po = po_win[0]
                k0 = max(0, tt - 4)
                pow_ = po

                def sc_win(kt):
                    pss = sbank()
                    k.pe.matmul(out=pss[:, :], lhsT=KwT[:, kt * 128:(kt + 1) * 128], rhs=qf, start=True, stop=True)
                    e = enext()
                    k.act.activation(out=e[:, :], in_=pss[:, :], func=AF.Exp, scale=0.125)
                    ev = e[:, :].rearrange("p (r t) -> p r t", r=4)
                    if kt == tt:
                        k.pool.tensor_tensor(out=ev, in0=ev, in1=tri[:, 0:1, :].bc([128, 4, 128]), op=ALU.mult)
                    elif kt == tt - 4:
                        k.pool.tensor_tensor(out=ev, in0=ev, in1=tri[:, 1:2, :].bc([128, 4, 128]), op=ALU.mult)
                    return e

                def pv_win(kt, e):
                    for r in range(4):
                        k.pe.matmul(out=pow_[:, r, :], lhsT=e[:, r * 128:(r + 1) * 128], rhs=Vw[:, kt, :], start=(kt == k0 and r == 0), stop=(kt == tt), skip_group_check=True)

                pipe(range(k0, tt + 1), sc_win, pv_win)
                k.dve.tensor_scalar(out=rr[:, 0:4], in0=po[:, :, 64], scalar1=1e-30, scalar2=None, op0=ALU.max)
                k.dve.reciprocal(out=rr[:, 4:8], in_=rr[:, 0:4])
                k.dve.tensor_tensor(out=rr[:, 8:12], in0=rr[:, 4:8], in1=gv3[:, g * 4:(g + 1) * 4, 2], op=ALU.mult)
                k.dve.tensor_tensor(out=tmp4[:, :, :], in0=po[:, :, 0:64], in1=rr[:, 8:12].unsq(2).bc([128, 4, 64]), op=ALU.mult)
                k.dve.tensor_tensor(out=accb[:, :, :], in0=acc[:, :, :], in1=tmp4[:, :, :], op=ALU.add)
                k.dma(AO[t0:t0 + 128, 512 + g * 256:512 + (g + 1) * 256], accb[:, :, :].rearrange("p r d -> p (r d)"))


def layer_tail(k, L, layer, mix, w_out, h_in, h_out, y_out):
    Tn = L['Tn']; NT = L['NT']; identb = L['identb']; H1 = L['H1']; ACTT = L['ACTT']
    rmsnorm_T = L['rmsnorm_T']; load_w_bf16 = L['load_w_bf16']
    xp = L['xp']; xs = L['xs']
    groups = [(i * 512, min(512, Tn - i * 512)) for i in range((Tn + 511) // 512)] + [(Tn, 16)]
    with k.scope():
        Wo = k.sb('Wo', [128, 8, 1024], BF16); load_w_bf16(Wo, w_out, 1024, 1024)
        Wg = k.sb('Wg', [128, 8, DFF], BF16); load_w_bf16(Wg, L['ffn_g'][layer], 1024, DFF)
        Wu = k.sb('Wu', [128, 8, DFF], BF16); load_w_bf16(Wu, L['ffn_u'][layer], 1024, DFF)
        gbc = k.sb('gbc', [128, 1024]); k.dma(gbc[:, :], L['norm_ffn'][layer:layer + 1, :].pbc(128))
        mt = k.sb('mt', [128, 1024], BF16); mT = k.sb('mT', [128, 8, 128], BF16)
        ht = [k.sb('ht', [128, 1024]) for _ in range(2)]
        junk = k.sb('junk', [128, 1024], BF16); xn = k.sb('xn', [128, 1024], BF16); ss = k.sb('ss', [128, 4])
        xnT = k.sb('xnT', [128, 8, 512], BF16); xnT1 = k.sb('xnT1', [128, 8, 128], BF16)
        sg = k.sb('sg', [128, 512], BF16); at = [k.sb('at', [128, 512], BF16) for _ in range(2)]
        pT = k.ps('pT', [128, 8, 128], BF16)
        pA = [k.ps('pA', [128, 512]) for _ in range(2)]
        pG = [k.ps('pG', [128, 512]) for _ in range(2)]; pU = [k.ps('pU', [128, 512]) for _ in range(2)]
        ci = 0
        for (g0, gn) in groups:
            ntile = (gn + 127) // 128
            for j in range(ntile):
                r0 = g0 + j * 128; rows = min(128, gn - j * 128)
                h = ht[ci % 2]; ci += 1
                k.dma(mt[:rows, :], mix[r0:r0 + rows, :])
                if h_in is None:
                    k.dma(h[:rows, :], xp[r0:r0 + rows, :] if r0 < Tn else xs[0:16, :])
                else:
                    k.dma(h[:rows, :], h_in[r0:r0 + rows, :])
                for kk in range(8):
                    k.pe.transpose(out=pT[:, kk, :rows], in_=mt[:rows, kk * 128:(kk + 1) * 128], identity=identb[:rows, :rows])
                k.act.copy(out=mT[:, :, :rows], in_=pT[:, :, :rows])
                for c in range(2):
                    ps = pA[c]
                    for kk in range(8):
                        k.pe.matmul(out=ps[:rows, :], lhsT=mT[:, kk, :rows], rhs=Wo[:, kk, c * 512:(c + 1) * 512], start=(kk == 0), stop=(kk == 7))
                    k.dve.tensor_tensor(out=h[:rows, c * 512:(c + 1) * 512], in0=h[:rows, c * 512:(c + 1) * 512], in1=ps[:rows, :], op=ALU.add)
                k.dma(H1[r0:r0 + rows, :], h[:rows, :])
                rmsnorm_T(h, rows, gbc, xn, pT, xnT1, ss, junk)
                k.dve.tensor_copy(out=xnT[:, :, j * 128:j * 128 + rows], in_=xnT1[:, :, :rows])
            for hc in range(22):
                pg = pG[hc % 2]; pu = pU[hc % 2]
                for kk in range(8):
                    k.pe.matmul(out=pg[:, :gn], lhsT=Wg[:, kk, hc * 128:(hc + 1) * 128], rhs=xnT[:, kk, :gn], start=(kk == 0), stop=(kk == 7))
                for kk in range(8):
                    k.pe.matmul(out=pu[:, :gn], lhsT=Wu[:, kk, hc * 128:(hc + 1) * 128], rhs=xnT[:, kk, :gn], start=(kk == 0), stop=(kk == 7))
                a = at[hc % 2]
                k.act.activation(out=sg[:, :gn], in_=pg[:, :gn], func=AF.Silu)
                k.dve.tensor_tensor(out=a[:, :gn], in0=sg[:, :gn], in1=pu[:, :gn], op=ALU.mult)
                k.dma(ACTT[hc, :, g0:g0 + gn], a[:, :gn])
    with k.scope():
        Wd = k.sb('Wd', [128, 22, 1024], BF16); load_w_bf16(Wd, L['ffn_d'][layer], DFF, 1024)
        gbc = k.sb('gbc', [128, 1024])
        if y_out is not None:
            k.dma(gbc[:, :], L['norm_final'][0:1, :].pbc(128))
        aT = [k.sb('aT', [128, 22, 128], BF16) for _ in range(2)]
        ht = [k.sb('ht', [128, 1024]) for _ in range(2)]
        junk = k.sb('junk', [128, 1024]); ss = k.sb('ss', [128, 4]); yt = k.sb('yt', [128, 1024])
        pA = [k.ps('pA', [128, 512]) for _ in range(4)]
        ci = 0
        for ti, (r0, rows) in enumerate(L['tiles']):
            a = aT[ti % 2]; h = ht[ti % 2]
            k.dma(a[:, :, :rows], ACTT[:, :, r0:r0 + rows].rearrange("c p t -> p c t"))
            k.dma(h[:rows, :], H1[r0:r0 + rows, :])
            for c in range(2):
                ps = pA[ci % 4]; ci += 1
                for hc in range(22):
                    k.pe.matmul(out=ps[:rows, :], lhsT=a[:, hc, :rows], rhs=Wd[:, hc, c * 512:(c + 1) * 512], start=(hc == 0), stop=(hc == 21))
                k.dve.tensor_tensor(out=h[:rows, c * 512:(c + 1) * 512], in0=h[:rows, c * 512:(c + 1) * 512], in1=ps[:rows, :], op=ALU.add)
            if y_out is None:
                k.dma(H1[r0:r0 + rows, :], h[:rows, :])
            else:
                k.act.activation(out=junk[:rows, :], in_=h[:rows, :], func=AF.Square, accum_out=ss[:rows, 0:1])
                k.dve.tensor_scalar(out=ss[:rows, 1:2], in0=ss[:rows, 0:1], scalar1=1.0 / 1024, scalar2=1e-6, op0=ALU.mult, op1=ALU.add)
                k.act.activation(out=ss[:rows, 3:4], in_=ss[:rows, 1:2], func=AF.Sqrt)
                k.dve.reciprocal(out=ss[:rows, 2:3], in_=ss[:rows, 3:4])
                k.dve.scalar_tensor_tensor(out=yt[:rows, :], in0=h[:rows, :], scalar=ss[:rows, 2:3], in1=gbc[:rows, :], op0=ALU.mult, op1=ALU.mult)
                if r0 < Tn:
                    k.dma(y_out[0][r0:r0 + rows, :], yt[:rows, :])
                else:
                    k.dma(y_out[1][0:16, :], yt[:16, :])


def phase_proj1(k, L):
    Tn = L['Tn']; NT = L['NT']; identb = L['identb']; identf = L['identf']; H1 = L['H1']
    NBLK = L['NBLK']; NBLKP = L['NBLKP']
    rmsnorm_T = L['rmsnorm_T']; load_w_bf16 = L['load_w_bf16']
    QT2 = L['QT2']; KT2 = L['KT2']; VT2 = L['VT2']; NS2 = L['NS2']
    with k.scope():
        W1 = k.sb('W1', [128, 8, ODC], BF16); load_w_bf16(W1, L['w_in1'], 1024, ODC)
        gbc = k.sb('gbc', [128, 1024]); k.dma(gbc[:, :], L['norm_mix'][1:2, :].pbc(128))
        xt = [k.sb('xt', [128, 1024]) for _ in range(2)]
        junk = k.sb('junk', [128, 1024], BF16); xn = k.sb('xn', [128, 1024], BF16); ss = k.sb('ss', [128, 4])
        xnT = k.sb('xnT', [128, 8, 128], BF16)
        proj = [k.sb('proj', [128, ODC]) for _ in range(2)]
        cs = k.sb('cs', [128, 64])
        tmp = [k.sb('tmp%d' % i, [128, 16, 32]) for i in range(4)]
        qb = k.sb('qb', [128, 16, 64], BF16); kb = k.sb('kb', [128, 4, 64], BF16)
        vb = k.sb('vb', [128, 4, 65], BF16)
        k.op('dve', lambda e: e.memset(vb[:, :, :].ap, 1.0), (), (vb,))
        ones = k.sb('ones', [128, 1]); k.op('dve', lambda e: e.memset(ones[:, :].ap, 1.0), (), (ones,))
        qT = k.sb('qT', [64, 16, 128], BF16); kT = k.sb('kT', [64, 4, 128], BF16)
        qTf = k.sb('qTf', [64, 16, 128])
        meansT = k.sb('meansT', [64, 4, NBLKP])
        k.op('dve', lambda e: e.memset(meansT[:, :, :].ap, 0.0), (), (meansT,))
        gbm = k.sb('gbm', [128, 2, NBLKP])
        score = k.sb('score', [128, 16, NBLKP]); m8 = k.sb('m8', [128, 16, 8]); thr = k.sb('thr', [128, 16])
        negs = k.sb('negs', [128, 16, NBLKP]); negb = k.sb('negb', [128, 16, NBLKP], BF16)
        pT = k.ps('pT', [128, 8, 128], BF16)
        pA = [k.ps('pA', [128, 512]) for _ in range(2)]
        pQ = k.ps('pQ', [64, 8, 128], BF16)
        pK = k.ps('pK', [64, 4, 128], BF16)
        pQf = [k.ps('pQf', [64, 4, 128]) for _ in range(1)]
        pM = k.ps('pM', [64, 4, NBLKP])
        pGt = k.ps('pGt', [128, 16, NBLKP])
        ci = 0
        for ti, (r0, rows) in enumerate(L['tiles']):
            x = xt[ti % 2]; pj = proj[ti % 2]
            k.dma(x[:rows, :], H1[r0:r0 + rows, :])
            k.dma(cs[:rows, :], L['ropecs'][r0:r0 + rows, :])
            rmsnorm_T(x, rows, gbc, xn, pT, xnT, ss, junk)
            for c0 in range(0, ODC, 512):
                ps = pA[ci % 2]; ci += 1
                for kk in range(8):
                    k.pe.matmul(out=ps[:rows, :], lhsT=xnT[:, kk, :rows], rhs=W1[:, kk, c0:c0 + 512], start=(kk == 0), stop=(kk == 7))
                (k.act.copy if ci % 2 else k.dve.tensor_copy)(out=pj[:rows, c0:c0 + 512], in_=ps[:rows, :])
            cosb = lambda n: cs[:rows, 0:32].unsq(1).bc([rows, n, 32])
            sinb = lambda n: cs[:rows, 32:64].unsq(1).bc([rows, n, 32])
            for vi, (c0, n) in enumerate([(0, 16), (1024, 4)]):
                xv = pj[:rows, c0:c0 + n * 64].rearrange("p (h d) -> p h d", h=n)
                E1 = k.dve if vi == 0 else k.pool
                x1 = xv[:, :, 0:32]; x2 = xv[:, :, 32:64]
                t = [tt_[:rows, 0:n, :] for tt_ in tmp]
                E1.tensor_tensor(out=t[0], in0=x1, in1=cosb(n), op=ALU.mult)
                E1.tensor_tensor(out=t[1], in0=x2, in1=sinb(n), op=ALU.mult)
                E1.tensor_tensor(out=t[2], in0=x2, in1=cosb(n), op=ALU.mult)
                E1.tensor_tensor(out=t[3], in0=x1, in1=sinb(n), op=ALU.mult)
                E1.tensor_tensor(out=x1, in0=t[0], in1=t[1], op=ALU.subtract)
                E1.tensor_tensor(out=x2, in0=t[2], in1=t[3], op=ALU.add)
            qv = pj[:rows, 0:1024].rearrange("p (h d) -> p h d", h=16)
            kv_ = pj[:rows, 1024:1280].rearrange("p (h d) -> p h d", h=4)
            vv = pj[:rows, 1280:1536].rearrange("p (h d) -> p h d", h=4)
            k.dve.tensor_copy(out=qb[:rows, :, :], in_=qv)
            k.pool.tensor_copy(out=kb[:rows, :, :], in_=kv_)
            k.pool.tensor_copy(out=vb[:rows, :, 0:64], in_=vv)
            if r0 < Tn:
                k.dma(L['o_moba_p'][r0:r0 + rows, :], pj[:rows, 1024:1536])
            else:
                k.dma(L['o_moba_s'][0:16, :], pj[:16, 1024:1536])
            for hh in range(2):
                for h in range(8):
                    k.pe.transpose(out=pQ[:, h, :rows], in_=qb[:rows, hh * 8 + h, :], identity=identb[:rows, :rows])
                k.act.copy(out=qT[:, hh * 8:(hh + 1) * 8, :rows], in_=pQ[:, :, :rows])
            for h in range(4):
                k.pe.transpose(out=pK[:, h, :rows], in_=kb[:rows, h, :], identity=identb[:rows, :rows])
            k.dve.tensor_copy(out=kT[:, :, :rows], in_=pK[:, :, :rows])
            k.dma(QT2[:, :, r0:r0 + rows], qT[:, :, :rows])
            k.dma(KT2[:, :, r0:r0 + rows], kT[:, :, :rows])
            k.dma(VT2[r0:r0 + rows, :, :], vb[:rows, :, :])
            for hq in range(4):
                pq = pQf[0]
                for h in range(4):
                    k.pe.transpose(out=pq[:, h, :rows], in_=pj[:rows, (hq * 4 + h) * 64:(hq * 4 + h + 1) * 64], identity=identf[:rows, :rows])
                k.act.copy(out=qTf[:, hq * 4:(hq + 1) * 4, :rows], in_=pq[:, :, :rows])
            if r0 >= Tn:
                k.dma(L['QF2'][:, :, :], qTf[:, :, 0:16])
                continue
            blk = r0 // 256
            for h in range(16):
                k.pe.matmul(out=pGt[:rows, h, :], lhsT=qTf[:, h, :rows], rhs=meansT[:, h // 4, :], start=(h == 0), stop=(h == 15), skip_group_check=True)
            k.dma(gbm[:rows, :, :], L['GBM'][r0:r0 + rows, :, :])
            k.dve.tensor_tensor(out=score[:rows, :, :], in0=pGt[:rows, :, :], in1=gbm[:rows, 0:1, :].bc([rows, 16, NBLKP]), op=ALU.add)
            for h in range(16):
                k.dve.max(out=m8[:rows, h, :], in_=score[:rows, h, :])
            k.dve.tensor_scalar(out=thr[:rows, :], in0=m8[:rows, :, 2], scalar1=-1.0e8, scalar2=None, op0=ALU.max)
            k.dve.tensor_tensor(out=negs[:rows, :, :], in0=score[:rows, :, :], in1=thr[:rows, :].unsq(2).bc([rows, 16, NBLKP]), op=ALU.is_lt)
            k.dve.tensor_tensor(out=negs[:rows, :, :], in0=negs[:rows, :, :], in1=gbm[:rows, 1:2, :].bc([rows, 16, NBLKP]), op=ALU.mult)
            k.dve.tensor_scalar(out=negb[:rows, :, :], in0=negs[:rows, :, :], scalar1=NEG, scalar2=None, op0=ALU.mult)
            k.dma(NS2[r0:r0 + rows, :, :], negb[:rows, :, :])
            for g in range(4):
                first = (r0 % 256 == 0)
                k.pe.matmul(out=pM[:, g, blk:blk + 1], lhsT=pj[:rows, 1024 + g * 64:1024 + (g + 1) * 64], rhs=ones[:rows, 0:1],
                            start=(ti == 0 and g == 0), stop=not first, skip_group_check=True)
            if r0 % 256 == 128:
                k.dve.tensor_scalar(out=meansT[:, :, blk:blk + 1], in0=pM[:, :, blk:blk + 1], scalar1=1.0 / 2048, scalar2=None, op0=ALU.mult)


def phase_moba_prompt(k, L):
    Tn = L['Tn']; NT = L['NT']; NBLK = L['NBLK']; NBLKP = L['NBLKP']; identb = L['identb']
    QT2 = L['QT2']; KT2 = L['KT2']; VT2 = L['VT2']; NS2 = L['NS2']; MO = L['MO']
    with k.scope():
        tri = k.sb('tri', [128, 2, 128], BF16); k.dma(tri[:, :, :], L['TRIt'][:, :, :])
        ps_s = [k.ps('ps_s', [128, 512]) for _ in range(3)]
        po_ = [k.ps('po', [128, 4, 65]) for _ in range(2)]
        ps_t = k.ps('ps_t', [NBLKP, 4, 128], BF16)
        KA = NBLKP + 64
        K2 = k.sb('K2', [KA, Tn], BF16); V2 = k.sb('V2', [128, NT, 65], BF16)
        k.dma(K2[0:NBLKP, :], L['EEXP2'][0:NBLKP, 0:Tn])
        q4 = [k.sb('q4', [KA, 4, 128], BF16) for _ in range(2)]
        ns = [k.sb('ns', [128, 4, NBLKP], BF16) for _ in range(2)]
        negT4 = k.sb('negT4', [NBLKP, 4, 128], BF16)
        ebuf = [k.sb('ebuf', [128, 512], BF16) for _ in range(3)]
        rr = k.sb('rr', [128, 8]); accb = k.sb('accb', [128, 4, 64], BF16)
        cnt = [0]
        for g in range(4):
            k.dma(K2[NBLKP:KA, :], KT2[:, g, 0:Tn])
            k.dma(V2[:, :, :], VT2[0:Tn, g, :].rearrange("(kt p) c -> p kt c", p=128))
            def loads(tt):
                k.dma(q4[tt % 2][NBLKP:KA, :, :], QT2[:, g * 4:(g + 1) * 4, tt * 128:tt * 128 + 128])
                k.dma(ns[tt % 2][:, :, :], NS2[tt * 128:tt * 128 + 128, g * 4:(g + 1) * 4, :])

            loads(0)
            for tt in range(NT):
                t0 = tt * 128
                q = q4[tt % 2]; n_ = ns[tt % 2]
                if tt + 1 < NT:
                    loads(tt + 1)
                for r in range(4):
                    k.pe.transpose(out=ps_t[:, r, :], in_=n_[:, r, :], identity=identb[:, :])
                k.dve.tensor_copy(out=q[0:NBLKP, :, :], in_=ps_t[:, :, :])
                qf = q[:, :, :].rearrange("p r t -> p (r t)")
                po = po_[tt % 2]
                def sc_m(kt):
                    pss = ps_s[cnt[0] % 3]
                    k.pe.matmul(out=pss[:, :], lhsT=K2[:, kt * 128:(kt + 1) * 128], rhs=qf, start=True, stop=True)
                    e = ebuf[cnt[0] % 3]; cnt[0] += 1
                    k.act.activation(out=e[:, :], in_=pss[:, :], func=AF.Exp, scale=0.125)
                    if kt == tt:
                        ev = e[:, :].rearrange("p (r t) -> p r t", r=4)
                        k.pool.tensor_tensor(out=ev, in0=ev, in1=tri[:, 0:1, :].bc([128, 4, 128]), op=ALU.mult)
                    return e

                def pv_m(kt, e):
                    for r in range(4):
                        k.pe.matmul(out=po[:, r, :], lhsT=e[:, r * 128:(r + 1) * 128], rhs=V2[:, kt, :], start=(kt == 0 and r == 0), stop=(kt == tt), skip_group_check=True)

                pipe(range(tt + 1), sc_m, pv_m)
                k.dve.tensor_scalar(out=rr[:, 0:4], in0=po[:, :, 64], scalar1=1e-30, scalar2=None, op0=ALU.max)
                k.dve.reciprocal(out=rr[:, 4:8], in_=rr[:, 0:4])
                k.dve.tensor_tensor(out=accb[:, :, :], in0=po[:, :, 0:64], in1=rr[:, 4:8].unsq(2).bc([128, 4, 64]), op=ALU.mult)
                k.dma(MO[t0:t0 + 128, g * 256:(g + 1) * 256], accb[:, :, :].rearrange("p r d -> p (r d)"))


def page_indices(k, L):
    P = L['P']
    pti = k.sb('pti', [128, 4 * P], I32); ptf = k.sb('ptf', [128, 4 * P]); io = k.sb('io', [128, 1])
    idx = k.sb('idx', [128, 4 * P], I32)
    k.dma(pti[:, :], L['ptab'][:, :].rearrange("b p -> (b p)").unsq(0).pbc(128) if False else L['ptab'][:, :].rearrange("(o b) p -> o (b p)", o=1).pbc(128))
    k.dma(io[:, :], L['IOTA'][:, :])
    k.dve.tensor_copy(out=ptf[:, :], in_=pti[:, :])
    k.dve.tensor_scalar(out=ptf[:, :], in0=ptf[:, :], scalar1=128.0, scalar2=None, op0=ALU.mult)
    k.dve.tensor_tensor(out=ptf[:, :], in0=ptf[:, :], in1=io[:, 0:1].bc([128, 4 * P]), op=ALU.add)
    k.dve.tensor_copy(out=idx[:, :], in_=ptf[:, :])
    return idx


def gather_page(k, pg, cache, idx, col):
    k.raw16('pool', lambda e: e.indirect_dma_start(out=pg[:, :].ap, out_offset=None, in_=cache[:, :].ap,
                                                   in_offset=bass.IndirectOffsetOnAxis(ap=idx[:, col:col + 1].ap, axis=0)),
            reads=[cache.t if isinstance(cache, V) else cache, idx], writes=[pg])


def phase_nsa_sample(k, L):
    Tn = L['Tn']; P = L['P']; LP = L['LP']; NSELS = L['NSELS']; NBS = L['NBS']; NBTS = L['NBTS']
    QT = L['QT']; KT = L['KT']; VT = L['VT']; GT = L['GT']; AO = L['AO']; identb = L['identb']; identf = L['identf']
    cache = L['cache_nsa']; st_win = L['st_win']
    NJ = LP // 64
    with k.scope():
        ps_s = [k.ps('ps_s', [128, 512]) for _ in range(2)]
        W = load_cmp_weights(k, L, ps_s[0])
        idx = page_indices(k, L)
        eexp = k.sb('eexp', [NJ, LP], BF16); k.dma(eexp[:, :], L['EEXP'][0:NJ, 0:LP])
        smk = k.sb('smk', [128, 2, 16], BF16); k.dma(smk[:, :, :], L['SMK'][:, :, :])
        fbs = k.sb('fbs', [4, NSELS]); k.dma(fbs[:, :], L['FBs'][:, :])
        pX = [k.ps('pX', [64, 4, 128]) for _ in range(1)]
        pXb = [k.ps('pXb', [64, 8, 128], BF16) for _ in range(1)]
        stg = [k.sb('stg', [128, 384], BF16) for _ in range(2)]
        po_a = [k.ps('po_a', [4, 2, 65 + NSELS]) for _ in range(2)]
        po_b = k.ps('po_b', [4, 4, 65])
        ps_t = k.ps('ps_t', [128, 16], BF16)
        si = [0]

        def sbank():
            b = ps_s[si[0] % 2]; si[0] += 1
            return b

        pg = [k.sb('pg', [128, 512]) for _ in range(3)]
        KcTs = k.sb('KcTs', [64, 2, LP], BF16); VcTs = k.sb('VcTs', [64, 2, LP], BF16); KsTs = k.sb('KsTs', [64, 2, LP], BF16)
        Vss = k.sb('Vss', [128, P, 2, 65], BF16)
        k.op('dve', lambda e: e.memset(Vss[:, :, :, 64:65].ap, 1.0), (), (Vss,))
        KwTs = k.sb('KwTs', [64, 2, 512], BF16); Vws = k.sb('Vws', [128, 4, 2, 65], BF16)
        k.op('dve', lambda e: e.memset(Vws[:, :, :, 64:65].ap, 1.0), (), (Vws,))
        wt = [k.sb('wt', [128, 256]) for _ in range(2)]
        KcCs = k.sb('KcCs', [64, NBTS * 128], BF16)
        VcCs = k.sb('VcCs', [128, NBTS, 65 + NSELS], BF16)
        k.dma(VcCs[:, :, 65:65 + NSELS], L['OVs'][:, :, :])
        k.op('dve', lambda e: e.memset(VcCs[:, :, 64:65].ap, 1.0), (), (VcCs,))
        hx = k.sb('hx', [64, 512]); ta = k.sb('ta', [64, 512]); tb = k.sb('tb', [64, 512])
        hk = k.sb('hk', [64, 512], BF16); hv_ = k.sb('hv_', [64, 512], BF16)
        qs_all = k.sb('qs_all', [64, 8, 16], BF16); k.dma(qs_all[:, :, :], QT[:, :, Tn:Tn + 16])
        knew = k.sb('knew', [64, 6, 16], BF16); k.dma(knew[:, :, :], KT[:, :, Tn:Tn + 16])
        vnew = [k.sb('vnew', [4, 2, 2, 65], BF16) for _ in range(2)]
        gts = [k.sb('gts', [4, 24]) for _ in range(2)]
        q16 = k.sb('q16', [64, 4, 4], BF16)
        ebuf = [k.sb('ebuf', [128, 16], BF16) for _ in range(4)]
        ei = [0]

        def enext():
            b = ebuf[ei[0] % 4]; ei[0] += 1
            return b

        acc = k.sb('acc', [4, 8, 64]); accb = k.sb('accb', [4, 8, 64], BF16); tmp4 = k.sb('tmp4', [4, 4, 64])
        rr = k.sb('rr', [4, 16]); imp = k.sb('imp', [4, NSELS]); score = k.sb('score', [4, NSELS]); sc2 = k.sb('sc2', [4, NSELS])
        m8 = k.sb('m8', [4, 16]); negs = k.sb('negs', [4, NSELS], BF16)
        negT16 = k.sb('negT16', [NJ, 4, 4], BF16)
        SD = L['cfg'].get('sdbg', 99)
        for bl in range(4 if SD >= 99 else 1):
            vn = vnew[bl % 2]; gt_ = gts[bl % 2]
            if SD < 1:
                break
            import os
            if os.environ.get('SKIPVN') != '1':
                k.dma(vn[:, :, :, :], VT[Tn + bl * 4:Tn + bl * 4 + 4, :, :, :])
                k.dma(gt_[:, :], GT[Tn + bl * 4:Tn + bl * 4 + 4, :])
            gv3 = gt_[:, :].rearrange("p (h b) -> p h b", b=3)
            for lp0 in range(min(2, P)):
                gather_page(k, pg[lp0 % 3], cache, idx, bl * P + lp0)
            for lp in range(P):
                pgt = pg[lp % 3]
                if lp + 2 < P:
                    gather_page(k, pg[(lp + 2) % 3], cache, idx, bl * P + lp + 2)
                sg_ = stg[lp % 2]
                k.dve.tensor_copy(out=sg_[:, :], in_=pgt[:, 0:384])
                k.pool.tensor_copy(out=Vss[:, lp, :, 0:64], in_=pgt[:, 384:512].rearrange("p (g d) -> p g d", g=2))
                px0 = pXb[0]
                for i in range(6):
                    k.pe.transpose(out=px0[:, i, :], in_=sg_[:, i * 64:(i + 1) * 64], identity=identb[:, :])
                k.dve.tensor_copy(out=KcTs[:, :, lp * 128:(lp + 1) * 128], in_=px0[:, 0:2, :])
                k.dve.tensor_copy(out=VcTs[:, :, lp * 128:(lp + 1) * 128], in_=px0[:, 2:4, :])
                k.dve.tensor_copy(out=KsTs[:, :, lp * 128:(lp + 1) * 128], in_=px0[:, 4:6, :])
            if SD < 2:
                break
            for wi in range(4):
                w_ = wt[wi % 2]
                k.dma(w_[:, :], st_win[bl, wi * 128:(wi + 1) * 128, :])
                px1 = pX[0]
                for g in range(2):
                    k.pe.transpose(out=px1[:, g, :], in_=w_[:, g * 64:(g + 1) * 64], identity=identf[:, :])
                k.dve.tensor_copy(out=KwTs[:, :, wi * 128:(wi + 1) * 128], in_=px1[:, 0:2, :])
                k.pool.tensor_copy(out=Vws[:, wi, :, 0:64], in_=w_[:, 128:256].rearrange("p (g d) -> p g d", g=2))
            if SD < 3:
                break
            for g in range(2):
                pb = sbank()
                compress(k, W, 0, KcTs[:, g, :], NBS, pb, hx, ta, tb, hk)
                pb2 = sbank()
                k.pe.matmul(out=pb2[0:64, 0:NBS], lhsT=W[0][1][:, :], rhs=hk[:, 0:NBS], start=True, stop=True)
                k.dve.tensor_copy(out=KcCs[:, 0:NBS], in_=pb2[0:64, 0:NBS])
                pb = sbank()
                compress(k, W, 1, VcTs[:, g, :], NBS, pb, hx, ta, tb, hv_)
                for ni in range(NBTS):
                    nn = min(128, NBS - ni * 128)
                    pb3 = sbank()
                    k.pe.matmul(out=pb3[:nn, 0:64], lhsT=hv_[:, ni * 128:ni * 128 + nn], rhs=W[1][1][:, :], start=True, stop=True)
                    k.dve.tensor_copy(out=VcCs[:nn, ni, 0:64], in_=pb3[:nn, 0:64])
                if SD < 4:
                    continue
                k.dve.tensor_copy(out=q16[:, :, :], in_=qs_all[:, g * 4:(g + 1) * 4, bl * 4:(bl + 1) * 4])
                qf = q16[:, :, :].rearrange("p r t -> p (r t)")
                for ni in range(NBTS):
                    nn = min(128, NBS - ni * 128)
                    pss = sbank()
                    k.pe.matmul(out=pss[:nn, 0:16], lhsT=KcCs[:, ni * 128:ni * 128 + nn], rhs=qf, start=True, stop=True)
                    e = enext()
                    k.act.activation(out=e[:nn, :], in_=pss[:nn, 0:16], func=AF.Exp, scale=0.125)
                    for r in range(4):
                        k.pe.matmul(out=po_a[r // 2][:, r % 2, :], lhsT=e[:nn, r * 4:(r + 1) * 4], rhs=VcCs[:nn, ni, :],
                                    start=(ni == 0 and r % 2 == 0), stop=(ni == NBTS - 1), skip_group_check=True)
                for r in range(4):
                    po = po_a[r // 2][:, r % 2, :]
                    k.dve.tensor_scalar(out=rr[:, r:r + 1], in0=po[:, 64:65], scalar1=1e-30, scalar2=None, op0=ALU.max)
                    k.dve.reciprocal(out=rr[:, 4 + r:5 + r], in_=rr[:, r:r + 1])
                    k.dve.tensor_tensor(out=rr[:, 8 + r:9 + r], in0=rr[:, 4 + r:5 + r], in1=gv3[:, g * 4 + r, 0:1], op=ALU.mult)
                    k.dve.tensor_scalar(out=acc[:, g * 4 + r, :], in0=po[:, 0:64], scalar1=rr[:, 8 + r:9 + r], scalar2=None, op0=ALU.mult)
                    if r == 0:
                        k.dve.tensor_scalar(out=imp[:, :], in0=po[:, 65:65 + NSELS], scalar1=rr[:, 4 + r:5 + r], scalar2=None, op0=ALU.mult)
                    else:
                        k.dve.scalar_tensor_tensor(out=imp[:, :], in0=po[:, 65:65 + NSELS], scalar=rr[:, 4 + r:5 + r], in1=imp[:, :], op0=ALU.mult, op1=ALU.add)
                if SD < 5:
                    continue
                k.dve.tensor_tensor(out=score[:, :], in0=imp[:, :], in1=fbs[:, :], op=ALU.add)
                k.dve.max(out=m8[:, 0:8], in_=score[:, :])
                k.dve.match_replace(out=sc2[:, :], in_to_replace=m8[:, 0:8], in_values=score[:, :], imm_value=-3.0e38)
                k.dve.max(out=m8[:, 8:16], in_=sc2[:, :])
                k.dve.tensor_scalar(out=rr[:, 12:13], in0=m8[:, 15:16], scalar1=-1.0e8, scalar2=None, op0=ALU.max)
                k.dve.tensor_scalar(out=negs[:, :], in0=score[:, :], scalar1=rr[:, 12:13], scalar2=NEG, op0=ALU.is_lt, op1=ALU.mult)
                k.pe.transpose(out=ps_t[0:NJ, 0:4], in_=negs[:, 0:NJ], identity=identb[0:4, 0:4])
                k.dve.tensor_copy(out=negT16[:, :, :], in_=ps_t[0:NJ, 0:4].unsq(1).bc([NJ, 4, 4]))
                nf = negT16[:, :, :].rearrange("p r t -> p (r t)")
                if SD < 6:
                    continue
                def sc_s(kt):
                    pss = sbank()
                    e = enext()
                    if kt < P:
                        k.pe.matmul(out=pss[:, 0:16], lhsT=KsTs[:, g, kt * 128:(kt + 1) * 128], rhs=qf, start=True, stop=False)
                        k.pe.matmul(out=pss[:, 0:16], lhsT=eexp[:, kt * 128:(kt + 1) * 128], rhs=nf, start=False, stop=True)
                        k.act.activation(out=e[:, :], in_=pss[:, 0:16], func=AF.Exp, scale=0.125)
                        return (e, 128, Vss[:, kt, g, :])
                    k.pe.matmul(out=pss[0:4, 0:16], lhsT=knew[:, 2 + g, bl * 4:(bl + 1) * 4], rhs=qf, start=True, stop=True)
                    k.act.activation(out=e[0:4, :], in_=pss[0:4, 0:16], func=AF.Exp, scale=0.125)
                    k.pool.tensor_tensor(out=e[0:4, :], in0=e[0:4, :], in1=smk[0:4, 1, :], op=ALU.mult)
                    return (e, 4, vn[:, 0, g, :])

                def pv_s(kt, tup):
                    e, nk, vv = tup
                    for r in range(4):
                        k.pe.matmul(out=po_b[:, r, :], lhsT=e[:nk, r * 4:(r + 1) * 4], rhs=vv, start=(kt == 0 and r == 0), stop=(kt == P), skip_group_check=True)

                pipe(range(P + 1), sc_s, pv_s)
                k.dve.tensor_scalar(out=rr[:, 0:4], in0=po_b[:, :, 64], scalar1=1e-30, scalar2=None, op0=ALU.max)
                k.dve.reciprocal(out=rr[:, 4:8], in_=rr[:, 0:4])
                k.dve.tensor_tensor(out=rr[:, 8:12], in0=rr[:, 4:8], in1=gv3[:, g * 4:(g + 1) * 4, 1], op=ALU.mult)
                k.dve.tensor_tensor(out=tmp4[:, :, :], in0=po_b[:, :, 0:64], in1=rr[:, 8:12].unsq(2).bc([4, 4, 64]), op=ALU.mult)
                k.dve.tensor_tensor(out=acc[:, g * 4:(g + 1) * 4, :], in0=acc[:, g * 4:(g + 1) * 4, :], in1=tmp4[:, :, :], op=ALU.add)
                if SD < 7:
                    continue
                def sc_w(kt):
                    pss = sbank()
                    e = enext()
                    if kt < 4:
                        k.pe.matmul(out=pss[:, 0:16], lhsT=KwTs[:, g, kt * 128:(kt + 1) * 128], rhs=qf, start=True, stop=True)
                        k.act.activation(out=e[:, :], in_=pss[:, 0:16], func=AF.Exp, scale=0.125)
                        if kt == 0:
                            k.pool.tensor_tensor(out=e[:, :], in0=e[:, :], in1=smk[:, 0, :], op=ALU.mult)
                        return (e, 128, Vws[:, kt, g, :])
                    k.pe.matmul(out=pss[0:4, 0:16], lhsT=knew[:, 4 + g, bl * 4:(bl + 1) * 4], rhs=qf, start=True, stop=True)
                    k.act.activation(out=e[0:4, :], in_=pss[0:4, 0:16], func=AF.Exp, scale=0.125)
                    k.pool.tensor_tensor(out=e[0:4, :], in0=e[0:4, :], in1=smk[0:4, 1, :], op=ALU.mult)
                    return (e, 4, vn[:, 1, g, :])

                def pv_w(kt, tup):
                    e, nk, vv = tup
                    for r in range(4):
                        k.pe.matmul(out=po_b[:, r, :], lhsT=e[:nk, r * 4:(r + 1) * 4], rhs=vv, start=(kt == 0 and r == 0), stop=(kt == 4), skip_group_check=True)

                pipe(range(5), sc_w, pv_w)
                k.dve.tensor_scalar(out=rr[:, 0:4], in0=po_b[:, :, 64], scalar1=1e-30, scalar2=None, op0=ALU.max)
                k.dve.reciprocal(out=rr[:, 4:8], in_=rr[:, 0:4])
                k.dve.tensor_tensor(out=rr[:, 8:12], in0=rr[:, 4:8], in1=gv3[:, g * 4:(g + 1) * 4, 2], op=ALU.mult)
                k.dve.tensor_tensor(out=tmp4[:, :, :], in0=po_b[:, :, 0:64], in1=rr[:, 8:12].unsq(2).bc([4, 4, 64]), op=ALU.mult)
                k.dve.tensor_tensor(out=acc[:, g * 4:(g + 1) * 4, :], in0=acc[:, g * 4:(g + 1) * 4, :], in1=tmp4[:, :, :], op=ALU.add)
            k.dve.tensor_copy(out=accb[:, :, :], in_=acc[:, :, :])
            k.dma(AO[Tn + bl * 4:Tn + bl * 4 + 4, 512:1024], accb[:, :, :].rearrange("p h d -> p (h d)"))


def phase_moba_sample(k, L):
    Tn = L['Tn']; P = L['P']; LP = L['LP']; NBLKS = L['NBLKS']; NBLKSP = L['NBLKSP']
    QT2 = L['QT2']; KT2 = L['KT2']; VT2 = L['VT2']; MO = L['MO']; identb = L['identb']; identf = L['identf']
    cache = L['cache_moba']
    with k.scope():
        idx = page_indices(k, L)
        eexp = k.sb('eexp', [NBLKSP, LP], BF16); k.dma(eexp[:, :], L['EEXP2'][0:NBLKSP, 0:LP])
        smk = k.sb('smk', [128, 2, 16], BF16); k.dma(smk[:, :, :], L['SMK'][:, :, :])
        ones = k.sb('ones', [128, 1]); k.op('dve', lambda e: e.memset(ones[:, :].ap, 1.0), (), (ones,))
        ps_s = [k.ps('ps_s', [128, 512]) for _ in range(2)]
        pXb = [k.ps('pXb', [64, 4, 128], BF16) for _ in range(2)]
        stg = [k.sb('stg', [128, 256], BF16) for _ in range(2)]; stf = [k.sb('stf', [128, 256]) for _ in range(2)]
        pM = k.ps('pM', [64, 4, NBLKSP])
        pGt = k.ps('pGt', [4, 16, NBLKSP])
        po_b = k.ps('po_b', [4, 4, 65])
        ps_t = k.ps('ps_t', [NBLKSP, 16, 4], BF16)
        pg = [k.sb('pg', [128, 512]) for _ in range(3)]
        K2s = k.sb('K2s', [64, 4, LP], BF16); V2s = k.sb('V2s', [128, P, 4, 65], BF16)
        k.op('dve', lambda e: e.memset(V2s[:, :, :, 64:65].ap, 1.0), (), (V2s,))
        meansT = k.sb('meansT', [64, 4, NBLKSP])
        qf32 = k.sb('qf32', [64, 16, 16]); k.dma(qf32[:, :, :], L['QF2'][:, :, :])
        qs_all = k.sb('qs_all', [64, 16, 16], BF16); k.dma(qs_all[:, :, :], QT2[:, :, Tn:Tn + 16])
        knew = k.sb('knew', [64, 4, 16], BF16); k.dma(knew[:, :, :], KT2[:, :, Tn:Tn + 16])
        vnew = [k.sb('vnew', [4, 4, 65], BF16) for _ in range(2)]
        q16 = k.sb('q16', [64, 4, 4], BF16)
        score = k.sb('score', [4, 16, NBLKSP]); m8 = k.sb('m8', [4, 16, 8]); thr = k.sb('thr', [4, 16])
        negs = k.sb('negs', [4, 16, NBLKSP]); negb = k.sb('negb', [4, 16, NBLKSP], BF16)
        negT = k.sb('negT', [NBLKSP, 16, 4], BF16)
        ebuf = [k.sb('ebuf', [128, 16], BF16) for _ in range(4)]
        rr = k.sb('rr', [4, 8]); accb = k.sb('accb', [4, 16, 64], BF16)
        k.op('dve', lambda e: e.memset(score[:, :, :].ap, -2.0e9), (), (score,))
        cnt = [0]
        for bl in range(4):
            vn = vnew[bl % 2]
            k.dma(vn[:, :, :], VT2[Tn + bl * 4:Tn + bl * 4 + 4, :, :])
            for lp0 in range(min(2, P)):
                gather_page(k, pg[lp0 % 3], cache, idx, bl * P + lp0)
            for lp in range(P):
                pgt = pg[lp % 3]
                if lp + 2 < P:
                    gather_page(k, pg[(lp + 2) % 3], cache, idx, bl * P + lp + 2)
                sg_ = stg[lp % 2]; sf_ = stf[lp % 2]
                k.dve.tensor_copy(out=sg_[:, :], in_=pgt[:, 0:256])
                k.dve.tensor_copy(out=sf_[:, :], in_=pgt[:, 0:256])
                k.pool.tensor_copy(out=V2s[:, lp, :, 0:64], in_=pgt[:, 256:512].rearrange("p (g d) -> p g d", g=4))
                px = pXb[lp % 2]
                for g in range(4):
                    k.pe.transpose(out=px[:, g, :], in_=sg_[:, g * 64:(g + 1) * 64], identity=identb[:, :])
                k.dve.tensor_copy(out=K2s[:, :, lp * 128:(lp + 1) * 128], in_=px[:, :, :])
                blk = lp // 2
                for g in range(4):
                    k.pe.matmul(out=pM[:, g, blk:blk + 1], lhsT=sf_[:, g * 64:(g + 1) * 64], rhs=ones[:, 0:1],
                                start=(lp == 0 and g == 0), stop=(lp % 2 == 1), skip_group_check=True)
            k.dve.tensor_scalar(out=meansT[:, :, 0:NBLKS], in0=pM[:, :, 0:NBLKS], scalar1=1.0 / 2048, scalar2=None, op0=ALU.mult)
            for h in range(16):
                k.pe.matmul(out=pGt[:, h, 0:NBLKS], lhsT=qf32[:, h, bl * 4:(bl + 1) * 4], rhs=meansT[:, h // 4, 0:NBLKS], start=(h == 0), stop=(h == 15), skip_group_check=True)
            k.dve.tensor_copy(out=score[:, :, 0:NBLKS], in_=pGt[:, :, 0:NBLKS])
            for h in range(16):
                k.dve.max(out=m8[:, h, :], in_=score[:, h, :])
            k.dve.tensor_scalar(out=thr[:, :], in0=m8[:, :, 2], scalar1=-1.0e8, scalar2=None, op0=ALU.max)
            k.dve.tensor_tensor(out=negs[:, :, :], in0=score[:, :, :], in1=thr[:, :].unsq(2).bc([4, 16, NBLKSP]), op=ALU.is_lt)
            k.dve.tensor_scalar(out=negb[:, :, :], in0=negs[:, :, :], scalar1=NEG, scalar2=None, op0=ALU.mult)
            for h in range(16):
                k.pe.transpose(out=ps_t[:, h, :], in_=negb[:, h, :], identity=identb[0:4, 0:4])
            k.dve.tensor_copy(out=negT[:, :, :], in_=ps_t[:, :, :])
            for g in range(4):
                k.dve.tensor_copy(out=q16[:, :, :], in_=qs_all[:, g * 4:(g + 1) * 4, bl * 4:(bl + 1) * 4])
                qf = q16[:, :, :].rearrange("p r t -> p (r t)")
                nf = negT[:, g * 4:(g + 1) * 4, :].rearrange("p r t -> p (r t)")
                def sc_ms(kt):
                    pss = ps_s[cnt[0] % 2]
                    e = ebuf[cnt[0] % 4]; cnt[0] += 1
                    if kt < P:
                        k.pe.matmul(out=pss[:, 0:16], lhsT=K2s[:, g, kt * 128:(kt + 1) * 128], rhs=qf, start=True, stop=False)
                        k.pe.matmul(out=pss[:, 0:16], lhsT=eexp[:, kt * 128:(kt + 1) * 128], rhs=nf, start=False, stop=True)
                        k.act.activation(out=e[:, :], in_=pss[:, 0:16], func=AF.Exp, scale=0.125)
                        return (e, 128, V2s[:, kt, g, :])
                    k.pe.matmul(out=pss[0:4, 0:16], lhsT=knew[:, g, bl * 4:(bl + 1) * 4], rhs=qf, start=True, stop=True)
                    k.act.activation(out=e[0:4, :], in_=pss[0:4, 0:16], func=AF.Exp, scale=0.125)
                    k.pool.tensor_tensor(out=e[0:4, :], in0=e[0:4, :], in1=smk[0:4, 1, :], op=ALU.mult)
                    return (e, 4, vn[:, g, :])

                def pv_ms(kt, tup):
                    e, nk, vv = tup
                    for r in range(4):
                        k.pe.matmul(out=po_b[:, r, :], lhsT=e[:nk, r * 4:(r + 1) * 4], rhs=vv, start=(kt == 0 and r == 0), stop=(kt == P), skip_group_check=True)

                pipe(range(P + 1), sc_ms, pv_ms)
                k.dve.tensor_scalar(out=rr[:, 0:4], in0=po_b[:, :, 64], scalar1=1e-30, scalar2=None, op0=ALU.max)
                k.dve.reciprocal(out=rr[:, 4:8], in_=rr[:, 0:4])
                k.dve.tensor_tensor(out=accb[:, g * 4:(g + 1) * 4, :], in0=po_b[:, :, 0:64], in1=rr[:, 4:8].unsq(2).bc([4, 4, 64]), op=ALU.mult)
            k.dma(MO[Tn + bl * 4:Tn + bl * 4 + 4, :], accb[:, :, :].rearrange("p h d -> p (h d)"))


def host_consts(cfg):
    Tn, P = cfg['T'], cfg['P']
    TR = Tn + 128
    LP = P * 128
    pos = np.concatenate([np.arange(Tn), np.tile(LP + np.arange(4), 4), np.zeros(112)]).astype(np.float32)
    half = 32
    inv = (10000.0 ** (-np.arange(half, dtype=np.float32) / half)).astype(np.float32)
    ang = pos[:, None] * inv[None, :]
    ropecs = np.concatenate([np.cos(ang), np.sin(ang)], axis=1).astype(np.float32)
    return {
        'ropecs': ropecs,
        'identb': np.eye(128, dtype=np.float32).astype(ml_dtypes.bfloat16),
        'identf': np.eye(128, dtype=np.float32),
        **nsa_consts(Tn, LP),
        'masks64': np.stack([np.triu(np.ones((64, 64), np.float32)), np.triu(np.ones((64, 64), np.float32), 1), np.tril(np.ones((64, 64), np.float32), -1)], axis=1),
    }


def nsa_consts(Tn, LP):
    bf = ml_dtypes.bfloat16
    NBLK = Tn // 256; NBLKP = max(8, NBLK)
    tq = np.arange(Tn)[:, None]; nb = np.arange(NBLKP)[None, :]
    gb0 = np.where(nb < tq // 256, 0.0, -1.0e9).astype(np.float32)
    gb1 = (nb != tq // 256).astype(np.float32)
    GBM = np.stack([gb0, gb1], axis=1)
    Wd_ = max(Tn, LP)
    EE2 = (np.arange(Wd_)[None, :] // 256 == np.arange(64)[:, None]).astype(np.float32).astype(bf)
    NSEL = Tn // 64; NB = Tn // 16 - 1; NBT = (NB + 127) // 128
    t = np.arange(Tn)[:, None]; j = np.arange(NSEL)[None, :]
    cur = t // 64
    forced = ((j == cur) | (j == cur - 1) | (j == 0)).astype(np.float32)
    FB = np.where(j <= cur, 100.0 * forced, -1.0e9).astype(np.float32)
    n = np.arange(NBT * 128)[:, None]
    lo = np.maximum(n * 16, j * 64); hi = np.minimum(n * 16 + 32, (j + 1) * 64)
    ov = (np.clip(hi - lo, 0, None).astype(np.float32) / 32.0)
    ov[NB:] = 0
    OV = ov.reshape(NBT, 128, NSEL).transpose(1, 0, 2).astype(bf)
    nl = np.arange(128)[:, None, None]; idx = np.arange(17)[None, :, None]; tl = np.arange(128)[None, None, :]
    CM = (16 * nl + 31 - 128 * idx <= tl).astype(np.float32).astype(bf)
    s_ = np.arange(128)[:, None]; t_ = np.arange(128)[None, :]
    TRI = np.stack([(s_ <= t_), (s_ > t_)], axis=1).astype(np.float32).astype(bf)
    W = max(Tn, LP)
    EE = (np.arange(W)[None, :] // 64 == np.arange(128)[:, None]).astype(np.float32).astype(bf)
    NSELS = LP // 64 + 1; NBS = LP // 16 - 1; NBTS = (NBS + 127) // 128
    js = np.arange(NSELS)[None, :]
    FBs = np.tile((100.0 * ((js == 0) | (js == NSELS - 2) | (js == NSELS - 1))).astype(np.float32), (4, 1))
    ns_ = np.arange(NBTS * 128)[:, None]
    lo = np.maximum(ns_ * 16, js * 64); hi = np.minimum(ns_ * 16 + 32, (js + 1) * 64)
    ovs = (np.clip(hi - lo, 0, None).astype(np.float32) / 32.0); ovs[NBS:] = 0
    OVs = ovs.reshape(NBTS, 128, NSELS).transpose(1, 0, 2).astype(bf)
    i_ = np.arange(128)[:, None]; tq_ = np.tile(np.arange(4), 4)[None, :]
    SMK = np.stack([(i_ > tq_), (i_ <= tq_)], axis=1).astype(np.float32).astype(bf)
    return {'FBt': FB, 'OVt': OV, 'CMt': CM, 'TRIt': TRI, 'EEXP': EE, 'GBM': GBM, 'EEXP2': EE2,
            'FBs': FBs, 'OVs': OVs, 'SMK': SMK, 'IOTA': np.arange(128, dtype=np.float32).reshape(128, 1)}


def make_in_maps(cfg, inputs, ncores):
    Tn, P = cfg['T'], cfg['P']
    hc = host_consts(cfg)
    B = inputs['x_prompt'].shape[0]
    maps = []
    for c in range(ncores):
        b = c % B
        sl = slice(4 * c, 4 * c + 4)
        m = {
            'xp': np.ascontiguousarray(inputs['x_prompt'][b]),
            'xs': np.ascontiguousarray(inputs['x_sample'][sl]).reshape(16, 1024),
            'st_win': np.ascontiguousarray(inputs['state_win_kv'][0, sl]).reshape(4, 512, 256),
            'st_wkv': np.ascontiguousarray(inputs['state_wkv'][0, sl]),
            'st_shift': np.ascontiguousarray(inputs['state_shift'][0, sl]),
            'ptab': np.ascontiguousarray(inputs['page_table'][sl]).astype(np.int32),
            'norm_mix': inputs['norm_mix'], 'norm_ffn': inputs['norm_ffn'], 'norm_final': inputs['norm_final'].reshape(1, 1024),
            'w_in0': inputs['even_w_in'][0], 'w_out0': inputs['even_w_out'][0],
            'gate_b': inputs['nsa_gate_b'][0].reshape(1, 24),
            'cache_nsa': inputs['cache_nsa_kv'][0].reshape(-1, 512), 'cache_moba': inputs['cache_moba_kv'][0].reshape(-1, 512),
            'w_in1': inputs['odd_w_in'][0], 'w_out1': inputs['odd_w_out'][0],
            'ffn_g0': inputs['ffn_w_gate'][0], 'ffn_g1': inputs['ffn_w_gate'][1], 'ffn_u0': inputs['ffn_w_up'][0], 'ffn_u1': inputs['ffn_w_up'][1],
            'ffn_d0': inputs['ffn_w_down'][0], 'ffn_d1': inputs['ffn_w_down'][1],
            'cmp_w1': inputs['nsa_cmp_w1'][0], 'cmp_pe': inputs['nsa_cmp_pe'][0], 'cmp_w2': inputs['nsa_cmp_w2'][0],
            'rw_mu': inputs['rwkv_mu'][0].reshape(1, RWC),
            'rw_vec': np.stack([inputs['rwkv_w0'][0], inputs['rwkv_a0'][0], inputs['rwkv_k_k'][0], inputs['rwkv_k_a'][0],
                                inputs['rwkv_r_k'][0].reshape(512), inputs['rwkv_ln_g'][0], inputs['rwkv_ln_b'][0]]).astype(np.float32),
            'rw_wup': inputs['rwkv_w_up'][0], 'rw_aup': inputs['rwkv_a_up'][0], 'rw_gup': inputs['rwkv_g_up'][0],
        }
        m.update(hc)
        maps.append(m)
    return maps


CFG_FULL = {'T': 4096, 'P': 64, 'NPHYS': 2560, 'stages': 'ABCSDEFMG'}


def kernel(**inputs):
    cfg = dict(CFG_FULL)
    inputs = {k_: np.asarray(v) for k_, v in inputs.items()}
    nc, kb = build(cfg)
    maps = make_in_maps(cfg, inputs, 8)
    res = run_bass_kernel_spmd(nc, maps, core_ids=list(range(8)))
    R = res.results
    f32 = np.float32
    y_p = np.stack([R[c]['y_p'] for c in range(4)]).astype(f32)
    y_s = np.concatenate([R[c]['y_s'].reshape(4, 4, 1024) for c in range(8)]).astype(f32)
    nsa_p = np.stack([R[c]['o_nsa_p'].reshape(4096, 4, 2, 64) for c in range(4)])[None].astype(f32)
    nsa_s = np.concatenate([R[c]['o_nsa_s'].reshape(4, 4, 4, 2, 64) for c in range(8)])[None].astype(f32)
    moba_p = np.stack([R[c]['o_moba_p'].reshape(4096, 2, 4, 64) for c in range(4)])[None].astype(f32)
    moba_s = np.concatenate([R[c]['o_moba_s'].reshape(4, 4, 2, 4, 64) for c in range(8)])[None].astype(f32)
    win_p = np.stack([R[c]['o_win_p'].reshape(512, 2, 2, 64) for c in range(4)])[None].astype(f32)
    win_s = np.concatenate([R[c]['o_win_s'].reshape(4, 512, 2, 2, 64) for c in range(8)])[None].astype(f32)
    wkv_p = np.stack([R[c]['o_wkv_p'] for c in range(4)])[None].astype(f32)
    wkv_s = np.concatenate([R[c]['o_wkv_s'] for c in range(8)])[None].astype(f32)
    sh_p = np.concatenate([R[c]['o_sh_p'] for c in range(4)])[None].astype(f32)
    sh_s = np.concatenate([R[c]['o_sh_s'] for c in range(8)])[None].astype(f32)
    return (y_p, y_s, nsa_p, nsa_s, moba_p, moba_s, win_p, win_s, wkv_p, wkv_s, sh_p, sh_s)
```
